# Optimizing a Trainium2 kernel written in Bass

```python
import jax
import jax.numpy as jnp
from jax import lax
import numpy as np

D_MODEL = 2048
BATCH = 32
SEQ = 256
DEPTH = 2
DEC_BATCH = 8
DEC_SEQ = 2048
PAST_LEN = 512

GRID_W = 64
N_EVEN = (DEPTH + 1) // 2
N_ODD = DEPTH // 2
HEAD_DIM = 128
A_WIDTH = D_MODEL // 2
B_WIDTH = D_MODEL - A_WIDTH
A_HEADS = A_WIDTH // HEAD_DIM
B_HEADS = B_WIDTH // HEAD_DIM
EVEN_SPLITS = (A_WIDTH,) * 5 + (B_WIDTH,) * 3
EVEN_IN = sum(EVEN_SPLITS)
HGRN_CHUNK = 32
NA_ROWS = 8
NA_COLS = 16
MLA_HEADS = D_MODEL // 128
QK_NOPE = 128
QK_ROPE = 64
V_DIM = 128
Q_RANK = D_MODEL // 4
KV_RANK = D_MODEL // 4
ODD_SPLITS = (Q_RANK, KV_RANK, QK_ROPE)
ODD_IN = sum(ODD_SPLITS)
MLA_SCALE = (QK_NOPE + QK_ROPE) ** -0.5
Q_BLOCK = 128
D_FF = 4 * D_MODEL
ROPE_BASE = 10000.0
EPS = 1e-6
NEG = -1e30
F32 = jnp.float32

kernel_name = 'hybrid_dit_hgrn2_natten_mla_step'


def _rmsnorm(x, g):
    xf = x.astype(F32)
    y = xf * lax.rsqrt(jnp.mean(xf * xf, axis=-1, keepdims=True) + EPS)
    return (y * g.astype(F32)).astype(x.dtype)


def _split(p, sizes):
    out, o = [], 0
    for s in sizes:
        out.append(p[..., o:o + s])
        o += s
    return out


def _heads(a, n):
    return a.reshape(a.shape[0], a.shape[1], n, -1)


def _modulation(cvec, w, b):
    m = jax.nn.silu(cvec) @ w + b
    return m.reshape(m.shape[0], 1, 6, D_MODEL)


def _pre(x, g, mod, k):
    return _rmsnorm(x, g) * (1 + mod[:, :, k + 1]) + mod[:, :, k]


def _post(x, y, g, mod, k):
    return x + mod[:, :, k + 2] * _rmsnorm(y, g)


def _mlp(h, w1, w2):
    a = jax.nn.relu(h @ w1)
    return (a * a) @ w2


def _hgrn_scan(q, k, v, logf, s0):
    B, T, H, _ = q.shape
    n = T // HGRN_CHUNK

    def chunks(a):
        return a.reshape(B, n, HGRN_CHUNK, H, a.shape[-1]).transpose(1, 0, 3, 2, 4)

    tril = jnp.tril(jnp.ones((HGRN_CHUNK, HGRN_CHUNK), bool))[:, :, None]

    def step(S, inp):
        qc, kc, vc, gc = inp
        b = jnp.cumsum(gc, axis=2)
        dec = jnp.where(tril, jnp.exp(jnp.minimum(b[:, :, :, None, :] - b[:, :, None, :, :], 0.0)), 0.0)
        a = jnp.einsum('bhtk,bhsk,bhtsk->bhts', qc, kc, dec)
        o = jnp.einsum('bhtk,bhkv->bhtv', qc * jnp.exp(b), S) + jnp.einsum('bhts,bhsv->bhtv', a, vc)
        b_last = b[:, :, -1, :]
        S_new = jnp.exp(b_last)[..., None] * S + jnp.einsum(
            'bhsk,bhsv->bhkv', kc * jnp.exp(b_last[:, :, None, :] - b), vc)
        return S_new, o

    S, o = lax.scan(step, s0.astype(F32), (chunks(q), chunks(k), chunks(v), chunks(logf)))
    return o.transpose(1, 0, 3, 2, 4).reshape(B, T, H, v.shape[-1]), S


def _hgrn_gate(f_raw, lb):
    f = lb + (1.0 - lb) * jax.nn.sigmoid(f_raw.astype(F32))
    return _heads(1.0 - f, A_HEADS), _heads(jnp.log(f), A_HEADS)


def _hgrn_mixer(q_a, f_fw, f_bw, i_a, g_a, lb_fw, lb_bw, norm_g, s0_fw, s0_bw):
    B, T, _ = q_a.shape
    flip = lambda a: jnp.flip(a, axis=1)
    q = _heads(jax.nn.silu(q_a.astype(F32)) * HEAD_DIM ** -0.5, A_HEADS)
    v = _heads(i_a.astype(F32), A_HEADS)
    k_fw, lf_fw = _hgrn_gate(f_fw, lb_fw)
    k_bw, lf_bw = _hgrn_gate(f_bw, lb_bw)
    o_fw, s_fw = _hgrn_scan(q, k_fw, v, lf_fw, s0_fw)
    o_bw, s_bw = _hgrn_scan(flip(q), flip(k_bw), flip(v), flip(lf_bw), s0_bw)
    o = _rmsnorm(o_fw + flip(o_bw), norm_g) * jax.nn.silu(_heads(g_a.astype(F32), A_HEADS))
    return o.reshape(B, T, A_WIDTH).astype(q_a.dtype), s_fw, s_bw


def _dense_attn(q, k, v):
    s = jnp.einsum('bqhd,bkhd->bhqk', q, k, preferred_element_type=F32) * q.shape[-1] ** -0.5
    p = jax.nn.softmax(s, axis=-1).astype(v.dtype)
    o = jnp.einsum('bhqk,bkhd->bqhd', p, v)
    return o.reshape(o.shape[0], o.shape[1], -1)


def _neighbourhood_attn(q, k, v, k_ctx, v_ctx, rpb):
    B, S, H, Dh = q.shape
    rows = S // GRID_W
    kr = min(NA_ROWS, rows)
    r = jnp.arange(rows)
    row_idx = jnp.clip(r - kr // 2, 0, rows - kr)[:, None] + jnp.arange(kr)[None, :]
    col = jnp.arange(GRID_W)
    cs = jnp.clip(col - NA_COLS // 2, 0, GRID_W - NA_COLS)
    col_ok = (col[None, :] >= cs[:, None]) & (col[None, :] < cs[:, None] + NA_COLS)
    ri = (row_idx - r[:, None] + NA_ROWS - 1)[:, None, :, None]
    ci = jnp.clip(col[None, :] - col[:, None] + NA_COLS - 1, 0, 2 * NA_COLS - 2)[None, :, None, :]
    bias = jnp.where(col_ok[:, None, :], rpb.astype(F32)[:, ri, ci], NEG)
    qg = q.reshape(B, rows, GRID_W, H, Dh)
    k_band = k.reshape(B, rows, GRID_W, H, Dh)[:, row_idx]
    v_band = v.reshape(B, rows, GRID_W, H, Dh)[:, row_idx]
    scale = Dh ** -0.5
    s_lat = jnp.einsum('brqhd,brikhd->bhrqik', qg, k_band, preferred_element_type=F32) * scale + bias[None]
    s_ctx = jnp.einsum('brqhd,blhd->bhrql', qg, k_ctx, preferred_element_type=F32) * scale
    n_lat = kr * GRID_W
    p = jax.nn.softmax(jnp.concatenate([s_lat.reshape(B, H, rows, GRID_W, n_lat), s_ctx], axis=-1),
                       axis=-1).astype(v.dtype)
    o = (jnp.einsum('bhrqik,brikhd->brqhd', p[..., :n_lat].reshape(B, H, rows, GRID_W, kr, GRID_W), v_band)
         + jnp.einsum('bhrql,blhd->brqhd', p[..., n_lat:], v_ctx))
    return o.reshape(B, S, H * Dh)


def _axial_angles(T):
    t = jnp.arange(T)
    half = QK_ROPE // 2
    inv = jnp.power(ROPE_BASE, -jnp.arange(0, half, 2, dtype=F32) / half)
    ang_r = (t // GRID_W).astype(F32)[:, None] * inv
    ang_c = (t % GRID_W).astype(F32)[:, None] * inv
    return ang_r, ang_c


def _rotate(x, ang):
    m = ang.shape[-1]
    cos, sin = jnp.cos(ang)[:, None, :], jnp.sin(ang)[:, None, :]
    x1, x2 = x[..., :m], x[..., m:]
    return jnp.concatenate([x1 * cos - x2 * sin, x2 * cos + x1 * sin], axis=-1)


def _rope2d(x, ang_r, ang_c):
    half = QK_ROPE // 2
    xf = x.astype(F32)
    return jnp.concatenate([_rotate(xf[..., :half], ang_r), _rotate(xf[..., half:], ang_c)],
                           axis=-1).astype(x.dtype)


def _even_context(h, lb_fw, lb_bw, w_in, norm_g, w_out):
    q_a, f_fw, f_bw, i_a, g_a, q_b, k_b, v_b = _split(h @ w_in, EVEN_SPLITS)
    s0 = jnp.zeros((h.shape[0], A_HEADS, HEAD_DIM, HEAD_DIM), F32)
    o_a, s_fw, s_bw = _hgrn_mixer(q_a, f_fw, f_bw, i_a, g_a, lb_fw, lb_bw, norm_g, s0, s0)
    k = _heads(k_b, B_HEADS)
    v = _heads(v_b, B_HEADS)
    o_b = _dense_attn(_heads(q_b, B_HEADS), k, v)
    return jnp.concatenate([o_a, o_b], axis=-1) @ w_out, s_fw, s_bw, k, v


def _even_latent(h, s_fw, s_bw, k_ctx, v_ctx, lb_fw, lb_bw, w_in, norm_g, rpb, w_out):
    q_a, f_fw, f_bw, i_a, g_a, q_b, k_b, v_b = _split(h @ w_in, EVEN_SPLITS)
    o_a, _, _ = _hgrn_mixer(q_a, f_fw, f_bw, i_a, g_a, lb_fw, lb_bw, norm_g, s_fw, s_bw)
    o_b = _neighbourhood_attn(_heads(q_b, B_HEADS), _heads(k_b, B_HEADS), _heads(v_b, B_HEADS),
                              k_ctx, v_ctx, rpb)
    return jnp.concatenate([o_a, o_b], axis=-1) @ w_out


def _mla_project(h, w_in, q_norm_g, w_uq, kv_norm_g):
    cq, ckv, kpe = _split(h @ w_in, ODD_SPLITS)
    q = _heads(_rmsnorm(cq, q_norm_g) @ w_uq, MLA_HEADS)
    return q[..., :QK_NOPE], q[..., QK_NOPE:], _rmsnorm(ckv, kv_norm_g), kpe


def _mla_expand(ckv_n, w_ukv):
    kv = _heads(ckv_n @ w_ukv, MLA_HEADS)
    return kv[..., :QK_NOPE], kv[..., QK_NOPE:]


def _mla_context(h, w_in, q_norm_g, w_uq, kv_norm_g, w_ukv, w_out):
    B, T, _ = h.shape
    q_nope, q_pe, ckv_n, kpe = _mla_project(h, w_in, q_norm_g, w_uq, kv_norm_g)
    k_nope, v = _mla_expand(ckv_n, w_ukv)
    s = (jnp.einsum('bqhd,bkhd->bhqk', q_nope, k_nope, preferred_element_type=F32)
         + jnp.einsum('bqhr,bkr->bhqk', q_pe, kpe, preferred_element_type=F32)) * MLA_SCALE
    p = jax.nn.softmax(s, axis=-1).astype(v.dtype)
    o = jnp.einsum('bhqk,bkhd->bqhd', p, v)
    return o.reshape(B, T, MLA_HEADS * V_DIM) @ w_out, ckv_n, kpe


def _mla_latent(h, ckv_ctx, kpe_ctx, w_in, q_norm_g, w_uq, kv_norm_g, w_ukv, w_out):
    B, T, _ = h.shape
    q_nope, q_pe, ckv_n, kpe = _mla_project(h, w_in, q_norm_g, w_uq, kv_norm_g)
    ang_r, ang_c = _axial_angles(T)
    q_rot = _rope2d(q_pe, ang_r, ang_c)
    k_rot = _rope2d(kpe[:, :, None, :], ang_r, ang_c)[:, :, 0]
    k_nope, v = _mla_expand(ckv_n, w_ukv)
    k_nope_c, v_c = _mla_expand(ckv_ctx, w_ukv)
    nb = T // Q_BLOCK

    def blocks(a):
        return a.reshape(B, nb, Q_BLOCK, *a.shape[2:]).swapaxes(0, 1)

    def attend(blk):
        qn, qr, qp = blk
        s_lat = (jnp.einsum('bqhd,bkhd->bhqk', qn, k_nope, preferred_element_type=F32)
                 + jnp.einsum('bqhr,bkr->bhqk', qr, k_rot, preferred_element_type=F32))
        s_ctx = (jnp.einsum('bqhd,bkhd->bhqk', qn, k_nope_c, preferred_element_type=F32)
                 + jnp.einsum('bqhr,bkr->bhqk', qp, kpe_ctx, preferred_element_type=F32))
        p = jax.nn.softmax(jnp.concatenate([s_lat, s_ctx], axis=-1) * MLA_SCALE, axis=-1).astype(v.dtype)
        return (jnp.einsum('bhqk,bkhd->bqhd', p[..., :T], v)
                + jnp.einsum('bhqk,bkhd->bqhd', p[..., T:], v_c))

    o = lax.map(attend, (blocks(q_nope), blocks(q_rot), blocks(q_pe)))
    return o.swapaxes(0, 1).reshape(B, T, MLA_HEADS * V_DIM) @ w_out


def setup_inputs(seed: int = 0) -> dict:
    key = jax.random.key(seed)
    ks = iter(jax.random.split(key, 32))

    def nrm(shape, scale):
        return jax.random.normal(next(ks), shape, F32) * scale

    def gain(shape):
        return 1.0 + nrm(shape, 0.02)

    return {
        'x_prompt': nrm((BATCH, SEQ, D_MODEL), 1.0),
        'x_sample': nrm((DEC_BATCH, DEC_SEQ, D_MODEL), 1.0),
        'state_hgrn_fwd': nrm((DEC_BATCH, N_EVEN, A_HEADS, HEAD_DIM, HEAD_DIM), 0.5),
        'state_hgrn_bwd': nrm((DEC_BATCH, N_EVEN, A_HEADS, HEAD_DIM, HEAD_DIM), 0.5),
        'cache_na_k': nrm((DEC_BATCH, N_EVEN, PAST_LEN, B_HEADS, HEAD_DIM), 1.0),
        'cache_na_v': nrm((DEC_BATCH, N_EVEN, PAST_LEN, B_HEADS, HEAD_DIM), 1.0),
        'cache_mla_ckv': nrm((DEC_BATCH, N_ODD, PAST_LEN, KV_RANK), 1.0),
        'cache_mla_kpe': nrm((DEC_BATCH, N_ODD, PAST_LEN, QK_ROPE), 1.0),
        'c': nrm((DEC_BATCH, D_MODEL), 1.0),
        'c_ctx': nrm((D_MODEL,), 1.0),
        'ada_w': nrm((DEPTH, D_MODEL, 6 * D_MODEL), 0.5 * D_MODEL ** -0.5),
        'ada_b': nrm((DEPTH, 6 * D_MODEL), 0.02),
        'norm_g': gain((DEPTH, 4, D_MODEL)),
        'hgrn_lb_fwd': nrm((DEPTH + 1, A_WIDTH), 0.5),
        'hgrn_lb_bwd': nrm((DEPTH + 1, A_WIDTH), 0.5),
        'w_in_even': nrm((N_EVEN, D_MODEL, EVEN_IN), D_MODEL ** -0.5),
        'hgrn_norm_g': gain((N_EVEN, HEAD_DIM)),
        'na_rpb': nrm((N_EVEN, B_HEADS, 2 * NA_ROWS - 1, 2 * NA_COLS - 1), 0.1),
        'w_out_even': nrm((N_EVEN, D_MODEL, D_MODEL), D_MODEL ** -0.5),
        'w_in_odd': nrm((N_ODD, D_MODEL, ODD_IN), D_MODEL ** -0.5),
        'mla_q_norm_g': gain((N_ODD, Q_RANK)),
        'w_uq': nrm((N_ODD, Q_RANK, MLA_HEADS * (QK_NOPE + QK_ROPE)), Q_RANK ** -0.5),
        'mla_kv_norm_g': gain((N_ODD, KV_RANK)),
        'w_ukv': nrm((N_ODD, KV_RANK, MLA_HEADS * (QK_NOPE + V_DIM)), KV_RANK ** -0.5),
        'w_out_odd': nrm((N_ODD, MLA_HEADS * V_DIM, D_MODEL), (MLA_HEADS * V_DIM) ** -0.5),
        'mlp_w1': nrm((DEPTH, D_MODEL, D_FF), D_MODEL ** -0.5),
        'mlp_w2': nrm((DEPTH, D_FF, D_MODEL), D_FF ** -0.5),
    }


def reference(x_prompt, x_sample, state_hgrn_fwd, state_hgrn_bwd, cache_na_k, cache_na_v,
              cache_mla_ckv, cache_mla_kpe, c, c_ctx, ada_w, ada_b, norm_g, hgrn_lb_fwd,
              hgrn_lb_bwd, w_in_even, hgrn_norm_g, na_rpb, w_out_even, w_in_odd, mla_q_norm_g,
              w_uq, mla_kv_norm_g, w_ukv, w_out_odd, mlp_w1, mlp_w2):
    lb_fw = jnp.cumsum(jax.nn.softmax(hgrn_lb_fwd.astype(F32), axis=0), axis=0)
    lb_bw = jnp.cumsum(jax.nn.softmax(hgrn_lb_bwd.astype(F32), axis=0), axis=0)
    xc, xs = x_prompt, x_sample
    new_sf, new_sb, new_nk, new_nv, new_ckv, new_kpe = [], [], [], [], [], []
    for l in range(DEPTH):
        j = l // 2
        mod_c = _modulation(c_ctx[None, :], ada_w[l], ada_b[l])
        mod_s = _modulation(c, ada_w[l], ada_b[l])
        hc = _pre(xc, norm_g[l, 0], mod_c, 0)
        hs = _pre(xs, norm_g[l, 0], mod_s, 0)
        if l % 2 == 0:
            yc, sf, sb, nk, nv = _even_context(hc, lb_fw[l], lb_bw[l], w_in_even[j], hgrn_norm_g[j],
                                               w_out_even[j])
            ys = _even_latent(hs, state_hgrn_fwd[:, j], state_hgrn_bwd[:, j], cache_na_k[:, j],
                              cache_na_v[:, j], lb_fw[l], lb_bw[l], w_in_even[j], hgrn_norm_g[j],
                              na_rpb[j], w_out_even[j])
            new_sf.append(sf)
            new_sb.append(sb)
            new_nk.append(nk)
            new_nv.append(nv)
        else:
            yc, ckv, kpe = _mla_context(hc, w_in_odd[j], mla_q_norm_g[j], w_uq[j], mla_kv_norm_g[j],
                                        w_ukv[j], w_out_odd[j])
            ys = _mla_latent(hs, cache_mla_ckv[:, j], cache_mla_kpe[:, j], w_in_odd[j], mla_q_norm_g[j],
                             w_uq[j], mla_kv_norm_g[j], w_ukv[j], w_out_odd[j])
            new_ckv.append(ckv)
            new_kpe.append(kpe)
        xc = _post(xc, yc, norm_g[l, 1], mod_c, 0)
        xs = _post(xs, ys, norm_g[l, 1], mod_s, 0)
        xc = _post(xc, _mlp(_pre(xc, norm_g[l, 2], mod_c, 3), mlp_w1[l], mlp_w2[l]), norm_g[l, 3], mod_c, 3)
        xs = _post(xs, _mlp(_pre(xs, norm_g[l, 2], mod_s, 3), mlp_w1[l], mlp_w2[l]), norm_g[l, 3], mod_s, 3)
    return (xc, xs, jnp.stack(new_sf, axis=1), jnp.stack(new_sb, axis=1), jnp.stack(new_nk, axis=1),
            jnp.stack(new_nv, axis=1), jnp.stack(new_ckv, axis=1), jnp.stack(new_kpe, axis=1))
```

```python
import numpy as np
import concourse.bass as bass
import concourse.mybir as mybir
from concourse.bass_utils import run_bass_kernel_spmd

F32 = mybir.dt.float32
BF16 = mybir.dt.bfloat16
I32 = mybir.dt.int32
AF = mybir.ActivationFunctionType
ALU = mybir.AluOpType
AX = mybir.AxisListType

COMPUTE = ('pe', 'act', 'dve', 'pool')
ALLENG = ('pe', 'act', 'dve', 'pool', 'sp')


class DmaSem:
    __slots__ = ('name', 'count', 'handle')

    def __init__(self, name):
        self.name = name
        self.count = 0
        self.handle = None


class Res:
    __slots__ = ('name', 'w_ops', 'w_dma', 'r_ops', 'r_dma', 'had_read', 'sem')

    def __init__(self, name, sem=None):
        self.name = name
        self.w_ops = {}
        self.w_dma = {}
        self.r_ops = {}
        self.r_dma = {}
        self.had_read = False
        self.sem = sem


class Op:
    __slots__ = ('eng', 'fn', 'dep_ops', 'dep_dma', 'idx', 'signal', 'dma_sem')

    def __init__(self, eng, fn):
        self.eng = eng
        self.fn = fn
        self.dep_ops = {}
        self.dep_dma = {}
        self.idx = -1
        self.signal = False
        self.dma_sem = None


class K:
    def __init__(self, nc):
        self.nc = nc
        self.ops = {e: [] for e in ALLENG}
        self.known_ops = {e: {f: -1 for f in ALLENG} for e in ALLENG}
        self.known_dma = {e: {} for e in ALLENG}
        self.dma_sems = []
        self.out_sems = set()
        self.nres = 0

    def res(self, name=None):
        self.nres += 1
        return Res(name or f"r{self.nres}")

    def dsem(self, name):
        s = DmaSem(name)
        self.dma_sems.append(s)
        return s

    def _collect(self, eng, reads, writes):
        raw_ops, raw_dma, war_ops, war_dma = {}, {}, {}, {}
        for r in reads:
            for e, i in r.w_ops.items():
                if raw_ops.get(e, -1) < i:
                    raw_ops[e] = i
            for s, v in r.w_dma.items():
                if raw_dma.get(s, 0) < v:
                    raw_dma[s] = v
        for w in writes:
            for e, i in w.r_ops.items():
                if war_ops.get(e, -1) < i:
                    war_ops[e] = i
            for s, v in w.r_dma.items():
                if war_dma.get(s, 0) < v:
                    war_dma[s] = v
        dep_ops = {}
        for e, i in raw_ops.items():
            if e == eng and eng in ('pe', 'sp'):
                continue
            dep_ops[e] = i
        for e, i in war_ops.items():
            if e == eng:
                continue
            if dep_ops.get(e, -1) < i:
                dep_ops[e] = i
        dep_dma = dict(raw_dma)
        for s, v in war_dma.items():
            if dep_dma.get(s, 0) < v:
                dep_dma[s] = v
        ko = self.known_ops[eng]
        kd = self.known_dma[eng]
        dep_ops = {e: i for e, i in dep_ops.items() if ko[e] < i}
        dep_dma = {s: v for s, v in dep_dma.items() if kd.get(s, 0) < v}
        for e, i in dep_ops.items():
            ko[e] = i
        for s, v in dep_dma.items():
            kd[s] = v
        return dep_ops, dep_dma

    def _register(self, op, reads, writes, dma_evt=None):
        eng, idx = op.eng, op.idx
        for r in reads:
            r.had_read = True
            if dma_evt is None:
                r.r_ops[eng] = idx
            else:
                r.r_dma[dma_evt[0]] = dma_evt[1]
        for w in writes:
            if w.had_read:
                w.w_ops = {}
                w.w_dma = {}
                w.r_ops = {}
                w.r_dma = {}
                w.had_read = False
            if dma_evt is None:
                w.w_ops[eng] = idx
            else:
                w.w_dma[dma_evt[0]] = dma_evt[1]

    def op(self, eng, fn, reads=(), writes=()):
        o = Op(eng, fn)
        o.dep_ops, o.dep_dma = self._collect(eng, reads, writes)
        o.idx = len(self.ops[eng])
        self.ops[eng].append(o)
        self._register(o, reads, writes)
        return o

    def dma(self, q, out, in_, reads=(), writes=(), sem=None, is_output=False, **kw):
        if sem is None:
            for r in list(writes) + list(reads):
                if r.sem is not None:
                    sem = r.sem
                    break
        assert sem is not None, "dma needs a semaphore-bearing resource"
        o = Op(q, lambda e, out=out, in_=in_, kw=kw: e.dma_start(out=out, in_=in_, **kw))
        o.dep_ops, o.dep_dma = self._collect(q, reads, writes)
        o.idx = len(self.ops[q])
        self.ops[q].append(o)
        sem.count += 16
        o.dma_sem = sem
        self._register(o, reads, writes, dma_evt=(sem, sem.count))
        if is_output:
            self.out_sems.add(sem)
        return o

    def barrier(self):
        o = Op('sp', lambda e: e.nop())
        for e in COMPUTE:
            n = len(self.ops[e])
            if n > 0 and self.known_ops['sp'][e] < n - 1:
                last = n - 1
                while last >= 0 and (self.ops[e][last].fn is None or self.ops[e][last].dma_sem is not None):
                    last -= 1
                if last >= 0 and self.known_ops['sp'][e] < last:
                    o.dep_ops[e] = last
        for s in self.dma_sems:
            if self.known_dma['sp'].get(s, 0) < s.count:
                o.dep_dma[s] = s.count
        o.idx = len(self.ops['sp'])
        self.ops['sp'].append(o)
        for e in COMPUTE:
            w = Op(e, None)
            w.dep_ops = {'sp': o.idx}
            w.idx = len(self.ops[e])
            self.ops[e].append(w)
        for e in ALLENG:
            for f in ALLENG:
                self.known_ops[e][f] = len(self.ops[f]) - 1
            for s in self.dma_sems:
                self.known_dma[e][s] = s.count
            self.known_ops[e]['sp'] = o.idx

    def emit(self):
        nc = self.nc
        self.barrier()
        sig = {e: set() for e in ALLENG}
        for e in ALLENG:
            for o in self.ops[e]:
                for f, i in o.dep_ops.items():
                    sig[f].add(i)
        val = {}
        for e in ALLENG:
            val[e] = {i: k + 1 for k, i in enumerate(sorted(sig[e]))}
        self._cm = nc.cleanup_on_exit()
        self._cm.__enter__()
        esem = {e: nc.alloc_semaphore(f"eng_{e}") for e in ALLENG}
        for s in self.dma_sems:
            if s.count > 0:
                s.handle = nc.alloc_semaphore(f"d_{s.name}")
        engobj = {'pe': 'tensor', 'act': 'scalar', 'dve': 'vector', 'pool': 'gpsimd', 'sp': 'sync'}
        stats = {}

        def run(e):
            def body(eng):
                nw = 0
                for o in self.ops[e]:
                    for f, i in o.dep_ops.items():
                        eng.wait_ge(esem[f], val[f][i])
                        nw += 1
                    for s, v in o.dep_dma.items():
                        eng.wait_ge(s.handle, v)
                        nw += 1
                    if o.fn is None:
                        continue
                    ins = o.fn(eng)
                    if o.dma_sem is not None:
                        ins.then_inc(o.dma_sem.handle, 16)
                    elif o.idx in val[e]:
                        ins.then_inc(esem[e], 1)
                stats[e] = (len(self.ops[e]), nw)
            return body

        with nc.Block() as block:
            for e in ALLENG:
                getattr(block, engobj[e])(run(e))
        nc.all_engine_barrier()
        self._cm.__exit__(None, None, None)
        self.stats = stats
        return stats


import math

D_MODEL = 2048
NT = 3072
NCH = 16
D_FF = 8192
EPS = 1e-6


def I(method, *a, **kw):
    return lambda e: getattr(e, method)(*a, **kw)


class Arena:
    def __init__(self, nc, nbytes):
        self.t = nc.alloc_sbuf_tensor("arena", [128, nbytes // 4], F32)
        self.n = nbytes
        self.top = 0
        self.peak = 0

    def alloc(self, shape, dt=F32, parts=128):
        esz = 4 if dt == F32 else 2
        n = 1
        for s in shape:
            n *= s
        nb = (n * esz + 63) // 64 * 64
        assert self.top + nb <= self.n, f"arena overflow {self.top}+{nb}>{self.n}"
        o4 = self.top // 4
        a = self.t[0:parts, o4:o4 + nb // 4]
        self.top += nb
        self.peak = max(self.peak, self.top)
        if dt != F32:
            a = a.bitcast(dt)
        a = a[:, 0:n]
        if len(shape) == 2:
            a = a.rearrange("p (a b) -> p a b", a=shape[0])
        elif len(shape) == 3:
            a = a.rearrange("p (a b c) -> p a b c", a=shape[0], b=shape[1])
        return a

    def mark(self):
        return self.top

    def release(self, m):
        self.top = m


class Builder:
    def __init__(self, cfg):
        self.cfg = cfg
        nc = self.nc = bass.Bass("TRN2", target_bir_lowering=False)
        self.k = K(nc)
        self.din = {}
        self.dout = {}
        self.ar = Arena(nc, 204800)
        self.ps = nc.alloc_psum_tensor("ps", [128, 8, 512], F32)
        self.psr = [self.k.res(f"psum{b}") for b in range(8)]
        self.rr = {}

    def inp(self, name, shape, dt=F32):
        self.din[name] = self.nc.dram_tensor(name, list(shape), dt, kind="ExternalInput").ap()
        return self.din[name]

    def out(self, name, shape, dt=F32):
        self.dout[name] = self.nc.dram_tensor(name, list(shape), dt, kind="ExternalOutput").ap()
        return self.dout[name]

    def scr(self, name, shape, dt=F32):
        return self.nc.dram_tensor(name, list(shape), dt, kind="Internal").ap()

    def R(self, name, sem=False, sw=False):
        r = self.k.res(name)
        if not hasattr(self, "sem_pool"):
            self.sem_pool = []
            self.sem_i = 0
            self.sw_pool = []
            self.sw_i = 0
        if sw:
            if self.sw_i >= len(self.sw_pool):
                self.sw_pool.append(self.k.dsem(f"w{len(self.sw_pool)}"))
            r.sem = self.sw_pool[self.sw_i]
            self.sw_i += 1
        elif sem:
            if self.sem_i >= len(self.sem_pool):
                self.sem_pool.append(self.k.dsem(f"p{len(self.sem_pool)}"))
            r.sem = self.sem_pool[self.sem_i]
            self.sem_i += 1
        return r

    def phase(self):
        self.sem_i = self.sem_keep
        self.sw_i = 0

    def keep_sems(self):
        self.sem_keep = getattr(self, "sem_i", 0)

    def psb(self, b):
        return self.ps[:, b, :]

    def psb16(self, b):
        return self.ps[:, b, :].bitcast(BF16)

    def consts(self):
        k, ar = self.k, self.ar
        idf = self.inp("ident", [128, 128])
        self.ident_f = ar.alloc([128])
        self.ident_b = ar.alloc([128], BF16)
        self.ones_b = ar.alloc([128], BF16)
        self.r_ident_f = self.R("ident_f", sem=True)
        self.r_ident_b = self.R("ident_b")
        self.r_ones = self.R("ones_b")
        k.dma('sp', self.ident_f, idf, writes=[self.r_ident_f])
        k.op('dve', lambda e: e.tensor_copy(out=self.ident_b, in_=self.ident_f), reads=[self.r_ident_f], writes=[self.r_ident_b])
        k.op('dve', lambda e: e.memset(self.ones_b, 1.0), writes=[self.r_ones])
        self.ones_f = ar.alloc([128]); self.r_ones_f = self.R("ones_f")
        k.op('dve', lambda e: e.memset(self.ones_f, 1.0), writes=[self.r_ones_f])
        self.AB = [[ar.alloc([16, 4]) for s in range(2)] for l in range(2)]
        self.r_AB = [[self.R(f"AB{l}{s}") for s in range(2)] for l in range(2)]
        self.GROW = [[self.scr(f"grow{l}{s}", [2, D_MODEL]) for s in range(2)] for l in range(2)]
        self.r_GROW = [[self.R(f"grow{l}{s}") for s in range(2)] for l in range(2)]
        self.keep_sems()

    def modulation(self, l):
        k, ar, nc = self.k, self.ar, self.nc
        self.phase()
        m0 = ar.mark()
        cvec = self.din["cvec"]
        ada_w = self.din["ada_w"]
        ada_b = self.din["ada_b"]
        norm_g = self.din["norm_g"]
        crow = ar.alloc([D_MODEL], F32, parts=2)
        tmp = ar.alloc([D_MODEL], F32, parts=2)
        scb = ar.alloc([D_MODEL], BF16, parts=2)
        scT = ar.alloc([16, 2], BF16)
        mod = ar.alloc([6, D_MODEL], F32, parts=2)
        gn = ar.alloc([4, D_MODEL], F32, parts=2)
        wb = [ar.alloc([16, 512], BF16) for _ in range(3)]
        r_crow = self.R("crow", sem=True); r_tmp = self.R("tmp"); r_scb = self.R("scb"); r_scT = self.R("scT")
        r_mod = self.R("modrow", sem=True); r_gn = self.R("gn", sem=True)
        r_wb = [self.R(f"modw{i}", sw=True) for i in range(3)]
        k.dma('sp', crow, cvec, writes=[r_crow])
        k.dma('sp', mod, ada_b[l].rearrange("(o s d) -> o s d", o=1, s=6).to_broadcast([2, 6, D_MODEL]), writes=[r_mod])
        k.dma('sp', gn, norm_g[l].rearrange("(o s) d -> o s d", o=1).to_broadcast([2, 4, D_MODEL]), writes=[r_gn])
        if self.cfg.get("mod_stop") == 1:
            k.dma('sp', self.dout["dbg_mod0"], mod, reads=[r_mod], is_output=True); k.barrier(); ar.release(m0); return
        k.op('act', lambda e: e.activation(out=tmp, in_=crow, func=AF.Exp, scale=-1.0), reads=[r_crow], writes=[r_tmp])
        k.op('dve', lambda e: e.tensor_scalar_add(out=tmp, in0=tmp, scalar1=1.0), reads=[r_tmp], writes=[r_tmp])
        k.op('dve', lambda e: e.reciprocal(out=tmp, in_=tmp), reads=[r_tmp], writes=[r_tmp])
        k.op('dve', lambda e: e.tensor_tensor(out=scb, in0=tmp, in1=crow, op=ALU.mult), reads=[r_tmp, r_crow], writes=[r_scb])
        if self.cfg.get("mod_stop") == 2:
            k.dma('sp', self.dout["dbg_mod0"], mod, reads=[r_mod], is_output=True); k.barrier(); ar.release(m0); return
        pst = self.psb16(7)
        for c in range(16):
            k.op('pe', lambda e, c=c: e.transpose(out=pst[:, c * 2:c * 2 + 2], in_=scb[0:2, c * 128:(c + 1) * 128], identity=self.ident_b[0:2, 0:2]),
                 reads=[r_scb, self.r_ident_b], writes=[self.psr[7]])
        k.op('dve', lambda e: e.tensor_copy(out=scT.rearrange("p a b -> p (a b)"), in_=pst[:, 0:32]), reads=[self.psr[7]], writes=[r_scT])
        if self.cfg.get("mod_stop") == 3:
            k.dma('sp', self.dout["dbg_mod0"], mod, reads=[r_mod], is_output=True); k.barrier(); ar.release(m0); return
        for j in range(24):
            b = j % 3
            k.dma('pool', wb[b], ada_w[l, :, j * 512:(j + 1) * 512].rearrange("(k p) n -> p k n", p=128), writes=[r_wb[b]])
            pb = j % 2
            for kk in range(16):
                k.op('pe', lambda e, kk=kk, b=b, pb=pb: e.matmul(self.ps[0:2, pb, :], lhsT=scT[:, kk, :], rhs=wb[b][:, kk, :], start=(kk == 0), stop=(kk == 15)),
                     reads=[r_scT, r_wb[b]], writes=[self.psr[pb]])
            s_, off = divmod(j * 512, D_MODEL)
            k.op('dve', lambda e, pb=pb, s_=s_, off=off: e.tensor_tensor(out=mod[:, s_, off:off + 512], in0=self.ps[0:2, pb, :], in1=mod[:, s_, off:off + 512], op=ALU.add),
                 reads=[self.psr[pb], r_mod], writes=[r_mod])
        if self.cfg.get("mod_stop") == 4:
            k.dma('sp', self.dout["dbg_mod0"], mod, reads=[r_mod], is_output=True); k.barrier(); ar.release(m0); return
        for s in range(2):
            o = 3 * s
            k.op('dve', lambda e, o=o, s=s: e.scalar_tensor_tensor(out=mod[:, o + 1, :], in0=mod[:, o + 1, :], scalar=1.0, in1=gn[:, 2 * s, :], op0=ALU.add, op1=ALU.mult),
                 reads=[r_mod, r_gn], writes=[r_mod])
            k.op('dve', lambda e, o=o, s=s: e.tensor_tensor(out=mod[:, o + 2, :], in0=mod[:, o + 2, :], in1=gn[:, 2 * s + 1, :], op=ALU.mult),
                 reads=[r_mod, r_gn], writes=[r_mod])
        if self.cfg.get("mod_stop") == 5:
            k.dma('sp', self.dout["dbg_mod0"], mod, reads=[r_mod], is_output=True); k.barrier(); ar.release(m0); return
        for s in range(2):
            o = 3 * s
            k.dma('sp', self.GROW[l][s], mod[:, o + 2, :], reads=[r_mod], writes=[self.r_GROW[l][s]])
            if self.cfg.get("mod_stop") == 6:
                k.dma('sp', self.dout["dbg_mod0"], mod, reads=[r_mod], is_output=True); k.barrier(); ar.release(m0); return
            pf = self.psb(6)
            for c in range(16):
                for ab in range(2):
                    k.op('pe', lambda e, c=c, ab=ab, o=o: e.transpose(out=pf[:, c * 4 + 2 * ab:c * 4 + 2 * ab + 2], in_=mod[0:2, o + 1 - ab, c * 128:(c + 1) * 128], identity=self.ident_f[0:2, 0:2]),
                         reads=[r_mod, self.r_ident_f], writes=[self.psr[6]])
            if self.cfg.get("mod_stop") == 7:
                k.dma('sp', self.dout["dbg_mod0"], mod, reads=[r_mod], is_output=True); k.barrier(); ar.release(m0); return
            k.op('dve', lambda e, s=s: e.tensor_copy(out=self.AB[l][s].rearrange("p a b -> p (a b)"), in_=pf[:, 0:64]), reads=[self.psr[6]], writes=[self.r_AB[l][s]])
        if self.cfg.get("mod_stop") == 8:
            k.dma('sp', self.dout["dbg_mod0"], mod, reads=[r_mod], is_output=True); k.barrier(); ar.release(m0); return
        if self.cfg.get("dump_mod"):
            k.dma('sp', self.dout[f"dbg_mod{l}"], mod, reads=[r_mod], is_output=True)
        k.barrier()
        ar.release(m0)

    def pre_tile(self, xt, r_xt, hT, r_hT, col0, AB, r_AB, g, W):
        k = self.k
        junk, xn, st = W["junk"], W["xn"], W["st"]
        r_junk, r_xn, r_st = W["r_junk"], W["r_xn"], W["r_st"]
        ps_ = self.cfg.get('pre_stop', 99)
        if ps_ == 0: return
        k.op('dve', lambda e: e.memset(st[:, 0:1], 0.0), writes=[r_st])
        k.op('act', lambda e: e.activation(out=junk, in_=xt, func=AF.Square, accum_out=st[:, 0:1]), reads=[r_xt, r_st], writes=[r_junk, r_st])
        if ps_ == 1: return
        k.op('act', lambda e: e.activation(out=st[:, 1:2], in_=st[:, 0:1], func=AF.Ln, scale=1.0 / D_MODEL, bias=EPS), reads=[r_st], writes=[r_st])
        k.op('act', lambda e: e.activation(out=st[:, 2:3], in_=st[:, 1:2], func=AF.Exp, scale=-0.5), reads=[r_st], writes=[r_st])
        if ps_ == 2: return
        k.op('dve', lambda e: e.tensor_scalar_mul(out=xn, in0=xt, scalar1=st[:, 2:3]), reads=[r_xt, r_st], writes=[r_xn])
        if ps_ == 3: return
        for half in range(2):
            pb = 6 + half
            pst = self.psb16(pb)
            for c8 in range(8):
                c = half * 8 + c8
                k.op('pe', lambda e, c=c, c8=c8, pst=pst: e.transpose(out=pst[:, c8 * 128:(c8 + 1) * 128], in_=xn[:, c * 128:(c + 1) * 128], identity=self.ident_b),
                     reads=[r_xn, self.r_ident_b], writes=[self.psr[pb]])
            if ps_ == 4: return
            for c8 in range(8):
                c = half * 8 + c8
                if ps_ == 5 and c8 == 1: return
                if ps_ == 6 and c8 == 2: return
                src = pst[:, c8 * 128:(c8 + 1) * 128]
                dst = hT[:, c, col0:col0 + 128]
                if True:
                    k.op('act', lambda e, src=src, dst=dst, c=c: e.activation(out=dst, in_=src, func=AF.Identity, scale=AB[:, c, g:g + 1], bias=AB[:, c, 2 + g:3 + g]),
                         reads=[self.psr[pb], r_AB], writes=[r_hT])
                else:
                    k.op('dve', lambda e, src=src, dst=dst, c=c: e.tensor_scalar(out=dst, in0=src, scalar1=AB[:, c, g:g + 1], scalar2=AB[:, c, 2 + g:3 + g], op0=ALU.mult, op1=ALU.add),
                         reads=[self.psr[pb], r_AB], writes=[r_hT])

    def post_tile(self, xt, r_xt, ypieces, Grep, r_G, W):
        k = self.k
        junk, st, t = W["junk32"], W["st2"], W["t"]
        r_junk, r_st, r_t = W["r_junk32"], W["r_st2"], W["r_t"]
        npc = len(ypieces)
        k.op('dve', lambda e: e.memset(st[:, 0:4], 0.0), writes=[r_st])
        off = 0
        for i, (yp, r_y, n) in enumerate(ypieces):
            k.op('act', lambda e, yp=yp, i=i, n=n: e.activation(out=junk[:, 0:n], in_=yp, func=AF.Square, accum_out=st[:, i:i + 1]),
                 reads=[r_y, r_st], writes=[r_junk, r_st])
        k.op('dve', lambda e: e.tensor_reduce(out=st[:, 4:5], in_=st[:, 0:4], axis=AX.X, op=ALU.add), reads=[r_st], writes=[r_st])
        k.op('act', lambda e: e.activation(out=st[:, 5:6], in_=st[:, 4:5], func=AF.Ln, scale=1.0 / D_MODEL, bias=EPS), reads=[r_st], writes=[r_st])
        k.op('act', lambda e: e.activation(out=st[:, 6:7], in_=st[:, 5:6], func=AF.Exp, scale=-0.5), reads=[r_st], writes=[r_st])
        off = 0
        for i, (yp, r_y, n) in enumerate(ypieces):
            k.op('dve', lambda e, yp=yp, off=off, n=n: e.scalar_tensor_tensor(out=t[:, off:off + n], in0=yp, scalar=st[:, 6:7], in1=Grep[:, off:off + n], op0=ALU.mult, op1=ALU.mult),
                 reads=[r_y, r_st, r_G], writes=[r_t])
            off += n
        k.op('dve', lambda e: e.tensor_tensor(out=xt, in0=xt, in1=t, op=ALU.add), reads=[r_xt, r_t], writes=[r_xt])

    def load_grep(self, l, s):
        k, ar = self.k, self.ar
        G = [ar.alloc([D_MODEL]) for g in range(2)]
        r_G = [self.R(f"Grep{g}", sem=True) for g in range(2)]
        for g in range(2):
            k.dma('sp', G[g], self.GROW[l][s][g:g + 1, :].to_broadcast([128, D_MODEL]), reads=[self.r_GROW[l][s]], writes=[r_G[g]])
        return G, r_G

    def work_bufs(self):
        ar = self.ar
        W = {}
        W["junk"] = ar.alloc([D_MODEL], BF16); W["r_junk"] = self.R("junk")
        W["xn"] = ar.alloc([D_MODEL], BF16); W["r_xn"] = self.R("xn")
        W["st"] = ar.alloc([8]); W["r_st"] = self.R("st")
        W["st2"] = ar.alloc([8]); W["r_st2"] = self.R("st2")
        W["junk32"] = W["junk"]; W["r_junk32"] = W["r_junk"]
        W["t"] = ar.alloc([D_MODEL]); W["r_t"] = self.R("t")
        return W

    def mlp_phase(self, l, Xin, r_Xin, Xout, r_Xout, out_is_output=False):
        k, ar = self.k, self.ar
        self.phase()
        m0 = ar.mark()
        w1 = self.din["mlp_w1"][l]
        w2 = self.din["mlp_w2"][l]
        if not hasattr(self, "W1C"):
            self.W1C = self.scr("W1C", [16, 128, 16 * 512], BF16)
            self.W2C = self.scr("W2C", [32, 128, 4 * 1024], BF16)
        r_W1C = [self.R(f"w1c{j}") for j in range(16)]
        r_W2C = [self.R(f"w2c{j}") for j in range(32)]
        G, r_G = self.load_grep(l, 1)
        W = self.work_bufs()
        xt = [ar.alloc([D_MODEL]) for _ in range(2)]
        r_xt = [self.R(f"xt{i}", sem=True) for i in range(2)]
        hTs = [ar.alloc([16, 512], BF16) for _ in range(2)]; r_hTs = [self.R(f"hT{i}") for i in range(2)]
        ysbs = [h_.rearrange("p a b -> p (a b)").bitcast(F32).rearrange("p (s n) -> p s n", s=4) for h_ in hTs]
        aT = ar.alloc([64, 512], BF16); r_aT = self.R("aT")
        w1b = [ar.alloc([16, 512], BF16) for _ in range(2)]
        r_w1b = [self.R(f"w1b{i}", sw=True) for i in range(2)]
        w2b = [ar.alloc([4, 1024], BF16) for _ in range(2)]
        r_w2b = [self.R(f"w2b{i}", sw=True) for i in range(2)]
        w1s = [self.R(f"w1s{i}", sem=True).sem for i in range(2)]
        w2s = [self.R(f"w2s{i}", sem=True).sem for i in range(2)]
        rt = [ar.alloc([512]) for _ in range(2)]
        r_rt = [self.R(f"rt{i}") for i in range(2)]
        AB, r_AB = self.AB[l][1], self.r_AB[l][1]
        xi = 0
        w1i = 0
        w2i = 0
        NBLK = self.cfg.get('nblk', 6)
        xi_ = [0]

        def pre_one(blk_, sub_):
            g_ = 0 if blk_ < 2 else 1
            tile_ = blk_ * 4 + sub_
            b_ = xi_[0] % 2; xi_[0] += 1
            k.dma('sp', xt[b_], Xin[tile_ * 128:(tile_ + 1) * 128, :], reads=[r_Xin[tile_]], writes=[r_xt[b_]])
            self.pre_tile(xt[b_], r_xt[b_], hTs[blk_ % 2], r_hTs[blk_ % 2], sub_ * 128, AB, r_AB, g_, W)

        for sub in range(4):
            pre_one(0, sub)
        pending = []
        for blk in range(NBLK):
            g = 0 if blk < 2 else 1
            hT, r_hT, ysb = hTs[blk % 2], r_hTs[blk % 2], ysbs[blk % 2]
            if self.cfg.get("mlp_stop") == 1:
                k.barrier(); ar.release(m0); return
            for j in range(16):
                wb_i = w1i % 2; w1i += 1
                if blk == 0:
                    k.dma('pool', w1b[wb_i], w1[:, j * 512:(j + 1) * 512].rearrange("(k p) n -> p k n", p=128), writes=[r_w1b[wb_i]])
                    k.dma('sp', self.W1C[j], w1b[wb_i].rearrange("p a b -> p (a b)"), reads=[r_w1b[wb_i]], writes=[r_W1C[j]], sem=w1s[wb_i])
                else:
                    k.dma('pool', w1b[wb_i].rearrange("p a b -> p (a b)"), self.W1C[j], reads=[r_W1C[j]], writes=[r_w1b[wb_i]])
                for cc in range(4):
                    c = j * 4 + cc
                    pb = c % 2
                    for kk in range(16):
                        k.op('pe', lambda e, kk=kk, cc=cc, wb_i=wb_i, pb=pb, hT=hT: e.matmul(self.psb(pb), lhsT=w1b[wb_i][:, kk, cc * 128:(cc + 1) * 128], rhs=hT[:, kk, :], start=(kk == 0), stop=(kk == 15)),
                             reads=[r_w1b[wb_i], r_hT], writes=[self.psr[pb]])
                    k.op('act', lambda e, pb=pb: e.activation(out=rt[pb], in_=self.psb(pb), func=AF.Relu), reads=[self.psr[pb]], writes=[r_rt[pb]])
                    k.op('dve', lambda e, pb=pb, c=c: e.tensor_tensor(out=aT[:, c, :], in0=rt[pb], in1=rt[pb], op=ALU.mult), reads=[r_rt[pb]], writes=[r_aT])
                if j < 3 and pending:
                    post_one(*pending.pop(0))
                if j % 4 == 3 and blk + 1 < NBLK:
                    pre_one(blk + 1, j // 4)
            if self.cfg.get("mlp_stop") == 2:
                k.barrier(); ar.release(m0); return
            for half in range(2):
                for j in range(16):
                    wb_i = w2i % 2; w2i += 1
                    if blk == 0:
                        k.dma('pool', w2b[wb_i], w2[j * 512:(j + 1) * 512, half * 1024:(half + 1) * 1024].rearrange("(c p) n -> p c n", p=128), writes=[r_w2b[wb_i]])
                        k.dma('sp', self.W2C[half * 16 + j], w2b[wb_i].rearrange("p a b -> p (a b)"), reads=[r_w2b[wb_i]], writes=[r_W2C[half * 16 + j]], sem=w2s[wb_i])
                    else:
                        k.dma('pool', w2b[wb_i].rearrange("p a b -> p (a b)"), self.W2C[half * 16 + j], reads=[r_W2C[half * 16 + j]], writes=[r_w2b[wb_i]])
                    for cc in range(4):
                        c = j * 4 + cc
                        for sub in range(4):
                            for n in range(2):
                                pb = sub * 2 + n
                                k.op('pe', lambda e, c=c, cc=cc, sub=sub, n=n, pb=pb, wb_i=wb_i: e.matmul(self.psb(pb), lhsT=aT[:, c, sub * 128:(sub + 1) * 128], rhs=w2b[wb_i][:, cc, n * 512:(n + 1) * 512], start=(c == 0), stop=(c == 63)),
                                     reads=[r_aT, r_w2b[wb_i]] + ([r_hT] if False else []), writes=[self.psr[pb]])
                if self.cfg.get("mlp_stop") == 3:
                    k.barrier(); ar.release(m0); return
                if half == 0:
                    for sub in range(4):
                        for n in range(2):
                            pb = sub * 2 + n
                            eng = 'act' if n == 0 else 'dve'
                            if eng == 'act':
                                k.op('act', lambda e, sub=sub, n=n, pb=pb, ysb=ysb: e.copy(out=ysb[:, sub, n * 512:(n + 1) * 512], in_=self.psb(pb)), reads=[self.psr[pb]], writes=[r_hT])
                            else:
                                k.op('dve', lambda e, sub=sub, n=n, pb=pb, ysb=ysb: e.tensor_copy(out=ysb[:, sub, n * 512:(n + 1) * 512], in_=self.psb(pb)), reads=[self.psr[pb]], writes=[r_hT])
            if self.cfg.get("mlp_stop") == 4:
                k.barrier(); ar.release(m0); return
            def post_one(blk_, sub_, ysb_, r_hT_, g_):
                tile_ = blk_ * 4 + sub_
                b_ = xi_[0] % 2; xi_[0] += 1
                k.dma('sp', xt[b_], Xin[tile_ * 128:(tile_ + 1) * 128, :], reads=[r_Xin[tile_]], writes=[r_xt[b_]])
                yp_ = [(ysb_[:, sub_, :], r_hT_, 1024), (self.psb(sub_ * 2), self.psr[sub_ * 2], 512), (self.psb(sub_ * 2 + 1), self.psr[sub_ * 2 + 1], 512)]
                self.post_tile(xt[b_], r_xt[b_], yp_, G[g_], r_G[g_], W)
                k.dma('sp', Xout[tile_ * 128:(tile_ + 1) * 128, :], xt[b_], reads=[r_xt[b_]], writes=[r_Xout[tile_]], is_output=out_is_output)

            post_one(blk, 0, ysb, r_hT, g)
            for sub in range(1, 4):
                pending.append((blk, sub, ysb, r_hT, g))
        while pending:
            post_one(*pending.pop(0))
        k.barrier()
        ar.release(m0)


def _even_scratch(self):
    if hasattr(self, "QA"):
        return
    s = self.scr
    self.QA = s("QA", [8, 128, NT]); self.FF = s("FF", [8, 128, NT]); self.FB = s("FB", [8, 128, NT]); self.GA = s("GA", [8, 128, NT])
    self.QB = s("QB", [8, 128, NT], BF16); self.KB = s("KB", [8, 128, NT], BF16)
    self.VA = s("VA", [NT, 1024], BF16); self.VB = s("VB", [NT, 1024], BF16)
    self.OA = s("OA", [16, 128, NT], BF16) if not self.cfg.get("dump_oa") else self.out("OA", [16, 128, NT], BF16)
    self.r_E1 = self.R("E1out")
    self.r_OA = self.R("OAres")


def even_inproj(self, Xin, r_Xin):
    k, ar = self.k, self.ar
    self.phase()
    _even_scratch(self)
    m0 = ar.mark()
    l = 0
    w_in = self.din["w_in_even"]
    W = self.work_bufs()
    xt = [ar.alloc([D_MODEL]) for _ in range(2)]
    r_xt = [self.R(f"e1xt{i}", sem=True) for i in range(2)]
    hT = ar.alloc([16, NT], BF16); r_hT = self.R("hTall")
    wb = [ar.alloc([16, 512], BF16) for _ in range(2)]
    r_wb = [self.R(f"e1w{i}", sw=True) for i in range(2)]
    NST = 4
    stg = [ar.alloc([512]) for _ in range(NST)]
    r_stg = [self.R(f"e1stg{i}", sem=True) for i in range(NST)]
    AB, r_AB = self.AB[l][0], self.r_AB[l][0]
    for tile in range(24):
        g = 0 if tile < 8 else 1
        b = tile % 2
        k.dma('sp', xt[b], Xin[tile * 128:(tile + 1) * 128, :], reads=[r_Xin[tile]], writes=[r_xt[b]])
        self.pre_tile(xt[b], r_xt[b], hT, r_hT, tile * 128, AB, r_AB, g, W)
    fm_dst = {0: self.QA, 1: self.FF, 2: self.FB, 4: self.GA, 5: self.QB, 6: self.KB}
    si = 0
    pbi = 0
    nak, nav = self.dout["nak"], self.dout["nav"]
    for j in self.cfg.get('e1_js', range(16)):
        grp = j // 2
        wi = j % 2
        k.dma('pool', wb[wi], w_in[:, j * 512:(j + 1) * 512].rearrange("(k p) n -> p k n", p=128), writes=[r_wb[wi]])
        if grp in fm_dst:
            dst = fm_dst[grp]
            isbf = grp in (5, 6)
            for cc in range(4):
                head = (j % 2) * 4 + cc
                for tb in range(6):
                    pb = pbi % 4; pbi += 1
                    for kk in range(16):
                        k.op('pe', lambda e, kk=kk, cc=cc, wi=wi, tb=tb, pb=pb: e.matmul(self.psb(pb), lhsT=wb[wi][:, kk, cc * 128:(cc + 1) * 128], rhs=hT[:, kk, tb * 512:(tb + 1) * 512], start=(kk == 0), stop=(kk == 15)),
                             reads=[r_wb[wi], r_hT], writes=[self.psr[pb]])
                    s_ = si % NST; si += 1
                    so = stg[s_] if not isbf else stg[s_].bitcast(BF16)[:, 0:512]
                    scale = (128.0 ** -0.5) if grp == 5 else 1.0
                    if si % 2 == 0:
                        k.op('act', lambda e, so=so, pb=pb, scale=scale: e.activation(out=so, in_=self.psb(pb), func=AF.Copy, scale=scale), reads=[self.psr[pb]], writes=[r_stg[s_]])
                    else:
                        k.op('dve', lambda e, so=so, pb=pb, scale=scale: e.tensor_scalar_mul(out=so, in0=self.psb(pb), scalar1=scale), reads=[self.psr[pb]], writes=[r_stg[s_]])
                    k.dma('sp', dst[head, :, tb * 512:(tb + 1) * 512], so, reads=[r_stg[s_]], writes=[self.r_E1])
        if grp in (3, 6, 7):
            ntile = 8 if grp == 6 else 24
            col0 = (j % 2) * 512
            for t in range(ntile):
                pb = pbi % 4; pbi += 1
                for kk in range(16):
                    k.op('pe', lambda e, kk=kk, wi=wi, t=t, pb=pb: e.matmul(self.psb(pb), lhsT=hT[:, kk, t * 128:(t + 1) * 128], rhs=wb[wi][:, kk, :], start=(kk == 0), stop=(kk == 15)),
                         reads=[r_wb[wi], r_hT], writes=[self.psr[pb]])
                if grp in (3, 7):
                    s_ = si % NST; si += 1
                    so = stg[s_].bitcast(BF16)[:, 0:512]
                    k.op('act', lambda e, so=so, pb=pb: e.copy(out=so, in_=self.psb(pb)), reads=[self.psr[pb]], writes=[r_stg[s_]])
                    d = self.VA if grp == 3 else self.VB
                    k.dma('sp', d[t * 128:(t + 1) * 128, col0:col0 + 512], so, reads=[r_stg[s_]], writes=[self.r_E1])
                if grp in (6, 7) and t < 8:
                    s_ = si % NST; si += 1
                    so = stg[s_]
                    k.op('act', lambda e, so=so, pb=pb: e.copy(out=so, in_=self.psb(pb)), reads=[self.psr[pb]], writes=[r_stg[s_]])
                    d = nak if grp == 6 else nav
                    k.dma('sp', d[t * 128:(t + 1) * 128, col0:col0 + 512], so, reads=[r_stg[s_]], is_output=True)
    k.barrier()
    ar.release(m0)


Builder.even_inproj = even_inproj


def hgrn_phase(self):
    k, ar = self.k, self.ar
    self.phase()
    _even_scratch(self)
    m0 = ar.mark()
    CH = 64
    BT = 512
    cmask = ar.alloc([1024]); r_cmask = self.R("cmask", sem=True)
    trim = ar.alloc([128], parts=64); r_trim = self.R("trim", sem=True)
    k.dma('sp', cmask, self.din["cmask"], writes=[r_cmask])
    k.dma('sp', trim, self.din["trimask"], writes=[r_trim])
    gn = ar.alloc([1]); r_gn = self.R("hgn", sem=True)
    with self.nc.allow_non_contiguous_dma(reason="tiny"):
        k.dma('sp', gn, self.din["hgn"].rearrange("o p -> p o"), writes=[r_gn])
    lbr = ar.alloc([2, 1024], parts=3); r_lbr = self.R("lbr", sem=True)
    k.dma('sp', lbr, self.din["lbrows"].rearrange("d r c -> r d c"), writes=[r_lbr])
    k.op('act', I("activation", out=lbr, in_=lbr, func=AF.Exp), reads=[r_lbr], writes=[r_lbr])
    pf = self.psb(7)
    for d in range(2):
        for h in range(8):
            k.op('pe', I("transpose", out=pf[:, (d * 8 + h) * 3:(d * 8 + h) * 3 + 3], in_=lbr[0:3, d, h * 128:(h + 1) * 128], identity=self.ident_f[0:3, 0:3]),
                 reads=[r_lbr, self.r_ident_f], writes=[self.psr[7]])
    lbe = ar.alloc([16, 3]); r_lbe = self.R("lbe")
    omlb = ar.alloc([16]); r_omlb = self.R("omlb")
    lsum = ar.alloc([16]); r_lsum = self.R("lsum")
    k.op('dve', I("tensor_copy", out=lbe.rearrange("p a r -> p (a r)"), in_=pf[:, 0:48]), reads=[self.psr[7]], writes=[r_lbe])
    k.op('dve', I("tensor_reduce", out=lsum, in_=lbe, axis=AX.X, op=ALU.add), reads=[r_lbe], writes=[r_lsum])
    k.op('dve', I("reciprocal", out=lsum, in_=lsum), reads=[r_lsum], writes=[r_lsum])
    k.op('dve', I("tensor_tensor", out=omlb, in0=lbe[:, :, 0], in1=lsum, op=ALU.mult), reads=[r_lbe, r_lsum], writes=[r_omlb])
    lbv = ar.alloc([16]); lnom = ar.alloc([16])
    k.op('dve', I("tensor_copy", out=lbv, in_=omlb), reads=[r_omlb], writes=[r_omlb])
    k.op('dve', I("tensor_scalar", out=omlb, in0=omlb, scalar1=-1.0, scalar2=1.0, op0=ALU.mult, op1=ALU.add), reads=[r_omlb], writes=[r_omlb])
    k.op('act', I("activation", out=lnom, in_=omlb, func=AF.Ln), reads=[r_omlb], writes=[r_omlb])
    lnsc = ar.alloc([1])
    k.op('dve', I("memset", lnsc, float(math.log(128.0 ** -0.5))), writes=[r_omlb])
    k.barrier()
    NC_ = 4
    def f32buf(n=BT): return ar.alloc([n])
    qT = [f32buf() for _ in range(3)]; r_qT = [self.R(f"hq{i}", sem=True) for i in range(3)]
    fT = [f32buf() for _ in range(3)]; r_fT = [self.R(f"hf{i}", sem=True) for i in range(3)]
    kt = f32buf(); r_kt = self.R("kt")
    lf = f32buf(); r_lf = self.R("lf")
    bc = f32buf(); r_bc = self.R("bc")
    rv = f32buf(); r_rv = self.R("rv")
    sq = f32buf(); r_sq = self.R("sq")
    tm = f32buf(); r_tm = self.R("tm")
    tm2 = f32buf(); r_tm2 = self.R("tm2")
    vt = [[ar.alloc([8, 128], BF16, parts=64) for _ in range(2)] for _ in range(NC_)]
    r_vt = [[self.R(f"hv{c}{i}", sem=True) for i in range(2)] for c in range(NC_)]
    eb = [[f32buf() for _ in range(2)] for _ in range(NC_)]; r_eb = [[self.R(f"eb{c}{i}") for i in range(2)] for c in range(NC_)]
    qb = [[ar.alloc([BT], BF16) for _ in range(2)] for _ in range(NC_)]; r_qb = [[self.R(f"qb{c}{i}") for i in range(2)] for c in range(NC_)]
    kb = [[ar.alloc([BT], BF16) for _ in range(2)] for _ in range(NC_)]; r_kb = [[self.R(f"kb{c}{i}") for i in range(2)] for c in range(NC_)]
    kd = [[ar.alloc([BT], BF16) for _ in range(2)] for _ in range(NC_)]; r_kd = [[self.R(f"kd{c}{i}") for i in range(2)] for c in range(NC_)]
    S32 = [ar.alloc([128]) for _ in range(NC_)]; r_S32 = [self.R(f"S32{c}", sem=True) for c in range(NC_)]
    Sbf = [ar.alloc([128], BF16) for _ in range(NC_)]; r_Sbf = [self.R(f"Sbf{c}") for c in range(NC_)]
    Asb = [[ar.alloc([CH], BF16, parts=64) for _ in range(2)] for _ in range(NC_)]; r_Asb = [[self.R(f"Asb{c}{i}") for i in range(2)] for c in range(NC_)]
    kdt = [[ar.alloc([128], BF16, parts=64) for _ in range(2)] for _ in range(NC_)]; r_kdt = [[self.R(f"kdt{c}{i}") for i in range(2)] for c in range(NC_)]
    Oacc = [ar.alloc([2048]) for _ in range(2)]; r_Oacc = [self.R(f"Oacc{i}") for i in range(2)]
    Oacb = [ar.alloc([2048]) for _ in range(2)]; r_Oacb = [self.R(f"Oacb{i}") for i in range(2)]
    gT = [f32buf() for _ in range(2)]; r_gT = [self.R(f"hg{i}", sem=True) for i in range(2)]
    sq16 = ar.alloc([BT], BF16); r_sq16 = self.R("sq16")
    rstd = f32buf(); r_rstd = self.R("rstdh")
    ost = [ar.alloc([BT], BF16) for _ in range(2)]; r_ost = [self.R(f"host{i}", sem=True) for i in range(2)]
    def regA(c, p): return self.ps[0:CH, 2 * c, p * 64:p * 64 + CH]
    def regU(c, p): return self.ps[:, 2 * c, 128 + p * 128:256 + p * 128]
    def regOd(c, p): return self.ps[:, 2 * c, 384 + p * 64:448 + p * 64]
    def regK(c, p): return self.psb16(2 * c + 1)[0:CH, p * 128:(p + 1) * 128]
    def regOa(c, p): return self.ps[:, 2 * c + 1, 128 + p * 64:192 + p * 64]
    r_bD = [self.psr[2 * c] for c in range(NC_)]
    r_bA = [self.psr[2 * c + 1] for c in range(NC_)]
    r_pA = [[r_bD[c] for p in range(2)] for c in range(NC_)]
    r_pU = [[r_bD[c] for p in range(2)] for c in range(NC_)]
    r_pK = [[r_bA[c] for p in range(2)] for c in range(NC_)]
    r_pO = [[(r_bA[c] if c % 2 == 0 else r_bD[c]) for p in range(2)] for c in range(NC_)]
    r_fin = self.R("pfin")
    srcs = {0: self.FF, 1: self.FB}
    cnt = {"ld": 0, "fin": 0}
    stepc = [0] * NC_
    chc = [0] * NC_
    seqs = [(i * 256, 256, True, i) for i in range(4)] + [(1024, 2048, False, 0)]
    seqs = seqs[self.cfg.get('hg_s0', 0):self.cfg.get('hg_s1', 5)]
    def prep_chain(c, h, d, t0, bt, nblk, ncb, step, sl, par, blkof):
        blk = step if d == 0 else nblk - 1 - step
        blkof[c] = blk
        c0 = t0 + blk * bt
        li = cnt["ld"] % 3; cnt["ld"] += 1
        pi = stepc[c] % 2; stepc[c] += 1
        par[c] = pi
        k.dma('sp', qT[li][:, sl], self.QA[h, :, c0:c0 + bt], reads=[self.r_E1], writes=[r_qT[li]])
        k.dma('sp', fT[li][:, sl], srcs[d][h, :, c0:c0 + bt], reads=[self.r_E1], writes=[r_fT[li]])
        k.dma('sp', vt[c][pi][:, 0:ncb, :], self.VA[c0:c0 + bt, h * 128:(h + 1) * 128].rearrange("(c p) v -> p c v", p=CH), reads=[self.r_E1], writes=[r_vt[c][pi]])
        q_, f_ = qT[li][:, sl], fT[li][:, sl]
        hd = d * 8 + h
        lb_ap, lno_ap = lbv[:, hd:hd + 1], lnom[:, hd:hd + 1]
        k.op('act', I("activation", out=kt[:, sl], in_=f_, func=AF.Exp), reads=[r_fT[li]], writes=[r_kt])
        k.op('act', I("activation", out=tm[:, sl], in_=kt[:, sl], func=AF.Ln, bias=1.0), reads=[r_kt], writes=[r_tm])
        k.op('act', I("activation", out=lf[:, sl], in_=kt[:, sl], func=AF.Ln, bias=lb_ap), reads=[r_kt, r_omlb], writes=[r_lf])
        k.op('dve', I("tensor_tensor", out=lf[:, sl], in0=lf[:, sl], in1=tm[:, sl], op=ALU.subtract), reads=[r_lf, r_tm], writes=[r_lf])
        mF, mB = cmask[:, 0:bt], cmask[:, 512:512 + bt]
        if d == 0:
            k.op('dve', I("tensor_tensor_scan", out=bc[:, sl], data0=mF, data1=lf[:, sl], initial=0.0, op0=ALU.mult, op1=ALU.add), reads=[r_lf, r_cmask], writes=[r_bc])
            k.op('dve', I("tensor_tensor_scan", out=rv[:, sl][:, ::-1], data0=mB[:, ::-1], data1=lf[:, sl][:, ::-1], initial=0.0, op0=ALU.mult, op1=ALU.add), reads=[r_lf, r_cmask], writes=[r_rv])
        else:
            k.op('dve', I("tensor_tensor_scan", out=bc[:, sl][:, ::-1], data0=mB[:, ::-1], data1=lf[:, sl][:, ::-1], initial=0.0, op0=ALU.mult, op1=ALU.add), reads=[r_lf, r_cmask], writes=[r_bc])
            k.op('dve', I("tensor_tensor_scan", out=rv[:, sl], data0=mF, data1=lf[:, sl], initial=0.0, op0=ALU.mult, op1=ALU.add), reads=[r_lf, r_cmask], writes=[r_rv])
        dcol0 = CH - 1 if d == 0 else 0
        k.op('act', I("activation", out=eb[c][pi][:, 0:ncb], in_=bc[:, dcol0:bt:CH], func=AF.Exp), reads=[r_bc], writes=[r_eb[c][pi]])
        k.op('dve', I("tensor_tensor", out=rv[:, sl], in0=rv[:, sl], in1=lf[:, sl], op=ALU.subtract), reads=[r_rv, r_lf], writes=[r_rv])
        k.op('dve', I("tensor_tensor", out=rv[:, sl], in0=rv[:, sl], in1=tm[:, sl], op=ALU.subtract), reads=[r_rv, r_tm], writes=[r_rv])
        k.op('act', I("activation", out=kd[c][pi][:, sl], in_=rv[:, sl], func=AF.Exp, bias=lno_ap), reads=[r_rv, r_omlb], writes=[r_kd[c][pi]])
        k.op('dve', I("tensor_tensor", out=tm[:, sl], in0=tm[:, sl], in1=bc[:, sl], op=ALU.add), reads=[r_tm, r_bc], writes=[r_tm])
        k.op('act', I("activation", out=kb[c][pi][:, sl], in_=tm[:, sl], func=AF.Exp, scale=-1.0, bias=lno_ap), reads=[r_tm, r_omlb], writes=[r_kb[c][pi]])
        k.op('act', I("activation", out=sq[:, sl], in_=q_, func=AF.Exp, scale=-1.0), reads=[r_qT[li]], writes=[r_sq])
        k.op('act', I("activation", out=sq[:, sl], in_=sq[:, sl], func=AF.Ln, bias=1.0), reads=[r_sq], writes=[r_sq])
        k.op('dve', I("tensor_tensor", out=sq[:, sl], in0=bc[:, sl], in1=sq[:, sl], op=ALU.subtract), reads=[r_bc, r_sq], writes=[r_sq])
        k.op('act', I("activation", out=tm2[:, sl], in_=sq[:, sl], func=AF.Exp, bias=lnsc[:, 0:1]), reads=[r_sq, r_omlb], writes=[r_tm2])
        k.op('dve', I("tensor_tensor", out=qb[c][pi][:, sl], in0=q_, in1=tm2[:, sl], op=ALU.mult), reads=[r_qT[li], r_tm2], writes=[r_qb[c][pi]])

    def chunk_step(cstep, chains, bt, ncb, par, blkof):
        info = []
        for c, (h, d) in enumerate(chains):
            cc = cstep if d == 0 else ncb - 1 - cstep
            pp = chc[c] % 2; chc[c] += 1
            pi = par[c]
            cs = cc * CH
            info.append((c, h, d, cc, pp, pi, cs))
        for (c, h, d, cc, pp, pi, cs) in info:
            qbc, kbc, kdc = qb[c][pi][:, cs:cs + CH], kb[c][pi][:, cs:cs + CH], kd[c][pi][:, cs:cs + CH]
            k.op('pe', I("matmul", regA(c, pp), lhsT=kbc, rhs=qbc, start=True, stop=True), reads=[r_kb[c][pi], r_qb[c][pi]], writes=[r_pA[c][pp]])
            k.op('pe', I("transpose", out=regK(c, pp), in_=kdc, identity=self.ident_b), reads=[r_kd[c][pi], self.r_ident_b], writes=[r_pK[c][pp]])
        for (c, h, d, cc, pp, pi, cs) in info:
            mk = trim[:, 0:CH] if d == 0 else trim[:, CH:2 * CH]
            k.op('dve', I("tensor_tensor", out=Asb[c][pp], in0=regA(c, pp), in1=mk, op=ALU.mult), reads=[r_pA[c][pp], r_trim], writes=[r_Asb[c][pp]])
            k.op('act', I("copy", out=kdt[c][pp], in_=regK(c, pp)), reads=[r_pK[c][pp]], writes=[r_kdt[c][pp]])
        for (c, h, d, cc, pp, pi, cs) in info:
            qbc = qb[c][pi][:, cs:cs + CH]
            vch = vt[c][pi][:, cc, :]
            ro = regOa(c, pp) if d == 0 else regOd(c, pp)
            k.op('pe', I("matmul", ro, lhsT=Sbf[c], rhs=qbc, start=True, stop=False), reads=[r_Sbf[c], r_qb[c][pi]], writes=[r_pO[c][pp]])
            k.op('pe', I("matmul", ro, lhsT=vch, rhs=Asb[c][pp], start=False, stop=True), reads=[r_vt[c][pi], r_Asb[c][pp]], writes=[r_pO[c][pp]])
            k.op('pe', I("matmul", regU(c, pp), lhsT=kdt[c][pp], rhs=vch, start=True, stop=True), reads=[r_kdt[c][pp], r_vt[c][pi]], writes=[r_pU[c][pp]])
        for (c, h, d, cc, pp, pi, cs) in info:
            blk = blkof[c]
            oc = Oacc[c // 2][:, blk * bt + cs: blk * bt + cs + CH]
            if d == 0:
                k.op('act', I("copy", out=oc, in_=regOa(c, pp)), reads=[r_pO[c][pp]], writes=[r_Oacc[c // 2]])
            dec = eb[c][pi][:, cc:cc + 1]
            k.op('dve', I("scalar_tensor_tensor", out=S32[c], in0=S32[c], scalar=dec, in1=regU(c, pp), op0=ALU.mult, op1=ALU.add), reads=[r_S32[c], r_eb[c][pi], r_pU[c][pp]], writes=[r_S32[c]])
            k.op('act', I("copy", out=Sbf[c], in_=S32[c]), reads=[r_S32[c]], writes=[r_Sbf[c]])
        for (c, h, d, cc, pp, pi, cs) in info:
            if d == 1:
                blk = blkof[c]
                oc = Oacb[c // 2][:, blk * bt + cs: blk * bt + cs + CH]
                k.op('dve', I("tensor_copy", out=oc, in_=regOd(c, pp)), reads=[r_pO[c][pp]], writes=[r_Oacb[c // 2]])

    def finish_group(chains, t0, bt, nblk, sl, is_ctx, sidx, hp):
        if is_ctx:
            for c, (h, d) in enumerate(chains):
                dst = self.dout["nsf" if d == 0 else "nsb"]
                k.dma('sp', dst[sidx, h], S32[c], reads=[r_S32[c]], is_output=True)
        for hh in range(0 if self.cfg.get('hg_nofin') else 2):
            h = 2 * hp + hh
            for blk in range(nblk):
                c0 = t0 + blk * bt
                gi = cnt["fin"] % 2; cnt["fin"] += 1
                ob = Oacc[hh][:, blk * bt:(blk + 1) * bt]
                obb = Oacb[hh][:, blk * bt:(blk + 1) * bt]
                k.dma('sp', gT[gi][:, sl], self.GA[h, :, c0:c0 + bt], reads=[self.r_E1], writes=[r_gT[gi]])
                k.op('dve', I("tensor_tensor", out=ob, in0=ob, in1=obb, op=ALU.add), reads=[r_Oacc[hh], r_Oacb[hh]], writes=[r_Oacc[hh]])
                k.op('act', I("activation", out=sq16[:, sl], in_=ob, func=AF.Square), reads=[r_Oacc[hh]], writes=[r_sq16])
                pfin = self.ps[:, 7, 0:bt]
                fin_res = [r_bA[3]]
                k.op('pe', I("matmul", pfin, lhsT=self.ones_b, rhs=sq16[:, sl], start=True, stop=True), reads=[self.r_ones, r_sq16], writes=fin_res)
                k.op('act', I("activation", out=rstd[:, sl], in_=pfin, func=AF.Ln, scale=1.0 / 128, bias=EPS), reads=fin_res, writes=[r_rstd])
                g_ = gT[gi][:, sl]
                k.op('act', I("activation", out=tm[:, sl], in_=g_, func=AF.Exp, scale=-1.0), reads=[r_gT[gi]], writes=[r_tm])
                k.op('act', I("activation", out=tm[:, sl], in_=tm[:, sl], func=AF.Ln, bias=1.0), reads=[r_tm], writes=[r_tm])
                k.op('dve', I("scalar_tensor_tensor", out=rstd[:, sl], in0=rstd[:, sl], scalar=-0.5, in1=tm[:, sl], op0=ALU.mult, op1=ALU.subtract), reads=[r_rstd, r_tm], writes=[r_rstd])
                k.op('act', I("activation", out=rstd[:, sl], in_=rstd[:, sl], func=AF.Exp), reads=[r_rstd], writes=[r_rstd])
                k.op('dve', I("tensor_tensor", out=tm[:, sl], in0=ob, in1=g_, op=ALU.mult), reads=[r_Oacc[hh], r_gT[gi], r_tm], writes=[r_tm])
                k.op('dve', I("scalar_tensor_tensor", out=ost[gi][:, sl], in0=tm[:, sl], scalar=gn[:, 0:1], in1=rstd[:, sl], op0=ALU.mult, op1=ALU.mult), reads=[r_tm, r_rstd, r_gn], writes=[r_ost[gi]])
                k.dma('sp', self.OA[h, :, c0:c0 + bt], ost[gi][:, sl], reads=[r_ost[gi]], writes=[self.r_OA], is_output=bool(self.cfg.get("dump_oa")))

    items = []
    for (t0, T, is_ctx, sidx) in seqs:
        bt = min(BT, T)
        nblk = T // bt
        for hp in range(self.cfg.get('hg_nhp', 4)):
            for step in range(nblk):
                items.append((t0, T, is_ctx, sidx, hp, step))
    pars = [[0] * NC_ for _ in items]
    blkofs = [[0] * NC_ for _ in items]

    def do_prep(ii, c):
        (t0, T, is_ctx, sidx, hp, step) = items[ii]
        bt = min(BT, T); nblk = T // bt; ncb = bt // CH
        h, d = 2 * hp + (c // 2), c % 2
        prep_chain(c, h, d, t0, bt, nblk, ncb, step, slice(0, bt), pars[ii], blkofs[ii])

    for c in range(NC_):
        do_prep(0, c)
    for ii, (t0, T, is_ctx, sidx, hp, step) in enumerate(items):
        bt = min(BT, T); nblk = T // bt; ncb = bt // CH
        sl = slice(0, bt)
        chains = [(2 * hp + (c // 2), c % 2) for c in range(NC_)]
        if step == 0:
            for c, (h, d) in enumerate(chains):
                if is_ctx:
                    k.op('dve', I("memset", S32[c], 0.0), writes=[r_S32[c]])
                    k.op('dve', I("memset", Sbf[c], 0.0), writes=[r_Sbf[c]])
                else:
                    k.dma('sp', S32[c], self.din["st_f" if d == 0 else "st_b"][h], writes=[r_S32[c]])
                    k.op('act', I("copy", out=Sbf[c], in_=S32[c]), reads=[r_S32[c]], writes=[r_Sbf[c]])
        nxt = ii + 1 if ii + 1 < len(items) else None
        done = 0
        for cstep in range(ncb):
            chunk_step(cstep, chains, bt, ncb, pars[ii], blkofs[ii])
            if nxt is not None:
                want = ((cstep + 1) * NC_) // ncb
                while done < want:
                    do_prep(nxt, done); done += 1
        if nxt is not None:
            while done < NC_:
                do_prep(nxt, done); done += 1
        if step == nblk - 1:
            finish_group(chains, t0, bt, nblk, sl, is_ctx, sidx, hp)
    k.barrier()
    ar.release(m0)


Builder.hgrn_phase = hgrn_phase


def _attn_finish(self, po, r_po, rec, r_rec, ob16, r_ob16, pT, r_pT, ostg_slice, r_ostg):
    k = self.k
    k.op('dve', I("reciprocal", out=rec, in_=po[:, 128:129]), reads=[r_po], writes=[r_rec])
    k.op('dve', I("tensor_scalar_mul", out=ob16, in0=po[:, 0:128], scalar1=rec), reads=[r_po, r_rec], writes=[r_ob16])
    k.op('pe', I("transpose", out=pT, in_=ob16, identity=self.ident_b), reads=[r_ob16, self.r_ident_b], writes=[r_pT])
    k.op('act', I("copy", out=ostg_slice, in_=pT), reads=[r_pT], writes=[r_ostg])


def na_phase(self):
    k, ar = self.k, self.ar
    self.phase()
    _even_scratch(self)
    m0 = ar.mark()
    QT = [ar.alloc([2048], BF16) for _ in range(2)]; r_QT = [self.R(f"naQ{i}", sem=True) for i in range(2)]
    KT = [ar.alloc([2048], BF16) for _ in range(2)]; r_KT = [self.R(f"naK{i}", sem=True) for i in range(2)]
    vaug = [ar.alloc([16, 129], BF16) for _ in range(2)]; r_vaug = [self.R(f"naV{i}", sem=True) for i in range(2)]
    for i in range(2):
        k.op('pool', I("memset", vaug[i][:, :, 128:129], 1.0), writes=[r_vaug[i]])
    rec = [ar.alloc([1]) for _ in range(2)]; r_rec = [self.R(f"narec{i}") for i in range(2)]
    ob16 = [ar.alloc([128], BF16) for _ in range(2)]; r_ob16 = [self.R(f"naob{i}") for i in range(2)]
    ostg = [ar.alloc([512], BF16) for _ in range(2)]; r_ostg = [self.R(f"naost{i}", sem=True) for i in range(2)]
    Pc = [ar.alloc([4, 512], BF16) for _ in range(2)]; r_Pc = [self.R(f"naPc{i}") for i in range(2)]
    Praw = [ar.alloc([128], BF16) for _ in range(3)]; r_Praw = [self.R(f"naPr{i}") for i in range(3)]
    Pacc = [ar.alloc([512]) for _ in range(2)]; r_Pacc = [self.R(f"naPacc{i}") for i in range(2)]
    recb = ar.alloc([512]); r_recb = self.R("narecb")
    Pl = [ar.alloc([128], BF16) for _ in range(6)]; r_Pl = [self.R(f"naPl{i}") for i in range(6)]
    cnt = {"h": 0, "o": 0, "f": 0, "pl": 0, "pr": 0}
    for sq_ in range(4):
        t0 = sq_ * 256
        for h in range(8):
            hi = cnt["h"] % 2; cnt["h"] += 1
            k.dma('sp', QT[hi][:, 0:256], self.QB[h, :, t0:t0 + 256], reads=[self.r_E1], writes=[r_QT[hi]])
            k.dma('sp', KT[hi][:, 0:256], self.KB[h, :, t0:t0 + 256], reads=[self.r_E1], writes=[r_KT[hi]])
            k.dma('sp', vaug[hi][:, 0:2, 0:128], self.VB[t0:t0 + 256, h * 128:(h + 1) * 128].rearrange("(c p) v -> p c v", p=128), reads=[self.r_E1], writes=[r_vaug[hi]])
            pci = hi
            for kc in range(2):
                pb = kc
                k.op('pe', I("matmul", self.ps[:, pb, 0:256], lhsT=KT[hi][:, kc * 128:(kc + 1) * 128], rhs=QT[hi][:, 0:256], start=True, stop=True),
                     reads=[r_KT[hi], r_QT[hi]], writes=[self.psr[pb]])
                k.op('act', I("activation", out=Pc[pci][:, kc, 0:256], in_=self.ps[:, pb, 0:256], func=AF.Exp), reads=[self.psr[pb]], writes=[r_Pc[pci]])
            oi = cnt["o"] % 2; cnt["o"] += 1
            for qt in range(2):
                fi = cnt["f"] % 2; cnt["f"] += 1
                pv = 4 + fi
                po = self.ps[:, pv, 0:129]
                for kc in range(2):
                    k.op('pe', I("matmul", po, lhsT=Pc[pci][:, kc, qt * 128:(qt + 1) * 128], rhs=vaug[hi][:, kc, :], start=(kc == 0), stop=(kc == 1)),
                         reads=[r_Pc[pci], r_vaug[hi]], writes=[self.psr[pv]])
                pT = self.psb16(6)[:, fi * 128:(fi + 1) * 128]
                _attn_finish(self, po, self.psr[pv], rec[fi], r_rec[fi], ob16[fi], r_ob16[fi], pT, self.psr[6], ostg[oi][:, qt * 128:(qt + 1) * 128], r_ostg[oi])
            k.dma('sp', self.OA[8 + h, :, t0:t0 + 256], ostg[oi][:, 0:256], reads=[r_ostg[oi]], writes=[self.r_OA], is_output=bool(self.cfg.get("dump_oa")))
    colm = ar.alloc([64]); r_colm = self.R("colm", sem=True)
    k.dma('sp', colm, self.din["colmask"], writes=[r_colm])
    Traw = ar.alloc([14, 64]); r_Traw = self.R("Traw", sem=True)
    Cexp = ar.alloc([14, 64], BF16); r_Cexp = self.R("Cexp")
    def ws(r): return min(max(r - 4, 0), 24)
    types = {}
    plan = []
    for j in range(16):
        lo = ws(2 * j) // 2
        hi_ = (ws(2 * j + 1) + 7) // 2
        lst = []
        for kc in range(lo, hi_ + 1):
            dl = 2 * kc - 2 * j
            valid = tuple(tuple(ws(2 * j + b) <= 2 * kc + a <= ws(2 * j + b) + 7 for b in range(2)) for a in range(2))
            key = (dl, valid)
            if key not in types:
                types[key] = len(types)
            lst.append((kc, types[key]))
        plan.append(lst)
    ntyp = len(types)
    EBt = ar.alloc([ntyp, 128], BF16); r_EBt = self.R("EBt")
    Kc32 = ar.alloc([4, 128]); r_Kc32 = self.R("Kc32", sem=True)
    Kc16 = ar.alloc([4, 128], BF16); r_Kc16 = self.R("Kc16")
    KcT = ar.alloc([512], BF16); r_KcT = self.R("KcT")
    Vc32 = ar.alloc([4, 128]); r_Vc32 = self.R("Vc32", sem=True)
    vaugc = ar.alloc([4, 129], BF16); r_vaugc = self.R("vaugc")
    k.op('pool', I("memset", vaugc[:, :, 128:129], 1.0), writes=[r_vaugc])
    rpbr = self.din["rpbr"]
    T0 = 1024
    for h in range(8):
        hi = cnt["h"] % 2; cnt["h"] += 1
        k.dma('sp', QT[hi], self.QB[h, :, T0:T0 + 2048], reads=[self.r_E1], writes=[r_QT[hi]])
        k.dma('sp', KT[hi], self.KB[h, :, T0:T0 + 2048], reads=[self.r_E1], writes=[r_KT[hi]])
        k.dma('sp', vaug[hi][:, :, 0:128], self.VB[T0:T0 + 2048, h * 128:(h + 1) * 128].rearrange("(c p) v -> p c v", p=128), reads=[self.r_E1], writes=[r_vaug[hi]])
        k.dma('sp', Kc32, self.din["cnak"][:, h, :].rearrange("(c p) d -> p c d", p=128), writes=[r_Kc32])
        k.dma('sp', Vc32, self.din["cnav"][:, h, :].rearrange("(c p) d -> p c d", p=128), writes=[r_Vc32])
        k.op('pool', I("tensor_copy", out=Kc16, in_=Kc32), reads=[r_Kc32], writes=[r_Kc16])
        k.op('pool', I("tensor_copy", out=vaugc[:, :, 0:128], in_=Vc32), reads=[r_Vc32], writes=[r_vaugc])
        for c in range(4):
            pT = self.psb16(6)[:, c * 128:(c + 1) * 128]
            k.op('pe', I("transpose", out=pT, in_=Kc16[:, c, :], identity=self.ident_b), reads=[r_Kc16, self.r_ident_b], writes=[self.psr[6]])
        k.op('act', I("copy", out=KcT, in_=self.psb16(6)[:, 0:512]), reads=[self.psr[6]], writes=[r_KcT])
        for half in range(2):
            src = bass.AP(tensor=rpbr.tensor, offset=h * 15 * 8192 + half * 8192 + 63, ap=[[127, 64], [8192, 14], [1, 64]])
            k.dma('sp', Traw[half * 64:(half + 1) * 64], src, writes=[r_Traw])
        k.op('act', I("activation", out=Traw, in_=Traw, func=AF.Exp), reads=[r_Traw], writes=[r_Traw])
        for i in range(14):
            k.op('dve', I("tensor_tensor", out=Cexp[:, i, :], in0=Traw[:, i, :], in1=colm, op=ALU.mult), reads=[r_Traw, r_colm], writes=[r_Cexp])
        for (dl, valid), ti in types.items():
            for b in range(2):
                k.op('pool', I("tensor_copy", out=EBt[:, ti, b * 64:(b + 1) * 64], in_=Cexp[:, dl - b + 7, :]), reads=[r_Cexp], writes=[r_EBt])
            for a in range(2):
                for b in range(2):
                    if not valid[a][b]:
                        k.op('pool', I("memset", EBt[a * 64:(a + 1) * 64, ti, b * 64:(b + 1) * 64], 0.0), reads=[r_EBt], writes=[r_EBt])
        for jg in range(4):
            pci = cnt["o"] % 2; cnt["o"] += 1
            po = self.ps[:, 4 + pci, :]
            r_po = self.psr[4 + pci]
            qs = slice(jg * 512, (jg + 1) * 512)
            for c in range(4):
                pb = c % 2
                k.op('pe', I("matmul", self.ps[:, pb, :], lhsT=KcT[:, c * 128:(c + 1) * 128], rhs=QT[hi][:, qs], start=True, stop=True),
                     reads=[r_KcT, r_QT[hi]], writes=[self.psr[pb]])
                k.op('act', I("activation", out=Pc[pci][:, c, :], in_=self.ps[:, pb, :], func=AF.Exp), reads=[self.psr[pb]], writes=[r_Pc[pci]])
                k.op('pe', I("matmul", po, lhsT=vaugc[:, c, 0:128], rhs=Pc[pci][:, c, :], start=(c == 0), stop=False), reads=[r_Pc[pci], r_vaugc], writes=[r_po])
                if c == 1:
                    k.op('dve', I("tensor_tensor", out=Pacc[pci], in0=Pc[pci][:, 0, :], in1=Pc[pci][:, 1, :], op=ALU.add), reads=[r_Pc[pci]], writes=[r_Pacc[pci]])
                elif c > 1:
                    k.op('dve', I("tensor_tensor", out=Pacc[pci], in0=Pacc[pci], in1=Pc[pci][:, c, :], op=ALU.add), reads=[r_Pc[pci], r_Pacc[pci]], writes=[r_Pacc[pci]])
            lat = []
            for jj in range(4):
                j = jg * 4 + jj
                for (kc, ti) in plan[j]:
                    lat.append((jj, j, kc, ti))
            NL = len(lat)
            sbk = []; pls_ = []; prs = []
            for n_ in range(NL):
                sbk.append((2, 3, 7)[cnt["pr"] % 3]); prs.append(cnt["pr"] % 3); cnt["pr"] += 1
                pls_.append(cnt["pl"] % 6); cnt["pl"] += 1

            def S_lat(n_):
                jj, j, kc, ti = lat[n_]
                pb = sbk[n_]
                k.op('pe', I("matmul", self.ps[:, pb, 0:128], lhsT=KT[hi][:, kc * 128:(kc + 1) * 128], rhs=QT[hi][:, j * 128:(j + 1) * 128], start=True, stop=True),
                     reads=[r_KT[hi], r_QT[hi]], writes=[self.psr[pb]])

            S_lat(0)
            if NL > 1:
                S_lat(1)
            for n_ in range(NL):
                jj, j, kc, ti = lat[n_]
                if n_ + 2 < NL:
                    S_lat(n_ + 2)
                pb, pr, pl = sbk[n_], prs[n_], pls_[n_]
                k.op('act', I("activation", out=Praw[pr], in_=self.ps[:, pb, 0:128], func=AF.Exp), reads=[self.psr[pb]], writes=[r_Praw[pr]])
                k.op('dve', I("tensor_tensor", out=Pl[pl], in0=Praw[pr], in1=EBt[:, ti, :], op=ALU.mult), reads=[r_Praw[pr], r_EBt], writes=[r_Pl[pl]])
                k.op('pe', I("matmul", po[:, jj * 128:(jj + 1) * 128], lhsT=vaug[hi][:, kc, 0:128], rhs=Pl[pl], start=False, stop=(n_ == NL - 1)),
                     reads=[r_Pl[pl], r_vaug[hi]], writes=[r_po])
                k.op('dve', I("tensor_tensor", out=Pacc[pci][:, jj * 128:(jj + 1) * 128], in0=Pacc[pci][:, jj * 128:(jj + 1) * 128], in1=Pl[pl], op=ALU.add), reads=[r_Pacc[pci], r_Pl[pl]], writes=[r_Pacc[pci]])
            pden = self.ps[:, 6, :]
            k.op('pe', I("matmul", pden, lhsT=self.ones_f, rhs=Pacc[pci], start=True, stop=True), reads=[self.r_ones_f, r_Pacc[pci]], writes=[self.psr[6]])
            k.op('dve', I("reciprocal", out=recb, in_=pden), reads=[self.psr[6]], writes=[r_recb])
            k.op('dve', I("tensor_tensor", out=ostg[pci], in0=po, in1=recb, op=ALU.mult), reads=[r_po, r_recb], writes=[r_ostg[pci]])
            k.dma('sp', self.OA[8 + h, :, T0 + jg * 512:T0 + (jg + 1) * 512], ostg[pci], reads=[r_ostg[pci]], writes=[self.r_OA], is_output=bool(self.cfg.get("dump_oa")))
    k.barrier()
    ar.release(m0)


Builder.na_phase = na_phase


def outproj_phase(self, l, OA, r_OA, w_out, Xin, r_Xin, Xout, r_Xout):
    k, ar = self.k, self.ar
    self.phase()
    m0 = ar.mark()
    G, r_G = self.load_grep(l, 0)
    W = self.work_bufs()
    wo = ar.alloc([16, 2048], BF16); r_wo = self.R("wo", sw=True)
    for n in range(4):
        k.dma('pool', wo[:, :, n * 512:(n + 1) * 512], w_out[:, n * 512:(n + 1) * 512].rearrange("(k p) n -> p k n", p=128), writes=[r_wo])
    ob = [ar.alloc([16, 512], BF16) for _ in range(2)]; r_ob = [self.R(f"opo{i}", sem=True) for i in range(2)]
    xt = [ar.alloc([D_MODEL]) for _ in range(2)]; r_xt = [self.R(f"opx{i}", sem=True) for i in range(2)]
    xi = 0
    for blk in range(6):
        g = 0 if blk < 2 else 1
        bi = blk % 2
        k.dma('sp', ob[bi], OA[:, :, blk * 512:(blk + 1) * 512].rearrange("c p t -> p c t"), reads=[r_OA], writes=[r_ob[bi]])
        for sub in range(4):
            tile = blk * 4 + sub
            pb0 = (tile % 2) * 4
            for n in range(4):
                for kk in range(16):
                    k.op('pe', I("matmul", self.psb(pb0 + n), lhsT=ob[bi][:, kk, sub * 128:(sub + 1) * 128], rhs=wo[:, kk, n * 512:(n + 1) * 512], start=(kk == 0), stop=(kk == 15)),
                         reads=[r_ob[bi], r_wo], writes=[self.psr[pb0 + n]])
            b = xi % 2; xi += 1
            k.dma('sp', xt[b], Xin[tile * 128:(tile + 1) * 128, :], reads=[r_Xin[tile]], writes=[r_xt[b]])
            yp = [(self.psb(pb0 + n), self.psr[pb0 + n], 512) for n in range(4)]
            self.post_tile(xt[b], r_xt[b], yp, G[g], r_G[g], W)
            k.dma('sp', Xout[tile * 128:(tile + 1) * 128, :], xt[b], reads=[r_xt[b]], writes=[r_Xout[tile]])
    k.barrier()
    ar.release(m0)


Builder.outproj_phase = outproj_phase


NTK = NT + 512
MLA_SCALE = 192.0 ** -0.5


def _mla_scratch(self):
    if hasattr(self, "CQT"):
        return
    s = self.scr
    self.CQT = s("CQT", [4, 128, NT], BF16)
    self.CKVT = s("CKVT", [4, 128, NTK], BF16)
    self.KPET = s("KPET", [64, NTK], BF16)
    self.KROT = s("KROT", [64, 2048], BF16)
    self.QN = s("QN", [16, 128, NT], BF16)
    self.QPE = s("QPE", [16, 64, NT], BF16)
    self.QROT = s("QROT", [16, 64, 2048], BF16)
    self.KN = s("KN", [16, 128, NTK], BF16)
    self.VM = s("VM", [NTK, 16, 128], BF16)
    self.OA2 = s("OA2", [16, 128, NT], BF16) if not self.cfg.get("dump_oa2") else self.out("OA2", [16, 128, NT], BF16)
    self.r_O1 = self.R("O1out"); self.r_O2 = self.R("O2out"); self.r_OA2 = self.R("OA2res")


def _rmsnorm_free(self, src_ps, r_src, n, gq, r_gq, out32, out16, r_out, W, st, r_st):
    k = self.k
    k.op('dve', I("memset", st[:, 0:1], 0.0), writes=[r_st])
    k.op('act', I("activation", out=W["junk"][:, 0:n], in_=src_ps, func=AF.Square, accum_out=st[:, 0:1]), reads=[r_src, r_st], writes=[W["r_junk"], r_st])
    k.op('act', I("activation", out=st[:, 1:2], in_=st[:, 0:1], func=AF.Ln, scale=1.0 / n, bias=EPS), reads=[r_st], writes=[r_st])
    k.op('act', I("activation", out=st[:, 2:3], in_=st[:, 1:2], func=AF.Exp, scale=-0.5), reads=[r_st], writes=[r_st])
    if out32 is not None:
        k.op('dve', I("scalar_tensor_tensor", out=out32, in0=src_ps, scalar=st[:, 2:3], in1=gq, op0=ALU.mult, op1=ALU.mult), reads=[r_src, r_st, r_gq], writes=[r_out])
        k.op('pool', I("tensor_copy", out=out16, in_=out32), reads=[r_out], writes=[r_out])
    else:
        k.op('dve', I("scalar_tensor_tensor", out=out16, in0=src_ps, scalar=st[:, 2:3], in1=gq, op0=ALU.mult, op1=ALU.mult), reads=[r_src, r_st, r_gq], writes=[r_out])


def mla_inproj(self, Xin, r_Xin):
    k, ar = self.k, self.ar
    self.phase()
    _mla_scratch(self)
    m0 = ar.mark()
    l = 1
    W = self.work_bufs()
    w_in = self.din["w_in_odd"]
    wb = ar.alloc([16, 1088], BF16); r_wb = self.R("o1w", sw=True)
    for (c0, c1) in ((0, 512), (512, 1024), (1024, 1088)):
        k.dma('pool', wb[:, :, c0:c1], w_in[:, c0:c1].rearrange("(k p) n -> p k n", p=128), writes=[r_wb])
    gq = ar.alloc([512]); r_gq = self.R("gq", sem=True)
    gkv = ar.alloc([512]); r_gkv = self.R("gkv", sem=True)
    k.dma('sp', gq, self.din["mla_qg"].to_broadcast([128, 512]), writes=[r_gq])
    k.dma('sp', gkv, self.din["mla_kvg"].to_broadcast([128, 512]), writes=[r_gkv])
    ropeP32 = ar.alloc([64], parts=64); r_ropeP32 = self.R("ropeP32", sem=True)
    ropeP = ar.alloc([64], BF16, parts=64); r_ropeP = self.R("ropeP")
    k.dma('sp', ropeP32, self.din["ropeP"], writes=[r_ropeP32])
    k.op('dve', I("tensor_copy", out=ropeP, in_=ropeP32), reads=[r_ropeP32], writes=[r_ropeP])
    cs = ar.alloc([2, 128], parts=64)
    r_cs = self.R("ropecs", sem=True)
    xt = [ar.alloc([D_MODEL]) for _ in range(2)]; r_xt = [self.R(f"o1x{i}", sem=True) for i in range(2)]
    hT = [ar.alloc([16, 128], BF16) for _ in range(2)]; r_hT = [self.R(f"o1h{i}") for i in range(2)]
    st = ar.alloc([8]); r_st = self.R("o1st")
    cq16 = [ar.alloc([512], BF16) for _ in range(2)]; r_cq16 = [self.R(f"cq16{i}") for i in range(2)]
    kv32 = [ar.alloc([512]) for _ in range(2)]; r_kv32 = [self.R(f"kv32{i}", sem=True) for i in range(2)]
    kv16 = [ar.alloc([512], BF16) for _ in range(2)]
    kp32 = [ar.alloc([64]) for _ in range(2)]; r_kp32 = [self.R(f"kp32{i}", sem=True) for i in range(2)]
    kp16 = [ar.alloc([64], BF16) for _ in range(2)]
    tq = [ar.alloc([4, 128], BF16) for _ in range(2)]; r_tq = [self.R(f"tq{i}", sem=True) for i in range(2)]
    tkv = [ar.alloc([4, 128], BF16) for _ in range(2)]; r_tkv = [self.R(f"tkv{i}", sem=True) for i in range(2)]
    tkp = [ar.alloc([128], BF16, parts=64) for _ in range(2)]; r_tkp = [self.R(f"tkp{i}", sem=True) for i in range(2)]
    trot = [ar.alloc([128], BF16, parts=64) for _ in range(2)]; r_trot = [self.R(f"trot{i}", sem=True) for i in range(2)]
    t1 = ar.alloc([128], parts=64); r_t1 = self.R("ropet1")
    t2 = ar.alloc([128], parts=64); r_t2 = self.R("ropet2")
    AB, r_AB = self.AB[l][0], self.r_AB[l][0]
    nckv, nkpe = self.dout["nckv"], self.dout["nkpe"]
    for tile in range(28):
        b = tile % 2
        own = tile < 24
        if own:
            g = 0 if tile < 8 else 1
            k.dma('sp', xt[b], Xin[tile * 128:(tile + 1) * 128, :], reads=[r_Xin[tile]], writes=[r_xt[b]])
            self.pre_tile(xt[b], r_xt[b], hT[b], r_hT[b], 0, AB, r_AB, g, W)
            for n, (c0, c1) in enumerate(((0, 512), (512, 1024), (1024, 1088))):
                for kk in range(16):
                    k.op('pe', I("matmul", self.ps[:, n, 0:c1 - c0], lhsT=hT[b][:, kk, :], rhs=wb[:, kk, c0:c1], start=(kk == 0), stop=(kk == 15)),
                         reads=[r_hT[b], r_wb], writes=[self.psr[n]])
            _rmsnorm_free(self, self.ps[:, 0, :], self.psr[0], 512, gq, r_gq, None, cq16[b], r_cq16[b], W, st, r_st)
            _rmsnorm_free(self, self.ps[:, 1, :], self.psr[1], 512, gkv, r_gkv, kv32[b], kv16[b], r_kv32[b], W, st, r_st)
            k.op('act', I("copy", out=kp32[b], in_=self.ps[:, 2, 0:64]), reads=[self.psr[2]], writes=[r_kp32[b]])
            k.op('pool', I("tensor_copy", out=kp16[b], in_=kp32[b]), reads=[r_kp32[b]], writes=[r_kp32[b]])
            if tile < 8:
                k.dma('sp', nckv[tile * 128:(tile + 1) * 128, :], kv32[b], reads=[r_kv32[b]], is_output=True)
                k.dma('sp', nkpe[tile * 128:(tile + 1) * 128, :], kp32[b], reads=[r_kp32[b]], is_output=True)
        else:
            ct = tile - 24
            k.dma('sp', kv32[b], self.din["cckv"][ct * 128:(ct + 1) * 128, :], writes=[r_kv32[b]])
            k.dma('sp', kp32[b], self.din["ckpe"][ct * 128:(ct + 1) * 128, :], writes=[r_kp32[b]])
            k.op('pool', I("tensor_copy", out=kv16[b], in_=kv32[b]), reads=[r_kv32[b]], writes=[r_kv32[b]])
            k.op('pool', I("tensor_copy", out=kp16[b], in_=kp32[b]), reads=[r_kp32[b]], writes=[r_kp32[b]])
        if own:
            p3 = self.psb16(3)
            for c in range(4):
                k.op('pe', I("transpose", out=p3[:, c * 128:(c + 1) * 128], in_=cq16[b][:, c * 128:(c + 1) * 128], identity=self.ident_b), reads=[r_cq16[b], self.r_ident_b], writes=[self.psr[3]])
            k.op('act', I("copy", out=tq[b].rearrange("p a b -> p (a b)"), in_=p3[:, 0:512]), reads=[self.psr[3]], writes=[r_tq[b]])
            k.dma('sp', self.CQT[:, :, tile * 128:(tile + 1) * 128].rearrange("c p t -> p c t"), tq[b], reads=[r_tq[b]], writes=[self.r_O1])
        p4 = self.psb16(4)
        for c in range(4):
            k.op('pe', I("transpose", out=p4[:, c * 128:(c + 1) * 128], in_=kv16[b][:, c * 128:(c + 1) * 128], identity=self.ident_b), reads=[r_kv32[b], self.r_ident_b], writes=[self.psr[4]])
        k.op('act', I("copy", out=tkv[b].rearrange("p a b -> p (a b)"), in_=p4[:, 0:512]), reads=[self.psr[4]], writes=[r_tkv[b]])
        k.dma('sp', self.CKVT[:, :, tile * 128:(tile + 1) * 128].rearrange("c p t -> p c t"), tkv[b], reads=[r_tkv[b]], writes=[self.r_O1])
        p5 = self.psb16(5)
        k.op('pe', I("transpose", out=p5[0:64, 0:128], in_=kp16[b], identity=self.ident_b), reads=[r_kp32[b], self.r_ident_b], writes=[self.psr[5]])
        k.op('act', I("copy", out=tkp[b], in_=p5[0:64, 0:128]), reads=[self.psr[5]], writes=[r_tkp[b]])
        k.dma('sp', self.KPET[:, tile * 128:(tile + 1) * 128], tkp[b], reads=[r_tkp[b]], writes=[self.r_O1])
        if own and tile >= 8:
            lt = tile - 8
            k.dma('sp', cs, self.din["ropecs"][:, :, lt * 128:(lt + 1) * 128], writes=[r_cs])
            k.op('pe', I("matmul", self.ps[0:64, 6, 0:128], lhsT=ropeP, rhs=tkp[b], start=True, stop=True), reads=[r_ropeP, r_tkp[b]], writes=[self.psr[6]])
            k.op('dve', I("tensor_tensor", out=t1, in0=self.ps[0:64, 6, 0:128], in1=cs[:, 1, :], op=ALU.mult), reads=[self.psr[6], r_cs], writes=[r_t1])
            k.op('pool', I("tensor_tensor", out=t2, in0=tkp[b], in1=cs[:, 0, :], op=ALU.mult), reads=[r_tkp[b], r_cs], writes=[r_t2])
            k.op('pool', I("tensor_tensor", out=trot[b], in0=t1, in1=t2, op=ALU.add), reads=[r_t1, r_t2], writes=[r_trot[b]])
            k.dma('sp', self.KROT[:, lt * 128:(lt + 1) * 128], trot[b], reads=[r_trot[b]], writes=[self.r_O1])
    k.barrier()
    ar.release(m0)


def mla_proj(self):
    k, ar = self.k, self.ar
    self.phase()
    _mla_scratch(self)
    m0 = ar.mark()
    wq = ar.alloc([4, 3072], BF16); r_wq = self.R("wq", sw=True)
    wkv = ar.alloc([4, 4096], BF16); r_wkv = self.R("wkv", sw=True)
    for n in range(6):
        k.dma('pool', wq[:, :, n * 512:(n + 1) * 512], self.din["w_uq"][:, n * 512:(n + 1) * 512].rearrange("(k p) n -> p k n", p=128), writes=[r_wq])
    for n in range(8):
        k.dma('pool', wkv[:, :, n * 512:(n + 1) * 512], self.din["w_ukv"][:, n * 512:(n + 1) * 512].rearrange("(k p) n -> p k n", p=128), writes=[r_wkv])
    cqT = ar.alloc([4, NT], BF16); r_cqT = self.R("cqT", sem=True)
    ckvT = ar.alloc([4, NTK], BF16); r_ckvT = self.R("ckvT", sem=True)
    k.dma('sp', cqT, self.CQT.rearrange("c p t -> p c t"), reads=[self.r_O1], writes=[r_cqT])
    k.dma('sp', ckvT, self.CKVT.rearrange("c p t -> p c t"), reads=[self.r_O1], writes=[r_ckvT])
    ropeP32 = ar.alloc([64], parts=64); r_ropeP32 = self.R("ropeP32b", sem=True)
    ropeP = ar.alloc([64], BF16, parts=64); r_ropeP = self.R("ropePb")
    k.dma('sp', ropeP32, self.din["ropeP"], writes=[r_ropeP32])
    k.op('dve', I("tensor_copy", out=ropeP, in_=ropeP32), reads=[r_ropeP32], writes=[r_ropeP])
    cs = ar.alloc([2, 2048], parts=64); r_cs = self.R("ropecs2", sem=True)
    k.dma('sp', cs, self.din["ropecs"], writes=[r_cs])
    NST = 4
    stg = [ar.alloc([512], BF16) for _ in range(NST)]; r_stg = [self.R(f"o2s{i}", sem=True) for i in range(NST)]
    t1 = ar.alloc([512], parts=64); r_t1 = self.R("o2t1")
    t2 = ar.alloc([512], parts=64); r_t2 = self.R("o2t2")
    si = 0; pbi = 0
    wqv = wq.rearrange("p k (h d) -> p k h d", h=16)
    wkvv = wkv.rearrange("p k (h d) -> p k h d", h=16)
    for h in range(16):
        for tb in range(6):
            ts = slice(tb * 512, (tb + 1) * 512)
            pb = pbi % 3; pbi += 1
            for kk in range(4):
                k.op('pe', I("matmul", self.psb(pb), lhsT=wqv[:, kk, h, 0:128], rhs=cqT[:, kk, ts], start=(kk == 0), stop=(kk == 3)), reads=[r_wq, r_cqT], writes=[self.psr[pb]])
            s_ = si % NST; si += 1
            k.op('act', I("activation", out=stg[s_], in_=self.psb(pb), func=AF.Copy, scale=MLA_SCALE), reads=[self.psr[pb]], writes=[r_stg[s_]])
            k.dma('sp', self.QN[h, :, ts], stg[s_], reads=[r_stg[s_]], writes=[self.r_O2])
            pb = pbi % 3; pbi += 1
            for kk in range(4):
                k.op('pe', I("matmul", self.ps[0:64, pb, :], lhsT=wqv[:, kk, h, 128:192], rhs=cqT[:, kk, ts], start=(kk == 0), stop=(kk == 3)), reads=[r_wq, r_cqT], writes=[self.psr[pb]])
            s_ = si % NST; si += 1
            qpe = stg[s_][0:64, :]
            k.op('act', I("activation", out=qpe, in_=self.ps[0:64, pb, :], func=AF.Copy, scale=MLA_SCALE), reads=[self.psr[pb]], writes=[r_stg[s_]])
            k.dma('sp', self.QPE[h, :, ts], qpe, reads=[r_stg[s_]], writes=[self.r_O2])
            if tb >= 2:
                lt = tb - 2
                ls = slice(lt * 512, (lt + 1) * 512)
                k.op('pe', I("matmul", self.ps[0:64, 6, :], lhsT=ropeP, rhs=qpe, start=True, stop=True), reads=[r_ropeP, r_stg[s_]], writes=[self.psr[6]])
                k.op('dve', I("tensor_tensor", out=t1, in0=self.ps[0:64, 6, :], in1=cs[:, 1, ls], op=ALU.mult), reads=[self.psr[6], r_cs], writes=[r_t1])
                k.op('pool', I("tensor_tensor", out=t2, in0=qpe, in1=cs[:, 0, ls], op=ALU.mult), reads=[r_stg[s_], r_cs], writes=[r_t2])
                s2 = si % NST; si += 1
                qrot = stg[s2][0:64, :]
                k.op('pool', I("tensor_tensor", out=qrot, in0=t1, in1=t2, op=ALU.add), reads=[r_t1, r_t2], writes=[r_stg[s2]])
                k.dma('sp', self.QROT[h, :, ls], qrot, reads=[r_stg[s2]], writes=[self.r_O2])
        for tb in range(7):
            ts = slice(tb * 512, (tb + 1) * 512)
            pb = pbi % 3; pbi += 1
            for kk in range(4):
                k.op('pe', I("matmul", self.psb(pb), lhsT=wkvv[:, kk, h, 0:128], rhs=ckvT[:, kk, ts], start=(kk == 0), stop=(kk == 3)), reads=[r_wkv, r_ckvT], writes=[self.psr[pb]])
            s_ = si % NST; si += 1
            k.op('act', I("copy", out=stg[s_], in_=self.psb(pb)), reads=[self.psr[pb]], writes=[r_stg[s_]])
            k.dma('sp', self.KN[h, :, ts], stg[s_], reads=[r_stg[s_]], writes=[self.r_O2])
    for t in range(28):
        for hg in range(4):
            pb = 3 + (pbi % 2); pbi += 1
            for kk in range(4):
                k.op('pe', I("matmul", self.psb(pb).rearrange("p (h d) -> p h d", h=4), lhsT=ckvT[:, kk, t * 128:(t + 1) * 128], rhs=wkvv[:, kk, hg * 4:(hg + 1) * 4, 128:256], start=(kk == 0), stop=(kk == 3)),
                     reads=[r_wkv, r_ckvT], writes=[self.psr[pb]])
            s_ = si % NST; si += 1
            k.op('act', I("copy", out=stg[s_], in_=self.psb(pb)), reads=[self.psr[pb]], writes=[r_stg[s_]])
            k.dma('sp', self.VM[t * 128:(t + 1) * 128, hg * 4:(hg + 1) * 4, :], stg[s_].rearrange("p (h d) -> p h d", h=4), reads=[r_stg[s_]], writes=[self.r_O2])
    k.barrier()
    ar.release(m0)


def mla_attn(self):
    k, ar = self.k, self.ar
    self.phase()
    _mla_scratch(self)
    m0 = ar.mark()
    kpeT_f = ar.alloc([NTK], BF16); r_kpeT = self.R("kpeTall", sem=True)
    krot_f = ar.alloc([2048], BF16); r_krot = self.R("krotall", sem=True)
    k.op('dve', I("memset", kpeT_f[64:128], 0.0), writes=[r_kpeT])
    k.op('dve', I("memset", krot_f[64:128], 0.0), writes=[r_krot])
    k.dma('sp', kpeT_f[0:64], self.KPET, reads=[self.r_O1], writes=[r_kpeT])
    k.dma('sp', krot_f[0:64], self.KROT, reads=[self.r_O1], writes=[r_krot])
    kpeT, krot = kpeT_f, krot_f
    QN = [ar.alloc([NT], BF16) for _ in range(2)]; r_QN = [self.R(f"aQN{i}", sem=True) for i in range(2)]
    QP = [ar.alloc([NT], BF16) for _ in range(2)]; r_QP = [self.R(f"aQP{i}", sem=True) for i in range(2)]
    QR = [ar.alloc([2048], BF16) for _ in range(2)]; r_QR = [self.R(f"aQR{i}", sem=True) for i in range(2)]
    for i in range(2):
        k.op('dve', I("memset", QP[i][64:128], 0.0), writes=[r_QP[i]])
        k.op('dve', I("memset", QR[i][64:128], 0.0), writes=[r_QR[i]])
    KN = [ar.alloc([NTK], BF16) for _ in range(2)]; r_KN = [self.R(f"aKN{i}", sem=True) for i in range(2)]
    va = [ar.alloc([28, 129], BF16) for _ in range(2)]; r_va = [self.R(f"aV{i}", sem=True) for i in range(2)]
    for i in range(2):
        k.op('pool', I("memset", va[i][:, :, 128:129], 1.0), writes=[r_va[i]])
    PT = [ar.alloc([512], BF16) for _ in range(5)]; r_PT = [self.R(f"aPT{i}") for i in range(5)]
    rec = [ar.alloc([1]) for _ in range(2)]; r_rec = [self.R(f"arec{i}") for i in range(2)]
    ob16 = [ar.alloc([128], BF16) for _ in range(2)]; r_ob16 = [self.R(f"aob{i}") for i in range(2)]
    ostg = [ar.alloc([512], BF16) for _ in range(2)]; r_ostg = [self.R(f"aost{i}", sem=True) for i in range(2)]
    Pacc = [ar.alloc([512]) for _ in range(2)]; r_Pacc = [self.R(f"aPacc{i}") for i in range(2)]
    recb = [ar.alloc([512]) for _ in range(2)]; r_recb = [self.R(f"arecb{i}") for i in range(2)]
    cnt = {"p": 0, "s": 0, "o": 0, "f": 0}
    def load_head(h):
        hi = h % 2
        k.dma('sp', QN[hi], self.QN[h], reads=[self.r_O2], writes=[r_QN[hi]])
        k.dma('sp', QP[hi][0:64], self.QPE[h], reads=[self.r_O2], writes=[r_QP[hi]])
        k.dma('sp', QR[hi][0:64], self.QROT[h], reads=[self.r_O2], writes=[r_QR[hi]])
        k.dma('sp', KN[hi], self.KN[h], reads=[self.r_O2], writes=[r_KN[hi]])
        k.dma('sp', va[hi][:, :, 0:128], self.VM[:, h, :].rearrange("(c p) d -> p c d", p=128), reads=[self.r_O2], writes=[r_va[hi]])

    flat = []
    jobinfo = {}
    jid = 0
    for h in range(16):
        hi = h % 2
        jobs = []
        for sq_ in range(4):
            t0 = sq_ * 256
            keys = [(t0 // 128 + c, kpeT[:, t0 + c * 128:t0 + (c + 1) * 128], QP[hi][:, t0:t0 + 256]) for c in range(2)]
            jobs.append((t0, 256, keys))
        for qb in range(4):
            q0 = 1024 + qb * 512
            lq = slice(qb * 512, (qb + 1) * 512)
            keys = [(8 + c, krot[:, c * 128:(c + 1) * 128], QR[hi][:, lq]) for c in range(16)]
            keys += [(24 + c, kpeT[:, NT + c * 128:NT + (c + 1) * 128], QP[hi][:, q0:q0 + 512]) for c in range(4)]
            jobs.append((q0, 512, keys))
        for (q0, nq, keys) in jobs:
            for n_, (kt, kr, qr) in enumerate(keys):
                flat.append((h, jid, q0, nq, n_, len(keys), kt, kr, qr))
            jid += 1
    NF = len(flat)
    sbs = [(0, 1, 7)[i % 3] for i in range(NF)]
    pis = [i % 5 for i in range(NF)]
    loaded = set()

    def S_ops(i):
        (h, jid_, q0, nq, n_, NK, kt, kr, qr) = flat[i]
        hi = h % 2
        if h not in loaded:
            load_head(h); loaded.add(h)
        sb_ = sbs[i]
        k.op('pe', I("matmul", self.ps[:, sb_, 0:nq], lhsT=KN[hi][:, kt * 128:(kt + 1) * 128], rhs=QN[hi][:, q0:q0 + nq], start=True, stop=False),
             reads=[r_KN[hi], r_QN[hi]], writes=[self.psr[sb_]])
        k.op('pe', I("matmul", self.ps[:, sb_, 0:nq], lhsT=kr, rhs=qr, start=False, stop=True),
             reads=[r_kpeT, r_krot, r_QP[hi], r_QR[hi]], writes=[self.psr[sb_]])

    load_head(0); loaded.add(0)
    S_ops(0); S_ops(1)
    for i in range(NF):
        (h, jid_, q0, nq, n_, NK, kt, kr, qr) = flat[i]
        hi = h % 2
        ji = jid_ % 2
        po = self.ps[:, 2 + ji, 0:nq]
        r_po = self.psr[2 + ji]
        if n_ == 0 and h + 1 < 16 and (h + 1) not in loaded and q0 == 256:
            load_head(h + 1); loaded.add(h + 1)
        if i + 2 < NF:
            S_ops(i + 2)
        sb_, pi = sbs[i], pis[i]
        k.op('act', I("activation", out=PT[pi][:, 0:nq], in_=self.ps[:, sb_, 0:nq], func=AF.Exp), reads=[self.psr[sb_]], writes=[r_PT[pi]])
        k.op('pe', I("matmul", po, lhsT=va[hi][:, kt, 0:128], rhs=PT[pi][:, 0:nq], start=(n_ == 0), stop=(n_ == NK - 1)),
             reads=[r_PT[pi], r_va[hi]], writes=[r_po])
        if n_ == 0:
            k.op('dve', I("tensor_copy", out=Pacc[ji][:, 0:nq], in_=PT[pi][:, 0:nq]), reads=[r_PT[pi]], writes=[r_Pacc[ji]])
        else:
            k.op('dve', I("tensor_tensor", out=Pacc[ji][:, 0:nq], in0=Pacc[ji][:, 0:nq], in1=PT[pi][:, 0:nq], op=ALU.add), reads=[r_Pacc[ji], r_PT[pi]], writes=[r_Pacc[ji]])
        if n_ == NK - 1:
            pden = self.ps[:, 4 + ji, 0:nq]
            k.op('pe', I("matmul", pden, lhsT=self.ones_f, rhs=Pacc[ji][:, 0:nq], start=True, stop=True), reads=[self.r_ones_f, r_Pacc[ji]], writes=[self.psr[4 + ji]])
            k.op('act', I("activation", out=recb[ji][:, 0:nq], in_=pden, func=AF.Ln), reads=[self.psr[4 + ji]], writes=[r_recb[ji]])
            k.op('act', I("activation", out=recb[ji][:, 0:nq], in_=recb[ji][:, 0:nq], func=AF.Exp, scale=-1.0), reads=[r_recb[ji]], writes=[r_recb[ji]])
            k.op('dve', I("tensor_tensor", out=ostg[ji][:, 0:nq], in0=po, in1=recb[ji][:, 0:nq], op=ALU.mult), reads=[r_po, r_recb[ji]], writes=[r_ostg[ji]])
            k.dma('sp', self.OA2[h, :, q0:q0 + nq], ostg[ji][:, 0:nq], reads=[r_ostg[ji]], writes=[self.r_OA2], is_output=bool(self.cfg.get("dump_oa2")))
    k.barrier()
    ar.release(m0)


Builder.mla_inproj = mla_inproj
Builder.mla_proj = mla_proj
Builder.mla_attn = mla_attn


def _host_consts():
    cm = np.ones((128, 1024), np.float32)
    t = np.arange(512)
    cm[:, 0:512][:, t % 64 == 0] = 0
    cm[:, 512:][:, t % 64 == 63] = 0
    s = np.arange(64)[:, None]
    tt = np.arange(64)[None, :]
    tri = np.concatenate([(s <= tt), (s >= tt)], 1).astype(np.float32)
    col = np.arange(64)
    cs_ = np.clip(col - 8, 0, 48)
    ok = (col[None, :] >= cs_[:, None]) & (col[None, :] < cs_[:, None] + 16)
    m = ok.T.astype(np.float32)
    colmask = np.concatenate([m, m], 0)
    P = np.zeros((64, 64), np.float32)
    for base in (0, 32):
        for i in range(16):
            P[base + i, base + i + 16] = -1.0
            P[base + i + 16, base + i] = 1.0
    ropeP = np.ascontiguousarray(P.T)
    tq = np.arange(2048)
    inv = np.power(np.float32(10000.0), -np.arange(0, 32, 2, dtype=np.float32) / np.float32(32)).astype(np.float32)
    ang_r = (tq // 64).astype(np.float32)[:, None] * inv
    ang_c = (tq % 64).astype(np.float32)[:, None] * inv
    ang = np.concatenate([ang_r, ang_r, ang_c, ang_c], 1).T
    ropecs = np.ascontiguousarray(np.stack([np.cos(ang), np.sin(ang)], 1).astype(np.float32))
    return {"ident": np.eye(128, dtype=np.float32), "cmask": cm, "trimask": tri, "colmask": colmask, "ropeP": ropeP, "ropecs": ropecs}


def build_program(cfg=None):
    B = Builder(cfg or {})
    B.inp("cvec", [2, 2048]); B.inp("ada_w", [2, 2048, 12288]); B.inp("ada_b", [2, 12288]); B.inp("norm_g", [2, 4, 2048])
    B.inp("w_in_even", [2048, 8192]); B.inp("cmask", [128, 1024]); B.inp("trimask", [64, 128]); B.inp("hgn", [1, 128]); B.inp("lbrows", [2, 3, 1024])
    B.inp("st_f", [8, 128, 128]); B.inp("st_b", [8, 128, 128]); B.inp("rpbr", [8, 15, 8192]); B.inp("colmask", [128, 64])
    B.inp("cnak", [512, 8, 128]); B.inp("cnav", [512, 8, 128]); B.inp("w_out_even", [2048, 2048])
    B.inp("mlp_w1", [2, 2048, 8192]); B.inp("mlp_w2", [2, 8192, 2048])
    B.inp("w_in_odd", [2048, 1088]); B.inp("mla_qg", [1, 512]); B.inp("mla_kvg", [1, 512]); B.inp("w_uq", [512, 3072]); B.inp("w_ukv", [512, 4096]); B.inp("w_out_odd", [2048, 2048])
    B.inp("cckv", [512, 512]); B.inp("ckpe", [512, 64]); B.inp("ropeP", [64, 64]); B.inp("ropecs", [64, 2, 2048])
    X0 = B.inp("xin", [NT, 2048])
    X4 = B.out("yout", [NT, 2048])
    B.out("nsf", [4, 8, 128, 128]); B.out("nsb", [4, 8, 128, 128]); B.out("nak", [1024, 1024]); B.out("nav", [1024, 1024])
    B.out("nckv", [1024, 512]); B.out("nkpe", [1024, 64])
    X1 = B.scr("X1", [NT, 2048]); X2 = B.scr("X2", [NT, 2048]); X3 = B.scr("X3", [NT, 2048])
    rX = [[B.R(f"X{i}_{t}") for t in range(24)] for i in range(5)]
    B.consts()
    B.modulation(0)
    B.modulation(1)
    B.even_inproj(X0, rX[0])
    B.hgrn_phase()
    B.na_phase()
    B.outproj_phase(0, B.OA, B.r_OA, B.din["w_out_even"], X0, rX[0], X1, rX[1])
    B.mlp_phase(0, X1, rX[1], X2, rX[2])
    B.mla_inproj(X2, rX[2])
    B.mla_proj()
    B.mla_attn()
    B.outproj_phase(1, B.OA2, B.r_OA2, B.din["w_out_odd"], X2, rX[2], X3, rX[3])
    B.mlp_phase(1, X3, rX[3], X4, rX[4], out_is_output=True)
    B.k.emit()
    return B


def kernel(x_prompt, x_sample, state_hgrn_fwd, state_hgrn_bwd, cache_na_k, cache_na_v, cache_mla_ckv, cache_mla_kpe,
           c, c_ctx, ada_w, ada_b, norm_g, hgrn_lb_fwd, hgrn_lb_bwd, w_in_even, hgrn_norm_g, na_rpb, w_out_even, w_in_odd,
           mla_q_norm_g, w_uq, mla_kv_norm_g, w_ukv, w_out_odd, mlp_w1, mlp_w2):
    f32 = lambda a: np.ascontiguousarray(np.asarray(a, dtype=np.float32))
    NCORE = 8
    B = build_program()
    hc = _host_consts()
    rpb = f32(na_rpb)[0]
    rp = np.zeros((8, 15, 128), np.float32)
    rp[:, :, 48:79] = rpb[:, :, ::-1]
    rpbr = np.ascontiguousarray(np.broadcast_to(rp[:, :, None, :], (8, 15, 64, 128))).reshape(8, 15, 8192)
    shared = {
        "ada_w": f32(ada_w), "ada_b": f32(ada_b), "norm_g": f32(norm_g), "w_in_even": f32(w_in_even)[0],
        "hgn": f32(hgrn_norm_g)[0:1], "lbrows": np.ascontiguousarray(np.stack([f32(hgrn_lb_fwd), f32(hgrn_lb_bwd)], 0)),
        "rpbr": rpbr, "w_out_even": f32(w_out_even)[0], "mlp_w1": f32(mlp_w1), "mlp_w2": f32(mlp_w2),
        "w_in_odd": f32(w_in_odd)[0], "mla_qg": f32(mla_q_norm_g)[0:1], "mla_kvg": f32(mla_kv_norm_g)[0:1],
        "w_uq": f32(w_uq)[0], "w_ukv": f32(w_ukv)[0], "w_out_odd": f32(w_out_odd)[0],
    }
    shared.update(hc)
    xp, xs = f32(x_prompt), f32(x_sample)
    sf, sb_ = f32(state_hgrn_fwd), f32(state_hgrn_bwd)
    cnk, cnv = f32(cache_na_k), f32(cache_na_v)
    cck, ckp = f32(cache_mla_ckv), f32(cache_mla_kpe)
    cc, cctx = f32(c), f32(c_ctx)
    in_maps = []
    for i in range(NCORE):
        m = dict(shared)
        m["xin"] = np.ascontiguousarray(np.concatenate([xp[4 * i:4 * i + 4].reshape(1024, 2048), xs[i]], 0))
        m["cvec"] = np.ascontiguousarray(np.stack([cctx, cc[i]], 0))
        m["st_f"] = np.ascontiguousarray(sf[i, 0]); m["st_b"] = np.ascontiguousarray(sb_[i, 0])
        m["cnak"] = np.ascontiguousarray(cnk[i, 0]); m["cnav"] = np.ascontiguousarray(cnv[i, 0])
        m["cckv"] = np.ascontiguousarray(cck[i, 0]); m["ckpe"] = np.ascontiguousarray(ckp[i, 0])
        in_maps.append(m)
    res = run_bass_kernel_spmd(B.nc, in_maps, core_ids=list(range(NCORE)))
    R_ = res.results
    y_prompt = np.concatenate([np.asarray(r["yout"])[:1024].reshape(4, 256, 2048) for r in R_], 0).astype(np.float32)
    y_sample = np.stack([np.asarray(r["yout"])[1024:] for r in R_], 0).astype(np.float32)
    nsf = np.concatenate([np.asarray(r["nsf"]).reshape(4, 1, 8, 128, 128) for r in R_], 0).astype(np.float32)
    nsb = np.concatenate([np.asarray(r["nsb"]).reshape(4, 1, 8, 128, 128) for r in R_], 0).astype(np.float32)
    nak = np.concatenate([np.asarray(r["nak"]).reshape(4, 1, 256, 8, 128) for r in R_], 0).astype(np.float32)
    nav = np.concatenate([np.asarray(r["nav"]).reshape(4, 1, 256, 8, 128) for r in R_], 0).astype(np.float32)
    nckv = np.concatenate([np.asarray(r["nckv"]).reshape(4, 1, 256, 512) for r in R_], 0).astype(np.float32)
    nkpe = np.concatenate([np.asarray(r["nkpe"]).reshape(4, 1, 256, 64) for r in R_], 0).astype(np.float32)
    return (y_prompt, y_sample, nsf, nsb, nak, nav, nckv, nkpe)
```

```python
import numpy as np
import concourse.bass as bass
import concourse.mybir as mybir
from concourse.bass_utils import run_bass_kernel_spmd

F32 = mybir.dt.float32
BF16 = mybir.dt.bfloat16
I32 = mybir.dt.int32
AF = mybir.ActivationFunctionType
ALU = mybir.AluOpType
AX = mybir.AxisListType

COMPUTE = ('pe', 'act', 'dve', 'pool')
ALLENG = ('pe', 'act', 'dve', 'pool', 'sp')


class DmaSem:
    __slots__ = ('name', 'count', 'handle')

    def __init__(self, name):
        self.name = name
        self.count = 0
        self.handle = None


class Res:
    __slots__ = ('name', 'w_ops', 'w_dma', 'r_ops', 'r_dma', 'had_read', 'sem', '_stsem', '_stphase')

    def __init__(self, name, sem=None):
        self.name = name
        self.w_ops = {}
        self.w_dma = {}
        self.r_ops = {}
        self.r_dma = {}
        self.had_read = False
        self.sem = sem


class Op:
    __slots__ = ('eng', 'fn', 'dep_ops', 'dep_dma', 'idx', 'signal', 'dma_sem')

    def __init__(self, eng, fn):
        self.eng = eng
        self.fn = fn
        self.dep_ops = {}
        self.dep_dma = {}
        self.idx = -1
        self.signal = False
        self.dma_sem = None


class K:
    def __init__(self, nc):
        self.nc = nc
        self.ops = {e: [] for e in ALLENG}
        self.known_ops = {e: {f: -1 for f in ALLENG} for e in ALLENG}
        self.known_dma = {e: {} for e in ALLENG}
        self.dma_sems = []
        self.out_sems = set()
        self.nres = 0

    def res(self, name=None):
        self.nres += 1
        return Res(name or f"r{self.nres}")

    def dsem(self, name):
        s = DmaSem(name)
        self.dma_sems.append(s)
        return s

    def _collect(self, eng, reads, writes):
        raw_ops, raw_dma, war_ops, war_dma = {}, {}, {}, {}
        for r in reads:
            for e, i in r.w_ops.items():
                if raw_ops.get(e, -1) < i:
                    raw_ops[e] = i
            for s, v in r.w_dma.items():
                if raw_dma.get(s, 0) < v:
                    raw_dma[s] = v
        for w in writes:
            for e, i in w.r_ops.items():
                if war_ops.get(e, -1) < i:
                    war_ops[e] = i
            for s, v in w.r_dma.items():
                if war_dma.get(s, 0) < v:
                    war_dma[s] = v
        dep_ops = {}
        for e, i in raw_ops.items():
            if e == eng and eng in ('pe', 'sp'):
                continue
            dep_ops[e] = i
        for e, i in war_ops.items():
            if e == eng:
                continue
            if dep_ops.get(e, -1) < i:
                dep_ops[e] = i
        dep_dma = dict(raw_dma)
        for s, v in war_dma.items():
            if dep_dma.get(s, 0) < v:
                dep_dma[s] = v
        ko = self.known_ops[eng]
        kd = self.known_dma[eng]
        dep_ops = {e: i for e, i in dep_ops.items() if ko[e] < i}
        dep_dma = {s: v for s, v in dep_dma.items() if kd.get(s, 0) < v}
        for e, i in dep_ops.items():
            ko[e] = i
        for s, v in dep_dma.items():
            kd[s] = v
        return dep_ops, dep_dma

    def _register(self, op, reads, writes, dma_evt=None):
        eng, idx = op.eng, op.idx
        for r in reads:
            r.had_read = True
            if dma_evt is None:
                r.r_ops[eng] = idx
            else:
                r.r_dma[dma_evt[0]] = dma_evt[1]
        for w in writes:
            if w.had_read:
                w.w_ops = {}
                w.w_dma = {}
                w.r_ops = {}
                w.r_dma = {}
                w.had_read = False
            if dma_evt is None:
                w.w_ops[eng] = idx
            else:
                w.w_dma[dma_evt[0]] = dma_evt[1]

    def op(self, eng, fn, reads=(), writes=()):
        o = Op(eng, fn)
        o.dep_ops, o.dep_dma = self._collect(eng, reads, writes)
        o.idx = len(self.ops[eng])
        self.ops[eng].append(o)
        self._register(o, reads, writes)
        return o

    def dma(self, q, out, in_, reads=(), writes=(), sem=None, is_output=False, **kw):
        if sem is None:
            for r in list(writes) + list(reads):
                if r.sem is not None:
                    sem = r.sem
                    break
        assert sem is not None, "dma needs a semaphore-bearing resource"
        o = Op(q, lambda e, out=out, in_=in_, kw=kw: e.dma_start(out=out, in_=in_, **kw))
        o.dep_ops, o.dep_dma = self._collect(q, reads, writes)
        o.idx = len(self.ops[q])
        self.ops[q].append(o)
        sem.count += 16
        o.dma_sem = sem
        self._register(o, reads, writes, dma_evt=(sem, sem.count))
        if is_output:
            self.out_sems.add(sem)
        return o

    def barrier(self):
        o = Op('sp', lambda e: e.nop())
        for e in COMPUTE:
            n = len(self.ops[e])
            if n > 0 and self.known_ops['sp'][e] < n - 1:
                last = n - 1
                while last >= 0 and (self.ops[e][last].fn is None or self.ops[e][last].dma_sem is not None):
                    last -= 1
                if last >= 0 and self.known_ops['sp'][e] < last:
                    o.dep_ops[e] = last
        for s in self.dma_sems:
            if self.known_dma['sp'].get(s, 0) < s.count:
                o.dep_dma[s] = s.count
        o.idx = len(self.ops['sp'])
        self.ops['sp'].append(o)
        for e in COMPUTE:
            w = Op(e, None)
            w.dep_ops = {'sp': o.idx}
            w.idx = len(self.ops[e])
            self.ops[e].append(w)
        for e in ALLENG:
            for f in ALLENG:
                self.known_ops[e][f] = len(self.ops[f]) - 1
            for s in self.dma_sems:
                self.known_dma[e][s] = s.count
            self.known_ops[e]['sp'] = o.idx

    def emit(self):
        nc = self.nc
        self.barrier()
        sig = {e: set() for e in ALLENG}
        for e in ALLENG:
            for o in self.ops[e]:
                for f, i in o.dep_ops.items():
                    sig[f].add(i)
        val = {}
        for e in ALLENG:
            val[e] = {i: k + 1 for k, i in enumerate(sorted(sig[e]))}
        self._cm = nc.cleanup_on_exit()
        self._cm.__enter__()
        esem = {e: nc.alloc_semaphore(f"eng_{e}") for e in ALLENG}
        for s in self.dma_sems:
            if s.count > 0:
                s.handle = nc.alloc_semaphore(f"d_{s.name}")
        engobj = {'pe': 'tensor', 'act': 'scalar', 'dve': 'vector', 'pool': 'gpsimd', 'sp': 'sync'}
        stats = {}

        def run(e):
            def body(eng):
                nw = 0
                for o in self.ops[e]:
                    for f, i in o.dep_ops.items():
                        eng.wait_ge(esem[f], val[f][i])
                        nw += 1
                    for s, v in o.dep_dma.items():
                        eng.wait_ge(s.handle, v)
                        nw += 1
                    if o.fn is None:
                        continue
                    ins = o.fn(eng)
                    if o.dma_sem is not None:
                        ins.then_inc(o.dma_sem.handle, 16)
                    elif o.idx in val[e]:
                        ins.then_inc(esem[e], 1)
                stats[e] = (len(self.ops[e]), nw)
            return body

        with nc.Block() as block:
            for e in ALLENG:
                getattr(block, engobj[e])(run(e))
        nc.all_engine_barrier()
        self._cm.__exit__(None, None, None)
        self.stats = stats
        return stats


import math

D_MODEL = 2048
NT = 3072
NCH = 16
D_FF = 8192
EPS = 1e-6


def I(method, *a, **kw):
    return lambda e: getattr(e, method)(*a, **kw)


class Arena:
    def __init__(self, nc, nbytes):
        self.t = nc.alloc_sbuf_tensor("arena", [128, nbytes // 4], F32)
        self.n = nbytes
        self.top = 0
        self.peak = 0

    def alloc(self, shape, dt=F32, parts=128):
        esz = 4 if dt == F32 else 2
        n = 1
        for s in shape:
            n *= s
        nb = (n * esz + 63) // 64 * 64
        assert self.top + nb <= self.n, f"arena overflow {self.top}+{nb}>{self.n}"
        o4 = self.top // 4
        a = self.t[0:parts, o4:o4 + nb // 4]
        self.top += nb
        self.peak = max(self.peak, self.top)
        if dt != F32:
            a = a.bitcast(dt)
        a = a[:, 0:n]
        if len(shape) == 2:
            a = a.rearrange("p (a b) -> p a b", a=shape[0])
        elif len(shape) == 3:
            a = a.rearrange("p (a b c) -> p a b c", a=shape[0], b=shape[1])
        return a

    def mark(self):
        return self.top

    def release(self, m):
        self.top = m


class Builder:
    def __init__(self, cfg):
        self.cfg = cfg
        nc = self.nc = bass.Bass("TRN2", target_bir_lowering=False)
        self.k = K(nc)
        self.din = {}
        self.dout = {}
        self.ar = Arena(nc, 204800)
        self.ps = nc.alloc_psum_tensor("ps", [128, 8, 512], F32)
        self.psr = [self.k.res(f"psum{b}") for b in range(8)]
        self.rr = {}

    def inp(self, name, shape, dt=F32):
        self.din[name] = self.nc.dram_tensor(name, list(shape), dt, kind="ExternalInput").ap()
        return self.din[name]

    def out(self, name, shape, dt=F32):
        self.dout[name] = self.nc.dram_tensor(name, list(shape), dt, kind="ExternalOutput").ap()
        return self.dout[name]

    def scr(self, name, shape, dt=F32):
        return self.nc.dram_tensor(name, list(shape), dt, kind="Internal").ap()

    def R(self, name, sem=False, sw=False):
        r = self.k.res(name)
        if not hasattr(self, "sem_pool"):
            self.sem_pool = []
            self.sem_i = 0
            self.sw_pool = []
            self.sw_i = 0
        if sw:
            if self.sw_i >= len(self.sw_pool):
                self.sw_pool.append(self.k.dsem(f"w{len(self.sw_pool)}"))
            r.sem = self.sw_pool[self.sw_i]
            self.sw_i += 1
        elif sem:
            if self.sem_i >= len(self.sem_pool):
                self.sem_pool.append(self.k.dsem(f"p{len(self.sem_pool)}"))
            r.sem = self.sem_pool[self.sem_i]
            self.sem_i += 1
        return r

    def phase(self):
        self.sem_i = self.sem_keep
        self.sw_i = 0
        self.phase_id = getattr(self, "phase_id", 0) + 1

    def keep_sems(self):
        self.sem_keep = getattr(self, "sem_i", 0)

    def store(self, out, in_, src_res, reads, writes, is_output=False):
        if not hasattr(src_res, "_stsem") or src_res._stphase != self.phase_id:
            src_res._stsem = self.R("stsem", sw=True).sem
            src_res._stphase = self.phase_id
        return self.k.dma('pool', out, in_, reads=reads, writes=writes, sem=src_res._stsem, is_output=is_output)

    def psb(self, b):
        return self.ps[:, b, :]

    def psb16(self, b):
        return self.ps[:, b, :].bitcast(BF16)

    def consts(self):
        k, ar = self.k, self.ar
        idf = self.inp("ident", [128, 128])
        self.ident_f = ar.alloc([128])
        self.ident_b = ar.alloc([128], BF16)
        self.ones_b = ar.alloc([128], BF16)
        self.r_ident_f = self.R("ident_f", sem=True)
        self.r_ident_b = self.R("ident_b")
        self.r_ones = self.R("ones_b")
        k.dma('sp', self.ident_f, idf, writes=[self.r_ident_f])
        k.op('dve', lambda e: e.tensor_copy(out=self.ident_b, in_=self.ident_f), reads=[self.r_ident_f], writes=[self.r_ident_b])
        k.op('dve', lambda e: e.memset(self.ones_b, 1.0), writes=[self.r_ones])
        self.ones_f = ar.alloc([128]); self.r_ones_f = self.R("ones_f")
        k.op('dve', lambda e: e.memset(self.ones_f, 1.0), writes=[self.r_ones_f])
        self.AB = [[ar.alloc([16, 4]) for s in range(2)] for l in range(2)]
        self.r_AB = [[self.R(f"AB{l}{s}") for s in range(2)] for l in range(2)]
        self.GROW = [[self.scr(f"grow{l}{s}", [2, D_MODEL]) for s in range(2)] for l in range(2)]
        self.r_GROW = [[self.R(f"grow{l}{s}") for s in range(2)] for l in range(2)]
        self.keep_sems()

    def modulation(self, l):
        k, ar, nc = self.k, self.ar, self.nc
        self.phase()
        m0 = ar.mark()
        cvec = self.din["cvec"]
        ada_w = self.din["ada_w"]
        ada_b = self.din["ada_b"]
        norm_g = self.din["norm_g"]
        crow = ar.alloc([D_MODEL], F32, parts=2)
        tmp = ar.alloc([D_MODEL], F32, parts=2)
        scb = ar.alloc([D_MODEL], BF16, parts=2)
        scT = ar.alloc([16, 2], BF16)
        mod = ar.alloc([6, D_MODEL], F32, parts=2)
        gn = ar.alloc([4, D_MODEL], F32, parts=2)
        wb = [ar.alloc([16, 512], BF16) for _ in range(3)]
        r_crow = self.R("crow", sem=True); r_tmp = self.R("tmp"); r_scb = self.R("scb"); r_scT = self.R("scT")
        r_mod = self.R("modrow", sem=True); r_gn = self.R("gn", sem=True)
        r_wb = [self.R(f"modw{i}", sw=True) for i in range(3)]
        k.dma('sp', crow, cvec, writes=[r_crow])
        k.dma('sp', mod, ada_b[l].rearrange("(o s d) -> o s d", o=1, s=6).to_broadcast([2, 6, D_MODEL]), writes=[r_mod])
        k.dma('sp', gn, norm_g[l].rearrange("(o s) d -> o s d", o=1).to_broadcast([2, 4, D_MODEL]), writes=[r_gn])
        if self.cfg.get("mod_stop") == 1:
            k.dma('sp', self.dout["dbg_mod0"], mod, reads=[r_mod], is_output=True); k.barrier(); ar.release(m0); return
        k.op('act', lambda e: e.activation(out=tmp, in_=crow, func=AF.Exp, scale=-1.0), reads=[r_crow], writes=[r_tmp])
        k.op('dve', lambda e: e.tensor_scalar_add(out=tmp, in0=tmp, scalar1=1.0), reads=[r_tmp], writes=[r_tmp])
        k.op('dve', lambda e: e.reciprocal(out=tmp, in_=tmp), reads=[r_tmp], writes=[r_tmp])
        k.op('dve', lambda e: e.tensor_tensor(out=scb, in0=tmp, in1=crow, op=ALU.mult), reads=[r_tmp, r_crow], writes=[r_scb])
        if self.cfg.get("mod_stop") == 2:
            k.dma('sp', self.dout["dbg_mod0"], mod, reads=[r_mod], is_output=True); k.barrier(); ar.release(m0); return
        pst = self.psb16(7)
        for c in range(16):
            k.op('pe', lambda e, c=c: e.transpose(out=pst[:, c * 2:c * 2 + 2], in_=scb[0:2, c * 128:(c + 1) * 128], identity=self.ident_b[0:2, 0:2]),
                 reads=[r_scb, self.r_ident_b], writes=[self.psr[7]])
        k.op('dve', lambda e: e.tensor_copy(out=scT.rearrange("p a b -> p (a b)"), in_=pst[:, 0:32]), reads=[self.psr[7]], writes=[r_scT])
        if self.cfg.get("mod_stop") == 3:
            k.dma('sp', self.dout["dbg_mod0"], mod, reads=[r_mod], is_output=True); k.barrier(); ar.release(m0); return
        for j in range(24):
            b = j % 3
            k.dma('pool', wb[b], ada_w[l, :, j * 512:(j + 1) * 512].rearrange("(k p) n -> p k n", p=128), writes=[r_wb[b]])
            pb = j % 2
            for kk in range(16):
                k.op('pe', lambda e, kk=kk, b=b, pb=pb: e.matmul(self.ps[0:2, pb, :], lhsT=scT[:, kk, :], rhs=wb[b][:, kk, :], start=(kk == 0), stop=(kk == 15)),
                     reads=[r_scT, r_wb[b]], writes=[self.psr[pb]])
            s_, off = divmod(j * 512, D_MODEL)
            k.op('dve', lambda e, pb=pb, s_=s_, off=off: e.tensor_tensor(out=mod[:, s_, off:off + 512], in0=self.ps[0:2, pb, :], in1=mod[:, s_, off:off + 512], op=ALU.add),
                 reads=[self.psr[pb], r_mod], writes=[r_mod])
        if self.cfg.get("mod_stop") == 4:
            k.dma('sp', self.dout["dbg_mod0"], mod, reads=[r_mod], is_output=True); k.barrier(); ar.release(m0); return
        for s in range(2):
            o = 3 * s
            k.op('dve', lambda e, o=o, s=s: e.scalar_tensor_tensor(out=mod[:, o + 1, :], in0=mod[:, o + 1, :], scalar=1.0, in1=gn[:, 2 * s, :], op0=ALU.add, op1=ALU.mult),
                 reads=[r_mod, r_gn], writes=[r_mod])
            k.op('dve', lambda e, o=o, s=s: e.tensor_tensor(out=mod[:, o + 2, :], in0=mod[:, o + 2, :], in1=gn[:, 2 * s + 1, :], op=ALU.mult),
                 reads=[r_mod, r_gn], writes=[r_mod])
        if self.cfg.get("mod_stop") == 5:
            k.dma('sp', self.dout["dbg_mod0"], mod, reads=[r_mod], is_output=True); k.barrier(); ar.release(m0); return
        for s in range(2):
            o = 3 * s
            k.dma('sp', self.GROW[l][s], mod[:, o + 2, :], reads=[r_mod], writes=[self.r_GROW[l][s]])
            if self.cfg.get("mod_stop") == 6:
                k.dma('sp', self.dout["dbg_mod0"], mod, reads=[r_mod], is_output=True); k.barrier(); ar.release(m0); return
            pf = self.psb(6)
            for c in range(16):
                for ab in range(2):
                    k.op('pe', lambda e, c=c, ab=ab, o=o: e.transpose(out=pf[:, c * 4 + 2 * ab:c * 4 + 2 * ab + 2], in_=mod[0:2, o + 1 - ab, c * 128:(c + 1) * 128], identity=self.ident_f[0:2, 0:2]),
                         reads=[r_mod, self.r_ident_f], writes=[self.psr[6]])
            if self.cfg.get("mod_stop") == 7:
                k.dma('sp', self.dout["dbg_mod0"], mod, reads=[r_mod], is_output=True); k.barrier(); ar.release(m0); return
            k.op('dve', lambda e, s=s: e.tensor_copy(out=self.AB[l][s].rearrange("p a b -> p (a b)"), in_=pf[:, 0:64]), reads=[self.psr[6]], writes=[self.r_AB[l][s]])
        if self.cfg.get("mod_stop") == 8:
            k.dma('sp', self.dout["dbg_mod0"], mod, reads=[r_mod], is_output=True); k.barrier(); ar.release(m0); return
        if self.cfg.get("dump_mod"):
            k.dma('sp', self.dout[f"dbg_mod{l}"], mod, reads=[r_mod], is_output=True)
        k.barrier()
        ar.release(m0)

    def pre_tile(self, xt, r_xt, hT, r_hT, col0, AB, r_AB, g, W):
        k = self.k
        junk, xn, st = W["junk"], W["xn"], W["st"]
        r_junk, r_xn, r_st = W["r_junk"], W["r_xn"], W["r_st"]
        ps_ = self.cfg.get('pre_stop', 99)
        if ps_ == 0: return
        k.op('dve', lambda e: e.memset(st[:, 0:1], 0.0), writes=[r_st])
        k.op('act', lambda e: e.activation(out=junk, in_=xt, func=AF.Square, accum_out=st[:, 0:1]), reads=[r_xt, r_st], writes=[r_junk, r_st])
        if ps_ == 1: return
        k.op('act', lambda e: e.activation(out=st[:, 1:2], in_=st[:, 0:1], func=AF.Ln, scale=1.0 / D_MODEL, bias=EPS), reads=[r_st], writes=[r_st])
        k.op('act', lambda e: e.activation(out=st[:, 2:3], in_=st[:, 1:2], func=AF.Exp, scale=-0.5), reads=[r_st], writes=[r_st])
        if ps_ == 2: return
        k.op('dve', lambda e: e.tensor_scalar_mul(out=xn, in0=xt, scalar1=st[:, 2:3]), reads=[r_xt, r_st], writes=[r_xn])
        if ps_ == 3: return
        for half in range(2):
            pb = 6 + half
            pst = self.psb16(pb)
            for c8 in range(8):
                c = half * 8 + c8
                k.op('pe', lambda e, c=c, c8=c8, pst=pst: e.transpose(out=pst[:, c8 * 128:(c8 + 1) * 128], in_=xn[:, c * 128:(c + 1) * 128], identity=self.ident_b),
                     reads=[r_xn, self.r_ident_b], writes=[self.psr[pb]])
            if ps_ == 4: return
            for c8 in range(8):
                c = half * 8 + c8
                if ps_ == 5 and c8 == 1: return
                if ps_ == 6 and c8 == 2: return
                src = pst[:, c8 * 128:(c8 + 1) * 128]
                dst = hT[:, c, col0:col0 + 128]
                if True:
                    k.op('act', lambda e, src=src, dst=dst, c=c: e.activation(out=dst, in_=src, func=AF.Identity, scale=AB[:, c, g:g + 1], bias=AB[:, c, 2 + g:3 + g]),
                         reads=[self.psr[pb], r_AB], writes=[r_hT])
                else:
                    k.op('dve', lambda e, src=src, dst=dst, c=c: e.tensor_scalar(out=dst, in0=src, scalar1=AB[:, c, g:g + 1], scalar2=AB[:, c, 2 + g:3 + g], op0=ALU.mult, op1=ALU.add),
                         reads=[self.psr[pb], r_AB], writes=[r_hT])

    def post_tile(self, xt, r_xt, ypieces, Grep, r_G, W):
        k = self.k
        junk, st, t = W["junk32"], W["st2"], W["t"]
        r_junk, r_st, r_t = W["r_junk32"], W["r_st2"], W["r_t"]
        npc = len(ypieces)
        k.op('dve', lambda e: e.memset(st[:, 0:4], 0.0), writes=[r_st])
        off = 0
        for i, (yp, r_y, n) in enumerate(ypieces):
            k.op('act', lambda e, yp=yp, i=i, n=n: e.activation(out=junk[:, 0:n], in_=yp, func=AF.Square, accum_out=st[:, i:i + 1]),
                 reads=[r_y, r_st], writes=[r_junk, r_st])
        k.op('dve', lambda e: e.tensor_reduce(out=st[:, 4:5], in_=st[:, 0:4], axis=AX.X, op=ALU.add), reads=[r_st], writes=[r_st])
        k.op('act', lambda e: e.activation(out=st[:, 5:6], in_=st[:, 4:5], func=AF.Ln, scale=1.0 / D_MODEL, bias=EPS), reads=[r_st], writes=[r_st])
        k.op('act', lambda e: e.activation(out=st[:, 6:7], in_=st[:, 5:6], func=AF.Exp, scale=-0.5), reads=[r_st], writes=[r_st])
        off = 0
        for i, (yp, r_y, n) in enumerate(ypieces):
            k.op('dve', lambda e, yp=yp, off=off, n=n: e.scalar_tensor_tensor(out=t[:, off:off + n], in0=yp, scalar=st[:, 6:7], in1=Grep[:, off:off + n], op0=ALU.mult, op1=ALU.mult),
                 reads=[r_y, r_st, r_G], writes=[r_t])
            off += n
        k.op('dve', lambda e: e.tensor_tensor(out=xt, in0=xt, in1=t, op=ALU.add), reads=[r_xt, r_t], writes=[r_xt])

    def load_grep(self, l, s):
        k, ar = self.k, self.ar
        G = [ar.alloc([D_MODEL]) for g in range(2)]
        r_G = [self.R(f"Grep{g}", sem=True) for g in range(2)]
        for g in range(2):
            k.dma('sp', G[g], self.GROW[l][s][g:g + 1, :].to_broadcast([128, D_MODEL]), reads=[self.r_GROW[l][s]], writes=[r_G[g]])
        return G, r_G

    def work_bufs(self):
        ar = self.ar
        W = {}
        W["junk"] = ar.alloc([D_MODEL], BF16); W["r_junk"] = self.R("junk")
        W["xn"] = ar.alloc([D_MODEL], BF16); W["r_xn"] = self.R("xn")
        W["st"] = ar.alloc([8]); W["r_st"] = self.R("st")
        W["st2"] = ar.alloc([8]); W["r_st2"] = self.R("st2")
        W["junk32"] = W["junk"]; W["r_junk32"] = W["r_junk"]
        W["t"] = ar.alloc([D_MODEL]); W["r_t"] = self.R("t")
        return W

    def mlp_phase(self, l, Xin, r_Xin, Xout, r_Xout, out_is_output=False):
        k, ar = self.k, self.ar
        self.phase()
        m0 = ar.mark()
        w1 = self.din["mlp_w1"][l]
        w2 = self.din["mlp_w2"][l]
        if not hasattr(self, "W1C"):
            self.W1C = self.scr("W1C", [16, 128, 16 * 512], BF16)
            self.W2C = self.scr("W2C", [32, 128, 4 * 1024], BF16)
        r_W1C = [self.R(f"w1c{j}") for j in range(16)]
        r_W2C = [self.R(f"w2c{j}") for j in range(32)]
        G, r_G = self.load_grep(l, 1)
        W = self.work_bufs()
        xt = [ar.alloc([D_MODEL]) for _ in range(2)]
        r_xt = [self.R(f"xt{i}", sem=True) for i in range(2)]
        hTs = [ar.alloc([16, 512], BF16) for _ in range(2)]; r_hTs = [self.R(f"hT{i}") for i in range(2)]
        ysbs = [h_.rearrange("p a b -> p (a b)").bitcast(F32).rearrange("p (s n) -> p s n", s=4) for h_ in hTs]
        aT = ar.alloc([64, 512], BF16); r_aT = self.R("aT")
        w1b = [ar.alloc([16, 512], BF16) for _ in range(2)]
        r_w1b = [self.R(f"w1b{i}", sw=True) for i in range(2)]
        w2b = [ar.alloc([4, 1024], BF16) for _ in range(2)]
        r_w2b = [self.R(f"w2b{i}", sw=True) for i in range(2)]
        w1s = [self.R(f"w1s{i}", sem=True).sem for i in range(2)]
        w2s = [self.R(f"w2s{i}", sem=True).sem for i in range(2)]
        rt = [ar.alloc([512]) for _ in range(2)]
        r_rt = [self.R(f"rt{i}") for i in range(2)]
        AB, r_AB = self.AB[l][1], self.r_AB[l][1]
        xi = 0
        w1i = 0
        w2i = 0
        NBLK = self.cfg.get('nblk', 6)
        xi_ = [0]

        def pre_one(blk_, sub_):
            g_ = 0 if blk_ < 2 else 1
            tile_ = blk_ * 4 + sub_
            b_ = xi_[0] % 2; xi_[0] += 1
            k.dma('sp', xt[b_], Xin[tile_ * 128:(tile_ + 1) * 128, :], reads=[r_Xin[tile_]], writes=[r_xt[b_]])
            self.pre_tile(xt[b_], r_xt[b_], hTs[blk_ % 2], r_hTs[blk_ % 2], sub_ * 128, AB, r_AB, g_, W)

        for sub in range(4):
            pre_one(0, sub)
        pending = []
        for blk in range(NBLK):
            g = 0 if blk < 2 else 1
            hT, r_hT, ysb = hTs[blk % 2], r_hTs[blk % 2], ysbs[blk % 2]
            if self.cfg.get("mlp_stop") == 1:
                k.barrier(); ar.release(m0); return
            for j in range(16):
                wb_i = w1i % 2; w1i += 1
                if blk == 0:
                    k.dma('pool', w1b[wb_i], w1[:, j * 512:(j + 1) * 512].rearrange("(k p) n -> p k n", p=128), writes=[r_w1b[wb_i]])
                    k.dma('sp', self.W1C[j], w1b[wb_i].rearrange("p a b -> p (a b)"), reads=[r_w1b[wb_i]], writes=[r_W1C[j]], sem=w1s[wb_i])
                else:
                    k.dma('pool', w1b[wb_i].rearrange("p a b -> p (a b)"), self.W1C[j], reads=[r_W1C[j]], writes=[r_w1b[wb_i]])
                for cc in range(4):
                    c = j * 4 + cc
                    pb = c % 2
                    for kk in range(16):
                        k.op('pe', lambda e, kk=kk, cc=cc, wb_i=wb_i, pb=pb, hT=hT: e.matmul(self.psb(pb), lhsT=w1b[wb_i][:, kk, cc * 128:(cc + 1) * 128], rhs=hT[:, kk, :], start=(kk == 0), stop=(kk == 15)),
                             reads=[r_w1b[wb_i], r_hT], writes=[self.psr[pb]])
                    k.op('act', lambda e, pb=pb: e.activation(out=rt[pb], in_=self.psb(pb), func=AF.Relu), reads=[self.psr[pb]], writes=[r_rt[pb]])
                    k.op('dve', lambda e, pb=pb, c=c: e.tensor_tensor(out=aT[:, c, :], in0=rt[pb], in1=rt[pb], op=ALU.mult), reads=[r_rt[pb]], writes=[r_aT])
                if j < 3 and pending:
                    post_one(*pending.pop(0))
                if j % 4 == 3 and blk + 1 < NBLK:
                    pre_one(blk + 1, j // 4)
            if self.cfg.get("mlp_stop") == 2:
                k.barrier(); ar.release(m0); return
            for half in range(2):
                for j in range(16):
                    wb_i = w2i % 2; w2i += 1
                    if blk == 0:
                        k.dma('pool', w2b[wb_i], w2[j * 512:(j + 1) * 512, half * 1024:(half + 1) * 1024].rearrange("(c p) n -> p c n", p=128), writes=[r_w2b[wb_i]])
                        k.dma('sp', self.W2C[half * 16 + j], w2b[wb_i].rearrange("p a b -> p (a b)"), reads=[r_w2b[wb_i]], writes=[r_W2C[half * 16 + j]], sem=w2s[wb_i])
                    else:
                        k.dma('pool', w2b[wb_i].rearrange("p a b -> p (a b)"), self.W2C[half * 16 + j], reads=[r_W2C[half * 16 + j]], writes=[r_w2b[wb_i]])
                    for cc in range(4):
                        c = j * 4 + cc
                        for sub in range(4):
                            for n in range(2):
                                pb = sub * 2 + n
                                k.op('pe', lambda e, c=c, cc=cc, sub=sub, n=n, pb=pb, wb_i=wb_i: e.matmul(self.psb(pb), lhsT=aT[:, c, sub * 128:(sub + 1) * 128], rhs=w2b[wb_i][:, cc, n * 512:(n + 1) * 512], start=(c == 0), stop=(c == 63)),
                                     reads=[r_aT, r_w2b[wb_i]] + ([r_hT] if False else []), writes=[self.psr[pb]])
                if self.cfg.get("mlp_stop") == 3:
                    k.barrier(); ar.release(m0); return
                if half == 0:
                    for sub in range(4):
                        for n in range(2):
                            pb = sub * 2 + n
                            eng = 'act' if n == 0 else 'dve'
                            if eng == 'act':
                                k.op('act', lambda e, sub=sub, n=n, pb=pb, ysb=ysb: e.copy(out=ysb[:, sub, n * 512:(n + 1) * 512], in_=self.psb(pb)), reads=[self.psr[pb]], writes=[r_hT])
                            else:
                                k.op('dve', lambda e, sub=sub, n=n, pb=pb, ysb=ysb: e.tensor_copy(out=ysb[:, sub, n * 512:(n + 1) * 512], in_=self.psb(pb)), reads=[self.psr[pb]], writes=[r_hT])
            if self.cfg.get("mlp_stop") == 4:
                k.barrier(); ar.release(m0); return
            def post_one(blk_, sub_, ysb_, r_hT_, g_):
                tile_ = blk_ * 4 + sub_
                b_ = xi_[0] % 2; xi_[0] += 1
                k.dma('sp', xt[b_], Xin[tile_ * 128:(tile_ + 1) * 128, :], reads=[r_Xin[tile_]], writes=[r_xt[b_]])
                yp_ = [(ysb_[:, sub_, :], r_hT_, 1024), (self.psb(sub_ * 2), self.psr[sub_ * 2], 512), (self.psb(sub_ * 2 + 1), self.psr[sub_ * 2 + 1], 512)]
                self.post_tile(xt[b_], r_xt[b_], yp_, G[g_], r_G[g_], W)
                k.dma('sp', Xout[tile_ * 128:(tile_ + 1) * 128, :], xt[b_], reads=[r_xt[b_]], writes=[r_Xout[tile_]], is_output=out_is_output)

            post_one(blk, 0, ysb, r_hT, g)
            for sub in range(1, 4):
                pending.append((blk, sub, ysb, r_hT, g))
        while pending:
            post_one(*pending.pop(0))
        k.barrier()
        ar.release(m0)


def _even_scratch(self):
    if hasattr(self, "QA"):
        return
    s = self.scr
    self.QA = s("QA", [8, 128, NT]); self.FF = s("FF", [8, 128, NT]); self.FB = s("FB", [8, 128, NT]); self.GA = s("GA", [8, 128, NT])
    self.QB = s("QB", [8, 128, NT], BF16); self.KB = s("KB", [8, 128, NT], BF16)
    self.VA = s("VA", [NT, 1024], BF16); self.VB = s("VB", [NT, 1024], BF16)
    self.OA = s("OA", [16, 128, NT], BF16) if not self.cfg.get("dump_oa") else self.out("OA", [16, 128, NT], BF16)
    self.r_E1 = self.R("E1out")
    self.r_OA = self.R("OAres")


def even_inproj(self, Xin, r_Xin):
    k, ar = self.k, self.ar
    self.phase()
    _even_scratch(self)
    m0 = ar.mark()
    l = 0
    w_in = self.din["w_in_even"]
    W = self.work_bufs()
    xt = [ar.alloc([D_MODEL]) for _ in range(2)]
    r_xt = [self.R(f"e1xt{i}", sem=True) for i in range(2)]
    hT = ar.alloc([16, NT], BF16); r_hT = self.R("hTall")
    wb = [ar.alloc([16, 512], BF16) for _ in range(2)]
    r_wb = [self.R(f"e1w{i}", sw=True) for i in range(2)]
    NST = 4
    stg = [ar.alloc([512]) for _ in range(NST)]
    r_stg = [self.R(f"e1stg{i}", sem=True) for i in range(NST)]
    AB, r_AB = self.AB[l][0], self.r_AB[l][0]
    for tile in range(24):
        g = 0 if tile < 8 else 1
        b = tile % 2
        k.dma('sp', xt[b], Xin[tile * 128:(tile + 1) * 128, :], reads=[r_Xin[tile]], writes=[r_xt[b]])
        self.pre_tile(xt[b], r_xt[b], hT, r_hT, tile * 128, AB, r_AB, g, W)
    fm_dst = {0: self.QA, 1: self.FF, 2: self.FB, 4: self.GA, 5: self.QB, 6: self.KB}
    si = 0
    pbi = 0
    nak, nav = self.dout["nak"], self.dout["nav"]
    for j in self.cfg.get('e1_js', range(16)):
        grp = j // 2
        wi = j % 2
        k.dma('pool', wb[wi], w_in[:, j * 512:(j + 1) * 512].rearrange("(k p) n -> p k n", p=128), writes=[r_wb[wi]])
        if grp in fm_dst:
            dst = fm_dst[grp]
            isbf = grp in (5, 6)
            for cc in range(4):
                head = (j % 2) * 4 + cc
                for tb in range(6):
                    pb = pbi % 4; pbi += 1
                    for kk in range(16):
                        k.op('pe', lambda e, kk=kk, cc=cc, wi=wi, tb=tb, pb=pb: e.matmul(self.psb(pb), lhsT=wb[wi][:, kk, cc * 128:(cc + 1) * 128], rhs=hT[:, kk, tb * 512:(tb + 1) * 512], start=(kk == 0), stop=(kk == 15)),
                             reads=[r_wb[wi], r_hT], writes=[self.psr[pb]])
                    s_ = si % NST; si += 1
                    so = stg[s_] if not isbf else stg[s_].bitcast(BF16)[:, 0:512]
                    scale = (128.0 ** -0.5) if grp == 5 else 1.0
                    if si % 2 == 0:
                        k.op('act', lambda e, so=so, pb=pb, scale=scale: e.activation(out=so, in_=self.psb(pb), func=AF.Copy, scale=scale), reads=[self.psr[pb]], writes=[r_stg[s_]])
                    else:
                        k.op('dve', lambda e, so=so, pb=pb, scale=scale: e.tensor_scalar_mul(out=so, in0=self.psb(pb), scalar1=scale), reads=[self.psr[pb]], writes=[r_stg[s_]])
                    k.dma('sp', dst[head, :, tb * 512:(tb + 1) * 512], so, reads=[r_stg[s_]], writes=[self.r_E1])
        if grp in (3, 6, 7):
            ntile = 8 if grp == 6 else 24
            col0 = (j % 2) * 512
            for t in range(ntile):
                pb = pbi % 4; pbi += 1
                for kk in range(16):
                    k.op('pe', lambda e, kk=kk, wi=wi, t=t, pb=pb: e.matmul(self.psb(pb), lhsT=hT[:, kk, t * 128:(t + 1) * 128], rhs=wb[wi][:, kk, :], start=(kk == 0), stop=(kk == 15)),
                         reads=[r_wb[wi], r_hT], writes=[self.psr[pb]])
                if grp in (3, 7):
                    s_ = si % NST; si += 1
                    so = stg[s_].bitcast(BF16)[:, 0:512]
                    k.op('act', lambda e, so=so, pb=pb: e.copy(out=so, in_=self.psb(pb)), reads=[self.psr[pb]], writes=[r_stg[s_]])
                    d = self.VA if grp == 3 else self.VB
                    k.dma('sp', d[t * 128:(t + 1) * 128, col0:col0 + 512], so, reads=[r_stg[s_]], writes=[self.r_E1])
                if grp in (6, 7) and t < 8:
                    s_ = si % NST; si += 1
                    so = stg[s_]
                    k.op('act', lambda e, so=so, pb=pb: e.copy(out=so, in_=self.psb(pb)), reads=[self.psr[pb]], writes=[r_stg[s_]])
                    d = nak if grp == 6 else nav
                    k.dma('sp', d[t * 128:(t + 1) * 128, col0:col0 + 512], so, reads=[r_stg[s_]], is_output=True)
    k.barrier()
    ar.release(m0)


Builder.even_inproj = even_inproj


def hgrn_phase(self):
    k, ar = self.k, self.ar
    self.phase()
    _even_scratch(self)
    m0 = ar.mark()
    CH = 64
    BT = 512
    cmask = ar.alloc([1024]); r_cmask = self.R("cmask", sem=True)
    trim = ar.alloc([128], parts=64); r_trim = self.R("trim", sem=True)
    k.dma('sp', cmask, self.din["cmask"], writes=[r_cmask])
    k.dma('sp', trim, self.din["trimask"], writes=[r_trim])
    gn = ar.alloc([1]); r_gn = self.R("hgn", sem=True)
    with self.nc.allow_non_contiguous_dma(reason="tiny"):
        k.dma('sp', gn, self.din["hgn"].rearrange("o p -> p o"), writes=[r_gn])
    lbr = ar.alloc([2, 1024], parts=3); r_lbr = self.R("lbr", sem=True)
    k.dma('sp', lbr, self.din["lbrows"].rearrange("d r c -> r d c"), writes=[r_lbr])
    k.op('act', I("activation", out=lbr, in_=lbr, func=AF.Exp), reads=[r_lbr], writes=[r_lbr])
    pf = self.psb(7)
    for d in range(2):
        for h in range(8):
            k.op('pe', I("transpose", out=pf[:, (d * 8 + h) * 3:(d * 8 + h) * 3 + 3], in_=lbr[0:3, d, h * 128:(h + 1) * 128], identity=self.ident_f[0:3, 0:3]),
                 reads=[r_lbr, self.r_ident_f], writes=[self.psr[7]])
    lbe = ar.alloc([16, 3]); r_lbe = self.R("lbe")
    omlb = ar.alloc([16]); r_omlb = self.R("omlb")
    lsum = ar.alloc([16]); r_lsum = self.R("lsum")
    k.op('dve', I("tensor_copy", out=lbe.rearrange("p a r -> p (a r)"), in_=pf[:, 0:48]), reads=[self.psr[7]], writes=[r_lbe])
    k.op('dve', I("tensor_reduce", out=lsum, in_=lbe, axis=AX.X, op=ALU.add), reads=[r_lbe], writes=[r_lsum])
    k.op('dve', I("reciprocal", out=lsum, in_=lsum), reads=[r_lsum], writes=[r_lsum])
    k.op('dve', I("tensor_tensor", out=omlb, in0=lbe[:, :, 0], in1=lsum, op=ALU.mult), reads=[r_lbe, r_lsum], writes=[r_omlb])
    lbv = ar.alloc([16]); lnom = ar.alloc([16])
    k.op('dve', I("tensor_copy", out=lbv, in_=omlb), reads=[r_omlb], writes=[r_omlb])
    k.op('dve', I("tensor_scalar", out=omlb, in0=omlb, scalar1=-1.0, scalar2=1.0, op0=ALU.mult, op1=ALU.add), reads=[r_omlb], writes=[r_omlb])
    k.op('act', I("activation", out=lnom, in_=omlb, func=AF.Ln), reads=[r_omlb], writes=[r_omlb])
    lnsc = ar.alloc([1])
    k.op('dve', I("memset", lnsc, float(math.log(128.0 ** -0.5))), writes=[r_omlb])
    k.barrier()
    NC_ = 4
    def f32buf(n=BT): return ar.alloc([n])
    qT = [f32buf() for _ in range(3)]; r_qT = [self.R(f"hq{i}", sem=True) for i in range(3)]
    fT = [f32buf() for _ in range(3)]; r_fT = [self.R(f"hf{i}", sem=True) for i in range(3)]
    kt = f32buf(); r_kt = self.R("kt")
    lf = f32buf(); r_lf = self.R("lf")
    bc = f32buf(); r_bc = self.R("bc")
    rv = f32buf(); r_rv = self.R("rv")
    sq = f32buf(); r_sq = self.R("sq")
    tm = f32buf(); r_tm = self.R("tm")
    tm2 = f32buf(); r_tm2 = self.R("tm2")
    vt = [[ar.alloc([8, 128], BF16, parts=64) for _ in range(2)] for _ in range(NC_)]
    r_vt = [[self.R(f"hv{c}{i}", sem=True) for i in range(2)] for c in range(NC_)]
    eb = [[f32buf() for _ in range(2)] for _ in range(NC_)]; r_eb = [[self.R(f"eb{c}{i}") for i in range(2)] for c in range(NC_)]
    qb = [[ar.alloc([BT], BF16) for _ in range(2)] for _ in range(NC_)]; r_qb = [[self.R(f"qb{c}{i}") for i in range(2)] for c in range(NC_)]
    kb = [[ar.alloc([BT], BF16) for _ in range(2)] for _ in range(NC_)]; r_kb = [[self.R(f"kb{c}{i}") for i in range(2)] for c in range(NC_)]
    kd = [[ar.alloc([BT], BF16) for _ in range(2)] for _ in range(NC_)]; r_kd = [[self.R(f"kd{c}{i}") for i in range(2)] for c in range(NC_)]
    S32 = [ar.alloc([128]) for _ in range(NC_)]; r_S32 = [self.R(f"S32{c}", sem=True) for c in range(NC_)]
    Sbf = [ar.alloc([128], BF16) for _ in range(NC_)]; r_Sbf = [self.R(f"Sbf{c}") for c in range(NC_)]
    Asb = [[ar.alloc([CH], BF16, parts=64) for _ in range(2)] for _ in range(NC_)]; r_Asb = [[self.R(f"Asb{c}{i}") for i in range(2)] for c in range(NC_)]
    kdt = [[ar.alloc([128], BF16, parts=64) for _ in range(2)] for _ in range(NC_)]; r_kdt = [[self.R(f"kdt{c}{i}") for i in range(2)] for c in range(NC_)]
    Oacc = [ar.alloc([2048]) for _ in range(2)]; r_Oacc = [self.R(f"Oacc{i}") for i in range(2)]
    Oacb = [ar.alloc([2048]) for _ in range(2)]; r_Oacb = [self.R(f"Oacb{i}") for i in range(2)]
    gT = [f32buf() for _ in range(2)]; r_gT = [self.R(f"hg{i}", sem=True) for i in range(2)]
    sq16 = ar.alloc([BT], BF16); r_sq16 = self.R("sq16")
    rstd = f32buf(); r_rstd = self.R("rstdh")
    ost = [ar.alloc([BT], BF16) for _ in range(2)]; r_ost = [self.R(f"host{i}", sem=True) for i in range(2)]
    def regA(c, p): return self.ps[0:CH, 2 * c, p * 64:p * 64 + CH]
    def regU(c, p): return self.ps[:, 2 * c, 128 + p * 128:256 + p * 128]
    def regOd(c, p): return self.ps[:, 2 * c, 384 + p * 64:448 + p * 64]
    def regK(c, p): return self.psb16(2 * c + 1)[0:CH, p * 128:(p + 1) * 128]
    def regOa(c, p): return self.ps[:, 2 * c + 1, 128 + p * 64:192 + p * 64]
    r_bD = [self.psr[2 * c] for c in range(NC_)]
    r_bA = [self.psr[2 * c + 1] for c in range(NC_)]
    r_pA = [[r_bD[c] for p in range(2)] for c in range(NC_)]
    r_pU = [[r_bD[c] for p in range(2)] for c in range(NC_)]
    r_pK = [[r_bA[c] for p in range(2)] for c in range(NC_)]
    r_pO = [[(r_bA[c] if c % 2 == 0 else r_bD[c]) for p in range(2)] for c in range(NC_)]
    r_fin = self.R("pfin")
    srcs = {0: self.FF, 1: self.FB}
    cnt = {"ld": 0, "fin": 0}
    stepc = [0] * NC_
    chc = [0] * NC_
    seqs = [(i * 256, 256, True, i) for i in range(4)] + [(1024, 2048, False, 0)]
    seqs = seqs[self.cfg.get('hg_s0', 0):self.cfg.get('hg_s1', 5)]
    def prep_chain(c, h, d, t0, bt, nblk, ncb, step, sl, par, blkof):
        blk = step if d == 0 else nblk - 1 - step
        blkof[c] = blk
        c0 = t0 + blk * bt
        li = cnt["ld"] % 3; cnt["ld"] += 1
        pi = stepc[c] % 2; stepc[c] += 1
        par[c] = pi
        k.dma('sp', qT[li][:, sl], self.QA[h, :, c0:c0 + bt], reads=[self.r_E1], writes=[r_qT[li]])
        k.dma('sp', fT[li][:, sl], srcs[d][h, :, c0:c0 + bt], reads=[self.r_E1], writes=[r_fT[li]])
        k.dma('sp', vt[c][pi][:, 0:ncb, :], self.VA[c0:c0 + bt, h * 128:(h + 1) * 128].rearrange("(c p) v -> p c v", p=CH), reads=[self.r_E1], writes=[r_vt[c][pi]])
        q_, f_ = qT[li][:, sl], fT[li][:, sl]
        hd = d * 8 + h
        lb_ap, lno_ap = lbv[:, hd:hd + 1], lnom[:, hd:hd + 1]
        k.op('act', I("activation", out=kt[:, sl], in_=f_, func=AF.Exp), reads=[r_fT[li]], writes=[r_kt])
        k.op('act', I("activation", out=tm[:, sl], in_=kt[:, sl], func=AF.Ln, bias=1.0), reads=[r_kt], writes=[r_tm])
        k.op('act', I("activation", out=lf[:, sl], in_=kt[:, sl], func=AF.Ln, bias=lb_ap), reads=[r_kt, r_omlb], writes=[r_lf])
        k.op('dve', I("tensor_tensor", out=lf[:, sl], in0=lf[:, sl], in1=tm[:, sl], op=ALU.subtract), reads=[r_lf, r_tm], writes=[r_lf])
        mF, mB = cmask[:, 0:bt], cmask[:, 512:512 + bt]
        if d == 0:
            k.op('dve', I("tensor_tensor_scan", out=bc[:, sl], data0=mF, data1=lf[:, sl], initial=0.0, op0=ALU.mult, op1=ALU.add), reads=[r_lf, r_cmask], writes=[r_bc])
            k.op('dve', I("tensor_tensor_scan", out=rv[:, sl][:, ::-1], data0=mB[:, ::-1], data1=lf[:, sl][:, ::-1], initial=0.0, op0=ALU.mult, op1=ALU.add), reads=[r_lf, r_cmask], writes=[r_rv])
        else:
            k.op('dve', I("tensor_tensor_scan", out=bc[:, sl][:, ::-1], data0=mB[:, ::-1], data1=lf[:, sl][:, ::-1], initial=0.0, op0=ALU.mult, op1=ALU.add), reads=[r_lf, r_cmask], writes=[r_bc])
            k.op('dve', I("tensor_tensor_scan", out=rv[:, sl], data0=mF, data1=lf[:, sl], initial=0.0, op0=ALU.mult, op1=ALU.add), reads=[r_lf, r_cmask], writes=[r_rv])
        dcol0 = CH - 1 if d == 0 else 0
        k.op('act', I("activation", out=eb[c][pi][:, 0:ncb], in_=bc[:, dcol0:bt:CH], func=AF.Exp), reads=[r_bc], writes=[r_eb[c][pi]])
        k.op('dve', I("tensor_tensor", out=rv[:, sl], in0=rv[:, sl], in1=lf[:, sl], op=ALU.subtract), reads=[r_rv, r_lf], writes=[r_rv])
        k.op('dve', I("tensor_tensor", out=rv[:, sl], in0=rv[:, sl], in1=tm[:, sl], op=ALU.subtract), reads=[r_rv, r_tm], writes=[r_rv])
        k.op('act', I("activation", out=kd[c][pi][:, sl], in_=rv[:, sl], func=AF.Exp, bias=lno_ap), reads=[r_rv, r_omlb], writes=[r_kd[c][pi]])
        k.op('dve', I("tensor_tensor", out=tm[:, sl], in0=tm[:, sl], in1=bc[:, sl], op=ALU.add), reads=[r_tm, r_bc], writes=[r_tm])
        k.op('act', I("activation", out=kb[c][pi][:, sl], in_=tm[:, sl], func=AF.Exp, scale=-1.0, bias=lno_ap), reads=[r_tm, r_omlb], writes=[r_kb[c][pi]])
        k.op('act', I("activation", out=sq[:, sl], in_=q_, func=AF.Exp, scale=-1.0), reads=[r_qT[li]], writes=[r_sq])
        k.op('act', I("activation", out=sq[:, sl], in_=sq[:, sl], func=AF.Ln, bias=1.0), reads=[r_sq], writes=[r_sq])
        k.op('dve', I("tensor_tensor", out=sq[:, sl], in0=bc[:, sl], in1=sq[:, sl], op=ALU.subtract), reads=[r_bc, r_sq], writes=[r_sq])
        k.op('act', I("activation", out=tm2[:, sl], in_=sq[:, sl], func=AF.Exp, bias=lnsc[:, 0:1]), reads=[r_sq, r_omlb], writes=[r_tm2])
        k.op('dve', I("tensor_tensor", out=qb[c][pi][:, sl], in0=q_, in1=tm2[:, sl], op=ALU.mult), reads=[r_qT[li], r_tm2], writes=[r_qb[c][pi]])

    def chunk_step(cstep, chains, bt, ncb, par, blkof):
        info = []
        for c, (h, d) in enumerate(chains):
            cc = cstep if d == 0 else ncb - 1 - cstep
            pp = chc[c] % 2; chc[c] += 1
            pi = par[c]
            cs = cc * CH
            info.append((c, h, d, cc, pp, pi, cs))
        for (c, h, d, cc, pp, pi, cs) in info:
            qbc, kbc, kdc = qb[c][pi][:, cs:cs + CH], kb[c][pi][:, cs:cs + CH], kd[c][pi][:, cs:cs + CH]
            k.op('pe', I("matmul", regA(c, pp), lhsT=kbc, rhs=qbc, start=True, stop=True), reads=[r_kb[c][pi], r_qb[c][pi]], writes=[r_pA[c][pp]])
            k.op('pe', I("transpose", out=regK(c, pp), in_=kdc, identity=self.ident_b), reads=[r_kd[c][pi], self.r_ident_b], writes=[r_pK[c][pp]])
        for (c, h, d, cc, pp, pi, cs) in info:
            mk = trim[:, 0:CH] if d == 0 else trim[:, CH:2 * CH]
            k.op('dve', I("tensor_tensor", out=Asb[c][pp], in0=regA(c, pp), in1=mk, op=ALU.mult), reads=[r_pA[c][pp], r_trim], writes=[r_Asb[c][pp]])
            k.op('act', I("copy", out=kdt[c][pp], in_=regK(c, pp)), reads=[r_pK[c][pp]], writes=[r_kdt[c][pp]])
        for (c, h, d, cc, pp, pi, cs) in info:
            qbc = qb[c][pi][:, cs:cs + CH]
            vch = vt[c][pi][:, cc, :]
            ro = regOa(c, pp) if d == 0 else regOd(c, pp)
            k.op('pe', I("matmul", ro, lhsT=Sbf[c], rhs=qbc, start=True, stop=False), reads=[r_Sbf[c], r_qb[c][pi]], writes=[r_pO[c][pp]])
            k.op('pe', I("matmul", ro, lhsT=vch, rhs=Asb[c][pp], start=False, stop=True), reads=[r_vt[c][pi], r_Asb[c][pp]], writes=[r_pO[c][pp]])
            k.op('pe', I("matmul", regU(c, pp), lhsT=kdt[c][pp], rhs=vch, start=True, stop=True), reads=[r_kdt[c][pp], r_vt[c][pi]], writes=[r_pU[c][pp]])
        for (c, h, d, cc, pp, pi, cs) in info:
            blk = blkof[c]
            oc = Oacc[c // 2][:, blk * bt + cs: blk * bt + cs + CH]
            if d == 0:
                k.op('act', I("copy", out=oc, in_=regOa(c, pp)), reads=[r_pO[c][pp]], writes=[r_Oacc[c // 2]])
            dec = eb[c][pi][:, cc:cc + 1]
            k.op('dve', I("scalar_tensor_tensor", out=S32[c], in0=S32[c], scalar=dec, in1=regU(c, pp), op0=ALU.mult, op1=ALU.add), reads=[r_S32[c], r_eb[c][pi], r_pU[c][pp]], writes=[r_S32[c]])
            k.op('act', I("copy", out=Sbf[c], in_=S32[c]), reads=[r_S32[c]], writes=[r_Sbf[c]])
        for (c, h, d, cc, pp, pi, cs) in info:
            if d == 1:
                blk = blkof[c]
                oc = Oacb[c // 2][:, blk * bt + cs: blk * bt + cs + CH]
                k.op('dve', I("tensor_copy", out=oc, in_=regOd(c, pp)), reads=[r_pO[c][pp]], writes=[r_Oacb[c // 2]])

    def finish_group(chains, t0, bt, nblk, sl, is_ctx, sidx, hp):
        if is_ctx:
            for c, (h, d) in enumerate(chains):
                dst = self.dout["nsf" if d == 0 else "nsb"]
                self.store(dst[sidx, h], S32[c], r_S32[c], reads=[r_S32[c]], writes=[], is_output=True)
        for hh in range(0 if self.cfg.get('hg_nofin') else 2):
            h = 2 * hp + hh
            for blk in range(nblk):
                c0 = t0 + blk * bt
                gi = cnt["fin"] % 2; cnt["fin"] += 1
                ob = Oacc[hh][:, blk * bt:(blk + 1) * bt]
                obb = Oacb[hh][:, blk * bt:(blk + 1) * bt]
                k.dma('sp', gT[gi][:, sl], self.GA[h, :, c0:c0 + bt], reads=[self.r_E1], writes=[r_gT[gi]])
                k.op('dve', I("tensor_tensor", out=ob, in0=ob, in1=obb, op=ALU.add), reads=[r_Oacc[hh], r_Oacb[hh]], writes=[r_Oacc[hh]])
                k.op('act', I("activation", out=sq16[:, sl], in_=ob, func=AF.Square), reads=[r_Oacc[hh]], writes=[r_sq16])
                pfin = self.ps[:, 7, 0:bt]
                fin_res = [r_bA[3]]
                k.op('pe', I("matmul", pfin, lhsT=self.ones_b, rhs=sq16[:, sl], start=True, stop=True), reads=[self.r_ones, r_sq16], writes=fin_res)
                k.op('act', I("activation", out=rstd[:, sl], in_=pfin, func=AF.Ln, scale=1.0 / 128, bias=EPS), reads=fin_res, writes=[r_rstd])
                g_ = gT[gi][:, sl]
                k.op('act', I("activation", out=tm[:, sl], in_=g_, func=AF.Exp, scale=-1.0), reads=[r_gT[gi]], writes=[r_tm])
                k.op('act', I("activation", out=tm[:, sl], in_=tm[:, sl], func=AF.Ln, bias=1.0), reads=[r_tm], writes=[r_tm])
                k.op('dve', I("scalar_tensor_tensor", out=rstd[:, sl], in0=rstd[:, sl], scalar=-0.5, in1=tm[:, sl], op0=ALU.mult, op1=ALU.subtract), reads=[r_rstd, r_tm], writes=[r_rstd])
                k.op('act', I("activation", out=rstd[:, sl], in_=rstd[:, sl], func=AF.Exp), reads=[r_rstd], writes=[r_rstd])
                k.op('dve', I("tensor_tensor", out=tm[:, sl], in0=ob, in1=g_, op=ALU.mult), reads=[r_Oacc[hh], r_gT[gi], r_tm], writes=[r_tm])
                k.op('dve', I("scalar_tensor_tensor", out=ost[gi][:, sl], in0=tm[:, sl], scalar=gn[:, 0:1], in1=rstd[:, sl], op0=ALU.mult, op1=ALU.mult), reads=[r_tm, r_rstd, r_gn], writes=[r_ost[gi]])
                self.store(self.OA[h, :, c0:c0 + bt], ost[gi][:, sl], r_ost[gi], reads=[r_ost[gi]], writes=[self.r_OA], is_output=bool(self.cfg.get("dump_oa")))

    items = []
    for (t0, T, is_ctx, sidx) in seqs:
        bt = min(BT, T)
        nblk = T // bt
        for hp in range(self.cfg.get('hg_nhp', 4)):
            for step in range(nblk):
                items.append((t0, T, is_ctx, sidx, hp, step))
    pars = [[0] * NC_ for _ in items]
    blkofs = [[0] * NC_ for _ in items]

    def do_prep(ii, c):
        (t0, T, is_ctx, sidx, hp, step) = items[ii]
        bt = min(BT, T); nblk = T // bt; ncb = bt // CH
        h, d = 2 * hp + (c // 2), c % 2
        prep_chain(c, h, d, t0, bt, nblk, ncb, step, slice(0, bt), pars[ii], blkofs[ii])

    for c in range(NC_):
        do_prep(0, c)
    for ii, (t0, T, is_ctx, sidx, hp, step) in enumerate(items):
        bt = min(BT, T); nblk = T // bt; ncb = bt // CH
        sl = slice(0, bt)
        chains = [(2 * hp + (c // 2), c % 2) for c in range(NC_)]
        if step == 0:
            for c, (h, d) in enumerate(chains):
                if is_ctx:
                    k.op('dve', I("memset", S32[c], 0.0), writes=[r_S32[c]])
                    k.op('dve', I("memset", Sbf[c], 0.0), writes=[r_Sbf[c]])
                else:
                    k.dma('sp', S32[c], self.din["st_f" if d == 0 else "st_b"][h], writes=[r_S32[c]])
                    k.op('act', I("copy", out=Sbf[c], in_=S32[c]), reads=[r_S32[c]], writes=[r_Sbf[c]])
        nxt = ii + 1 if ii + 1 < len(items) else None
        done = 0
        for cstep in range(ncb):
            chunk_step(cstep, chains, bt, ncb, pars[ii], blkofs[ii])
            if nxt is not None:
                want = ((cstep + 1) * NC_) // ncb
                while done < want:
                    do_prep(nxt, done); done += 1
        if nxt is not None:
            while done < NC_:
                do_prep(nxt, done); done += 1
        if step == nblk - 1:
            finish_group(chains, t0, bt, nblk, sl, is_ctx, sidx, hp)
    k.barrier()
    ar.release(m0)


Builder.hgrn_phase = hgrn_phase


def _attn_finish(self, po, r_po, rec, r_rec, ob16, r_ob16, pT, r_pT, ostg_slice, r_ostg):
    k = self.k
    k.op('dve', I("reciprocal", out=rec, in_=po[:, 128:129]), reads=[r_po], writes=[r_rec])
    k.op('dve', I("tensor_scalar_mul", out=ob16, in0=po[:, 0:128], scalar1=rec), reads=[r_po, r_rec], writes=[r_ob16])
    k.op('pe', I("transpose", out=pT, in_=ob16, identity=self.ident_b), reads=[r_ob16, self.r_ident_b], writes=[r_pT])
    k.op('act', I("copy", out=ostg_slice, in_=pT), reads=[r_pT], writes=[r_ostg])


def na_phase(self):
    k, ar = self.k, self.ar
    self.phase()
    _even_scratch(self)
    m0 = ar.mark()
    QT = [ar.alloc([2048], BF16) for _ in range(2)]; r_QT = [self.R(f"naQ{i}", sem=True) for i in range(2)]
    KT = [ar.alloc([2048], BF16) for _ in range(2)]; r_KT = [self.R(f"naK{i}", sem=True) for i in range(2)]
    vaug = [ar.alloc([16, 129], BF16) for _ in range(2)]; r_vaug = [self.R(f"naV{i}", sem=True) for i in range(2)]
    for i in range(2):
        k.op('pool', I("memset", vaug[i][:, :, 128:129], 1.0), writes=[r_vaug[i]])
    rec = [ar.alloc([1]) for _ in range(2)]; r_rec = [self.R(f"narec{i}") for i in range(2)]
    ob16 = [ar.alloc([128], BF16) for _ in range(2)]; r_ob16 = [self.R(f"naob{i}") for i in range(2)]
    ostg = [ar.alloc([512], BF16) for _ in range(2)]; r_ostg = [self.R(f"naost{i}", sem=True) for i in range(2)]
    Pc = [ar.alloc([4, 512], BF16) for _ in range(2)]; r_Pc = [self.R(f"naPc{i}") for i in range(2)]
    Praw = [ar.alloc([128], BF16) for _ in range(3)]; r_Praw = [self.R(f"naPr{i}") for i in range(3)]
    Pacc = [ar.alloc([512]) for _ in range(2)]; r_Pacc = [self.R(f"naPacc{i}") for i in range(2)]
    recb = ar.alloc([512]); r_recb = self.R("narecb")
    Pl = [ar.alloc([128], BF16) for _ in range(6)]; r_Pl = [self.R(f"naPl{i}") for i in range(6)]
    cnt = {"h": 0, "o": 0, "f": 0, "pl": 0, "pr": 0}
    for sq_ in range(4):
        t0 = sq_ * 256
        for h in range(8):
            hi = cnt["h"] % 2; cnt["h"] += 1
            k.dma('sp', QT[hi][:, 0:256], self.QB[h, :, t0:t0 + 256], reads=[self.r_E1], writes=[r_QT[hi]])
            k.dma('sp', KT[hi][:, 0:256], self.KB[h, :, t0:t0 + 256], reads=[self.r_E1], writes=[r_KT[hi]])
            k.dma('sp', vaug[hi][:, 0:2, 0:128], self.VB[t0:t0 + 256, h * 128:(h + 1) * 128].rearrange("(c p) v -> p c v", p=128), reads=[self.r_E1], writes=[r_vaug[hi]])
            pci = hi
            for kc in range(2):
                pb = kc
                k.op('pe', I("matmul", self.ps[:, pb, 0:256], lhsT=KT[hi][:, kc * 128:(kc + 1) * 128], rhs=QT[hi][:, 0:256], start=True, stop=True),
                     reads=[r_KT[hi], r_QT[hi]], writes=[self.psr[pb]])
                k.op('act', I("activation", out=Pc[pci][:, kc, 0:256], in_=self.ps[:, pb, 0:256], func=AF.Exp), reads=[self.psr[pb]], writes=[r_Pc[pci]])
            oi = cnt["o"] % 2; cnt["o"] += 1
            for qt in range(2):
                fi = cnt["f"] % 2; cnt["f"] += 1
                pv = 4 + fi
                po = self.ps[:, pv, 0:129]
                for kc in range(2):
                    k.op('pe', I("matmul", po, lhsT=Pc[pci][:, kc, qt * 128:(qt + 1) * 128], rhs=vaug[hi][:, kc, :], start=(kc == 0), stop=(kc == 1)),
                         reads=[r_Pc[pci], r_vaug[hi]], writes=[self.psr[pv]])
                pT = self.psb16(6)[:, fi * 128:(fi + 1) * 128]
                _attn_finish(self, po, self.psr[pv], rec[fi], r_rec[fi], ob16[fi], r_ob16[fi], pT, self.psr[6], ostg[oi][:, qt * 128:(qt + 1) * 128], r_ostg[oi])
            self.store(self.OA[8 + h, :, t0:t0 + 256], ostg[oi][:, 0:256], r_ostg[oi], reads=[r_ostg[oi]], writes=[self.r_OA], is_output=bool(self.cfg.get("dump_oa")))
    colm = ar.alloc([64]); r_colm = self.R("colm", sem=True)
    k.dma('sp', colm, self.din["colmask"], writes=[r_colm])
    Traw = ar.alloc([14, 64]); r_Traw = self.R("Traw", sem=True)
    Cexp = ar.alloc([14, 64], BF16); r_Cexp = self.R("Cexp")
    def ws(r): return min(max(r - 4, 0), 24)
    types = {}
    plan = []
    for j in range(16):
        lo = ws(2 * j) // 2
        hi_ = (ws(2 * j + 1) + 7) // 2
        lst = []
        for kc in range(lo, hi_ + 1):
            dl = 2 * kc - 2 * j
            valid = tuple(tuple(ws(2 * j + b) <= 2 * kc + a <= ws(2 * j + b) + 7 for b in range(2)) for a in range(2))
            key = (dl, valid)
            if key not in types:
                types[key] = len(types)
            lst.append((kc, types[key]))
        plan.append(lst)
    ntyp = len(types)
    EBt = ar.alloc([ntyp, 128], BF16); r_EBt = self.R("EBt")
    Kc32 = ar.alloc([4, 128]); r_Kc32 = self.R("Kc32", sem=True)
    Kc16 = ar.alloc([4, 128], BF16); r_Kc16 = self.R("Kc16")
    KcT = ar.alloc([512], BF16); r_KcT = self.R("KcT")
    Vc32 = ar.alloc([4, 128]); r_Vc32 = self.R("Vc32", sem=True)
    vaugc = ar.alloc([4, 129], BF16); r_vaugc = self.R("vaugc")
    k.op('pool', I("memset", vaugc[:, :, 128:129], 1.0), writes=[r_vaugc])
    rpbr = self.din["rpbr"]
    T0 = 1024
    for h in range(8):
        hi = cnt["h"] % 2; cnt["h"] += 1
        k.dma('sp', QT[hi], self.QB[h, :, T0:T0 + 2048], reads=[self.r_E1], writes=[r_QT[hi]])
        k.dma('sp', KT[hi], self.KB[h, :, T0:T0 + 2048], reads=[self.r_E1], writes=[r_KT[hi]])
        k.dma('sp', vaug[hi][:, :, 0:128], self.VB[T0:T0 + 2048, h * 128:(h + 1) * 128].rearrange("(c p) v -> p c v", p=128), reads=[self.r_E1], writes=[r_vaug[hi]])
        k.dma('sp', Kc32, self.din["cnak"][:, h, :].rearrange("(c p) d -> p c d", p=128), writes=[r_Kc32])
        k.dma('sp', Vc32, self.din["cnav"][:, h, :].rearrange("(c p) d -> p c d", p=128), writes=[r_Vc32])
        k.op('pool', I("tensor_copy", out=Kc16, in_=Kc32), reads=[r_Kc32], writes=[r_Kc16])
        k.op('pool', I("tensor_copy", out=vaugc[:, :, 0:128], in_=Vc32), reads=[r_Vc32], writes=[r_vaugc])
        for c in range(4):
            pT = self.psb16(6)[:, c * 128:(c + 1) * 128]
            k.op('pe', I("transpose", out=pT, in_=Kc16[:, c, :], identity=self.ident_b), reads=[r_Kc16, self.r_ident_b], writes=[self.psr[6]])
        k.op('act', I("copy", out=KcT, in_=self.psb16(6)[:, 0:512]), reads=[self.psr[6]], writes=[r_KcT])
        for half in range(2):
            src = bass.AP(tensor=rpbr.tensor, offset=h * 15 * 8192 + half * 8192 + 63, ap=[[127, 64], [8192, 14], [1, 64]])
            k.dma('sp', Traw[half * 64:(half + 1) * 64], src, writes=[r_Traw])
        k.op('act', I("activation", out=Traw, in_=Traw, func=AF.Exp), reads=[r_Traw], writes=[r_Traw])
        for i in range(14):
            k.op('dve', I("tensor_tensor", out=Cexp[:, i, :], in0=Traw[:, i, :], in1=colm, op=ALU.mult), reads=[r_Traw, r_colm], writes=[r_Cexp])
        for (dl, valid), ti in types.items():
            for b in range(2):
                k.op('pool', I("tensor_copy", out=EBt[:, ti, b * 64:(b + 1) * 64], in_=Cexp[:, dl - b + 7, :]), reads=[r_Cexp], writes=[r_EBt])
            for a in range(2):
                for b in range(2):
                    if not valid[a][b]:
                        k.op('pool', I("memset", EBt[a * 64:(a + 1) * 64, ti, b * 64:(b + 1) * 64], 0.0), reads=[r_EBt], writes=[r_EBt])
        for jg in range(4):
            pci = cnt["o"] % 2; cnt["o"] += 1
            po = self.ps[:, 4 + pci, :]
            r_po = self.psr[4 + pci]
            qs = slice(jg * 512, (jg + 1) * 512)
            for c in range(4):
                pb = c % 2
                k.op('pe', I("matmul", self.ps[:, pb, :], lhsT=KcT[:, c * 128:(c + 1) * 128], rhs=QT[hi][:, qs], start=True, stop=True),
                     reads=[r_KcT, r_QT[hi]], writes=[self.psr[pb]])
                k.op('act', I("activation", out=Pc[pci][:, c, :], in_=self.ps[:, pb, :], func=AF.Exp), reads=[self.psr[pb]], writes=[r_Pc[pci]])
                k.op('pe', I("matmul", po, lhsT=vaugc[:, c, 0:128], rhs=Pc[pci][:, c, :], start=(c == 0), stop=False), reads=[r_Pc[pci], r_vaugc], writes=[r_po])
                if c == 1:
                    k.op('dve', I("tensor_tensor", out=Pacc[pci], in0=Pc[pci][:, 0, :], in1=Pc[pci][:, 1, :], op=ALU.add), reads=[r_Pc[pci]], writes=[r_Pacc[pci]])
                elif c > 1:
                    k.op('dve', I("tensor_tensor", out=Pacc[pci], in0=Pacc[pci], in1=Pc[pci][:, c, :], op=ALU.add), reads=[r_Pc[pci], r_Pacc[pci]], writes=[r_Pacc[pci]])
            lat = []
            for jj in range(4):
                j = jg * 4 + jj
                for (kc, ti) in plan[j]:
                    lat.append((jj, j, kc, ti))
            NL = len(lat)
            sbk = []; pls_ = []; prs = []
            for n_ in range(NL):
                sbk.append((2, 3, 7)[cnt["pr"] % 3]); prs.append(cnt["pr"] % 3); cnt["pr"] += 1
                pls_.append(cnt["pl"] % 6); cnt["pl"] += 1

            def S_lat(n_):
                jj, j, kc, ti = lat[n_]
                pb = sbk[n_]
                k.op('pe', I("matmul", self.ps[:, pb, 0:128], lhsT=KT[hi][:, kc * 128:(kc + 1) * 128], rhs=QT[hi][:, j * 128:(j + 1) * 128], start=True, stop=True),
                     reads=[r_KT[hi], r_QT[hi]], writes=[self.psr[pb]])

            S_lat(0)
            if NL > 1:
                S_lat(1)
            for n_ in range(NL):
                jj, j, kc, ti = lat[n_]
                if n_ + 2 < NL:
                    S_lat(n_ + 2)
                pb, pr, pl = sbk[n_], prs[n_], pls_[n_]
                k.op('act', I("activation", out=Praw[pr], in_=self.ps[:, pb, 0:128], func=AF.Exp), reads=[self.psr[pb]], writes=[r_Praw[pr]])
                k.op('dve', I("tensor_tensor", out=Pl[pl], in0=Praw[pr], in1=EBt[:, ti, :], op=ALU.mult), reads=[r_Praw[pr], r_EBt], writes=[r_Pl[pl]])
                k.op('pe', I("matmul", po[:, jj * 128:(jj + 1) * 128], lhsT=vaug[hi][:, kc, 0:128], rhs=Pl[pl], start=False, stop=(n_ == NL - 1)),
                     reads=[r_Pl[pl], r_vaug[hi]], writes=[r_po])
                k.op('dve', I("tensor_tensor", out=Pacc[pci][:, jj * 128:(jj + 1) * 128], in0=Pacc[pci][:, jj * 128:(jj + 1) * 128], in1=Pl[pl], op=ALU.add), reads=[r_Pacc[pci], r_Pl[pl]], writes=[r_Pacc[pci]])
            pden = self.ps[:, 6, :]
            k.op('pe', I("matmul", pden, lhsT=self.ones_f, rhs=Pacc[pci], start=True, stop=True), reads=[self.r_ones_f, r_Pacc[pci]], writes=[self.psr[6]])
            k.op('dve', I("reciprocal", out=recb, in_=pden), reads=[self.psr[6]], writes=[r_recb])
            k.op('dve', I("tensor_tensor", out=ostg[pci], in0=po, in1=recb, op=ALU.mult), reads=[r_po, r_recb], writes=[r_ostg[pci]])
            self.store(self.OA[8 + h, :, T0 + jg * 512:T0 + (jg + 1) * 512], ostg[pci], r_ostg[pci], reads=[r_ostg[pci]], writes=[self.r_OA], is_output=bool(self.cfg.get("dump_oa")))
    k.barrier()
    ar.release(m0)


Builder.na_phase = na_phase


def outproj_phase(self, l, OA, r_OA, w_out, Xin, r_Xin, Xout, r_Xout):
    k, ar = self.k, self.ar
    self.phase()
    m0 = ar.mark()
    G, r_G = self.load_grep(l, 0)
    W = self.work_bufs()
    wo = ar.alloc([16, 2048], BF16); r_wo = self.R("wo", sw=True)
    for n in range(4):
        k.dma('pool', wo[:, :, n * 512:(n + 1) * 512], w_out[:, n * 512:(n + 1) * 512].rearrange("(k p) n -> p k n", p=128), writes=[r_wo])
    ob = [ar.alloc([16, 512], BF16) for _ in range(2)]; r_ob = [self.R(f"opo{i}", sem=True) for i in range(2)]
    xt = [ar.alloc([D_MODEL]) for _ in range(2)]; r_xt = [self.R(f"opx{i}", sem=True) for i in range(2)]
    xi = 0
    for blk in range(6):
        g = 0 if blk < 2 else 1
        bi = blk % 2
        k.dma('sp', ob[bi], OA[:, :, blk * 512:(blk + 1) * 512].rearrange("c p t -> p c t"), reads=[r_OA], writes=[r_ob[bi]])
        for sub in range(4):
            tile = blk * 4 + sub
            pb0 = (tile % 2) * 4
            for n in range(4):
                for kk in range(16):
                    k.op('pe', I("matmul", self.psb(pb0 + n), lhsT=ob[bi][:, kk, sub * 128:(sub + 1) * 128], rhs=wo[:, kk, n * 512:(n + 1) * 512], start=(kk == 0), stop=(kk == 15)),
                         reads=[r_ob[bi], r_wo], writes=[self.psr[pb0 + n]])
            b = xi % 2; xi += 1
            k.dma('sp', xt[b], Xin[tile * 128:(tile + 1) * 128, :], reads=[r_Xin[tile]], writes=[r_xt[b]])
            yp = [(self.psb(pb0 + n), self.psr[pb0 + n], 512) for n in range(4)]
            self.post_tile(xt[b], r_xt[b], yp, G[g], r_G[g], W)
            self.store(Xout[tile * 128:(tile + 1) * 128, :], xt[b], r_xt[b], reads=[r_xt[b]], writes=[r_Xout[tile]])
    k.barrier()
    ar.release(m0)


Builder.outproj_phase = outproj_phase


NTK = NT + 512
MLA_SCALE = 192.0 ** -0.5


def _mla_scratch(self):
    if hasattr(self, "CQT"):
        return
    s = self.scr
    self.CQT = s("CQT", [4, 128, NT], BF16)
    self.CKVT = s("CKVT", [4, 128, NTK], BF16)
    self.KPET = s("KPET", [64, NTK], BF16)
    self.KROT = s("KROT", [64, 2048], BF16)
    self.QN = s("QN", [16, 128, NT], BF16)
    self.QPE = s("QPE", [16, 64, NT], BF16)
    self.QROT = s("QROT", [16, 64, 2048], BF16)
    self.KN = s("KN", [16, 128, NTK], BF16)
    self.VM = s("VM", [NTK, 16, 128], BF16)
    self.OA2 = s("OA2", [16, 128, NT], BF16) if not self.cfg.get("dump_oa2") else self.out("OA2", [16, 128, NT], BF16)
    self.r_O1 = self.R("O1out"); self.r_O2 = self.R("O2out"); self.r_OA2 = self.R("OA2res")


def _rmsnorm_free(self, src_ps, r_src, n, gq, r_gq, out32, out16, r_out, W, st, r_st):
    k = self.k
    k.op('dve', I("memset", st[:, 0:1], 0.0), writes=[r_st])
    k.op('act', I("activation", out=W["junk"][:, 0:n], in_=src_ps, func=AF.Square, accum_out=st[:, 0:1]), reads=[r_src, r_st], writes=[W["r_junk"], r_st])
    k.op('act', I("activation", out=st[:, 1:2], in_=st[:, 0:1], func=AF.Ln, scale=1.0 / n, bias=EPS), reads=[r_st], writes=[r_st])
    k.op('act', I("activation", out=st[:, 2:3], in_=st[:, 1:2], func=AF.Exp, scale=-0.5), reads=[r_st], writes=[r_st])
    if out32 is not None:
        k.op('dve', I("scalar_tensor_tensor", out=out32, in0=src_ps, scalar=st[:, 2:3], in1=gq, op0=ALU.mult, op1=ALU.mult), reads=[r_src, r_st, r_gq], writes=[r_out])
        k.op('pool', I("tensor_copy", out=out16, in_=out32), reads=[r_out], writes=[r_out])
    else:
        k.op('dve', I("scalar_tensor_tensor", out=out16, in0=src_ps, scalar=st[:, 2:3], in1=gq, op0=ALU.mult, op1=ALU.mult), reads=[r_src, r_st, r_gq], writes=[r_out])


def mla_inproj(self, Xin, r_Xin):
    k, ar = self.k, self.ar
    self.phase()
    _mla_scratch(self)
    m0 = ar.mark()
    l = 1
    W = self.work_bufs()
    w_in = self.din["w_in_odd"]
    wb = ar.alloc([16, 1088], BF16); r_wb = self.R("o1w", sw=True)
    for (c0, c1) in ((0, 512), (512, 1024), (1024, 1088)):
        k.dma('pool', wb[:, :, c0:c1], w_in[:, c0:c1].rearrange("(k p) n -> p k n", p=128), writes=[r_wb])
    gq = ar.alloc([512]); r_gq = self.R("gq", sem=True)
    gkv = ar.alloc([512]); r_gkv = self.R("gkv", sem=True)
    k.dma('sp', gq, self.din["mla_qg"].to_broadcast([128, 512]), writes=[r_gq])
    k.dma('sp', gkv, self.din["mla_kvg"].to_broadcast([128, 512]), writes=[r_gkv])
    ropeP32 = ar.alloc([64], parts=64); r_ropeP32 = self.R("ropeP32", sem=True)
    ropeP = ar.alloc([64], BF16, parts=64); r_ropeP = self.R("ropeP")
    k.dma('sp', ropeP32, self.din["ropeP"], writes=[r_ropeP32])
    k.op('dve', I("tensor_copy", out=ropeP, in_=ropeP32), reads=[r_ropeP32], writes=[r_ropeP])
    cs = ar.alloc([2, 128], parts=64)
    r_cs = self.R("ropecs", sem=True)
    xt = [ar.alloc([D_MODEL]) for _ in range(2)]; r_xt = [self.R(f"o1x{i}", sem=True) for i in range(2)]
    hT = [ar.alloc([16, 128], BF16) for _ in range(2)]; r_hT = [self.R(f"o1h{i}") for i in range(2)]
    st = ar.alloc([8]); r_st = self.R("o1st")
    cq16 = [ar.alloc([512], BF16) for _ in range(2)]; r_cq16 = [self.R(f"cq16{i}") for i in range(2)]
    kv32 = [ar.alloc([512]) for _ in range(2)]; r_kv32 = [self.R(f"kv32{i}", sem=True) for i in range(2)]
    kv16 = [ar.alloc([512], BF16) for _ in range(2)]
    kp32 = [ar.alloc([64]) for _ in range(2)]; r_kp32 = [self.R(f"kp32{i}", sem=True) for i in range(2)]
    kp16 = [ar.alloc([64], BF16) for _ in range(2)]
    tq = [ar.alloc([4, 128], BF16) for _ in range(2)]; r_tq = [self.R(f"tq{i}", sem=True) for i in range(2)]
    tkv = [ar.alloc([4, 128], BF16) for _ in range(2)]; r_tkv = [self.R(f"tkv{i}", sem=True) for i in range(2)]
    tkp = [ar.alloc([128], BF16, parts=64) for _ in range(2)]; r_tkp = [self.R(f"tkp{i}", sem=True) for i in range(2)]
    trot = [ar.alloc([128], BF16, parts=64) for _ in range(2)]; r_trot = [self.R(f"trot{i}", sem=True) for i in range(2)]
    t1 = ar.alloc([128], parts=64); r_t1 = self.R("ropet1")
    t2 = ar.alloc([128], parts=64); r_t2 = self.R("ropet2")
    AB, r_AB = self.AB[l][0], self.r_AB[l][0]
    nckv, nkpe = self.dout["nckv"], self.dout["nkpe"]
    for tile in range(28):
        b = tile % 2
        own = tile < 24
        if own:
            g = 0 if tile < 8 else 1
            k.dma('sp', xt[b], Xin[tile * 128:(tile + 1) * 128, :], reads=[r_Xin[tile]], writes=[r_xt[b]])
            self.pre_tile(xt[b], r_xt[b], hT[b], r_hT[b], 0, AB, r_AB, g, W)
            for n, (c0, c1) in enumerate(((0, 512), (512, 1024), (1024, 1088))):
                for kk in range(16):
                    k.op('pe', I("matmul", self.ps[:, n, 0:c1 - c0], lhsT=hT[b][:, kk, :], rhs=wb[:, kk, c0:c1], start=(kk == 0), stop=(kk == 15)),
                         reads=[r_hT[b], r_wb], writes=[self.psr[n]])
            _rmsnorm_free(self, self.ps[:, 0, :], self.psr[0], 512, gq, r_gq, None, cq16[b], r_cq16[b], W, st, r_st)
            _rmsnorm_free(self, self.ps[:, 1, :], self.psr[1], 512, gkv, r_gkv, kv32[b], kv16[b], r_kv32[b], W, st, r_st)
            k.op('act', I("copy", out=kp32[b], in_=self.ps[:, 2, 0:64]), reads=[self.psr[2]], writes=[r_kp32[b]])
            k.op('pool', I("tensor_copy", out=kp16[b], in_=kp32[b]), reads=[r_kp32[b]], writes=[r_kp32[b]])
            if tile < 8:
                k.dma('sp', nckv[tile * 128:(tile + 1) * 128, :], kv32[b], reads=[r_kv32[b]], is_output=True)
                k.dma('sp', nkpe[tile * 128:(tile + 1) * 128, :], kp32[b], reads=[r_kp32[b]], is_output=True)
        else:
            ct = tile - 24
            k.dma('sp', kv32[b], self.din["cckv"][ct * 128:(ct + 1) * 128, :], writes=[r_kv32[b]])
            k.dma('sp', kp32[b], self.din["ckpe"][ct * 128:(ct + 1) * 128, :], writes=[r_kp32[b]])
            k.op('pool', I("tensor_copy", out=kv16[b], in_=kv32[b]), reads=[r_kv32[b]], writes=[r_kv32[b]])
            k.op('pool', I("tensor_copy", out=kp16[b], in_=kp32[b]), reads=[r_kp32[b]], writes=[r_kp32[b]])
        if own:
            p3 = self.psb16(3)
            for c in range(4):
                k.op('pe', I("transpose", out=p3[:, c * 128:(c + 1) * 128], in_=cq16[b][:, c * 128:(c + 1) * 128], identity=self.ident_b), reads=[r_cq16[b], self.r_ident_b], writes=[self.psr[3]])
            k.op('act', I("copy", out=tq[b].rearrange("p a b -> p (a b)"), in_=p3[:, 0:512]), reads=[self.psr[3]], writes=[r_tq[b]])
            k.dma('sp', self.CQT[:, :, tile * 128:(tile + 1) * 128].rearrange("c p t -> p c t"), tq[b], reads=[r_tq[b]], writes=[self.r_O1])
        p4 = self.psb16(4)
        for c in range(4):
            k.op('pe', I("transpose", out=p4[:, c * 128:(c + 1) * 128], in_=kv16[b][:, c * 128:(c + 1) * 128], identity=self.ident_b), reads=[r_kv32[b], self.r_ident_b], writes=[self.psr[4]])
        k.op('act', I("copy", out=tkv[b].rearrange("p a b -> p (a b)"), in_=p4[:, 0:512]), reads=[self.psr[4]], writes=[r_tkv[b]])
        k.dma('sp', self.CKVT[:, :, tile * 128:(tile + 1) * 128].rearrange("c p t -> p c t"), tkv[b], reads=[r_tkv[b]], writes=[self.r_O1])
        p5 = self.psb16(5)
        k.op('pe', I("transpose", out=p5[0:64, 0:128], in_=kp16[b], identity=self.ident_b), reads=[r_kp32[b], self.r_ident_b], writes=[self.psr[5]])
        k.op('act', I("copy", out=tkp[b], in_=p5[0:64, 0:128]), reads=[self.psr[5]], writes=[r_tkp[b]])
        k.dma('sp', self.KPET[:, tile * 128:(tile + 1) * 128], tkp[b], reads=[r_tkp[b]], writes=[self.r_O1])
        if own and tile >= 8:
            lt = tile - 8
            k.dma('sp', cs, self.din["ropecs"][:, :, lt * 128:(lt + 1) * 128], writes=[r_cs])
            k.op('pe', I("matmul", self.ps[0:64, 6, 0:128], lhsT=ropeP, rhs=tkp[b], start=True, stop=True), reads=[r_ropeP, r_tkp[b]], writes=[self.psr[6]])
            k.op('dve', I("tensor_tensor", out=t1, in0=self.ps[0:64, 6, 0:128], in1=cs[:, 1, :], op=ALU.mult), reads=[self.psr[6], r_cs], writes=[r_t1])
            k.op('pool', I("tensor_tensor", out=t2, in0=tkp[b], in1=cs[:, 0, :], op=ALU.mult), reads=[r_tkp[b], r_cs], writes=[r_t2])
            k.op('pool', I("tensor_tensor", out=trot[b], in0=t1, in1=t2, op=ALU.add), reads=[r_t1, r_t2], writes=[r_trot[b]])
            k.dma('sp', self.KROT[:, lt * 128:(lt + 1) * 128], trot[b], reads=[r_trot[b]], writes=[self.r_O1])
    k.barrier()
    ar.release(m0)


def mla_proj(self):
    k, ar = self.k, self.ar
    self.phase()
    _mla_scratch(self)
    m0 = ar.mark()
    wq = ar.alloc([4, 3072], BF16); r_wq = self.R("wq", sw=True)
    wkv = ar.alloc([4, 4096], BF16); r_wkv = self.R("wkv", sw=True)
    for n in range(6):
        k.dma('pool', wq[:, :, n * 512:(n + 1) * 512], self.din["w_uq"][:, n * 512:(n + 1) * 512].rearrange("(k p) n -> p k n", p=128), writes=[r_wq])
    for n in range(8):
        k.dma('pool', wkv[:, :, n * 512:(n + 1) * 512], self.din["w_ukv"][:, n * 512:(n + 1) * 512].rearrange("(k p) n -> p k n", p=128), writes=[r_wkv])
    cqT = ar.alloc([4, NT], BF16); r_cqT = self.R("cqT", sem=True)
    ckvT = ar.alloc([4, NTK], BF16); r_ckvT = self.R("ckvT", sem=True)
    k.dma('sp', cqT, self.CQT.rearrange("c p t -> p c t"), reads=[self.r_O1], writes=[r_cqT])
    k.dma('sp', ckvT, self.CKVT.rearrange("c p t -> p c t"), reads=[self.r_O1], writes=[r_ckvT])
    ropeP32 = ar.alloc([64], parts=64); r_ropeP32 = self.R("ropeP32b", sem=True)
    ropeP = ar.alloc([64], BF16, parts=64); r_ropeP = self.R("ropePb")
    k.dma('sp', ropeP32, self.din["ropeP"], writes=[r_ropeP32])
    k.op('dve', I("tensor_copy", out=ropeP, in_=ropeP32), reads=[r_ropeP32], writes=[r_ropeP])
    cs = ar.alloc([2, 2048], parts=64); r_cs = self.R("ropecs2", sem=True)
    k.dma('sp', cs, self.din["ropecs"], writes=[r_cs])
    NST = 4
    stg = [ar.alloc([512], BF16) for _ in range(NST)]; r_stg = [self.R(f"o2s{i}", sem=True) for i in range(NST)]
    t1 = ar.alloc([512], parts=64); r_t1 = self.R("o2t1")
    t2 = ar.alloc([512], parts=64); r_t2 = self.R("o2t2")
    si = 0; pbi = 0
    wqv = wq.rearrange("p k (h d) -> p k h d", h=16)
    wkvv = wkv.rearrange("p k (h d) -> p k h d", h=16)
    for h in range(16):
        for tb in range(6):
            ts = slice(tb * 512, (tb + 1) * 512)
            pb = pbi % 3; pbi += 1
            for kk in range(4):
                k.op('pe', I("matmul", self.psb(pb), lhsT=wqv[:, kk, h, 0:128], rhs=cqT[:, kk, ts], start=(kk == 0), stop=(kk == 3)), reads=[r_wq, r_cqT], writes=[self.psr[pb]])
            s_ = si % NST; si += 1
            k.op('act', I("activation", out=stg[s_], in_=self.psb(pb), func=AF.Copy, scale=MLA_SCALE), reads=[self.psr[pb]], writes=[r_stg[s_]])
            k.dma('sp', self.QN[h, :, ts], stg[s_], reads=[r_stg[s_]], writes=[self.r_O2])
            pb = pbi % 3; pbi += 1
            for kk in range(4):
                k.op('pe', I("matmul", self.ps[0:64, pb, :], lhsT=wqv[:, kk, h, 128:192], rhs=cqT[:, kk, ts], start=(kk == 0), stop=(kk == 3)), reads=[r_wq, r_cqT], writes=[self.psr[pb]])
            s_ = si % NST; si += 1
            qpe = stg[s_][0:64, :]
            k.op('act', I("activation", out=qpe, in_=self.ps[0:64, pb, :], func=AF.Copy, scale=MLA_SCALE), reads=[self.psr[pb]], writes=[r_stg[s_]])
            k.dma('sp', self.QPE[h, :, ts], qpe, reads=[r_stg[s_]], writes=[self.r_O2])
            if tb >= 2:
                lt = tb - 2
                ls = slice(lt * 512, (lt + 1) * 512)
                k.op('pe', I("matmul", self.ps[0:64, 6, :], lhsT=ropeP, rhs=qpe, start=True, stop=True), reads=[r_ropeP, r_stg[s_]], writes=[self.psr[6]])
                k.op('dve', I("tensor_tensor", out=t1, in0=self.ps[0:64, 6, :], in1=cs[:, 1, ls], op=ALU.mult), reads=[self.psr[6], r_cs], writes=[r_t1])
                k.op('pool', I("tensor_tensor", out=t2, in0=qpe, in1=cs[:, 0, ls], op=ALU.mult), reads=[r_stg[s_], r_cs], writes=[r_t2])
                s2 = si % NST; si += 1
                qrot = stg[s2][0:64, :]
                k.op('pool', I("tensor_tensor", out=qrot, in0=t1, in1=t2, op=ALU.add), reads=[r_t1, r_t2], writes=[r_stg[s2]])
                k.dma('sp', self.QROT[h, :, ls], qrot, reads=[r_stg[s2]], writes=[self.r_O2])
        for tb in range(7):
            ts = slice(tb * 512, (tb + 1) * 512)
            pb = pbi % 3; pbi += 1
            for kk in range(4):
                k.op('pe', I("matmul", self.psb(pb), lhsT=wkvv[:, kk, h, 0:128], rhs=ckvT[:, kk, ts], start=(kk == 0), stop=(kk == 3)), reads=[r_wkv, r_ckvT], writes=[self.psr[pb]])
            s_ = si % NST; si += 1
            k.op('act', I("copy", out=stg[s_], in_=self.psb(pb)), reads=[self.psr[pb]], writes=[r_stg[s_]])
            k.dma('sp', self.KN[h, :, ts], stg[s_], reads=[r_stg[s_]], writes=[self.r_O2])
    for t in range(28):
        for hg in range(4):
            pb = 3 + (pbi % 2); pbi += 1
            for kk in range(4):
                k.op('pe', I("matmul", self.psb(pb).rearrange("p (h d) -> p h d", h=4), lhsT=ckvT[:, kk, t * 128:(t + 1) * 128], rhs=wkvv[:, kk, hg * 4:(hg + 1) * 4, 128:256], start=(kk == 0), stop=(kk == 3)),
                     reads=[r_wkv, r_ckvT], writes=[self.psr[pb]])
            s_ = si % NST; si += 1
            k.op('act', I("copy", out=stg[s_], in_=self.psb(pb)), reads=[self.psr[pb]], writes=[r_stg[s_]])
            k.dma('sp', self.VM[t * 128:(t + 1) * 128, hg * 4:(hg + 1) * 4, :], stg[s_].rearrange("p (h d) -> p h d", h=4), reads=[r_stg[s_]], writes=[self.r_O2])
    k.barrier()
    ar.release(m0)


def mla_attn(self):
    k, ar = self.k, self.ar
    self.phase()
    _mla_scratch(self)
    m0 = ar.mark()
    kpeT_f = ar.alloc([NTK], BF16); r_kpeT = self.R("kpeTall", sem=True)
    krot_f = ar.alloc([2048], BF16); r_krot = self.R("krotall", sem=True)
    k.op('dve', I("memset", kpeT_f[64:128], 0.0), writes=[r_kpeT])
    k.op('dve', I("memset", krot_f[64:128], 0.0), writes=[r_krot])
    k.dma('sp', kpeT_f[0:64], self.KPET, reads=[self.r_O1], writes=[r_kpeT])
    k.dma('sp', krot_f[0:64], self.KROT, reads=[self.r_O1], writes=[r_krot])
    kpeT, krot = kpeT_f, krot_f
    QN = [ar.alloc([NT], BF16) for _ in range(2)]; r_QN = [self.R(f"aQN{i}", sem=True) for i in range(2)]
    QP = [ar.alloc([NT], BF16) for _ in range(2)]; r_QP = [self.R(f"aQP{i}", sem=True) for i in range(2)]
    QR = [ar.alloc([2048], BF16) for _ in range(2)]; r_QR = [self.R(f"aQR{i}", sem=True) for i in range(2)]
    for i in range(2):
        k.op('dve', I("memset", QP[i][64:128], 0.0), writes=[r_QP[i]])
        k.op('dve', I("memset", QR[i][64:128], 0.0), writes=[r_QR[i]])
    KN = [ar.alloc([NTK], BF16) for _ in range(2)]; r_KN = [self.R(f"aKN{i}", sem=True) for i in range(2)]
    va = [ar.alloc([28, 129], BF16) for _ in range(2)]; r_va = [self.R(f"aV{i}", sem=True) for i in range(2)]
    for i in range(2):
        k.op('pool', I("memset", va[i][:, :, 128:129], 1.0), writes=[r_va[i]])
    PT = [ar.alloc([512], BF16) for _ in range(5)]; r_PT = [self.R(f"aPT{i}") for i in range(5)]
    rec = [ar.alloc([1]) for _ in range(2)]; r_rec = [self.R(f"arec{i}") for i in range(2)]
    ob16 = [ar.alloc([128], BF16) for _ in range(2)]; r_ob16 = [self.R(f"aob{i}") for i in range(2)]
    ostg = [ar.alloc([512], BF16) for _ in range(2)]; r_ostg = [self.R(f"aost{i}", sem=True) for i in range(2)]
    Pacc = [ar.alloc([512]) for _ in range(2)]; r_Pacc = [self.R(f"aPacc{i}") for i in range(2)]
    recb = [ar.alloc([512]) for _ in range(2)]; r_recb = [self.R(f"arecb{i}") for i in range(2)]
    cnt = {"p": 0, "s": 0, "o": 0, "f": 0}
    def load_head(h):
        hi = h % 2
        k.dma('sp', QN[hi], self.QN[h], reads=[self.r_O2], writes=[r_QN[hi]])
        k.dma('sp', QP[hi][0:64], self.QPE[h], reads=[self.r_O2], writes=[r_QP[hi]])
        k.dma('sp', QR[hi][0:64], self.QROT[h], reads=[self.r_O2], writes=[r_QR[hi]])
        k.dma('sp', KN[hi], self.KN[h], reads=[self.r_O2], writes=[r_KN[hi]])
        k.dma('sp', va[hi][:, :, 0:128], self.VM[:, h, :].rearrange("(c p) d -> p c d", p=128), reads=[self.r_O2], writes=[r_va[hi]])

    flat = []
    jobinfo = {}
    jid = 0
    for h in range(16):
        hi = h % 2
        jobs = []
        for sq_ in range(4):
            t0 = sq_ * 256
            keys = [(t0 // 128 + c, kpeT[:, t0 + c * 128:t0 + (c + 1) * 128], QP[hi][:, t0:t0 + 256]) for c in range(2)]
            jobs.append((t0, 256, keys))
        for qb in range(4):
            q0 = 1024 + qb * 512
            lq = slice(qb * 512, (qb + 1) * 512)
            keys = [(8 + c, krot[:, c * 128:(c + 1) * 128], QR[hi][:, lq]) for c in range(16)]
            keys += [(24 + c, kpeT[:, NT + c * 128:NT + (c + 1) * 128], QP[hi][:, q0:q0 + 512]) for c in range(4)]
            jobs.append((q0, 512, keys))
        for (q0, nq, keys) in jobs:
            for n_, (kt, kr, qr) in enumerate(keys):
                flat.append((h, jid, q0, nq, n_, len(keys), kt, kr, qr))
            jid += 1
    NF = len(flat)
    sbs = [(0, 1, 7)[i % 3] for i in range(NF)]
    pis = [i % 5 for i in range(NF)]
    loaded = set()

    def S_ops(i):
        (h, jid_, q0, nq, n_, NK, kt, kr, qr) = flat[i]
        hi = h % 2
        if h not in loaded:
            load_head(h); loaded.add(h)
        sb_ = sbs[i]
        k.op('pe', I("matmul", self.ps[:, sb_, 0:nq], lhsT=KN[hi][:, kt * 128:(kt + 1) * 128], rhs=QN[hi][:, q0:q0 + nq], start=True, stop=False),
             reads=[r_KN[hi], r_QN[hi]], writes=[self.psr[sb_]])
        k.op('pe', I("matmul", self.ps[:, sb_, 0:nq], lhsT=kr, rhs=qr, start=False, stop=True),
             reads=[r_kpeT, r_krot, r_QP[hi], r_QR[hi]], writes=[self.psr[sb_]])

    load_head(0); loaded.add(0)
    S_ops(0); S_ops(1)
    for i in range(NF):
        (h, jid_, q0, nq, n_, NK, kt, kr, qr) = flat[i]
        hi = h % 2
        ji = jid_ % 2
        po = self.ps[:, 2 + ji, 0:nq]
        r_po = self.psr[2 + ji]
        if n_ == 0 and h + 1 < 16 and (h + 1) not in loaded and q0 == 256:
            load_head(h + 1); loaded.add(h + 1)
        if i + 2 < NF:
            S_ops(i + 2)
        sb_, pi = sbs[i], pis[i]
        k.op('act', I("activation", out=PT[pi][:, 0:nq], in_=self.ps[:, sb_, 0:nq], func=AF.Exp), reads=[self.psr[sb_]], writes=[r_PT[pi]])
        k.op('pe', I("matmul", po, lhsT=va[hi][:, kt, 0:128], rhs=PT[pi][:, 0:nq], start=(n_ == 0), stop=(n_ == NK - 1)),
             reads=[r_PT[pi], r_va[hi]], writes=[r_po])
        if n_ == 0:
            k.op('dve', I("tensor_copy", out=Pacc[ji][:, 0:nq], in_=PT[pi][:, 0:nq]), reads=[r_PT[pi]], writes=[r_Pacc[ji]])
        else:
            k.op('dve', I("tensor_tensor", out=Pacc[ji][:, 0:nq], in0=Pacc[ji][:, 0:nq], in1=PT[pi][:, 0:nq], op=ALU.add), reads=[r_Pacc[ji], r_PT[pi]], writes=[r_Pacc[ji]])
        if n_ == NK - 1:
            pden = self.ps[:, 4 + ji, 0:nq]
            k.op('pe', I("matmul", pden, lhsT=self.ones_f, rhs=Pacc[ji][:, 0:nq], start=True, stop=True), reads=[self.r_ones_f, r_Pacc[ji]], writes=[self.psr[4 + ji]])
            k.op('act', I("activation", out=recb[ji][:, 0:nq], in_=pden, func=AF.Ln), reads=[self.psr[4 + ji]], writes=[r_recb[ji]])
            k.op('act', I("activation", out=recb[ji][:, 0:nq], in_=recb[ji][:, 0:nq], func=AF.Exp, scale=-1.0), reads=[r_recb[ji]], writes=[r_recb[ji]])
            k.op('dve', I("tensor_tensor", out=ostg[ji][:, 0:nq], in0=po, in1=recb[ji][:, 0:nq], op=ALU.mult), reads=[r_po, r_recb[ji]], writes=[r_ostg[ji]])
            self.store(self.OA2[h, :, q0:q0 + nq], ostg[ji][:, 0:nq], r_ostg[ji], reads=[r_ostg[ji]], writes=[self.r_OA2], is_output=bool(self.cfg.get("dump_oa2")))
    k.barrier()
    ar.release(m0)


Builder.mla_inproj = mla_inproj
Builder.mla_proj = mla_proj
Builder.mla_attn = mla_attn


def _host_consts():
    cm = np.ones((128, 1024), np.float32)
    t = np.arange(512)
    cm[:, 0:512][:, t % 64 == 0] = 0
    cm[:, 512:][:, t % 64 == 63] = 0
    s = np.arange(64)[:, None]
    tt = np.arange(64)[None, :]
    tri = np.concatenate([(s <= tt), (s >= tt)], 1).astype(np.float32)
    col = np.arange(64)
    cs_ = np.clip(col - 8, 0, 48)
    ok = (col[None, :] >= cs_[:, None]) & (col[None, :] < cs_[:, None] + 16)
    m = ok.T.astype(np.float32)
    colmask = np.concatenate([m, m], 0)
    P = np.zeros((64, 64), np.float32)
    for base in (0, 32):
        for i in range(16):
            P[base + i, base + i + 16] = -1.0
            P[base + i + 16, base + i] = 1.0
    ropeP = np.ascontiguousarray(P.T)
    tq = np.arange(2048)
    inv = np.power(np.float32(10000.0), -np.arange(0, 32, 2, dtype=np.float32) / np.float32(32)).astype(np.float32)
    ang_r = (tq // 64).astype(np.float32)[:, None] * inv
    ang_c = (tq % 64).astype(np.float32)[:, None] * inv
    ang = np.concatenate([ang_r, ang_r, ang_c, ang_c], 1).T
    ropecs = np.ascontiguousarray(np.stack([np.cos(ang), np.sin(ang)], 1).astype(np.float32))
    return {"ident": np.eye(128, dtype=np.float32), "cmask": cm, "trimask": tri, "colmask": colmask, "ropeP": ropeP, "ropecs": ropecs}


def build_program(cfg=None):
    B = Builder(cfg or {})
    B.inp("cvec", [2, 2048]); B.inp("ada_w", [2, 2048, 12288]); B.inp("ada_b", [2, 12288]); B.inp("norm_g", [2, 4, 2048])
    B.inp("w_in_even", [2048, 8192]); B.inp("cmask", [128, 1024]); B.inp("trimask", [64, 128]); B.inp("hgn", [1, 128]); B.inp("lbrows", [2, 3, 1024])
    B.inp("st_f", [8, 128, 128]); B.inp("st_b", [8, 128, 128]); B.inp("rpbr", [8, 15, 8192]); B.inp("colmask", [128, 64])
    B.inp("cnak", [512, 8, 128]); B.inp("cnav", [512, 8, 128]); B.inp("w_out_even", [2048, 2048])
    B.inp("mlp_w1", [2, 2048, 8192]); B.inp("mlp_w2", [2, 8192, 2048])
    B.inp("w_in_odd", [2048, 1088]); B.inp("mla_qg", [1, 512]); B.inp("mla_kvg", [1, 512]); B.inp("w_uq", [512, 3072]); B.inp("w_ukv", [512, 4096]); B.inp("w_out_odd", [2048, 2048])
    B.inp("cckv", [512, 512]); B.inp("ckpe", [512, 64]); B.inp("ropeP", [64, 64]); B.inp("ropecs", [64, 2, 2048])
    X0 = B.inp("xin", [NT, 2048])
    X4 = B.out("yout", [NT, 2048])
    B.out("nsf", [4, 8, 128, 128]); B.out("nsb", [4, 8, 128, 128]); B.out("nak", [1024, 1024]); B.out("nav", [1024, 1024])
    B.out("nckv", [1024, 512]); B.out("nkpe", [1024, 64])
    X1 = B.scr("X1", [NT, 2048]); X2 = B.scr("X2", [NT, 2048]); X3 = B.scr("X3", [NT, 2048])
    rX = [[B.R(f"X{i}_{t}") for t in range(24)] for i in range(5)]
    B.consts()
    B.modulation(0)
    B.modulation(1)
    B.even_inproj(X0, rX[0])
    B.hgrn_phase()
    B.na_phase()
    B.outproj_phase(0, B.OA, B.r_OA, B.din["w_out_even"], X0, rX[0], X1, rX[1])
    B.mlp_phase(0, X1, rX[1], X2, rX[2])
    B.mla_inproj(X2, rX[2])
    B.mla_proj()
    B.mla_attn()
    B.outproj_phase(1, B.OA2, B.r_OA2, B.din["w_out_odd"], X2, rX[2], X3, rX[3])
    B.mlp_phase(1, X3, rX[3], X4, rX[4], out_is_output=True)
    B.k.emit()
    return B


def kernel(x_prompt, x_sample, state_hgrn_fwd, state_hgrn_bwd, cache_na_k, cache_na_v, cache_mla_ckv, cache_mla_kpe,
           c, c_ctx, ada_w, ada_b, norm_g, hgrn_lb_fwd, hgrn_lb_bwd, w_in_even, hgrn_norm_g, na_rpb, w_out_even, w_in_odd,
           mla_q_norm_g, w_uq, mla_kv_norm_g, w_ukv, w_out_odd, mlp_w1, mlp_w2):
    f32 = lambda a: np.ascontiguousarray(np.asarray(a, dtype=np.float32))
    NCORE = 8
    B = build_program()
    hc = _host_consts()
    rpb = f32(na_rpb)[0]
    rp = np.zeros((8, 15, 128), np.float32)
    rp[:, :, 48:79] = rpb[:, :, ::-1]
    rpbr = np.ascontiguousarray(np.broadcast_to(rp[:, :, None, :], (8, 15, 64, 128))).reshape(8, 15, 8192)
    shared = {
        "ada_w": f32(ada_w), "ada_b": f32(ada_b), "norm_g": f32(norm_g), "w_in_even": f32(w_in_even)[0],
        "hgn": f32(hgrn_norm_g)[0:1], "lbrows": np.ascontiguousarray(np.stack([f32(hgrn_lb_fwd), f32(hgrn_lb_bwd)], 0)),
        "rpbr": rpbr, "w_out_even": f32(w_out_even)[0], "mlp_w1": f32(mlp_w1), "mlp_w2": f32(mlp_w2),
        "w_in_odd": f32(w_in_odd)[0], "mla_qg": f32(mla_q_norm_g)[0:1], "mla_kvg": f32(mla_kv_norm_g)[0:1],
        "w_uq": f32(w_uq)[0], "w_ukv": f32(w_ukv)[0], "w_out_odd": f32(w_out_odd)[0],
    }
    shared.update(hc)
    xp, xs = f32(x_prompt), f32(x_sample)
    sf, sb_ = f32(state_hgrn_fwd), f32(state_hgrn_bwd)
    cnk, cnv = f32(cache_na_k), f32(cache_na_v)
    cck, ckp = f32(cache_mla_ckv), f32(cache_mla_kpe)
    cc, cctx = f32(c), f32(c_ctx)
    in_maps = []
    for i in range(NCORE):
        m = dict(shared)
        m["xin"] = np.ascontiguousarray(np.concatenate([xp[4 * i:4 * i + 4].reshape(1024, 2048), xs[i]], 0))
        m["cvec"] = np.ascontiguousarray(np.stack([cctx, cc[i]], 0))
        m["st_f"] = np.ascontiguousarray(sf[i, 0]); m["st_b"] = np.ascontiguousarray(sb_[i, 0])
        m["cnak"] = np.ascontiguousarray(cnk[i, 0]); m["cnav"] = np.ascontiguousarray(cnv[i, 0])
        m["cckv"] = np.ascontiguousarray(cck[i, 0]); m["ckpe"] = np.ascontiguousarray(ckp[i, 0])
        in_maps.append(m)
    res = run_bass_kernel_spmd(B.nc, in_maps, core_ids=list(range(NCORE)))
    R_ = res.results
    y_prompt = np.concatenate([np.asarray(r["yout"])[:1024].reshape(4, 256, 2048) for r in R_], 0).astype(np.float32)
    y_sample = np.stack([np.asarray(r["yout"])[1024:] for r in R_], 0).astype(np.float32)
    nsf = np.concatenate([np.asarray(r["nsf"]).reshape(4, 1, 8, 128, 128) for r in R_], 0).astype(np.float32)
    nsb = np.concatenate([np.asarray(r["nsb"]).reshape(4, 1, 8, 128, 128) for r in R_], 0).astype(np.float32)
    nak = np.concatenate([np.asarray(r["nak"]).reshape(4, 1, 256, 8, 128) for r in R_], 0).astype(np.float32)
    nav = np.concatenate([np.asarray(r["nav"]).reshape(4, 1, 256, 8, 128) for r in R_], 0).astype(np.float32)
    nckv = np.concatenate([np.asarray(r["nckv"]).reshape(4, 1, 256, 512) for r in R_], 0).astype(np.float32)
    nkpe = np.concatenate([np.asarray(r["nkpe"]).reshape(4, 1, 256, 64) for r in R_], 0).astype(np.float32)
    return (y_prompt, y_sample, nsf, nsb, nak, nav, nckv, nkpe)
```

```python
import numpy as np
import concourse.bass as bass
import concourse.mybir as mybir
from concourse.bass_utils import run_bass_kernel_spmd

F32 = mybir.dt.float32
BF16 = mybir.dt.bfloat16
I32 = mybir.dt.int32
AF = mybir.ActivationFunctionType
ALU = mybir.AluOpType
AX = mybir.AxisListType

COMPUTE = ('pe', 'act', 'dve', 'pool')
ALLENG = ('pe', 'act', 'dve', 'pool', 'sp')


class DmaSem:
    __slots__ = ('name', 'count', 'handle')

    def __init__(self, name):
        self.name = name
        self.count = 0
        self.handle = None


class Res:
    __slots__ = ('name', 'w_ops', 'w_dma', 'r_ops', 'r_dma', 'had_read', 'sem', '_stsem', '_stphase')

    def __init__(self, name, sem=None):
        self.name = name
        self.w_ops = {}
        self.w_dma = {}
        self.r_ops = {}
        self.r_dma = {}
        self.had_read = False
        self.sem = sem


class Op:
    __slots__ = ('eng', 'fn', 'dep_ops', 'dep_dma', 'idx', 'signal', 'dma_sem')

    def __init__(self, eng, fn):
        self.eng = eng
        self.fn = fn
        self.dep_ops = {}
        self.dep_dma = {}
        self.idx = -1
        self.signal = False
        self.dma_sem = None


class K:
    def __init__(self, nc):
        self.nc = nc
        self.ops = {e: [] for e in ALLENG}
        self.known_ops = {e: {f: -1 for f in ALLENG} for e in ALLENG}
        self.known_dma = {e: {} for e in ALLENG}
        self.dma_sems = []
        self.out_sems = set()
        self.nres = 0

    def res(self, name=None):
        self.nres += 1
        return Res(name or f"r{self.nres}")

    def dsem(self, name):
        s = DmaSem(name)
        self.dma_sems.append(s)
        return s

    def _collect(self, eng, reads, writes):
        raw_ops, raw_dma, war_ops, war_dma = {}, {}, {}, {}
        for r in reads:
            for e, i in r.w_ops.items():
                if raw_ops.get(e, -1) < i:
                    raw_ops[e] = i
            for s, v in r.w_dma.items():
                if raw_dma.get(s, 0) < v:
                    raw_dma[s] = v
        for w in writes:
            for e, i in w.r_ops.items():
                if war_ops.get(e, -1) < i:
                    war_ops[e] = i
            for s, v in w.r_dma.items():
                if war_dma.get(s, 0) < v:
                    war_dma[s] = v
        dep_ops = {}
        for e, i in raw_ops.items():
            if e == eng and eng in ('pe', 'sp'):
                continue
            dep_ops[e] = i
        for e, i in war_ops.items():
            if e == eng:
                continue
            if dep_ops.get(e, -1) < i:
                dep_ops[e] = i
        dep_dma = dict(raw_dma)
        for s, v in war_dma.items():
            if dep_dma.get(s, 0) < v:
                dep_dma[s] = v
        ko = self.known_ops[eng]
        kd = self.known_dma[eng]
        dep_ops = {e: i for e, i in dep_ops.items() if ko[e] < i}
        dep_dma = {s: v for s, v in dep_dma.items() if kd.get(s, 0) < v}
        for e, i in dep_ops.items():
            ko[e] = i
        for s, v in dep_dma.items():
            kd[s] = v
        return dep_ops, dep_dma

    def _register(self, op, reads, writes, dma_evt=None):
        eng, idx = op.eng, op.idx
        for r in reads:
            r.had_read = True
            if dma_evt is None:
                r.r_ops[eng] = idx
            else:
                r.r_dma[dma_evt[0]] = dma_evt[1]
        for w in writes:
            if w.had_read:
                w.w_ops = {}
                w.w_dma = {}
                w.r_ops = {}
                w.r_dma = {}
                w.had_read = False
            if dma_evt is None:
                w.w_ops[eng] = idx
            else:
                w.w_dma[dma_evt[0]] = dma_evt[1]

    def op(self, eng, fn, reads=(), writes=()):
        o = Op(eng, fn)
        o.dep_ops, o.dep_dma = self._collect(eng, reads, writes)
        o.idx = len(self.ops[eng])
        self.ops[eng].append(o)
        self._register(o, reads, writes)
        return o

    def dma(self, q, out, in_, reads=(), writes=(), sem=None, is_output=False, **kw):
        if sem is None:
            for r in list(writes) + list(reads):
                if r.sem is not None:
                    sem = r.sem
                    break
        assert sem is not None, "dma needs a semaphore-bearing resource"
        o = Op(q, lambda e, out=out, in_=in_, kw=kw: e.dma_start(out=out, in_=in_, **kw))
        o.dep_ops, o.dep_dma = self._collect(q, reads, writes)
        o.idx = len(self.ops[q])
        self.ops[q].append(o)
        sem.count += 16
        o.dma_sem = sem
        self._register(o, reads, writes, dma_evt=(sem, sem.count))
        if is_output:
            self.out_sems.add(sem)
        return o

    def barrier(self):
        o = Op('sp', lambda e: e.nop())
        for e in COMPUTE:
            n = len(self.ops[e])
            if n > 0 and self.known_ops['sp'][e] < n - 1:
                last = n - 1
                while last >= 0 and (self.ops[e][last].fn is None or self.ops[e][last].dma_sem is not None):
                    last -= 1
                if last >= 0 and self.known_ops['sp'][e] < last:
                    o.dep_ops[e] = last
        for s in self.dma_sems:
            if self.known_dma['sp'].get(s, 0) < s.count:
                o.dep_dma[s] = s.count
        o.idx = len(self.ops['sp'])
        self.ops['sp'].append(o)
        for e in COMPUTE:
            w = Op(e, None)
            w.dep_ops = {'sp': o.idx}
            w.idx = len(self.ops[e])
            self.ops[e].append(w)
        for e in ALLENG:
            for f in ALLENG:
                self.known_ops[e][f] = len(self.ops[f]) - 1
            for s in self.dma_sems:
                self.known_dma[e][s] = s.count
            self.known_ops[e]['sp'] = o.idx

    def emit(self):
        nc = self.nc
        self.barrier()
        sig = {e: set() for e in ALLENG}
        for e in ALLENG:
            for o in self.ops[e]:
                for f, i in o.dep_ops.items():
                    sig[f].add(i)
        val = {}
        for e in ALLENG:
            val[e] = {i: k + 1 for k, i in enumerate(sorted(sig[e]))}
        self._cm = nc.cleanup_on_exit()
        self._cm.__enter__()
        esem = {e: nc.alloc_semaphore(f"eng_{e}") for e in ALLENG}
        for s in self.dma_sems:
            if s.count > 0:
                s.handle = nc.alloc_semaphore(f"d_{s.name}")
        engobj = {'pe': 'tensor', 'act': 'scalar', 'dve': 'vector', 'pool': 'gpsimd', 'sp': 'sync'}
        stats = {}

        def run(e):
            def body(eng):
                nw = 0
                for o in self.ops[e]:
                    for f, i in o.dep_ops.items():
                        eng.wait_ge(esem[f], val[f][i])
                        nw += 1
                    for s, v in o.dep_dma.items():
                        eng.wait_ge(s.handle, v)
                        nw += 1
                    if o.fn is None:
                        continue
                    ins = o.fn(eng)
                    if o.dma_sem is not None:
                        ins.then_inc(o.dma_sem.handle, 16)
                    elif o.idx in val[e]:
                        ins.then_inc(esem[e], 1)
                stats[e] = (len(self.ops[e]), nw)
            return body

        with nc.Block() as block:
            for e in ALLENG:
                getattr(block, engobj[e])(run(e))
        nc.all_engine_barrier()
        self._cm.__exit__(None, None, None)
        self.stats = stats
        return stats


import math

D_MODEL = 2048
NT = 3072
NCH = 16
D_FF = 8192
EPS = 1e-6


def I(method, *a, **kw):
    return lambda e: getattr(e, method)(*a, **kw)


class Arena:
    def __init__(self, nc, nbytes):
        self.t = nc.alloc_sbuf_tensor("arena", [128, nbytes // 4], F32)
        self.n = nbytes
        self.top = 0
        self.peak = 0

    def alloc(self, shape, dt=F32, parts=128):
        esz = 4 if dt == F32 else 2
        n = 1
        for s in shape:
            n *= s
        nb = (n * esz + 63) // 64 * 64
        assert self.top + nb <= self.n, f"arena overflow {self.top}+{nb}>{self.n}"
        o4 = self.top // 4
        a = self.t[0:parts, o4:o4 + nb // 4]
        self.top += nb
        self.peak = max(self.peak, self.top)
        if dt != F32:
            a = a.bitcast(dt)
        a = a[:, 0:n]
        if len(shape) == 2:
            a = a.rearrange("p (a b) -> p a b", a=shape[0])
        elif len(shape) == 3:
            a = a.rearrange("p (a b c) -> p a b c", a=shape[0], b=shape[1])
        return a

    def mark(self):
        return self.top

    def release(self, m):
        self.top = m


class Builder:
    def __init__(self, cfg):
        self.cfg = cfg
        nc = self.nc = bass.Bass("TRN2", target_bir_lowering=False)
        self.k = K(nc)
        self.din = {}
        self.dout = {}
        self.ar = Arena(nc, 204800)
        self.ps = nc.alloc_psum_tensor("ps", [128, 8, 512], F32)
        self.psr = [self.k.res(f"psum{b}") for b in range(8)]
        self.rr = {}

    def inp(self, name, shape, dt=F32):
        self.din[name] = self.nc.dram_tensor(name, list(shape), dt, kind="ExternalInput").ap()
        return self.din[name]

    def out(self, name, shape, dt=F32):
        self.dout[name] = self.nc.dram_tensor(name, list(shape), dt, kind="ExternalOutput").ap()
        return self.dout[name]

    def scr(self, name, shape, dt=F32):
        return self.nc.dram_tensor(name, list(shape), dt, kind="Internal").ap()

    def R(self, name, sem=False, sw=False):
        r = self.k.res(name)
        if not hasattr(self, "sem_pool"):
            self.sem_pool = []
            self.sem_i = 0
            self.sw_pool = []
            self.sw_i = 0
        if sw:
            if self.sw_i >= len(self.sw_pool):
                self.sw_pool.append(self.k.dsem(f"w{len(self.sw_pool)}"))
            r.sem = self.sw_pool[self.sw_i]
            self.sw_i += 1
        elif sem:
            if self.sem_i >= len(self.sem_pool):
                self.sem_pool.append(self.k.dsem(f"p{len(self.sem_pool)}"))
            r.sem = self.sem_pool[self.sem_i]
            self.sem_i += 1
        return r

    def phase(self):
        self.sem_i = self.sem_keep
        self.sw_i = 0
        self.phase_id = getattr(self, "phase_id", 0) + 1

    def keep_sems(self):
        self.sem_keep = getattr(self, "sem_i", 0)

    def store(self, out, in_, src_res, reads, writes, is_output=False):
        if not hasattr(src_res, "_stsem") or src_res._stphase != self.phase_id:
            src_res._stsem = self.R("stsem", sw=True).sem
            src_res._stphase = self.phase_id
        return self.k.dma('pool', out, in_, reads=reads, writes=writes, sem=src_res._stsem, is_output=is_output)

    def psb(self, b):
        return self.ps[:, b, :]

    def psb16(self, b):
        return self.ps[:, b, :].bitcast(BF16)

    def consts(self):
        k, ar = self.k, self.ar
        idf = self.inp("ident", [128, 128])
        self.ident_f = ar.alloc([128])
        self.ident_b = ar.alloc([128], BF16)
        self.ones_b = ar.alloc([128], BF16)
        self.r_ident_f = self.R("ident_f", sem=True)
        self.r_ident_b = self.R("ident_b")
        self.r_ones = self.R("ones_b")
        k.dma('sp', self.ident_f, idf, writes=[self.r_ident_f])
        k.op('dve', lambda e: e.tensor_copy(out=self.ident_b, in_=self.ident_f), reads=[self.r_ident_f], writes=[self.r_ident_b])
        k.op('dve', lambda e: e.memset(self.ones_b, 1.0), writes=[self.r_ones])
        self.ones_f = ar.alloc([128]); self.r_ones_f = self.R("ones_f")
        k.op('dve', lambda e: e.memset(self.ones_f, 1.0), writes=[self.r_ones_f])
        self.AB = [[ar.alloc([16, 4]) for s in range(2)] for l in range(2)]
        self.r_AB = [[self.R(f"AB{l}{s}") for s in range(2)] for l in range(2)]
        self.GROW = [[self.scr(f"grow{l}{s}", [2, D_MODEL]) for s in range(2)] for l in range(2)]
        self.r_GROW = [[self.R(f"grow{l}{s}") for s in range(2)] for l in range(2)]
        self.keep_sems()

    def modulation(self, l):
        k, ar, nc = self.k, self.ar, self.nc
        self.phase()
        m0 = ar.mark()
        cvec = self.din["cvec"]
        ada_w = self.din["ada_w"]
        ada_b = self.din["ada_b"]
        norm_g = self.din["norm_g"]
        crow = ar.alloc([D_MODEL], F32, parts=2)
        tmp = ar.alloc([D_MODEL], F32, parts=2)
        scb = ar.alloc([D_MODEL], BF16, parts=2)
        scT = ar.alloc([16, 2], BF16)
        mod = ar.alloc([6, D_MODEL], F32, parts=2)
        gn = ar.alloc([4, D_MODEL], F32, parts=2)
        wb = [ar.alloc([16, 512], BF16) for _ in range(3)]
        r_crow = self.R("crow", sem=True); r_tmp = self.R("tmp"); r_scb = self.R("scb"); r_scT = self.R("scT")
        r_mod = self.R("modrow", sem=True); r_gn = self.R("gn", sem=True)
        r_wb = [self.R(f"modw{i}", sw=True) for i in range(3)]
        k.dma('sp', crow, cvec, writes=[r_crow])
        k.dma('sp', mod, ada_b[l].rearrange("(o s d) -> o s d", o=1, s=6).to_broadcast([2, 6, D_MODEL]), writes=[r_mod])
        k.dma('sp', gn, norm_g[l].rearrange("(o s) d -> o s d", o=1).to_broadcast([2, 4, D_MODEL]), writes=[r_gn])
        if self.cfg.get("mod_stop") == 1:
            k.dma('sp', self.dout["dbg_mod0"], mod, reads=[r_mod], is_output=True); k.barrier(); ar.release(m0); return
        k.op('act', lambda e: e.activation(out=tmp, in_=crow, func=AF.Exp, scale=-1.0), reads=[r_crow], writes=[r_tmp])
        k.op('dve', lambda e: e.tensor_scalar_add(out=tmp, in0=tmp, scalar1=1.0), reads=[r_tmp], writes=[r_tmp])
        k.op('dve', lambda e: e.reciprocal(out=tmp, in_=tmp), reads=[r_tmp], writes=[r_tmp])
        k.op('dve', lambda e: e.tensor_tensor(out=scb, in0=tmp, in1=crow, op=ALU.mult), reads=[r_tmp, r_crow], writes=[r_scb])
        if self.cfg.get("mod_stop") == 2:
            k.dma('sp', self.dout["dbg_mod0"], mod, reads=[r_mod], is_output=True); k.barrier(); ar.release(m0); return
        pst = self.psb16(7)
        for c in range(16):
            k.op('pe', lambda e, c=c: e.transpose(out=pst[:, c * 2:c * 2 + 2], in_=scb[0:2, c * 128:(c + 1) * 128], identity=self.ident_b[0:2, 0:2]),
                 reads=[r_scb, self.r_ident_b], writes=[self.psr[7]])
        k.op('dve', lambda e: e.tensor_copy(out=scT.rearrange("p a b -> p (a b)"), in_=pst[:, 0:32]), reads=[self.psr[7]], writes=[r_scT])
        if self.cfg.get("mod_stop") == 3:
            k.dma('sp', self.dout["dbg_mod0"], mod, reads=[r_mod], is_output=True); k.barrier(); ar.release(m0); return
        for j in range(24):
            b = j % 3
            k.dma('pool', wb[b], ada_w[l, :, j * 512:(j + 1) * 512].rearrange("(k p) n -> p k n", p=128), writes=[r_wb[b]])
            pb = j % 2
            for kk in range(16):
                k.op('pe', lambda e, kk=kk, b=b, pb=pb: e.matmul(self.ps[0:2, pb, :], lhsT=scT[:, kk, :], rhs=wb[b][:, kk, :], start=(kk == 0), stop=(kk == 15)),
                     reads=[r_scT, r_wb[b]], writes=[self.psr[pb]])
            s_, off = divmod(j * 512, D_MODEL)
            k.op('dve', lambda e, pb=pb, s_=s_, off=off: e.tensor_tensor(out=mod[:, s_, off:off + 512], in0=self.ps[0:2, pb, :], in1=mod[:, s_, off:off + 512], op=ALU.add),
                 reads=[self.psr[pb], r_mod], writes=[r_mod])
        if self.cfg.get("mod_stop") == 4:
            k.dma('sp', self.dout["dbg_mod0"], mod, reads=[r_mod], is_output=True); k.barrier(); ar.release(m0); return
        for s in range(2):
            o = 3 * s
            k.op('dve', lambda e, o=o, s=s: e.scalar_tensor_tensor(out=mod[:, o + 1, :], in0=mod[:, o + 1, :], scalar=1.0, in1=gn[:, 2 * s, :], op0=ALU.add, op1=ALU.mult),
                 reads=[r_mod, r_gn], writes=[r_mod])
            k.op('dve', lambda e, o=o, s=s: e.tensor_tensor(out=mod[:, o + 2, :], in0=mod[:, o + 2, :], in1=gn[:, 2 * s + 1, :], op=ALU.mult),
                 reads=[r_mod, r_gn], writes=[r_mod])
        if self.cfg.get("mod_stop") == 5:
            k.dma('sp', self.dout["dbg_mod0"], mod, reads=[r_mod], is_output=True); k.barrier(); ar.release(m0); return
        for s in range(2):
            o = 3 * s
            k.dma('sp', self.GROW[l][s], mod[:, o + 2, :], reads=[r_mod], writes=[self.r_GROW[l][s]])
            if self.cfg.get("mod_stop") == 6:
                k.dma('sp', self.dout["dbg_mod0"], mod, reads=[r_mod], is_output=True); k.barrier(); ar.release(m0); return
            pf = self.psb(6)
            for c in range(16):
                for ab in range(2):
                    k.op('pe', lambda e, c=c, ab=ab, o=o: e.transpose(out=pf[:, c * 4 + 2 * ab:c * 4 + 2 * ab + 2], in_=mod[0:2, o + 1 - ab, c * 128:(c + 1) * 128], identity=self.ident_f[0:2, 0:2]),
                         reads=[r_mod, self.r_ident_f], writes=[self.psr[6]])
            if self.cfg.get("mod_stop") == 7:
                k.dma('sp', self.dout["dbg_mod0"], mod, reads=[r_mod], is_output=True); k.barrier(); ar.release(m0); return
            k.op('dve', lambda e, s=s: e.tensor_copy(out=self.AB[l][s].rearrange("p a b -> p (a b)"), in_=pf[:, 0:64]), reads=[self.psr[6]], writes=[self.r_AB[l][s]])
        if self.cfg.get("mod_stop") == 8:
            k.dma('sp', self.dout["dbg_mod0"], mod, reads=[r_mod], is_output=True); k.barrier(); ar.release(m0); return
        if self.cfg.get("dump_mod"):
            k.dma('sp', self.dout[f"dbg_mod{l}"], mod, reads=[r_mod], is_output=True)
        k.barrier()
        ar.release(m0)

    def pre_tile(self, xt, r_xt, hT, r_hT, col0, AB, r_AB, g, W):
        k = self.k
        junk, xn, st = W["junk"], W["xn"], W["st"]
        r_junk, r_xn, r_st = W["r_junk"], W["r_xn"], W["r_st"]
        ps_ = self.cfg.get('pre_stop', 99)
        if ps_ == 0: return
        k.op('dve', lambda e: e.memset(st[:, 0:1], 0.0), writes=[r_st])
        k.op('act', lambda e: e.activation(out=junk, in_=xt, func=AF.Square, accum_out=st[:, 0:1]), reads=[r_xt, r_st], writes=[r_junk, r_st])
        if ps_ == 1: return
        k.op('act', lambda e: e.activation(out=st[:, 1:2], in_=st[:, 0:1], func=AF.Ln, scale=1.0 / D_MODEL, bias=EPS), reads=[r_st], writes=[r_st])
        k.op('act', lambda e: e.activation(out=st[:, 2:3], in_=st[:, 1:2], func=AF.Exp, scale=-0.5), reads=[r_st], writes=[r_st])
        if ps_ == 2: return
        k.op('dve', lambda e: e.tensor_scalar_mul(out=xn, in0=xt, scalar1=st[:, 2:3]), reads=[r_xt, r_st], writes=[r_xn])
        if ps_ == 3: return
        for half in range(2):
            pb = 6 + half
            pst = self.psb16(pb)
            for c8 in range(8):
                c = half * 8 + c8
                k.op('pe', lambda e, c=c, c8=c8, pst=pst: e.transpose(out=pst[:, c8 * 128:(c8 + 1) * 128], in_=xn[:, c * 128:(c + 1) * 128], identity=self.ident_b),
                     reads=[r_xn, self.r_ident_b], writes=[self.psr[pb]])
            if ps_ == 4: return
            for c8 in range(8):
                c = half * 8 + c8
                if ps_ == 5 and c8 == 1: return
                if ps_ == 6 and c8 == 2: return
                src = pst[:, c8 * 128:(c8 + 1) * 128]
                dst = hT[:, c, col0:col0 + 128]
                if True:
                    k.op('act', lambda e, src=src, dst=dst, c=c: e.activation(out=dst, in_=src, func=AF.Identity, scale=AB[:, c, g:g + 1], bias=AB[:, c, 2 + g:3 + g]),
                         reads=[self.psr[pb], r_AB], writes=[r_hT])
                else:
                    k.op('dve', lambda e, src=src, dst=dst, c=c: e.tensor_scalar(out=dst, in0=src, scalar1=AB[:, c, g:g + 1], scalar2=AB[:, c, 2 + g:3 + g], op0=ALU.mult, op1=ALU.add),
                         reads=[self.psr[pb], r_AB], writes=[r_hT])

    def post_tile(self, xt, r_xt, ypieces, Grep, r_G, W):
        k = self.k
        junk, st, t = W["junk32"], W["st2"], W["t"]
        r_junk, r_st, r_t = W["r_junk32"], W["r_st2"], W["r_t"]
        npc = len(ypieces)
        k.op('dve', lambda e: e.memset(st[:, 0:4], 0.0), writes=[r_st])
        off = 0
        for i, (yp, r_y, n) in enumerate(ypieces):
            k.op('act', lambda e, yp=yp, i=i, n=n: e.activation(out=junk[:, 0:n], in_=yp, func=AF.Square, accum_out=st[:, i:i + 1]),
                 reads=[r_y, r_st], writes=[r_junk, r_st])
        k.op('dve', lambda e: e.tensor_reduce(out=st[:, 4:5], in_=st[:, 0:4], axis=AX.X, op=ALU.add), reads=[r_st], writes=[r_st])
        k.op('act', lambda e: e.activation(out=st[:, 5:6], in_=st[:, 4:5], func=AF.Ln, scale=1.0 / D_MODEL, bias=EPS), reads=[r_st], writes=[r_st])
        k.op('act', lambda e: e.activation(out=st[:, 6:7], in_=st[:, 5:6], func=AF.Exp, scale=-0.5), reads=[r_st], writes=[r_st])
        off = 0
        for i, (yp, r_y, n) in enumerate(ypieces):
            k.op('dve', lambda e, yp=yp, off=off, n=n: e.scalar_tensor_tensor(out=t[:, off:off + n], in0=yp, scalar=st[:, 6:7], in1=Grep[:, off:off + n], op0=ALU.mult, op1=ALU.mult),
                 reads=[r_y, r_st, r_G], writes=[r_t])
            off += n
        k.op('dve', lambda e: e.tensor_tensor(out=xt, in0=xt, in1=t, op=ALU.add), reads=[r_xt, r_t], writes=[r_xt])

    def load_grep(self, l, s):
        k, ar = self.k, self.ar
        G = [ar.alloc([D_MODEL]) for g in range(2)]
        r_G = [self.R(f"Grep{g}", sem=True) for g in range(2)]
        for g in range(2):
            k.dma('sp', G[g], self.GROW[l][s][g:g + 1, :].to_broadcast([128, D_MODEL]), reads=[self.r_GROW[l][s]], writes=[r_G[g]])
        return G, r_G

    def work_bufs(self):
        ar = self.ar
        W = {}
        W["junk"] = ar.alloc([D_MODEL], BF16); W["r_junk"] = self.R("junk")
        W["xn"] = ar.alloc([D_MODEL], BF16); W["r_xn"] = self.R("xn")
        W["st"] = ar.alloc([8]); W["r_st"] = self.R("st")
        W["st2"] = ar.alloc([8]); W["r_st2"] = self.R("st2")
        W["junk32"] = W["junk"]; W["r_junk32"] = W["r_junk"]
        W["t"] = ar.alloc([D_MODEL]); W["r_t"] = self.R("t")
        return W

    def mlp_phase(self, l, Xin, r_Xin, Xout, r_Xout, out_is_output=False):
        k, ar = self.k, self.ar
        self.phase()
        m0 = ar.mark()
        w1 = self.din["mlp_w1"][l]
        w2 = self.din["mlp_w2"][l]
        if not hasattr(self, "W1C"):
            self.W1C = self.scr("W1C", [16, 128, 16 * 512], BF16)
            self.W2C = self.scr("W2C", [32, 128, 4 * 1024], BF16)
        r_W1C = [self.R(f"w1c{j}") for j in range(16)]
        r_W2C = [self.R(f"w2c{j}") for j in range(32)]
        G, r_G = self.load_grep(l, 1)
        W = self.work_bufs()
        xt = [ar.alloc([D_MODEL]) for _ in range(2)]
        r_xt = [self.R(f"xt{i}", sem=True) for i in range(2)]
        hTs = [ar.alloc([16, 512], BF16) for _ in range(2)]; r_hTs = [self.R(f"hT{i}") for i in range(2)]
        ysbs = [h_.rearrange("p a b -> p (a b)").bitcast(F32).rearrange("p (s n) -> p s n", s=4) for h_ in hTs]
        aT = ar.alloc([64, 512], BF16); r_aT = self.R("aT")
        w1b = [ar.alloc([16, 512], BF16) for _ in range(2)]
        r_w1b = [self.R(f"w1b{i}", sw=True) for i in range(2)]
        w2b = [ar.alloc([4, 1024], BF16) for _ in range(2)]
        r_w2b = [self.R(f"w2b{i}", sw=True) for i in range(2)]
        w1s = [self.R(f"w1s{i}", sem=True).sem for i in range(2)]
        w2s = [self.R(f"w2s{i}", sem=True).sem for i in range(2)]
        rt = [ar.alloc([512]) for _ in range(2)]
        r_rt = [self.R(f"rt{i}") for i in range(2)]
        AB, r_AB = self.AB[l][1], self.r_AB[l][1]
        xi = 0
        w1i = 0
        w2i = 0
        NBLK = self.cfg.get('nblk', 6)
        xi_ = [0]

        def pre_one(blk_, sub_):
            g_ = 0 if blk_ < 2 else 1
            tile_ = blk_ * 4 + sub_
            b_ = xi_[0] % 2; xi_[0] += 1
            k.dma('sp', xt[b_], Xin[tile_ * 128:(tile_ + 1) * 128, :], reads=[r_Xin[tile_]], writes=[r_xt[b_]])
            self.pre_tile(xt[b_], r_xt[b_], hTs[blk_ % 2], r_hTs[blk_ % 2], sub_ * 128, AB, r_AB, g_, W)

        for sub in range(4):
            pre_one(0, sub)
        pending = []
        for blk in range(NBLK):
            g = 0 if blk < 2 else 1
            hT, r_hT, ysb = hTs[blk % 2], r_hTs[blk % 2], ysbs[blk % 2]
            if self.cfg.get("mlp_stop") == 1:
                k.barrier(); ar.release(m0); return
            for j in range(16):
                wb_i = w1i % 2; w1i += 1
                if blk == 0:
                    k.dma('pool', w1b[wb_i], w1[:, j * 512:(j + 1) * 512].rearrange("(k p) n -> p k n", p=128), writes=[r_w1b[wb_i]])
                    k.dma('sp', self.W1C[j], w1b[wb_i].rearrange("p a b -> p (a b)"), reads=[r_w1b[wb_i]], writes=[r_W1C[j]], sem=w1s[wb_i])
                else:
                    k.dma('pool', w1b[wb_i].rearrange("p a b -> p (a b)"), self.W1C[j], reads=[r_W1C[j]], writes=[r_w1b[wb_i]])
                for cc in range(4):
                    c = j * 4 + cc
                    pb = c % 2
                    for kk in range(16):
                        k.op('pe', lambda e, kk=kk, cc=cc, wb_i=wb_i, pb=pb, hT=hT: e.matmul(self.psb(pb), lhsT=w1b[wb_i][:, kk, cc * 128:(cc + 1) * 128], rhs=hT[:, kk, :], start=(kk == 0), stop=(kk == 15)),
                             reads=[r_w1b[wb_i], r_hT], writes=[self.psr[pb]])
                    k.op('act', lambda e, pb=pb: e.activation(out=rt[pb], in_=self.psb(pb), func=AF.Relu), reads=[self.psr[pb]], writes=[r_rt[pb]])
                    k.op('dve', lambda e, pb=pb, c=c: e.tensor_tensor(out=aT[:, c, :], in0=rt[pb], in1=rt[pb], op=ALU.mult), reads=[r_rt[pb]], writes=[r_aT])
                if j < 3 and pending:
                    post_one(*pending.pop(0))
                if j % 4 == 3 and blk + 1 < NBLK:
                    pre_one(blk + 1, j // 4)
            if self.cfg.get("mlp_stop") == 2:
                k.barrier(); ar.release(m0); return
            for half in range(2):
                for j in range(16):
                    wb_i = w2i % 2; w2i += 1
                    if blk == 0:
                        k.dma('pool', w2b[wb_i], w2[j * 512:(j + 1) * 512, half * 1024:(half + 1) * 1024].rearrange("(c p) n -> p c n", p=128), writes=[r_w2b[wb_i]])
                        k.dma('sp', self.W2C[half * 16 + j], w2b[wb_i].rearrange("p a b -> p (a b)"), reads=[r_w2b[wb_i]], writes=[r_W2C[half * 16 + j]], sem=w2s[wb_i])
                    else:
                        k.dma('pool', w2b[wb_i].rearrange("p a b -> p (a b)"), self.W2C[half * 16 + j], reads=[r_W2C[half * 16 + j]], writes=[r_w2b[wb_i]])
                    for cc in range(4):
                        c = j * 4 + cc
                        for sub in range(4):
                            for n in range(2):
                                pb = sub * 2 + n
                                k.op('pe', lambda e, c=c, cc=cc, sub=sub, n=n, pb=pb, wb_i=wb_i: e.matmul(self.psb(pb), lhsT=aT[:, c, sub * 128:(sub + 1) * 128], rhs=w2b[wb_i][:, cc, n * 512:(n + 1) * 512], start=(c == 0), stop=(c == 63)),
                                     reads=[r_aT, r_w2b[wb_i]] + ([r_hT] if False else []), writes=[self.psr[pb]])
                if self.cfg.get("mlp_stop") == 3:
                    k.barrier(); ar.release(m0); return
                if half == 0:
                    for sub in range(4):
                        for n in range(2):
                            pb = sub * 2 + n
                            eng = 'act' if n == 0 else 'dve'
                            if eng == 'act':
                                k.op('act', lambda e, sub=sub, n=n, pb=pb, ysb=ysb: e.copy(out=ysb[:, sub, n * 512:(n + 1) * 512], in_=self.psb(pb)), reads=[self.psr[pb]], writes=[r_hT])
                            else:
                                k.op('dve', lambda e, sub=sub, n=n, pb=pb, ysb=ysb: e.tensor_copy(out=ysb[:, sub, n * 512:(n + 1) * 512], in_=self.psb(pb)), reads=[self.psr[pb]], writes=[r_hT])
            if self.cfg.get("mlp_stop") == 4:
                k.barrier(); ar.release(m0); return
            def post_one(blk_, sub_, ysb_, r_hT_, g_):
                tile_ = blk_ * 4 + sub_
                b_ = xi_[0] % 2; xi_[0] += 1
                k.dma('sp', xt[b_], Xin[tile_ * 128:(tile_ + 1) * 128, :], reads=[r_Xin[tile_]], writes=[r_xt[b_]])
                yp_ = [(ysb_[:, sub_, :], r_hT_, 1024), (self.psb(sub_ * 2), self.psr[sub_ * 2], 512), (self.psb(sub_ * 2 + 1), self.psr[sub_ * 2 + 1], 512)]
                self.post_tile(xt[b_], r_xt[b_], yp_, G[g_], r_G[g_], W)
                k.dma('sp', Xout[tile_ * 128:(tile_ + 1) * 128, :], xt[b_], reads=[r_xt[b_]], writes=[r_Xout[tile_]], is_output=out_is_output)

            post_one(blk, 0, ysb, r_hT, g)
            for sub in range(1, 4):
                pending.append((blk, sub, ysb, r_hT, g))
        while pending:
            post_one(*pending.pop(0))
        k.barrier()
        ar.release(m0)


def _even_scratch(self):
    if hasattr(self, "QA"):
        return
    s = self.scr
    self.QA = s("QA", [8, 128, NT]); self.FF = s("FF", [8, 128, NT]); self.FB = s("FB", [8, 128, NT]); self.GA = s("GA", [8, 128, NT])
    self.QB = s("QB", [8, 128, NT], BF16); self.KB = s("KB", [8, 128, NT], BF16)
    self.VA = s("VA", [NT, 1024], BF16); self.VB = s("VB", [NT, 1024], BF16)
    self.OA = s("OA", [16, 128, NT], BF16) if not self.cfg.get("dump_oa") else self.out("OA", [16, 128, NT], BF16)
    self.r_E1 = self.R("E1out")
    self.r_OA = self.R("OAres")


def even_inproj(self, Xin, r_Xin):
    k, ar = self.k, self.ar
    self.phase()
    _even_scratch(self)
    m0 = ar.mark()
    l = 0
    w_in = self.din["w_in_even"]
    W = self.work_bufs()
    xt = [ar.alloc([D_MODEL]) for _ in range(2)]
    r_xt = [self.R(f"e1xt{i}", sem=True) for i in range(2)]
    hT = ar.alloc([16, NT], BF16); r_hT = self.R("hTall")
    wb = [ar.alloc([16, 512], BF16) for _ in range(2)]
    r_wb = [self.R(f"e1w{i}", sw=True) for i in range(2)]
    NST = 4
    stg = [ar.alloc([512]) for _ in range(NST)]
    r_stg = [self.R(f"e1stg{i}", sem=True) for i in range(NST)]
    AB, r_AB = self.AB[l][0], self.r_AB[l][0]
    for tile in range(24):
        g = 0 if tile < 8 else 1
        b = tile % 2
        k.dma('sp', xt[b], Xin[tile * 128:(tile + 1) * 128, :], reads=[r_Xin[tile]], writes=[r_xt[b]])
        self.pre_tile(xt[b], r_xt[b], hT, r_hT, tile * 128, AB, r_AB, g, W)
    fm_dst = {0: self.QA, 1: self.FF, 2: self.FB, 4: self.GA, 5: self.QB, 6: self.KB}
    si = 0
    pbi = 0
    nak, nav = self.dout["nak"], self.dout["nav"]
    for j in self.cfg.get('e1_js', range(16)):
        grp = j // 2
        wi = j % 2
        k.dma('pool', wb[wi], w_in[:, j * 512:(j + 1) * 512].rearrange("(k p) n -> p k n", p=128), writes=[r_wb[wi]])
        if grp in fm_dst:
            dst = fm_dst[grp]
            isbf = grp in (5, 6)
            for cc in range(4):
                head = (j % 2) * 4 + cc
                for tb in range(6):
                    pb = pbi % 4; pbi += 1
                    for kk in range(16):
                        k.op('pe', lambda e, kk=kk, cc=cc, wi=wi, tb=tb, pb=pb: e.matmul(self.psb(pb), lhsT=wb[wi][:, kk, cc * 128:(cc + 1) * 128], rhs=hT[:, kk, tb * 512:(tb + 1) * 512], start=(kk == 0), stop=(kk == 15)),
                             reads=[r_wb[wi], r_hT], writes=[self.psr[pb]])
                    s_ = si % NST; si += 1
                    so = stg[s_] if not isbf else stg[s_].bitcast(BF16)[:, 0:512]
                    scale = (128.0 ** -0.5) if grp == 5 else 1.0
                    if si % 2 == 0:
                        k.op('act', lambda e, so=so, pb=pb, scale=scale: e.activation(out=so, in_=self.psb(pb), func=AF.Copy, scale=scale), reads=[self.psr[pb]], writes=[r_stg[s_]])
                    else:
                        k.op('dve', lambda e, so=so, pb=pb, scale=scale: e.tensor_scalar_mul(out=so, in0=self.psb(pb), scalar1=scale), reads=[self.psr[pb]], writes=[r_stg[s_]])
                    k.dma('sp', dst[head, :, tb * 512:(tb + 1) * 512], so, reads=[r_stg[s_]], writes=[self.r_E1])
        if grp in (3, 6, 7):
            ntile = 8 if grp == 6 else 24
            col0 = (j % 2) * 512
            for t in range(ntile):
                pb = pbi % 4; pbi += 1
                for kk in range(16):
                    k.op('pe', lambda e, kk=kk, wi=wi, t=t, pb=pb: e.matmul(self.psb(pb), lhsT=hT[:, kk, t * 128:(t + 1) * 128], rhs=wb[wi][:, kk, :], start=(kk == 0), stop=(kk == 15)),
                         reads=[r_wb[wi], r_hT], writes=[self.psr[pb]])
                if grp in (3, 7):
                    s_ = si % NST; si += 1
                    so = stg[s_].bitcast(BF16)[:, 0:512]
                    k.op('act', lambda e, so=so, pb=pb: e.copy(out=so, in_=self.psb(pb)), reads=[self.psr[pb]], writes=[r_stg[s_]])
                    d = self.VA if grp == 3 else self.VB
                    k.dma('sp', d[t * 128:(t + 1) * 128, col0:col0 + 512], so, reads=[r_stg[s_]], writes=[self.r_E1])
                if grp in (6, 7) and t < 8:
                    s_ = si % NST; si += 1
                    so = stg[s_]
                    k.op('act', lambda e, so=so, pb=pb: e.copy(out=so, in_=self.psb(pb)), reads=[self.psr[pb]], writes=[r_stg[s_]])
                    d = nak if grp == 6 else nav
                    k.dma('sp', d[t * 128:(t + 1) * 128, col0:col0 + 512], so, reads=[r_stg[s_]], is_output=True)
    k.barrier()
    ar.release(m0)


Builder.even_inproj = even_inproj


def hgrn_phase(self):
    k, ar = self.k, self.ar
    self.phase()
    _even_scratch(self)
    m0 = ar.mark()
    CH = 64
    BT = 512
    cmask = ar.alloc([1024]); r_cmask = self.R("cmask", sem=True)
    trim = ar.alloc([128], parts=64); r_trim = self.R("trim", sem=True)
    k.dma('sp', cmask, self.din["cmask"], writes=[r_cmask])
    k.dma('sp', trim, self.din["trimask"], writes=[r_trim])
    gn = ar.alloc([1]); r_gn = self.R("hgn", sem=True)
    with self.nc.allow_non_contiguous_dma(reason="tiny"):
        k.dma('sp', gn, self.din["hgn"].rearrange("o p -> p o"), writes=[r_gn])
    lbr = ar.alloc([2, 1024], parts=3); r_lbr = self.R("lbr", sem=True)
    k.dma('sp', lbr, self.din["lbrows"].rearrange("d r c -> r d c"), writes=[r_lbr])
    k.op('act', I("activation", out=lbr, in_=lbr, func=AF.Exp), reads=[r_lbr], writes=[r_lbr])
    pf = self.psb(7)
    for d in range(2):
        for h in range(8):
            k.op('pe', I("transpose", out=pf[:, (d * 8 + h) * 3:(d * 8 + h) * 3 + 3], in_=lbr[0:3, d, h * 128:(h + 1) * 128], identity=self.ident_f[0:3, 0:3]),
                 reads=[r_lbr, self.r_ident_f], writes=[self.psr[7]])
    lbe = ar.alloc([16, 3]); r_lbe = self.R("lbe")
    omlb = ar.alloc([16]); r_omlb = self.R("omlb")
    lsum = ar.alloc([16]); r_lsum = self.R("lsum")
    k.op('dve', I("tensor_copy", out=lbe.rearrange("p a r -> p (a r)"), in_=pf[:, 0:48]), reads=[self.psr[7]], writes=[r_lbe])
    k.op('dve', I("tensor_reduce", out=lsum, in_=lbe, axis=AX.X, op=ALU.add), reads=[r_lbe], writes=[r_lsum])
    k.op('dve', I("reciprocal", out=lsum, in_=lsum), reads=[r_lsum], writes=[r_lsum])
    k.op('dve', I("tensor_tensor", out=omlb, in0=lbe[:, :, 0], in1=lsum, op=ALU.mult), reads=[r_lbe, r_lsum], writes=[r_omlb])
    lbv = ar.alloc([16]); lnom = ar.alloc([16])
    k.op('dve', I("tensor_copy", out=lbv, in_=omlb), reads=[r_omlb], writes=[r_omlb])
    k.op('dve', I("tensor_scalar", out=omlb, in0=omlb, scalar1=-1.0, scalar2=1.0, op0=ALU.mult, op1=ALU.add), reads=[r_omlb], writes=[r_omlb])
    k.op('act', I("activation", out=lnom, in_=omlb, func=AF.Ln), reads=[r_omlb], writes=[r_omlb])
    lnsc = ar.alloc([1])
    k.op('dve', I("memset", lnsc, float(math.log(128.0 ** -0.5))), writes=[r_omlb])
    k.barrier()
    NC_ = 4
    def f32buf(n=BT): return ar.alloc([n])
    qT = [f32buf() for _ in range(3)]; r_qT = [self.R(f"hq{i}", sem=True) for i in range(3)]
    fT = [f32buf() for _ in range(3)]; r_fT = [self.R(f"hf{i}", sem=True) for i in range(3)]
    kt = f32buf(); r_kt = self.R("kt")
    lf = f32buf(); r_lf = self.R("lf")
    bc = f32buf(); r_bc = self.R("bc")
    rv = f32buf(); r_rv = self.R("rv")
    sq = f32buf(); r_sq = self.R("sq")
    tm = f32buf(); r_tm = self.R("tm")
    tm2 = f32buf(); r_tm2 = self.R("tm2")
    vt = [[ar.alloc([8, 128], BF16, parts=64) for _ in range(2)] for _ in range(NC_)]
    r_vt = [[self.R(f"hv{c}{i}", sem=True) for i in range(2)] for c in range(NC_)]
    eb = [[f32buf() for _ in range(2)] for _ in range(NC_)]; r_eb = [[self.R(f"eb{c}{i}") for i in range(2)] for c in range(NC_)]
    qb = [[ar.alloc([BT], BF16) for _ in range(2)] for _ in range(NC_)]; r_qb = [[self.R(f"qb{c}{i}") for i in range(2)] for c in range(NC_)]
    kb = [[ar.alloc([BT], BF16) for _ in range(2)] for _ in range(NC_)]; r_kb = [[self.R(f"kb{c}{i}") for i in range(2)] for c in range(NC_)]
    kd = [[ar.alloc([BT], BF16) for _ in range(2)] for _ in range(NC_)]; r_kd = [[self.R(f"kd{c}{i}") for i in range(2)] for c in range(NC_)]
    S32 = [ar.alloc([128]) for _ in range(NC_)]; r_S32 = [self.R(f"S32{c}", sem=True) for c in range(NC_)]
    Sbf = [ar.alloc([128], BF16) for _ in range(NC_)]; r_Sbf = [self.R(f"Sbf{c}") for c in range(NC_)]
    Asb = [[ar.alloc([CH], BF16, parts=64) for _ in range(2)] for _ in range(NC_)]; r_Asb = [[self.R(f"Asb{c}{i}") for i in range(2)] for c in range(NC_)]
    kdt = [[ar.alloc([128], BF16, parts=64) for _ in range(2)] for _ in range(NC_)]; r_kdt = [[self.R(f"kdt{c}{i}") for i in range(2)] for c in range(NC_)]
    Oacc = [ar.alloc([2048]) for _ in range(2)]; r_Oacc = [self.R(f"Oacc{i}") for i in range(2)]
    Oacb = [ar.alloc([2048]) for _ in range(2)]; r_Oacb = [self.R(f"Oacb{i}") for i in range(2)]
    gT = [f32buf() for _ in range(2)]; r_gT = [self.R(f"hg{i}", sem=True) for i in range(2)]
    sq16 = ar.alloc([BT], BF16); r_sq16 = self.R("sq16")
    rstd = f32buf(); r_rstd = self.R("rstdh")
    ost = [ar.alloc([BT], BF16) for _ in range(2)]; r_ost = [self.R(f"host{i}", sem=True) for i in range(2)]
    def regA(c, p): return self.ps[0:CH, 2 * c, p * 64:p * 64 + CH]
    def regU(c, p): return self.ps[:, 2 * c, 128 + p * 128:256 + p * 128]
    def regOd(c, p): return self.ps[:, 2 * c, 384 + p * 64:448 + p * 64]
    def regK(c, p): return self.psb16(2 * c + 1)[0:CH, p * 128:(p + 1) * 128]
    def regOa(c, p): return self.ps[:, 2 * c + 1, 128 + p * 64:192 + p * 64]
    r_bD = [self.psr[2 * c] for c in range(NC_)]
    r_bA = [self.psr[2 * c + 1] for c in range(NC_)]
    r_pA = [[r_bD[c] for p in range(2)] for c in range(NC_)]
    r_pU = [[r_bD[c] for p in range(2)] for c in range(NC_)]
    r_pK = [[r_bA[c] for p in range(2)] for c in range(NC_)]
    r_pO = [[(r_bA[c] if c % 2 == 0 else r_bD[c]) for p in range(2)] for c in range(NC_)]
    r_fin = self.R("pfin")
    srcs = {0: self.FF, 1: self.FB}
    cnt = {"ld": 0, "fin": 0}
    stepc = [0] * NC_
    chc = [0] * NC_
    seqs = [(i * 256, 256, True, i) for i in range(4)] + [(1024, 2048, False, 0)]
    seqs = seqs[self.cfg.get('hg_s0', 0):self.cfg.get('hg_s1', 5)]
    def prep_chain(c, h, d, t0, bt, nblk, ncb, step, sl, par, blkof):
        blk = step if d == 0 else nblk - 1 - step
        blkof[c] = blk
        c0 = t0 + blk * bt
        li = cnt["ld"] % 3; cnt["ld"] += 1
        pi = stepc[c] % 2; stepc[c] += 1
        par[c] = pi
        k.dma('sp', qT[li][:, sl], self.QA[h, :, c0:c0 + bt], reads=[self.r_E1], writes=[r_qT[li]])
        k.dma('sp', fT[li][:, sl], srcs[d][h, :, c0:c0 + bt], reads=[self.r_E1], writes=[r_fT[li]])
        k.dma('sp', vt[c][pi][:, 0:ncb, :], self.VA[c0:c0 + bt, h * 128:(h + 1) * 128].rearrange("(c p) v -> p c v", p=CH), reads=[self.r_E1], writes=[r_vt[c][pi]])
        q_, f_ = qT[li][:, sl], fT[li][:, sl]
        hd = d * 8 + h
        lb_ap, lno_ap = lbv[:, hd:hd + 1], lnom[:, hd:hd + 1]
        k.op('act', I("activation", out=kt[:, sl], in_=f_, func=AF.Exp), reads=[r_fT[li]], writes=[r_kt])
        k.op('act', I("activation", out=tm[:, sl], in_=kt[:, sl], func=AF.Ln, bias=1.0), reads=[r_kt], writes=[r_tm])
        k.op('act', I("activation", out=lf[:, sl], in_=kt[:, sl], func=AF.Ln, bias=lb_ap), reads=[r_kt, r_omlb], writes=[r_lf])
        k.op('dve', I("tensor_tensor", out=lf[:, sl], in0=lf[:, sl], in1=tm[:, sl], op=ALU.subtract), reads=[r_lf, r_tm], writes=[r_lf])
        mF, mB = cmask[:, 0:bt], cmask[:, 512:512 + bt]
        if d == 0:
            k.op('dve', I("tensor_tensor_scan", out=bc[:, sl], data0=mF, data1=lf[:, sl], initial=0.0, op0=ALU.mult, op1=ALU.add), reads=[r_lf, r_cmask], writes=[r_bc])
            k.op('dve', I("tensor_tensor_scan", out=rv[:, sl][:, ::-1], data0=mB[:, ::-1], data1=lf[:, sl][:, ::-1], initial=0.0, op0=ALU.mult, op1=ALU.add), reads=[r_lf, r_cmask], writes=[r_rv])
        else:
            k.op('dve', I("tensor_tensor_scan", out=bc[:, sl][:, ::-1], data0=mB[:, ::-1], data1=lf[:, sl][:, ::-1], initial=0.0, op0=ALU.mult, op1=ALU.add), reads=[r_lf, r_cmask], writes=[r_bc])
            k.op('dve', I("tensor_tensor_scan", out=rv[:, sl], data0=mF, data1=lf[:, sl], initial=0.0, op0=ALU.mult, op1=ALU.add), reads=[r_lf, r_cmask], writes=[r_rv])
        dcol0 = CH - 1 if d == 0 else 0
        k.op('act', I("activation", out=eb[c][pi][:, 0:ncb], in_=bc[:, dcol0:bt:CH], func=AF.Exp), reads=[r_bc], writes=[r_eb[c][pi]])
        k.op('dve', I("tensor_tensor", out=rv[:, sl], in0=rv[:, sl], in1=lf[:, sl], op=ALU.subtract), reads=[r_rv, r_lf], writes=[r_rv])
        k.op('dve', I("tensor_tensor", out=rv[:, sl], in0=rv[:, sl], in1=tm[:, sl], op=ALU.subtract), reads=[r_rv, r_tm], writes=[r_rv])
        k.op('act', I("activation", out=kd[c][pi][:, sl], in_=rv[:, sl], func=AF.Exp, bias=lno_ap), reads=[r_rv, r_omlb], writes=[r_kd[c][pi]])
        k.op('dve', I("tensor_tensor", out=tm[:, sl], in0=tm[:, sl], in1=bc[:, sl], op=ALU.add), reads=[r_tm, r_bc], writes=[r_tm])
        k.op('act', I("activation", out=kb[c][pi][:, sl], in_=tm[:, sl], func=AF.Exp, scale=-1.0, bias=lno_ap), reads=[r_tm, r_omlb], writes=[r_kb[c][pi]])
        k.op('act', I("activation", out=sq[:, sl], in_=q_, func=AF.Exp, scale=-1.0), reads=[r_qT[li]], writes=[r_sq])
        k.op('act', I("activation", out=sq[:, sl], in_=sq[:, sl], func=AF.Ln, bias=1.0), reads=[r_sq], writes=[r_sq])
        k.op('dve', I("tensor_tensor", out=sq[:, sl], in0=bc[:, sl], in1=sq[:, sl], op=ALU.subtract), reads=[r_bc, r_sq], writes=[r_sq])
        k.op('act', I("activation", out=tm2[:, sl], in_=sq[:, sl], func=AF.Exp, bias=lnsc[:, 0:1]), reads=[r_sq, r_omlb], writes=[r_tm2])
        k.op('dve', I("tensor_tensor", out=qb[c][pi][:, sl], in0=q_, in1=tm2[:, sl], op=ALU.mult), reads=[r_qT[li], r_tm2], writes=[r_qb[c][pi]])

    def chunk_step(cstep, chains, bt, ncb, par, blkof):
        info = []
        for c, (h, d) in enumerate(chains):
            cc = cstep if d == 0 else ncb - 1 - cstep
            pp = chc[c] % 2; chc[c] += 1
            pi = par[c]
            cs = cc * CH
            info.append((c, h, d, cc, pp, pi, cs))
        for (c, h, d, cc, pp, pi, cs) in info:
            qbc, kbc, kdc = qb[c][pi][:, cs:cs + CH], kb[c][pi][:, cs:cs + CH], kd[c][pi][:, cs:cs + CH]
            k.op('pe', I("matmul", regA(c, pp), lhsT=kbc, rhs=qbc, start=True, stop=True), reads=[r_kb[c][pi], r_qb[c][pi]], writes=[r_pA[c][pp]])
            k.op('pe', I("transpose", out=regK(c, pp), in_=kdc, identity=self.ident_b), reads=[r_kd[c][pi], self.r_ident_b], writes=[r_pK[c][pp]])
        for (c, h, d, cc, pp, pi, cs) in info:
            mk = trim[:, 0:CH] if d == 0 else trim[:, CH:2 * CH]
            k.op('dve', I("tensor_tensor", out=Asb[c][pp], in0=regA(c, pp), in1=mk, op=ALU.mult), reads=[r_pA[c][pp], r_trim], writes=[r_Asb[c][pp]])
            k.op('act', I("copy", out=kdt[c][pp], in_=regK(c, pp)), reads=[r_pK[c][pp]], writes=[r_kdt[c][pp]])
        for (c, h, d, cc, pp, pi, cs) in info:
            qbc = qb[c][pi][:, cs:cs + CH]
            vch = vt[c][pi][:, cc, :]
            ro = regOa(c, pp) if d == 0 else regOd(c, pp)
            k.op('pe', I("matmul", ro, lhsT=Sbf[c], rhs=qbc, start=True, stop=False), reads=[r_Sbf[c], r_qb[c][pi]], writes=[r_pO[c][pp]])
            k.op('pe', I("matmul", ro, lhsT=vch, rhs=Asb[c][pp], start=False, stop=True), reads=[r_vt[c][pi], r_Asb[c][pp]], writes=[r_pO[c][pp]])
            k.op('pe', I("matmul", regU(c, pp), lhsT=kdt[c][pp], rhs=vch, start=True, stop=True), reads=[r_kdt[c][pp], r_vt[c][pi]], writes=[r_pU[c][pp]])
        for (c, h, d, cc, pp, pi, cs) in info:
            blk = blkof[c]
            oc = Oacc[c // 2][:, blk * bt + cs: blk * bt + cs + CH]
            if d == 0:
                k.op('act', I("copy", out=oc, in_=regOa(c, pp)), reads=[r_pO[c][pp]], writes=[r_Oacc[c // 2]])
            dec = eb[c][pi][:, cc:cc + 1]
            k.op('dve', I("scalar_tensor_tensor", out=S32[c], in0=S32[c], scalar=dec, in1=regU(c, pp), op0=ALU.mult, op1=ALU.add), reads=[r_S32[c], r_eb[c][pi], r_pU[c][pp]], writes=[r_S32[c]])
            k.op('act', I("copy", out=Sbf[c], in_=S32[c]), reads=[r_S32[c]], writes=[r_Sbf[c]])
        for (c, h, d, cc, pp, pi, cs) in info:
            if d == 1:
                blk = blkof[c]
                oc = Oacb[c // 2][:, blk * bt + cs: blk * bt + cs + CH]
                k.op('dve', I("tensor_copy", out=oc, in_=regOd(c, pp)), reads=[r_pO[c][pp]], writes=[r_Oacb[c // 2]])

    def finish_group(chains, t0, bt, nblk, sl, is_ctx, sidx, hp):
        if is_ctx:
            for c, (h, d) in enumerate(chains):
                dst = self.dout["nsf" if d == 0 else "nsb"]
                self.store(dst[sidx, h], S32[c], r_S32[c], reads=[r_S32[c]], writes=[], is_output=True)
        for hh in range(0 if self.cfg.get('hg_nofin') else 2):
            h = 2 * hp + hh
            for blk in range(nblk):
                c0 = t0 + blk * bt
                gi = cnt["fin"] % 2; cnt["fin"] += 1
                ob = Oacc[hh][:, blk * bt:(blk + 1) * bt]
                obb = Oacb[hh][:, blk * bt:(blk + 1) * bt]
                k.dma('sp', gT[gi][:, sl], self.GA[h, :, c0:c0 + bt], reads=[self.r_E1], writes=[r_gT[gi]])
                k.op('dve', I("tensor_tensor", out=ob, in0=ob, in1=obb, op=ALU.add), reads=[r_Oacc[hh], r_Oacb[hh]], writes=[r_Oacc[hh]])
                k.op('act', I("activation", out=sq16[:, sl], in_=ob, func=AF.Square), reads=[r_Oacc[hh]], writes=[r_sq16])
                pfin = self.ps[:, 7, 0:bt]
                fin_res = [r_bA[3]]
                k.op('pe', I("matmul", pfin, lhsT=self.ones_b, rhs=sq16[:, sl], start=True, stop=True), reads=[self.r_ones, r_sq16], writes=fin_res)
                k.op('act', I("activation", out=rstd[:, sl], in_=pfin, func=AF.Ln, scale=1.0 / 128, bias=EPS), reads=fin_res, writes=[r_rstd])
                g_ = gT[gi][:, sl]
                k.op('act', I("activation", out=tm[:, sl], in_=g_, func=AF.Exp, scale=-1.0), reads=[r_gT[gi]], writes=[r_tm])
                k.op('act', I("activation", out=tm[:, sl], in_=tm[:, sl], func=AF.Ln, bias=1.0), reads=[r_tm], writes=[r_tm])
                k.op('dve', I("scalar_tensor_tensor", out=rstd[:, sl], in0=rstd[:, sl], scalar=-0.5, in1=tm[:, sl], op0=ALU.mult, op1=ALU.subtract), reads=[r_rstd, r_tm], writes=[r_rstd])
                k.op('act', I("activation", out=rstd[:, sl], in_=rstd[:, sl], func=AF.Exp), reads=[r_rstd], writes=[r_rstd])
                k.op('dve', I("tensor_tensor", out=tm[:, sl], in0=ob, in1=g_, op=ALU.mult), reads=[r_Oacc[hh], r_gT[gi], r_tm], writes=[r_tm])
                k.op('dve', I("scalar_tensor_tensor", out=ost[gi][:, sl], in0=tm[:, sl], scalar=gn[:, 0:1], in1=rstd[:, sl], op0=ALU.mult, op1=ALU.mult), reads=[r_tm, r_rstd, r_gn], writes=[r_ost[gi]])
                self.store(self.OA[h, :, c0:c0 + bt], ost[gi][:, sl], r_ost[gi], reads=[r_ost[gi]], writes=[self.r_OA], is_output=bool(self.cfg.get("dump_oa")))

    items = []
    for (t0, T, is_ctx, sidx) in seqs:
        bt = min(BT, T)
        nblk = T // bt
        for hp in range(self.cfg.get('hg_nhp', 4)):
            for step in range(nblk):
                items.append((t0, T, is_ctx, sidx, hp, step))
    pars = [[0] * NC_ for _ in items]
    blkofs = [[0] * NC_ for _ in items]

    def do_prep(ii, c):
        (t0, T, is_ctx, sidx, hp, step) = items[ii]
        bt = min(BT, T); nblk = T // bt; ncb = bt // CH
        h, d = 2 * hp + (c // 2), c % 2
        prep_chain(c, h, d, t0, bt, nblk, ncb, step, slice(0, bt), pars[ii], blkofs[ii])

    for c in range(NC_):
        do_prep(0, c)
    for ii, (t0, T, is_ctx, sidx, hp, step) in enumerate(items):
        bt = min(BT, T); nblk = T // bt; ncb = bt // CH
        sl = slice(0, bt)
        chains = [(2 * hp + (c // 2), c % 2) for c in range(NC_)]
        if step == 0:
            for c, (h, d) in enumerate(chains):
                if is_ctx:
                    k.op('dve', I("memset", S32[c], 0.0), writes=[r_S32[c]])
                    k.op('dve', I("memset", Sbf[c], 0.0), writes=[r_Sbf[c]])
                else:
                    k.dma('sp', S32[c], self.din["st_f" if d == 0 else "st_b"][h], writes=[r_S32[c]])
                    k.op('act', I("copy", out=Sbf[c], in_=S32[c]), reads=[r_S32[c]], writes=[r_Sbf[c]])
        nxt = ii + 1 if ii + 1 < len(items) else None
        done = 0
        for cstep in range(ncb):
            chunk_step(cstep, chains, bt, ncb, pars[ii], blkofs[ii])
            if nxt is not None:
                want = ((cstep + 1) * NC_) // ncb
                while done < want:
                    do_prep(nxt, done); done += 1
        if nxt is not None:
            while done < NC_:
                do_prep(nxt, done); done += 1
        if step == nblk - 1:
            finish_group(chains, t0, bt, nblk, sl, is_ctx, sidx, hp)
    k.barrier()
    ar.release(m0)


Builder.hgrn_phase = hgrn_phase


def _attn_finish(self, po, r_po, rec, r_rec, ob16, r_ob16, pT, r_pT, ostg_slice, r_ostg):
    k = self.k
    k.op('dve', I("reciprocal", out=rec, in_=po[:, 128:129]), reads=[r_po], writes=[r_rec])
    k.op('dve', I("tensor_scalar_mul", out=ob16, in0=po[:, 0:128], scalar1=rec), reads=[r_po, r_rec], writes=[r_ob16])
    k.op('pe', I("transpose", out=pT, in_=ob16, identity=self.ident_b), reads=[r_ob16, self.r_ident_b], writes=[r_pT])
    k.op('act', I("copy", out=ostg_slice, in_=pT), reads=[r_pT], writes=[r_ostg])


def na_phase(self):
    k, ar = self.k, self.ar
    self.phase()
    _even_scratch(self)
    m0 = ar.mark()
    QT = [ar.alloc([2048], BF16) for _ in range(2)]; r_QT = [self.R(f"naQ{i}", sem=True) for i in range(2)]
    KT = [ar.alloc([2048], BF16) for _ in range(2)]; r_KT = [self.R(f"naK{i}", sem=True) for i in range(2)]
    vaug = [ar.alloc([16, 129], BF16) for _ in range(2)]; r_vaug = [self.R(f"naV{i}", sem=True) for i in range(2)]
    for i in range(2):
        k.op('pool', I("memset", vaug[i][:, :, 128:129], 1.0), writes=[r_vaug[i]])
    rec = [ar.alloc([1]) for _ in range(2)]; r_rec = [self.R(f"narec{i}") for i in range(2)]
    ob16 = [ar.alloc([128], BF16) for _ in range(2)]; r_ob16 = [self.R(f"naob{i}") for i in range(2)]
    ostg = [ar.alloc([512], BF16) for _ in range(2)]; r_ostg = [self.R(f"naost{i}", sem=True) for i in range(2)]
    Pc = [ar.alloc([4, 512], BF16) for _ in range(2)]; r_Pc = [self.R(f"naPc{i}") for i in range(2)]
    Praw = [ar.alloc([128], BF16) for _ in range(3)]; r_Praw = [self.R(f"naPr{i}") for i in range(3)]
    Pacc = [ar.alloc([512]) for _ in range(2)]; r_Pacc = [self.R(f"naPacc{i}") for i in range(2)]
    recb = ar.alloc([512]); r_recb = self.R("narecb")
    Pl = [ar.alloc([128], BF16) for _ in range(6)]; r_Pl = [self.R(f"naPl{i}") for i in range(6)]
    cnt = {"h": 0, "o": 0, "f": 0, "pl": 0, "pr": 0}
    for sq_ in range(4):
        t0 = sq_ * 256
        for h in range(8):
            hi = cnt["h"] % 2; cnt["h"] += 1
            k.dma('sp', QT[hi][:, 0:256], self.QB[h, :, t0:t0 + 256], reads=[self.r_E1], writes=[r_QT[hi]])
            k.dma('sp', KT[hi][:, 0:256], self.KB[h, :, t0:t0 + 256], reads=[self.r_E1], writes=[r_KT[hi]])
            k.dma('sp', vaug[hi][:, 0:2, 0:128], self.VB[t0:t0 + 256, h * 128:(h + 1) * 128].rearrange("(c p) v -> p c v", p=128), reads=[self.r_E1], writes=[r_vaug[hi]])
            pci = hi
            for kc in range(2):
                pb = kc
                k.op('pe', I("matmul", self.ps[:, pb, 0:256], lhsT=KT[hi][:, kc * 128:(kc + 1) * 128], rhs=QT[hi][:, 0:256], start=True, stop=True),
                     reads=[r_KT[hi], r_QT[hi]], writes=[self.psr[pb]])
                k.op('act', I("activation", out=Pc[pci][:, kc, 0:256], in_=self.ps[:, pb, 0:256], func=AF.Exp), reads=[self.psr[pb]], writes=[r_Pc[pci]])
            oi = cnt["o"] % 2; cnt["o"] += 1
            for qt in range(2):
                fi = cnt["f"] % 2; cnt["f"] += 1
                pv = 4 + fi
                po = self.ps[:, pv, 0:129]
                for kc in range(2):
                    k.op('pe', I("matmul", po, lhsT=Pc[pci][:, kc, qt * 128:(qt + 1) * 128], rhs=vaug[hi][:, kc, :], start=(kc == 0), stop=(kc == 1)),
                         reads=[r_Pc[pci], r_vaug[hi]], writes=[self.psr[pv]])
                pT = self.psb16(6)[:, fi * 128:(fi + 1) * 128]
                _attn_finish(self, po, self.psr[pv], rec[fi], r_rec[fi], ob16[fi], r_ob16[fi], pT, self.psr[6], ostg[oi][:, qt * 128:(qt + 1) * 128], r_ostg[oi])
            self.store(self.OA[8 + h, :, t0:t0 + 256], ostg[oi][:, 0:256], r_ostg[oi], reads=[r_ostg[oi]], writes=[self.r_OA], is_output=bool(self.cfg.get("dump_oa")))
    colm = ar.alloc([64]); r_colm = self.R("colm", sem=True)
    k.dma('sp', colm, self.din["colmask"], writes=[r_colm])
    Traw = ar.alloc([14, 64]); r_Traw = self.R("Traw", sem=True)
    Cexp = ar.alloc([14, 64], BF16); r_Cexp = self.R("Cexp")
    def ws(r): return min(max(r - 4, 0), 24)
    types = {}
    plan = []
    for j in range(16):
        lo = ws(2 * j) // 2
        hi_ = (ws(2 * j + 1) + 7) // 2
        lst = []
        for kc in range(lo, hi_ + 1):
            dl = 2 * kc - 2 * j
            valid = tuple(tuple(ws(2 * j + b) <= 2 * kc + a <= ws(2 * j + b) + 7 for b in range(2)) for a in range(2))
            key = (dl, valid)
            if key not in types:
                types[key] = len(types)
            lst.append((kc, types[key]))
        plan.append(lst)
    ntyp = len(types)
    EBt = ar.alloc([ntyp, 128], BF16); r_EBt = self.R("EBt")
    Kc32 = ar.alloc([4, 128]); r_Kc32 = self.R("Kc32", sem=True)
    Kc16 = ar.alloc([4, 128], BF16); r_Kc16 = self.R("Kc16")
    KcT = ar.alloc([512], BF16); r_KcT = self.R("KcT")
    Vc32 = ar.alloc([4, 128]); r_Vc32 = self.R("Vc32", sem=True)
    vaugc = ar.alloc([4, 129], BF16); r_vaugc = self.R("vaugc")
    k.op('pool', I("memset", vaugc[:, :, 128:129], 1.0), writes=[r_vaugc])
    rpbr = self.din["rpbr"]
    T0 = 1024
    for h in range(8):
        hi = cnt["h"] % 2; cnt["h"] += 1
        k.dma('sp', QT[hi], self.QB[h, :, T0:T0 + 2048], reads=[self.r_E1], writes=[r_QT[hi]])
        k.dma('sp', KT[hi], self.KB[h, :, T0:T0 + 2048], reads=[self.r_E1], writes=[r_KT[hi]])
        k.dma('sp', vaug[hi][:, :, 0:128], self.VB[T0:T0 + 2048, h * 128:(h + 1) * 128].rearrange("(c p) v -> p c v", p=128), reads=[self.r_E1], writes=[r_vaug[hi]])
        k.dma('sp', Kc32, self.din["cnak"][:, h, :].rearrange("(c p) d -> p c d", p=128), writes=[r_Kc32])
        k.dma('sp', Vc32, self.din["cnav"][:, h, :].rearrange("(c p) d -> p c d", p=128), writes=[r_Vc32])
        k.op('pool', I("tensor_copy", out=Kc16, in_=Kc32), reads=[r_Kc32], writes=[r_Kc16])
        k.op('pool', I("tensor_copy", out=vaugc[:, :, 0:128], in_=Vc32), reads=[r_Vc32], writes=[r_vaugc])
        for c in range(4):
            pT = self.psb16(6)[:, c * 128:(c + 1) * 128]
            k.op('pe', I("transpose", out=pT, in_=Kc16[:, c, :], identity=self.ident_b), reads=[r_Kc16, self.r_ident_b], writes=[self.psr[6]])
        k.op('act', I("copy", out=KcT, in_=self.psb16(6)[:, 0:512]), reads=[self.psr[6]], writes=[r_KcT])
        for half in range(2):
            src = bass.AP(tensor=rpbr.tensor, offset=h * 15 * 8192 + half * 8192 + 63, ap=[[127, 64], [8192, 14], [1, 64]])
            k.dma('sp', Traw[half * 64:(half + 1) * 64], src, writes=[r_Traw])
        k.op('act', I("activation", out=Traw, in_=Traw, func=AF.Exp), reads=[r_Traw], writes=[r_Traw])
        for i in range(14):
            k.op('dve', I("tensor_tensor", out=Cexp[:, i, :], in0=Traw[:, i, :], in1=colm, op=ALU.mult), reads=[r_Traw, r_colm], writes=[r_Cexp])
        for (dl, valid), ti in types.items():
            for b in range(2):
                k.op('pool', I("tensor_copy", out=EBt[:, ti, b * 64:(b + 1) * 64], in_=Cexp[:, dl - b + 7, :]), reads=[r_Cexp], writes=[r_EBt])
            for a in range(2):
                for b in range(2):
                    if not valid[a][b]:
                        k.op('pool', I("memset", EBt[a * 64:(a + 1) * 64, ti, b * 64:(b + 1) * 64], 0.0), reads=[r_EBt], writes=[r_EBt])
        for jg in range(4):
            pci = cnt["o"] % 2; cnt["o"] += 1
            po = self.ps[:, 4 + pci, :]
            r_po = self.psr[4 + pci]
            qs = slice(jg * 512, (jg + 1) * 512)
            for c in range(4):
                pb = c % 2
                k.op('pe', I("matmul", self.ps[:, pb, :], lhsT=KcT[:, c * 128:(c + 1) * 128], rhs=QT[hi][:, qs], start=True, stop=True),
                     reads=[r_KcT, r_QT[hi]], writes=[self.psr[pb]])
                k.op('act', I("activation", out=Pc[pci][:, c, :], in_=self.ps[:, pb, :], func=AF.Exp), reads=[self.psr[pb]], writes=[r_Pc[pci]])
                k.op('pe', I("matmul", po, lhsT=vaugc[:, c, 0:128], rhs=Pc[pci][:, c, :], start=(c == 0), stop=False), reads=[r_Pc[pci], r_vaugc], writes=[r_po])
                if c == 1:
                    k.op('dve', I("tensor_tensor", out=Pacc[pci], in0=Pc[pci][:, 0, :], in1=Pc[pci][:, 1, :], op=ALU.add), reads=[r_Pc[pci]], writes=[r_Pacc[pci]])
                elif c > 1:
                    k.op('dve', I("tensor_tensor", out=Pacc[pci], in0=Pacc[pci], in1=Pc[pci][:, c, :], op=ALU.add), reads=[r_Pc[pci], r_Pacc[pci]], writes=[r_Pacc[pci]])
            lat = []
            for jj in range(4):
                j = jg * 4 + jj
                for (kc, ti) in plan[j]:
                    lat.append((jj, j, kc, ti))
            NL = len(lat)
            sbk = []; pls_ = []; prs = []
            for n_ in range(NL):
                sbk.append((2, 3, 7)[cnt["pr"] % 3]); prs.append(cnt["pr"] % 3); cnt["pr"] += 1
                pls_.append(cnt["pl"] % 6); cnt["pl"] += 1

            def S_lat(n_):
                jj, j, kc, ti = lat[n_]
                pb = sbk[n_]
                k.op('pe', I("matmul", self.ps[:, pb, 0:128], lhsT=KT[hi][:, kc * 128:(kc + 1) * 128], rhs=QT[hi][:, j * 128:(j + 1) * 128], start=True, stop=True),
                     reads=[r_KT[hi], r_QT[hi]], writes=[self.psr[pb]])

            S_lat(0)
            if NL > 1:
                S_lat(1)
            for n_ in range(NL):
                jj, j, kc, ti = lat[n_]
                if n_ + 2 < NL:
                    S_lat(n_ + 2)
                pb, pr, pl = sbk[n_], prs[n_], pls_[n_]
                k.op('act', I("activation", out=Praw[pr], in_=self.ps[:, pb, 0:128], func=AF.Exp), reads=[self.psr[pb]], writes=[r_Praw[pr]])
                k.op('dve', I("tensor_tensor", out=Pl[pl], in0=Praw[pr], in1=EBt[:, ti, :], op=ALU.mult), reads=[r_Praw[pr], r_EBt], writes=[r_Pl[pl]])
                k.op('pe', I("matmul", po[:, jj * 128:(jj + 1) * 128], lhsT=vaug[hi][:, kc, 0:128], rhs=Pl[pl], start=False, stop=(n_ == NL - 1)),
                     reads=[r_Pl[pl], r_vaug[hi]], writes=[r_po])
                k.op('dve', I("tensor_tensor", out=Pacc[pci][:, jj * 128:(jj + 1) * 128], in0=Pacc[pci][:, jj * 128:(jj + 1) * 128], in1=Pl[pl], op=ALU.add), reads=[r_Pacc[pci], r_Pl[pl]], writes=[r_Pacc[pci]])
            pden = self.ps[:, 6, :]
            k.op('pe', I("matmul", pden, lhsT=self.ones_f, rhs=Pacc[pci], start=True, stop=True), reads=[self.r_ones_f, r_Pacc[pci]], writes=[self.psr[6]])
            k.op('dve', I("reciprocal", out=recb, in_=pden), reads=[self.psr[6]], writes=[r_recb])
            k.op('dve', I("tensor_tensor", out=ostg[pci], in0=po, in1=recb, op=ALU.mult), reads=[r_po, r_recb], writes=[r_ostg[pci]])
            self.store(self.OA[8 + h, :, T0 + jg * 512:T0 + (jg + 1) * 512], ostg[pci], r_ostg[pci], reads=[r_ostg[pci]], writes=[self.r_OA], is_output=bool(self.cfg.get("dump_oa")))
    k.barrier()
    ar.release(m0)


Builder.na_phase = na_phase


def outproj_phase(self, l, OA, r_OA, w_out, Xin, r_Xin, Xout, r_Xout):
    k, ar = self.k, self.ar
    self.phase()
    m0 = ar.mark()
    G, r_G = self.load_grep(l, 0)
    W = self.work_bufs()
    wo = ar.alloc([16, 2048], BF16); r_wo = self.R("wo", sw=True)
    for n in range(4):
        k.dma('pool', wo[:, :, n * 512:(n + 1) * 512], w_out[:, n * 512:(n + 1) * 512].rearrange("(k p) n -> p k n", p=128), writes=[r_wo])
    ob = [ar.alloc([16, 512], BF16) for _ in range(2)]; r_ob = [self.R(f"opo{i}", sem=True) for i in range(2)]
    xt = [ar.alloc([D_MODEL]) for _ in range(2)]; r_xt = [self.R(f"opx{i}", sem=True) for i in range(2)]
    xi = 0
    for blk in range(6):
        g = 0 if blk < 2 else 1
        bi = blk % 2
        k.dma('sp', ob[bi], OA[:, :, blk * 512:(blk + 1) * 512].rearrange("c p t -> p c t"), reads=[r_OA], writes=[r_ob[bi]])
        for sub in range(4):
            tile = blk * 4 + sub
            pb0 = (tile % 2) * 4
            for n in range(4):
                for kk in range(16):
                    k.op('pe', I("matmul", self.psb(pb0 + n), lhsT=ob[bi][:, kk, sub * 128:(sub + 1) * 128], rhs=wo[:, kk, n * 512:(n + 1) * 512], start=(kk == 0), stop=(kk == 15)),
                         reads=[r_ob[bi], r_wo], writes=[self.psr[pb0 + n]])
            b = xi % 2; xi += 1
            k.dma('sp', xt[b], Xin[tile * 128:(tile + 1) * 128, :], reads=[r_Xin[tile]], writes=[r_xt[b]])
            yp = [(self.psb(pb0 + n), self.psr[pb0 + n], 512) for n in range(4)]
            self.post_tile(xt[b], r_xt[b], yp, G[g], r_G[g], W)
            self.store(Xout[tile * 128:(tile + 1) * 128, :], xt[b], r_xt[b], reads=[r_xt[b]], writes=[r_Xout[tile]])
    k.barrier()
    ar.release(m0)


Builder.outproj_phase = outproj_phase


NTK = NT + 512
MLA_SCALE = 192.0 ** -0.5


def _mla_scratch(self):
    if hasattr(self, "CQT"):
        return
    s = self.scr
    self.CQT = s("CQT", [4, 128, NT], BF16)
    self.CKVT = s("CKVT", [4, 128, NTK], BF16)
    self.KPET = s("KPET", [64, NTK], BF16)
    self.KROT = s("KROT", [64, 2048], BF16)
    self.QN = s("QN", [16, 128, NT], BF16)
    self.QPE = s("QPE", [16, 64, NT], BF16)
    self.QROT = s("QROT", [16, 64, 2048], BF16)
    self.KN = s("KN", [16, 128, NTK], BF16)
    self.VM = s("VM", [NTK, 16, 128], BF16)
    self.OA2 = s("OA2", [16, 128, NT], BF16) if not self.cfg.get("dump_oa2") else self.out("OA2", [16, 128, NT], BF16)
    self.r_O1 = self.R("O1out"); self.r_O2 = self.R("O2out"); self.r_OA2 = self.R("OA2res")


def _rmsnorm_free(self, src_ps, r_src, n, gq, r_gq, out32, out16, r_out, W, st, r_st):
    k = self.k
    k.op('dve', I("memset", st[:, 0:1], 0.0), writes=[r_st])
    k.op('act', I("activation", out=W["junk"][:, 0:n], in_=src_ps, func=AF.Square, accum_out=st[:, 0:1]), reads=[r_src, r_st], writes=[W["r_junk"], r_st])
    k.op('act', I("activation", out=st[:, 1:2], in_=st[:, 0:1], func=AF.Ln, scale=1.0 / n, bias=EPS), reads=[r_st], writes=[r_st])
    k.op('act', I("activation", out=st[:, 2:3], in_=st[:, 1:2], func=AF.Exp, scale=-0.5), reads=[r_st], writes=[r_st])
    if out32 is not None:
        k.op('dve', I("scalar_tensor_tensor", out=out32, in0=src_ps, scalar=st[:, 2:3], in1=gq, op0=ALU.mult, op1=ALU.mult), reads=[r_src, r_st, r_gq], writes=[r_out])
        k.op('pool', I("tensor_copy", out=out16, in_=out32), reads=[r_out], writes=[r_out])
    else:
        k.op('dve', I("scalar_tensor_tensor", out=out16, in0=src_ps, scalar=st[:, 2:3], in1=gq, op0=ALU.mult, op1=ALU.mult), reads=[r_src, r_st, r_gq], writes=[r_out])


def mla_inproj(self, Xin, r_Xin):
    k, ar = self.k, self.ar
    self.phase()
    _mla_scratch(self)
    m0 = ar.mark()
    l = 1
    W = self.work_bufs()
    w_in = self.din["w_in_odd"]
    wb = ar.alloc([16, 1088], BF16); r_wb = self.R("o1w", sw=True)
    for (c0, c1) in ((0, 512), (512, 1024), (1024, 1088)):
        k.dma('pool', wb[:, :, c0:c1], w_in[:, c0:c1].rearrange("(k p) n -> p k n", p=128), writes=[r_wb])
    gq = ar.alloc([512]); r_gq = self.R("gq", sem=True)
    gkv = ar.alloc([512]); r_gkv = self.R("gkv", sem=True)
    k.dma('sp', gq, self.din["mla_qg"].to_broadcast([128, 512]), writes=[r_gq])
    k.dma('sp', gkv, self.din["mla_kvg"].to_broadcast([128, 512]), writes=[r_gkv])
    ropeP32 = ar.alloc([64], parts=64); r_ropeP32 = self.R("ropeP32", sem=True)
    ropeP = ar.alloc([64], BF16, parts=64); r_ropeP = self.R("ropeP")
    k.dma('sp', ropeP32, self.din["ropeP"], writes=[r_ropeP32])
    k.op('dve', I("tensor_copy", out=ropeP, in_=ropeP32), reads=[r_ropeP32], writes=[r_ropeP])
    cs = ar.alloc([2, 128], parts=64)
    r_cs = self.R("ropecs", sem=True)
    xt = [ar.alloc([D_MODEL]) for _ in range(2)]; r_xt = [self.R(f"o1x{i}", sem=True) for i in range(2)]
    hT = [ar.alloc([16, 128], BF16) for _ in range(2)]; r_hT = [self.R(f"o1h{i}") for i in range(2)]
    st = ar.alloc([8]); r_st = self.R("o1st")
    cq16 = [ar.alloc([512], BF16) for _ in range(2)]; r_cq16 = [self.R(f"cq16{i}") for i in range(2)]
    kv32 = [ar.alloc([512]) for _ in range(2)]; r_kv32 = [self.R(f"kv32{i}", sem=True) for i in range(2)]
    kv16 = [ar.alloc([512], BF16) for _ in range(2)]
    kp32 = [ar.alloc([64]) for _ in range(2)]; r_kp32 = [self.R(f"kp32{i}", sem=True) for i in range(2)]
    kp16 = [ar.alloc([64], BF16) for _ in range(2)]
    tq = [ar.alloc([4, 128], BF16) for _ in range(2)]; r_tq = [self.R(f"tq{i}", sem=True) for i in range(2)]
    tkv = [ar.alloc([4, 128], BF16) for _ in range(2)]; r_tkv = [self.R(f"tkv{i}", sem=True) for i in range(2)]
    tkp = [ar.alloc([128], BF16, parts=64) for _ in range(2)]; r_tkp = [self.R(f"tkp{i}", sem=True) for i in range(2)]
    trot = [ar.alloc([128], BF16, parts=64) for _ in range(2)]; r_trot = [self.R(f"trot{i}", sem=True) for i in range(2)]
    t1 = ar.alloc([128], parts=64); r_t1 = self.R("ropet1")
    t2 = ar.alloc([128], parts=64); r_t2 = self.R("ropet2")
    AB, r_AB = self.AB[l][0], self.r_AB[l][0]
    nckv, nkpe = self.dout["nckv"], self.dout["nkpe"]
    for tile in range(28):
        b = tile % 2
        own = tile < 24
        if own:
            g = 0 if tile < 8 else 1
            k.dma('sp', xt[b], Xin[tile * 128:(tile + 1) * 128, :], reads=[r_Xin[tile]], writes=[r_xt[b]])
            self.pre_tile(xt[b], r_xt[b], hT[b], r_hT[b], 0, AB, r_AB, g, W)
            for n, (c0, c1) in enumerate(((0, 512), (512, 1024), (1024, 1088))):
                for kk in range(16):
                    k.op('pe', I("matmul", self.ps[:, n, 0:c1 - c0], lhsT=hT[b][:, kk, :], rhs=wb[:, kk, c0:c1], start=(kk == 0), stop=(kk == 15)),
                         reads=[r_hT[b], r_wb], writes=[self.psr[n]])
            _rmsnorm_free(self, self.ps[:, 0, :], self.psr[0], 512, gq, r_gq, None, cq16[b], r_cq16[b], W, st, r_st)
            _rmsnorm_free(self, self.ps[:, 1, :], self.psr[1], 512, gkv, r_gkv, kv32[b], kv16[b], r_kv32[b], W, st, r_st)
            k.op('act', I("copy", out=kp32[b], in_=self.ps[:, 2, 0:64]), reads=[self.psr[2]], writes=[r_kp32[b]])
            k.op('pool', I("tensor_copy", out=kp16[b], in_=kp32[b]), reads=[r_kp32[b]], writes=[r_kp32[b]])
            if tile < 8:
                self.store(nckv[tile * 128:(tile + 1) * 128, :], kv32[b], r_kv32[b], reads=[r_kv32[b]], writes=[], is_output=True)
                self.store(nkpe[tile * 128:(tile + 1) * 128, :], kp32[b], r_kp32[b], reads=[r_kp32[b]], writes=[], is_output=True)
        else:
            ct = tile - 24
            k.dma('sp', kv32[b], self.din["cckv"][ct * 128:(ct + 1) * 128, :], writes=[r_kv32[b]])
            k.dma('sp', kp32[b], self.din["ckpe"][ct * 128:(ct + 1) * 128, :], writes=[r_kp32[b]])
            k.op('pool', I("tensor_copy", out=kv16[b], in_=kv32[b]), reads=[r_kv32[b]], writes=[r_kv32[b]])
            k.op('pool', I("tensor_copy", out=kp16[b], in_=kp32[b]), reads=[r_kp32[b]], writes=[r_kp32[b]])
        if own:
            p3 = self.psb16(3)
            for c in range(4):
                k.op('pe', I("transpose", out=p3[:, c * 128:(c + 1) * 128], in_=cq16[b][:, c * 128:(c + 1) * 128], identity=self.ident_b), reads=[r_cq16[b], self.r_ident_b], writes=[self.psr[3]])
            k.op('act', I("copy", out=tq[b].rearrange("p a b -> p (a b)"), in_=p3[:, 0:512]), reads=[self.psr[3]], writes=[r_tq[b]])
            self.store(self.CQT[:, :, tile * 128:(tile + 1) * 128].rearrange("c p t -> p c t"), tq[b], r_tq[b], reads=[r_tq[b]], writes=[self.r_O1])
        p4 = self.psb16(4)
        for c in range(4):
            k.op('pe', I("transpose", out=p4[:, c * 128:(c + 1) * 128], in_=kv16[b][:, c * 128:(c + 1) * 128], identity=self.ident_b), reads=[r_kv32[b], self.r_ident_b], writes=[self.psr[4]])
        k.op('act', I("copy", out=tkv[b].rearrange("p a b -> p (a b)"), in_=p4[:, 0:512]), reads=[self.psr[4]], writes=[r_tkv[b]])
        self.store(self.CKVT[:, :, tile * 128:(tile + 1) * 128].rearrange("c p t -> p c t"), tkv[b], r_tkv[b], reads=[r_tkv[b]], writes=[self.r_O1])
        p5 = self.psb16(5)
        k.op('pe', I("transpose", out=p5[0:64, 0:128], in_=kp16[b], identity=self.ident_b), reads=[r_kp32[b], self.r_ident_b], writes=[self.psr[5]])
        k.op('act', I("copy", out=tkp[b], in_=p5[0:64, 0:128]), reads=[self.psr[5]], writes=[r_tkp[b]])
        self.store(self.KPET[:, tile * 128:(tile + 1) * 128], tkp[b], r_tkp[b], reads=[r_tkp[b]], writes=[self.r_O1])
        if own and tile >= 8:
            lt = tile - 8
            k.dma('sp', cs, self.din["ropecs"][:, :, lt * 128:(lt + 1) * 128], writes=[r_cs])
            k.op('pe', I("matmul", self.ps[0:64, 6, 0:128], lhsT=ropeP, rhs=tkp[b], start=True, stop=True), reads=[r_ropeP, r_tkp[b]], writes=[self.psr[6]])
            k.op('dve', I("tensor_tensor", out=t1, in0=self.ps[0:64, 6, 0:128], in1=cs[:, 1, :], op=ALU.mult), reads=[self.psr[6], r_cs], writes=[r_t1])
            k.op('pool', I("tensor_tensor", out=t2, in0=tkp[b], in1=cs[:, 0, :], op=ALU.mult), reads=[r_tkp[b], r_cs], writes=[r_t2])
            k.op('pool', I("tensor_tensor", out=trot[b], in0=t1, in1=t2, op=ALU.add), reads=[r_t1, r_t2], writes=[r_trot[b]])
            self.store(self.KROT[:, lt * 128:(lt + 1) * 128], trot[b], r_trot[b], reads=[r_trot[b]], writes=[self.r_O1])
    k.barrier()
    ar.release(m0)


def mla_proj(self):
    k, ar = self.k, self.ar
    self.phase()
    _mla_scratch(self)
    m0 = ar.mark()
    wq = ar.alloc([4, 3072], BF16); r_wq = self.R("wq", sw=True)
    wkv = ar.alloc([4, 4096], BF16); r_wkv = self.R("wkv", sw=True)
    for n in range(6):
        k.dma('pool', wq[:, :, n * 512:(n + 1) * 512], self.din["w_uq"][:, n * 512:(n + 1) * 512].rearrange("(k p) n -> p k n", p=128), writes=[r_wq])
    for n in range(8):
        k.dma('pool', wkv[:, :, n * 512:(n + 1) * 512], self.din["w_ukv"][:, n * 512:(n + 1) * 512].rearrange("(k p) n -> p k n", p=128), writes=[r_wkv])
    cqT = ar.alloc([4, NT], BF16); r_cqT = self.R("cqT", sem=True)
    ckvT = ar.alloc([4, NTK], BF16); r_ckvT = self.R("ckvT", sem=True)
    k.dma('sp', cqT, self.CQT.rearrange("c p t -> p c t"), reads=[self.r_O1], writes=[r_cqT])
    k.dma('sp', ckvT, self.CKVT.rearrange("c p t -> p c t"), reads=[self.r_O1], writes=[r_ckvT])
    ropeP32 = ar.alloc([64], parts=64); r_ropeP32 = self.R("ropeP32b", sem=True)
    ropeP = ar.alloc([64], BF16, parts=64); r_ropeP = self.R("ropePb")
    k.dma('sp', ropeP32, self.din["ropeP"], writes=[r_ropeP32])
    k.op('dve', I("tensor_copy", out=ropeP, in_=ropeP32), reads=[r_ropeP32], writes=[r_ropeP])
    cs = ar.alloc([2, 2048], parts=64); r_cs = self.R("ropecs2", sem=True)
    k.dma('sp', cs, self.din["ropecs"], writes=[r_cs])
    NST = 4
    stg = [ar.alloc([512], BF16) for _ in range(NST)]; r_stg = [self.R(f"o2s{i}", sem=True) for i in range(NST)]
    t1 = ar.alloc([512], parts=64); r_t1 = self.R("o2t1")
    t2 = ar.alloc([512], parts=64); r_t2 = self.R("o2t2")
    si = 0; pbi = 0
    wqv = wq.rearrange("p k (h d) -> p k h d", h=16)
    wkvv = wkv.rearrange("p k (h d) -> p k h d", h=16)
    for h in range(16):
        for tb in range(6):
            ts = slice(tb * 512, (tb + 1) * 512)
            pb = pbi % 3; pbi += 1
            for kk in range(4):
                k.op('pe', I("matmul", self.psb(pb), lhsT=wqv[:, kk, h, 0:128], rhs=cqT[:, kk, ts], start=(kk == 0), stop=(kk == 3)), reads=[r_wq, r_cqT], writes=[self.psr[pb]])
            s_ = si % NST; si += 1
            k.op('act', I("activation", out=stg[s_], in_=self.psb(pb), func=AF.Copy, scale=MLA_SCALE), reads=[self.psr[pb]], writes=[r_stg[s_]])
            k.dma('sp', self.QN[h, :, ts], stg[s_], reads=[r_stg[s_]], writes=[self.r_O2])
            pb = pbi % 3; pbi += 1
            for kk in range(4):
                k.op('pe', I("matmul", self.ps[0:64, pb, :], lhsT=wqv[:, kk, h, 128:192], rhs=cqT[:, kk, ts], start=(kk == 0), stop=(kk == 3)), reads=[r_wq, r_cqT], writes=[self.psr[pb]])
            s_ = si % NST; si += 1
            qpe = stg[s_][0:64, :]
            k.op('act', I("activation", out=qpe, in_=self.ps[0:64, pb, :], func=AF.Copy, scale=MLA_SCALE), reads=[self.psr[pb]], writes=[r_stg[s_]])
            k.dma('sp', self.QPE[h, :, ts], qpe, reads=[r_stg[s_]], writes=[self.r_O2])
            if tb >= 2:
                lt = tb - 2
                ls = slice(lt * 512, (lt + 1) * 512)
                k.op('pe', I("matmul", self.ps[0:64, 6, :], lhsT=ropeP, rhs=qpe, start=True, stop=True), reads=[r_ropeP, r_stg[s_]], writes=[self.psr[6]])
                k.op('dve', I("tensor_tensor", out=t1, in0=self.ps[0:64, 6, :], in1=cs[:, 1, ls], op=ALU.mult), reads=[self.psr[6], r_cs], writes=[r_t1])
                k.op('pool', I("tensor_tensor", out=t2, in0=qpe, in1=cs[:, 0, ls], op=ALU.mult), reads=[r_stg[s_], r_cs], writes=[r_t2])
                s2 = si % NST; si += 1
                qrot = stg[s2][0:64, :]
                k.op('pool', I("tensor_tensor", out=qrot, in0=t1, in1=t2, op=ALU.add), reads=[r_t1, r_t2], writes=[r_stg[s2]])
                k.dma('sp', self.QROT[h, :, ls], qrot, reads=[r_stg[s2]], writes=[self.r_O2])
        for tb in range(7):
            ts = slice(tb * 512, (tb + 1) * 512)
            pb = pbi % 3; pbi += 1
            for kk in range(4):
                k.op('pe', I("matmul", self.psb(pb), lhsT=wkvv[:, kk, h, 0:128], rhs=ckvT[:, kk, ts], start=(kk == 0), stop=(kk == 3)), reads=[r_wkv, r_ckvT], writes=[self.psr[pb]])
            s_ = si % NST; si += 1
            k.op('act', I("copy", out=stg[s_], in_=self.psb(pb)), reads=[self.psr[pb]], writes=[r_stg[s_]])
            k.dma('sp', self.KN[h, :, ts], stg[s_], reads=[r_stg[s_]], writes=[self.r_O2])
    for t in range(28):
        for hg in range(4):
            pb = 3 + (pbi % 2); pbi += 1
            for kk in range(4):
                k.op('pe', I("matmul", self.psb(pb).rearrange("p (h d) -> p h d", h=4), lhsT=ckvT[:, kk, t * 128:(t + 1) * 128], rhs=wkvv[:, kk, hg * 4:(hg + 1) * 4, 128:256], start=(kk == 0), stop=(kk == 3)),
                     reads=[r_wkv, r_ckvT], writes=[self.psr[pb]])
            s_ = si % NST; si += 1
            k.op('act', I("copy", out=stg[s_], in_=self.psb(pb)), reads=[self.psr[pb]], writes=[r_stg[s_]])
            k.dma('sp', self.VM[t * 128:(t + 1) * 128, hg * 4:(hg + 1) * 4, :], stg[s_].rearrange("p (h d) -> p h d", h=4), reads=[r_stg[s_]], writes=[self.r_O2])
    k.barrier()
    ar.release(m0)


def mla_attn(self):
    k, ar = self.k, self.ar
    self.phase()
    _mla_scratch(self)
    m0 = ar.mark()
    kpeT_f = ar.alloc([NTK], BF16); r_kpeT = self.R("kpeTall", sem=True)
    krot_f = ar.alloc([2048], BF16); r_krot = self.R("krotall", sem=True)
    k.op('dve', I("memset", kpeT_f[64:128], 0.0), writes=[r_kpeT])
    k.op('dve', I("memset", krot_f[64:128], 0.0), writes=[r_krot])
    k.dma('sp', kpeT_f[0:64], self.KPET, reads=[self.r_O1], writes=[r_kpeT])
    k.dma('sp', krot_f[0:64], self.KROT, reads=[self.r_O1], writes=[r_krot])
    kpeT, krot = kpeT_f, krot_f
    QN = [ar.alloc([NT], BF16) for _ in range(2)]; r_QN = [self.R(f"aQN{i}", sem=True) for i in range(2)]
    QP = [ar.alloc([NT], BF16) for _ in range(2)]; r_QP = [self.R(f"aQP{i}", sem=True) for i in range(2)]
    QR = [ar.alloc([2048], BF16) for _ in range(2)]; r_QR = [self.R(f"aQR{i}", sem=True) for i in range(2)]
    for i in range(2):
        k.op('dve', I("memset", QP[i][64:128], 0.0), writes=[r_QP[i]])
        k.op('dve', I("memset", QR[i][64:128], 0.0), writes=[r_QR[i]])
    KN = [ar.alloc([NTK], BF16) for _ in range(2)]; r_KN = [self.R(f"aKN{i}", sem=True) for i in range(2)]
    va = [ar.alloc([28, 129], BF16) for _ in range(2)]; r_va = [self.R(f"aV{i}", sem=True) for i in range(2)]
    for i in range(2):
        k.op('pool', I("memset", va[i][:, :, 128:129], 1.0), writes=[r_va[i]])
    PT = [ar.alloc([512], BF16) for _ in range(5)]; r_PT = [self.R(f"aPT{i}") for i in range(5)]
    rec = [ar.alloc([1]) for _ in range(2)]; r_rec = [self.R(f"arec{i}") for i in range(2)]
    ob16 = [ar.alloc([128], BF16) for _ in range(2)]; r_ob16 = [self.R(f"aob{i}") for i in range(2)]
    ostg = [ar.alloc([512], BF16) for _ in range(2)]; r_ostg = [self.R(f"aost{i}", sem=True) for i in range(2)]
    Pacc = [ar.alloc([512]) for _ in range(2)]; r_Pacc = [self.R(f"aPacc{i}") for i in range(2)]
    recb = [ar.alloc([512]) for _ in range(2)]; r_recb = [self.R(f"arecb{i}") for i in range(2)]
    cnt = {"p": 0, "s": 0, "o": 0, "f": 0}
    def load_head(h):
        hi = h % 2
        k.dma('sp', QN[hi], self.QN[h], reads=[self.r_O2], writes=[r_QN[hi]])
        k.dma('sp', QP[hi][0:64], self.QPE[h], reads=[self.r_O2], writes=[r_QP[hi]])
        k.dma('sp', QR[hi][0:64], self.QROT[h], reads=[self.r_O2], writes=[r_QR[hi]])
        k.dma('sp', KN[hi], self.KN[h], reads=[self.r_O2], writes=[r_KN[hi]])
        k.dma('sp', va[hi][:, :, 0:128], self.VM[:, h, :].rearrange("(c p) d -> p c d", p=128), reads=[self.r_O2], writes=[r_va[hi]])

    flat = []
    jobinfo = {}
    jid = 0
    for h in range(16):
        hi = h % 2
        jobs = []
        for sq_ in range(4):
            t0 = sq_ * 256
            keys = [(t0 // 128 + c, kpeT[:, t0 + c * 128:t0 + (c + 1) * 128], QP[hi][:, t0:t0 + 256]) for c in range(2)]
            jobs.append((t0, 256, keys))
        for qb in range(4):
            q0 = 1024 + qb * 512
            lq = slice(qb * 512, (qb + 1) * 512)
            keys = [(8 + c, krot[:, c * 128:(c + 1) * 128], QR[hi][:, lq]) for c in range(16)]
            keys += [(24 + c, kpeT[:, NT + c * 128:NT + (c + 1) * 128], QP[hi][:, q0:q0 + 512]) for c in range(4)]
            jobs.append((q0, 512, keys))
        for (q0, nq, keys) in jobs:
            for n_, (kt, kr, qr) in enumerate(keys):
                flat.append((h, jid, q0, nq, n_, len(keys), kt, kr, qr))
            jid += 1
    NF = len(flat)
    sbs = [(0, 1, 7)[i % 3] for i in range(NF)]
    pis = [i % 5 for i in range(NF)]
    loaded = set()

    def S_ops(i):
        (h, jid_, q0, nq, n_, NK, kt, kr, qr) = flat[i]
        hi = h % 2
        if h not in loaded:
            load_head(h); loaded.add(h)
        sb_ = sbs[i]
        k.op('pe', I("matmul", self.ps[:, sb_, 0:nq], lhsT=KN[hi][:, kt * 128:(kt + 1) * 128], rhs=QN[hi][:, q0:q0 + nq], start=True, stop=False),
             reads=[r_KN[hi], r_QN[hi]], writes=[self.psr[sb_]])
        k.op('pe', I("matmul", self.ps[:, sb_, 0:nq], lhsT=kr, rhs=qr, start=False, stop=True),
             reads=[r_kpeT, r_krot, r_QP[hi], r_QR[hi]], writes=[self.psr[sb_]])

    load_head(0); loaded.add(0)
    S_ops(0); S_ops(1)
    for i in range(NF):
        (h, jid_, q0, nq, n_, NK, kt, kr, qr) = flat[i]
        hi = h % 2
        ji = jid_ % 2
        po = self.ps[:, 2 + ji, 0:nq]
        r_po = self.psr[2 + ji]
        if n_ == 0 and h + 1 < 16 and (h + 1) not in loaded and q0 == 256:
            load_head(h + 1); loaded.add(h + 1)
        if i + 2 < NF:
            S_ops(i + 2)
        sb_, pi = sbs[i], pis[i]
        k.op('act', I("activation", out=PT[pi][:, 0:nq], in_=self.ps[:, sb_, 0:nq], func=AF.Exp), reads=[self.psr[sb_]], writes=[r_PT[pi]])
        k.op('pe', I("matmul", po, lhsT=va[hi][:, kt, 0:128], rhs=PT[pi][:, 0:nq], start=(n_ == 0), stop=(n_ == NK - 1)),
             reads=[r_PT[pi], r_va[hi]], writes=[r_po])
        if n_ == 0:
            k.op('dve', I("tensor_copy", out=Pacc[ji][:, 0:nq], in_=PT[pi][:, 0:nq]), reads=[r_PT[pi]], writes=[r_Pacc[ji]])
        else:
            k.op('dve', I("tensor_tensor", out=Pacc[ji][:, 0:nq], in0=Pacc[ji][:, 0:nq], in1=PT[pi][:, 0:nq], op=ALU.add), reads=[r_Pacc[ji], r_PT[pi]], writes=[r_Pacc[ji]])
        if n_ == NK - 1:
            pden = self.ps[:, 4 + ji, 0:nq]
            k.op('pe', I("matmul", pden, lhsT=self.ones_f, rhs=Pacc[ji][:, 0:nq], start=True, stop=True), reads=[self.r_ones_f, r_Pacc[ji]], writes=[self.psr[4 + ji]])
            k.op('act', I("activation", out=recb[ji][:, 0:nq], in_=pden, func=AF.Ln), reads=[self.psr[4 + ji]], writes=[r_recb[ji]])
            k.op('act', I("activation", out=recb[ji][:, 0:nq], in_=recb[ji][:, 0:nq], func=AF.Exp, scale=-1.0), reads=[r_recb[ji]], writes=[r_recb[ji]])
            k.op('dve', I("tensor_tensor", out=ostg[ji][:, 0:nq], in0=po, in1=recb[ji][:, 0:nq], op=ALU.mult), reads=[r_po, r_recb[ji]], writes=[r_ostg[ji]])
            self.store(self.OA2[h, :, q0:q0 + nq], ostg[ji][:, 0:nq], r_ostg[ji], reads=[r_ostg[ji]], writes=[self.r_OA2], is_output=bool(self.cfg.get("dump_oa2")))
    k.barrier()
    ar.release(m0)


Builder.mla_inproj = mla_inproj
Builder.mla_proj = mla_proj
Builder.mla_attn = mla_attn


def _host_consts():
    cm = np.ones((128, 1024), np.float32)
    t = np.arange(512)
    cm[:, 0:512][:, t % 64 == 0] = 0
    cm[:, 512:][:, t % 64 == 63] = 0
    s = np.arange(64)[:, None]
    tt = np.arange(64)[None, :]
    tri = np.concatenate([(s <= tt), (s >= tt)], 1).astype(np.float32)
    col = np.arange(64)
    cs_ = np.clip(col - 8, 0, 48)
    ok = (col[None, :] >= cs_[:, None]) & (col[None, :] < cs_[:, None] + 16)
    m = ok.T.astype(np.float32)
    colmask = np.concatenate([m, m], 0)
    P = np.zeros((64, 64), np.float32)
    for base in (0, 32):
        for i in range(16):
            P[base + i, base + i + 16] = -1.0
            P[base + i + 16, base + i] = 1.0
    ropeP = np.ascontiguousarray(P.T)
    tq = np.arange(2048)
    inv = np.power(np.float32(10000.0), -np.arange(0, 32, 2, dtype=np.float32) / np.float32(32)).astype(np.float32)
    ang_r = (tq // 64).astype(np.float32)[:, None] * inv
    ang_c = (tq % 64).astype(np.float32)[:, None] * inv
    ang = np.concatenate([ang_r, ang_r, ang_c, ang_c], 1).T
    ropecs = np.ascontiguousarray(np.stack([np.cos(ang), np.sin(ang)], 1).astype(np.float32))
    return {"ident": np.eye(128, dtype=np.float32), "cmask": cm, "trimask": tri, "colmask": colmask, "ropeP": ropeP, "ropecs": ropecs}


def build_program(cfg=None):
    B = Builder(cfg or {})
    B.inp("cvec", [2, 2048]); B.inp("ada_w", [2, 2048, 12288]); B.inp("ada_b", [2, 12288]); B.inp("norm_g", [2, 4, 2048])
    B.inp("w_in_even", [2048, 8192]); B.inp("cmask", [128, 1024]); B.inp("trimask", [64, 128]); B.inp("hgn", [1, 128]); B.inp("lbrows", [2, 3, 1024])
    B.inp("st_f", [8, 128, 128]); B.inp("st_b", [8, 128, 128]); B.inp("rpbr", [8, 15, 8192]); B.inp("colmask", [128, 64])
    B.inp("cnak", [512, 8, 128]); B.inp("cnav", [512, 8, 128]); B.inp("w_out_even", [2048, 2048])
    B.inp("mlp_w1", [2, 2048, 8192]); B.inp("mlp_w2", [2, 8192, 2048])
    B.inp("w_in_odd", [2048, 1088]); B.inp("mla_qg", [1, 512]); B.inp("mla_kvg", [1, 512]); B.inp("w_uq", [512, 3072]); B.inp("w_ukv", [512, 4096]); B.inp("w_out_odd", [2048, 2048])
    B.inp("cckv", [512, 512]); B.inp("ckpe", [512, 64]); B.inp("ropeP", [64, 64]); B.inp("ropecs", [64, 2, 2048])
    X0 = B.inp("xin", [NT, 2048])
    X4 = B.out("yout", [NT, 2048])
    B.out("nsf", [4, 8, 128, 128]); B.out("nsb", [4, 8, 128, 128]); B.out("nak", [1024, 1024]); B.out("nav", [1024, 1024])
    B.out("nckv", [1024, 512]); B.out("nkpe", [1024, 64])
    X1 = B.scr("X1", [NT, 2048]); X2 = B.scr("X2", [NT, 2048]); X3 = B.scr("X3", [NT, 2048])
    rX = [[B.R(f"X{i}_{t}") for t in range(24)] for i in range(5)]
    B.consts()
    B.modulation(0)
    B.modulation(1)
    B.even_inproj(X0, rX[0])
    B.hgrn_phase()
    B.na_phase()
    B.outproj_phase(0, B.OA, B.r_OA, B.din["w_out_even"], X0, rX[0], X1, rX[1])
    B.mlp_phase(0, X1, rX[1], X2, rX[2])
    B.mla_inproj(X2, rX[2])
    B.mla_proj()
    B.mla_attn()
    B.outproj_phase(1, B.OA2, B.r_OA2, B.din["w_out_odd"], X2, rX[2], X3, rX[3])
    B.mlp_phase(1, X3, rX[3], X4, rX[4], out_is_output=True)
    B.k.emit()
    return B


def kernel(x_prompt, x_sample, state_hgrn_fwd, state_hgrn_bwd, cache_na_k, cache_na_v, cache_mla_ckv, cache_mla_kpe,
           c, c_ctx, ada_w, ada_b, norm_g, hgrn_lb_fwd, hgrn_lb_bwd, w_in_even, hgrn_norm_g, na_rpb, w_out_even, w_in_odd,
           mla_q_norm_g, w_uq, mla_kv_norm_g, w_ukv, w_out_odd, mlp_w1, mlp_w2):
    f32 = lambda a: np.ascontiguousarray(np.asarray(a, dtype=np.float32))
    NCORE = 8
    B = build_program()
    hc = _host_consts()
    rpb = f32(na_rpb)[0]
    rp = np.zeros((8, 15, 128), np.float32)
    rp[:, :, 48:79] = rpb[:, :, ::-1]
    rpbr = np.ascontiguousarray(np.broadcast_to(rp[:, :, None, :], (8, 15, 64, 128))).reshape(8, 15, 8192)
    shared = {
        "ada_w": f32(ada_w), "ada_b": f32(ada_b), "norm_g": f32(norm_g), "w_in_even": f32(w_in_even)[0],
        "hgn": f32(hgrn_norm_g)[0:1], "lbrows": np.ascontiguousarray(np.stack([f32(hgrn_lb_fwd), f32(hgrn_lb_bwd)], 0)),
        "rpbr": rpbr, "w_out_even": f32(w_out_even)[0], "mlp_w1": f32(mlp_w1), "mlp_w2": f32(mlp_w2),
        "w_in_odd": f32(w_in_odd)[0], "mla_qg": f32(mla_q_norm_g)[0:1], "mla_kvg": f32(mla_kv_norm_g)[0:1],
        "w_uq": f32(w_uq)[0], "w_ukv": f32(w_ukv)[0], "w_out_odd": f32(w_out_odd)[0],
    }
    shared.update(hc)
    xp, xs = f32(x_prompt), f32(x_sample)
    sf, sb_ = f32(state_hgrn_fwd), f32(state_hgrn_bwd)
    cnk, cnv = f32(cache_na_k), f32(cache_na_v)
    cck, ckp = f32(cache_mla_ckv), f32(cache_mla_kpe)
    cc, cctx = f32(c), f32(c_ctx)
    in_maps = []
    for i in range(NCORE):
        m = dict(shared)
        m["xin"] = np.ascontiguousarray(np.concatenate([xp[4 * i:4 * i + 4].reshape(1024, 2048), xs[i]], 0))
        m["cvec"] = np.ascontiguousarray(np.stack([cctx, cc[i]], 0))
        m["st_f"] = np.ascontiguousarray(sf[i, 0]); m["st_b"] = np.ascontiguousarray(sb_[i, 0])
        m["cnak"] = np.ascontiguousarray(cnk[i, 0]); m["cnav"] = np.ascontiguousarray(cnv[i, 0])
        m["cckv"] = np.ascontiguousarray(cck[i, 0]); m["ckpe"] = np.ascontiguousarray(ckp[i, 0])
        in_maps.append(m)
    res = run_bass_kernel_spmd(B.nc, in_maps, core_ids=list(range(NCORE)))
    R_ = res.results
    y_prompt = np.concatenate([np.asarray(r["yout"])[:1024].reshape(4, 256, 2048) for r in R_], 0).astype(np.float32)
    y_sample = np.stack([np.asarray(r["yout"])[1024:] for r in R_], 0).astype(np.float32)
    nsf = np.concatenate([np.asarray(r["nsf"]).reshape(4, 1, 8, 128, 128) for r in R_], 0).astype(np.float32)
    nsb = np.concatenate([np.asarray(r["nsb"]).reshape(4, 1, 8, 128, 128) for r in R_], 0).astype(np.float32)
    nak = np.concatenate([np.asarray(r["nak"]).reshape(4, 1, 256, 8, 128) for r in R_], 0).astype(np.float32)
    nav = np.concatenate([np.asarray(r["nav"]).reshape(4, 1, 256, 8, 128) for r in R_], 0).astype(np.float32)
    nckv = np.concatenate([np.asarray(r["nckv"]).reshape(4, 1, 256, 512) for r in R_], 0).astype(np.float32)
    nkpe = np.concatenate([np.asarray(r["nkpe"]).reshape(4, 1, 256, 64) for r in R_], 0).astype(np.float32)
    return (y_prompt, y_sample, nsf, nsb, nak, nav, nckv, nkpe)
```

```python
import numpy as np
import concourse.bass as bass
import concourse.mybir as mybir
from concourse.bass_utils import run_bass_kernel_spmd

F32 = mybir.dt.float32
BF16 = mybir.dt.bfloat16
I32 = mybir.dt.int32
AF = mybir.ActivationFunctionType
ALU = mybir.AluOpType
AX = mybir.AxisListType

COMPUTE = ('pe', 'act', 'dve', 'pool')
ALLENG = ('pe', 'act', 'dve', 'pool', 'sp')


class DmaSem:
    __slots__ = ('name', 'count', 'handle')

    def __init__(self, name):
        self.name = name
        self.count = 0
        self.handle = None


class Res:
    __slots__ = ('name', 'w_ops', 'w_dma', 'r_ops', 'r_dma', 'had_read', 'sem', '_stsem', '_stphase')

    def __init__(self, name, sem=None):
        self.name = name
        self.w_ops = {}
        self.w_dma = {}
        self.r_ops = {}
        self.r_dma = {}
        self.had_read = False
        self.sem = sem


class Op:
    __slots__ = ('eng', 'fn', 'dep_ops', 'dep_dma', 'idx', 'signal', 'dma_sem')

    def __init__(self, eng, fn):
        self.eng = eng
        self.fn = fn
        self.dep_ops = {}
        self.dep_dma = {}
        self.idx = -1
        self.signal = False
        self.dma_sem = None


class K:
    def __init__(self, nc):
        self.nc = nc
        self.ops = {e: [] for e in ALLENG}
        self.known_ops = {e: {f: -1 for f in ALLENG} for e in ALLENG}
        self.known_dma = {e: {} for e in ALLENG}
        self.dma_sems = []
        self.out_sems = set()
        self.nres = 0

    def res(self, name=None):
        self.nres += 1
        return Res(name or f"r{self.nres}")

    def dsem(self, name):
        s = DmaSem(name)
        self.dma_sems.append(s)
        return s

    def _collect(self, eng, reads, writes):
        raw_ops, raw_dma, war_ops, war_dma = {}, {}, {}, {}
        for r in reads:
            for e, i in r.w_ops.items():
                if raw_ops.get(e, -1) < i:
                    raw_ops[e] = i
            for s, v in r.w_dma.items():
                if raw_dma.get(s, 0) < v:
                    raw_dma[s] = v
        for w in writes:
            for e, i in w.r_ops.items():
                if war_ops.get(e, -1) < i:
                    war_ops[e] = i
            for s, v in w.r_dma.items():
                if war_dma.get(s, 0) < v:
                    war_dma[s] = v
        dep_ops = {}
        for e, i in raw_ops.items():
            if e == eng and eng in ('pe', 'sp'):
                continue
            dep_ops[e] = i
        for e, i in war_ops.items():
            if e == eng:
                continue
            if dep_ops.get(e, -1) < i:
                dep_ops[e] = i
        dep_dma = dict(raw_dma)
        for s, v in war_dma.items():
            if dep_dma.get(s, 0) < v:
                dep_dma[s] = v
        ko = self.known_ops[eng]
        kd = self.known_dma[eng]
        dep_ops = {e: i for e, i in dep_ops.items() if ko[e] < i}
        dep_dma = {s: v for s, v in dep_dma.items() if kd.get(s, 0) < v}
        for e, i in dep_ops.items():
            ko[e] = i
        for s, v in dep_dma.items():
            kd[s] = v
        return dep_ops, dep_dma

    def _register(self, op, reads, writes, dma_evt=None):
        eng, idx = op.eng, op.idx
        for r in reads:
            r.had_read = True
            if dma_evt is None:
                r.r_ops[eng] = idx
            else:
                r.r_dma[dma_evt[0]] = dma_evt[1]
        for w in writes:
            if w.had_read:
                w.w_ops = {}
                w.w_dma = {}
                w.r_ops = {}
                w.r_dma = {}
                w.had_read = False
            if dma_evt is None:
                w.w_ops[eng] = idx
            else:
                w.w_dma[dma_evt[0]] = dma_evt[1]

    def op(self, eng, fn, reads=(), writes=()):
        o = Op(eng, fn)
        o.dep_ops, o.dep_dma = self._collect(eng, reads, writes)
        o.idx = len(self.ops[eng])
        self.ops[eng].append(o)
        self._register(o, reads, writes)
        return o

    def dma(self, q, out, in_, reads=(), writes=(), sem=None, is_output=False, **kw):
        if sem is None:
            for r in list(writes) + list(reads):
                if r.sem is not None:
                    sem = r.sem
                    break
        assert sem is not None, "dma needs a semaphore-bearing resource"
        o = Op(q, lambda e, out=out, in_=in_, kw=kw: e.dma_start(out=out, in_=in_, **kw))
        o.dep_ops, o.dep_dma = self._collect(q, reads, writes)
        o.idx = len(self.ops[q])
        self.ops[q].append(o)
        sem.count += 16
        o.dma_sem = sem
        self._register(o, reads, writes, dma_evt=(sem, sem.count))
        if is_output:
            self.out_sems.add(sem)
        return o

    def barrier(self):
        o = Op('sp', lambda e: e.nop())
        for e in COMPUTE:
            n = len(self.ops[e])
            if n > 0 and self.known_ops['sp'][e] < n - 1:
                last = n - 1
                while last >= 0 and (self.ops[e][last].fn is None or self.ops[e][last].dma_sem is not None):
                    last -= 1
                if last >= 0 and self.known_ops['sp'][e] < last:
                    o.dep_ops[e] = last
        for s in self.dma_sems:
            if self.known_dma['sp'].get(s, 0) < s.count:
                o.dep_dma[s] = s.count
        o.idx = len(self.ops['sp'])
        self.ops['sp'].append(o)
        for e in COMPUTE:
            w = Op(e, None)
            w.dep_ops = {'sp': o.idx}
            w.idx = len(self.ops[e])
            self.ops[e].append(w)
        for e in ALLENG:
            for f in ALLENG:
                self.known_ops[e][f] = len(self.ops[f]) - 1
            for s in self.dma_sems:
                self.known_dma[e][s] = s.count
            self.known_ops[e]['sp'] = o.idx

    def emit(self):
        nc = self.nc
        self.barrier()
        sig = {e: set() for e in ALLENG}
        for e in ALLENG:
            for o in self.ops[e]:
                for f, i in o.dep_ops.items():
                    sig[f].add(i)
        val = {}
        for e in ALLENG:
            val[e] = {i: k + 1 for k, i in enumerate(sorted(sig[e]))}
        self._cm = nc.cleanup_on_exit()
        self._cm.__enter__()
        esem = {e: nc.alloc_semaphore(f"eng_{e}") for e in ALLENG}
        for s in self.dma_sems:
            if s.count > 0:
                s.handle = nc.alloc_semaphore(f"d_{s.name}")
        engobj = {'pe': 'tensor', 'act': 'scalar', 'dve': 'vector', 'pool': 'gpsimd', 'sp': 'sync'}
        stats = {}

        def run(e):
            def body(eng):
                nw = 0
                for o in self.ops[e]:
                    for f, i in o.dep_ops.items():
                        eng.wait_ge(esem[f], val[f][i])
                        nw += 1
                    for s, v in o.dep_dma.items():
                        eng.wait_ge(s.handle, v)
                        nw += 1
                    if o.fn is None:
                        continue
                    ins = o.fn(eng)
                    if o.dma_sem is not None:
                        ins.then_inc(o.dma_sem.handle, 16)
                    elif o.idx in val[e]:
                        ins.then_inc(esem[e], 1)
                stats[e] = (len(self.ops[e]), nw)
            return body

        with nc.Block() as block:
            for e in ALLENG:
                getattr(block, engobj[e])(run(e))
        nc.all_engine_barrier()
        self._cm.__exit__(None, None, None)
        self.stats = stats
        return stats


import math

D_MODEL = 2048
NT = 3072
NCH = 16
D_FF = 8192
EPS = 1e-6


def I(method, *a, **kw):
    return lambda e: getattr(e, method)(*a, **kw)


class Arena:
    def __init__(self, nc, nbytes):
        self.t = nc.alloc_sbuf_tensor("arena", [128, nbytes // 4], F32)
        self.n = nbytes
        self.top = 0
        self.peak = 0

    def alloc(self, shape, dt=F32, parts=128):
        esz = 4 if dt == F32 else 2
        n = 1
        for s in shape:
            n *= s
        nb = (n * esz + 63) // 64 * 64
        assert self.top + nb <= self.n, f"arena overflow {self.top}+{nb}>{self.n}"
        o4 = self.top // 4
        a = self.t[0:parts, o4:o4 + nb // 4]
        self.top += nb
        self.peak = max(self.peak, self.top)
        if dt != F32:
            a = a.bitcast(dt)
        a = a[:, 0:n]
        if len(shape) == 2:
            a = a.rearrange("p (a b) -> p a b", a=shape[0])
        elif len(shape) == 3:
            a = a.rearrange("p (a b c) -> p a b c", a=shape[0], b=shape[1])
        return a

    def mark(self):
        return self.top

    def release(self, m):
        self.top = m


class Builder:
    def __init__(self, cfg):
        self.cfg = cfg
        nc = self.nc = bass.Bass("TRN2", target_bir_lowering=False)
        self.k = K(nc)
        self.din = {}
        self.dout = {}
        self.ar = Arena(nc, 204800)
        self.ps = nc.alloc_psum_tensor("ps", [128, 8, 512], F32)
        self.psr = [self.k.res(f"psum{b}") for b in range(8)]
        self.rr = {}

    def inp(self, name, shape, dt=F32):
        self.din[name] = self.nc.dram_tensor(name, list(shape), dt, kind="ExternalInput").ap()
        return self.din[name]

    def out(self, name, shape, dt=F32):
        self.dout[name] = self.nc.dram_tensor(name, list(shape), dt, kind="ExternalOutput").ap()
        return self.dout[name]

    def scr(self, name, shape, dt=F32):
        return self.nc.dram_tensor(name, list(shape), dt, kind="Internal").ap()

    def R(self, name, sem=False, sw=False):
        r = self.k.res(name)
        if not hasattr(self, "sem_pool"):
            self.sem_pool = []
            self.sem_i = 0
            self.sw_pool = []
            self.sw_i = 0
        if sw:
            if self.sw_i >= len(self.sw_pool):
                self.sw_pool.append(self.k.dsem(f"w{len(self.sw_pool)}"))
            r.sem = self.sw_pool[self.sw_i]
            self.sw_i += 1
        elif sem:
            if self.sem_i >= len(self.sem_pool):
                self.sem_pool.append(self.k.dsem(f"p{len(self.sem_pool)}"))
            r.sem = self.sem_pool[self.sem_i]
            self.sem_i += 1
        return r

    def phase(self):
        self.sem_i = self.sem_keep
        self.sw_i = 0
        self.phase_id = getattr(self, "phase_id", 0) + 1

    def keep_sems(self):
        self.sem_keep = getattr(self, "sem_i", 0)

    def store(self, out, in_, src_res, reads, writes, is_output=False):
        if not hasattr(src_res, "_stsem") or src_res._stphase != self.phase_id:
            src_res._stsem = self.R("stsem", sw=True).sem
            src_res._stphase = self.phase_id
        return self.k.dma('pool', out, in_, reads=reads, writes=writes, sem=src_res._stsem, is_output=is_output)

    def psb(self, b):
        return self.ps[:, b, :]

    def psb16(self, b):
        return self.ps[:, b, :].bitcast(BF16)

    def consts(self):
        k, ar = self.k, self.ar
        idf = self.inp("ident", [128, 128])
        self.ident_f = ar.alloc([128])
        self.ident_b = ar.alloc([128], BF16)
        self.ones_b = ar.alloc([128], BF16)
        self.r_ident_f = self.R("ident_f", sem=True)
        self.r_ident_b = self.R("ident_b")
        self.r_ones = self.R("ones_b")
        k.dma('sp', self.ident_f, idf, writes=[self.r_ident_f])
        k.op('dve', lambda e: e.tensor_copy(out=self.ident_b, in_=self.ident_f), reads=[self.r_ident_f], writes=[self.r_ident_b])
        k.op('dve', lambda e: e.memset(self.ones_b, 1.0), writes=[self.r_ones])
        self.ones_f = ar.alloc([128]); self.r_ones_f = self.R("ones_f")
        k.op('dve', lambda e: e.memset(self.ones_f, 1.0), writes=[self.r_ones_f])
        self.AB = [[ar.alloc([16, 4]) for s in range(2)] for l in range(2)]
        self.r_AB = [[self.R(f"AB{l}{s}") for s in range(2)] for l in range(2)]
        self.GROW = [[self.scr(f"grow{l}{s}", [2, D_MODEL]) for s in range(2)] for l in range(2)]
        self.r_GROW = [[self.R(f"grow{l}{s}") for s in range(2)] for l in range(2)]
        self.keep_sems()

    def modulation(self, l):
        k, ar, nc = self.k, self.ar, self.nc
        self.phase()
        m0 = ar.mark()
        cvec = self.din["cvec"]
        ada_w = self.din["ada_w"]
        ada_b = self.din["ada_b"]
        norm_g = self.din["norm_g"]
        crow = ar.alloc([D_MODEL], F32, parts=2)
        tmp = ar.alloc([D_MODEL], F32, parts=2)
        scb = ar.alloc([D_MODEL], BF16, parts=2)
        scT = ar.alloc([16, 2], BF16)
        mod = ar.alloc([6, D_MODEL], F32, parts=2)
        gn = ar.alloc([4, D_MODEL], F32, parts=2)
        wb = [ar.alloc([16, 512], BF16) for _ in range(3)]
        r_crow = self.R("crow", sem=True); r_tmp = self.R("tmp"); r_scb = self.R("scb"); r_scT = self.R("scT")
        r_mod = self.R("modrow", sem=True); r_gn = self.R("gn", sem=True)
        r_wb = [self.R(f"modw{i}", sw=True) for i in range(3)]
        k.dma('sp', crow, cvec, writes=[r_crow])
        k.dma('sp', mod, ada_b[l].rearrange("(o s d) -> o s d", o=1, s=6).to_broadcast([2, 6, D_MODEL]), writes=[r_mod])
        k.dma('sp', gn, norm_g[l].rearrange("(o s) d -> o s d", o=1).to_broadcast([2, 4, D_MODEL]), writes=[r_gn])
        if self.cfg.get("mod_stop") == 1:
            k.dma('sp', self.dout["dbg_mod0"], mod, reads=[r_mod], is_output=True); k.barrier(); ar.release(m0); return
        k.op('act', lambda e: e.activation(out=tmp, in_=crow, func=AF.Exp, scale=-1.0), reads=[r_crow], writes=[r_tmp])
        k.op('dve', lambda e: e.tensor_scalar_add(out=tmp, in0=tmp, scalar1=1.0), reads=[r_tmp], writes=[r_tmp])
        k.op('dve', lambda e: e.reciprocal(out=tmp, in_=tmp), reads=[r_tmp], writes=[r_tmp])
        k.op('dve', lambda e: e.tensor_tensor(out=scb, in0=tmp, in1=crow, op=ALU.mult), reads=[r_tmp, r_crow], writes=[r_scb])
        if self.cfg.get("mod_stop") == 2:
            k.dma('sp', self.dout["dbg_mod0"], mod, reads=[r_mod], is_output=True); k.barrier(); ar.release(m0); return
        pst = self.psb16(7)
        for c in range(16):
            k.op('pe', lambda e, c=c: e.transpose(out=pst[:, c * 2:c * 2 + 2], in_=scb[0:2, c * 128:(c + 1) * 128], identity=self.ident_b[0:2, 0:2]),
                 reads=[r_scb, self.r_ident_b], writes=[self.psr[7]])
        k.op('dve', lambda e: e.tensor_copy(out=scT.rearrange("p a b -> p (a b)"), in_=pst[:, 0:32]), reads=[self.psr[7]], writes=[r_scT])
        if self.cfg.get("mod_stop") == 3:
            k.dma('sp', self.dout["dbg_mod0"], mod, reads=[r_mod], is_output=True); k.barrier(); ar.release(m0); return
        for j in range(24):
            b = j % 3
            k.dma('pool', wb[b], ada_w[l, :, j * 512:(j + 1) * 512].rearrange("(k p) n -> p k n", p=128), writes=[r_wb[b]])
            pb = j % 2
            for kk in range(16):
                k.op('pe', lambda e, kk=kk, b=b, pb=pb: e.matmul(self.ps[0:2, pb, :], lhsT=scT[:, kk, :], rhs=wb[b][:, kk, :], start=(kk == 0), stop=(kk == 15)),
                     reads=[r_scT, r_wb[b]], writes=[self.psr[pb]])
            s_, off = divmod(j * 512, D_MODEL)
            k.op('dve', lambda e, pb=pb, s_=s_, off=off: e.tensor_tensor(out=mod[:, s_, off:off + 512], in0=self.ps[0:2, pb, :], in1=mod[:, s_, off:off + 512], op=ALU.add),
                 reads=[self.psr[pb], r_mod], writes=[r_mod])
        if self.cfg.get("mod_stop") == 4:
            k.dma('sp', self.dout["dbg_mod0"], mod, reads=[r_mod], is_output=True); k.barrier(); ar.release(m0); return
        for s in range(2):
            o = 3 * s
            k.op('dve', lambda e, o=o, s=s: e.scalar_tensor_tensor(out=mod[:, o + 1, :], in0=mod[:, o + 1, :], scalar=1.0, in1=gn[:, 2 * s, :], op0=ALU.add, op1=ALU.mult),
                 reads=[r_mod, r_gn], writes=[r_mod])
            k.op('dve', lambda e, o=o, s=s: e.tensor_tensor(out=mod[:, o + 2, :], in0=mod[:, o + 2, :], in1=gn[:, 2 * s + 1, :], op=ALU.mult),
                 reads=[r_mod, r_gn], writes=[r_mod])
        if self.cfg.get("mod_stop") == 5:
            k.dma('sp', self.dout["dbg_mod0"], mod, reads=[r_mod], is_output=True); k.barrier(); ar.release(m0); return
        for s in range(2):
            o = 3 * s
            k.dma('sp', self.GROW[l][s], mod[:, o + 2, :], reads=[r_mod], writes=[self.r_GROW[l][s]])
            if self.cfg.get("mod_stop") == 6:
                k.dma('sp', self.dout["dbg_mod0"], mod, reads=[r_mod], is_output=True); k.barrier(); ar.release(m0); return
            pf = self.psb(6)
            for c in range(16):
                for ab in range(2):
                    k.op('pe', lambda e, c=c, ab=ab, o=o: e.transpose(out=pf[:, c * 4 + 2 * ab:c * 4 + 2 * ab + 2], in_=mod[0:2, o + 1 - ab, c * 128:(c + 1) * 128], identity=self.ident_f[0:2, 0:2]),
                         reads=[r_mod, self.r_ident_f], writes=[self.psr[6]])
            if self.cfg.get("mod_stop") == 7:
                k.dma('sp', self.dout["dbg_mod0"], mod, reads=[r_mod], is_output=True); k.barrier(); ar.release(m0); return
            k.op('dve', lambda e, s=s: e.tensor_copy(out=self.AB[l][s].rearrange("p a b -> p (a b)"), in_=pf[:, 0:64]), reads=[self.psr[6]], writes=[self.r_AB[l][s]])
        if self.cfg.get("mod_stop") == 8:
            k.dma('sp', self.dout["dbg_mod0"], mod, reads=[r_mod], is_output=True); k.barrier(); ar.release(m0); return
        if self.cfg.get("dump_mod"):
            k.dma('sp', self.dout[f"dbg_mod{l}"], mod, reads=[r_mod], is_output=True)
        k.barrier()
        ar.release(m0)

    def pre_tile(self, xt, r_xt, hT, r_hT, col0, AB, r_AB, g, W, part=0):
        k = self.k
        junk, xn, st = W["junk"], W["xn"], W["st"]
        r_junk, r_xn, r_st = W["r_junk"], W["r_xn"], W["r_st"]
        ps_ = self.cfg.get('pre_stop', 99)
        if ps_ == 0: return
        if part != 2:
            self._pre_a(xt, r_xt, W)
        if part == 1:
            return
        self._pre_b(hT, r_hT, col0, AB, r_AB, g, W)

    def _pre_a(self, xt, r_xt, W):
        k = self.k
        junk, xn, st = W["junk"], W["xn"], W["st"]
        r_junk, r_xn, r_st = W["r_junk"], W["r_xn"], W["r_st"]
        ps_ = 99
        k.op('dve', lambda e: e.memset(st[:, 0:1], 0.0), writes=[r_st])
        k.op('act', lambda e: e.activation(out=junk, in_=xt, func=AF.Square, accum_out=st[:, 0:1]), reads=[r_xt, r_st], writes=[r_junk, r_st])
        if ps_ == 1: return
        k.op('act', lambda e: e.activation(out=st[:, 1:2], in_=st[:, 0:1], func=AF.Ln, scale=1.0 / D_MODEL, bias=EPS), reads=[r_st], writes=[r_st])
        k.op('act', lambda e: e.activation(out=st[:, 2:3], in_=st[:, 1:2], func=AF.Exp, scale=-0.5), reads=[r_st], writes=[r_st])
        if ps_ == 2: return
        k.op('dve', lambda e: e.tensor_scalar_mul(out=xn, in0=xt, scalar1=st[:, 2:3]), reads=[r_xt, r_st], writes=[r_xn])

    def _pre_b(self, hT, r_hT, col0, AB, r_AB, g, W):
        k = self.k
        xn, r_xn = W["xn"], W["r_xn"]
        ps_ = 99
        for half in range(2):
            pb = 6 + half
            pst = self.psb16(pb)
            for c8 in range(8):
                c = half * 8 + c8
                k.op('pe', lambda e, c=c, c8=c8, pst=pst: e.transpose(out=pst[:, c8 * 128:(c8 + 1) * 128], in_=xn[:, c * 128:(c + 1) * 128], identity=self.ident_b),
                     reads=[r_xn, self.r_ident_b], writes=[self.psr[pb]])
            if ps_ == 4: return
            for c8 in range(8):
                c = half * 8 + c8
                if ps_ == 5 and c8 == 1: return
                if ps_ == 6 and c8 == 2: return
                src = pst[:, c8 * 128:(c8 + 1) * 128]
                dst = hT[:, c, col0:col0 + 128]
                if True:
                    k.op('act', lambda e, src=src, dst=dst, c=c: e.activation(out=dst, in_=src, func=AF.Identity, scale=AB[:, c, g:g + 1], bias=AB[:, c, 2 + g:3 + g]),
                         reads=[self.psr[pb], r_AB], writes=[r_hT])
                else:
                    k.op('dve', lambda e, src=src, dst=dst, c=c: e.tensor_scalar(out=dst, in0=src, scalar1=AB[:, c, g:g + 1], scalar2=AB[:, c, 2 + g:3 + g], op0=ALU.mult, op1=ALU.add),
                         reads=[self.psr[pb], r_AB], writes=[r_hT])

    def post_tile(self, xt, r_xt, ypieces, Grep, r_G, W):
        k = self.k
        junk, st, t = W["junk32"], W["st2"], W["t"]
        r_junk, r_st, r_t = W["r_junk32"], W["r_st2"], W["r_t"]
        npc = len(ypieces)
        k.op('dve', lambda e: e.memset(st[:, 0:4], 0.0), writes=[r_st])
        off = 0
        for i, (yp, r_y, n) in enumerate(ypieces):
            k.op('act', lambda e, yp=yp, i=i, n=n: e.activation(out=junk[:, 0:n], in_=yp, func=AF.Square, accum_out=st[:, i:i + 1]),
                 reads=[r_y, r_st], writes=[r_junk, r_st])
        k.op('dve', lambda e: e.tensor_reduce(out=st[:, 4:5], in_=st[:, 0:4], axis=AX.X, op=ALU.add), reads=[r_st], writes=[r_st])
        k.op('act', lambda e: e.activation(out=st[:, 5:6], in_=st[:, 4:5], func=AF.Ln, scale=1.0 / D_MODEL, bias=EPS), reads=[r_st], writes=[r_st])
        k.op('act', lambda e: e.activation(out=st[:, 6:7], in_=st[:, 5:6], func=AF.Exp, scale=-0.5), reads=[r_st], writes=[r_st])
        off = 0
        for i, (yp, r_y, n) in enumerate(ypieces):
            k.op('dve', lambda e, yp=yp, off=off, n=n: e.scalar_tensor_tensor(out=t[:, off:off + n], in0=yp, scalar=st[:, 6:7], in1=Grep[:, off:off + n], op0=ALU.mult, op1=ALU.mult),
                 reads=[r_y, r_st, r_G], writes=[r_t])
            off += n
        k.op('dve', lambda e: e.tensor_tensor(out=xt, in0=xt, in1=t, op=ALU.add), reads=[r_xt, r_t], writes=[r_xt])

    def load_grep(self, l, s):
        k, ar = self.k, self.ar
        G = [ar.alloc([D_MODEL]) for g in range(2)]
        r_G = [self.R(f"Grep{g}", sem=True) for g in range(2)]
        for g in range(2):
            k.dma('sp', G[g], self.GROW[l][s][g:g + 1, :].to_broadcast([128, D_MODEL]), reads=[self.r_GROW[l][s]], writes=[r_G[g]])
        return G, r_G

    def work_bufs(self):
        ar = self.ar
        W = {}
        W["junk"] = ar.alloc([D_MODEL], BF16); W["r_junk"] = self.R("junk")
        W["xn"] = ar.alloc([D_MODEL], BF16); W["r_xn"] = self.R("xn")
        W["st"] = ar.alloc([8]); W["r_st"] = self.R("st")
        W["st2"] = ar.alloc([8]); W["r_st2"] = self.R("st2")
        W["junk32"] = W["junk"]; W["r_junk32"] = W["r_junk"]
        W["t"] = ar.alloc([D_MODEL]); W["r_t"] = self.R("t")
        return W

    def mlp_phase(self, l, Xin, r_Xin, Xout, r_Xout, out_is_output=False):
        k, ar = self.k, self.ar
        self.phase()
        m0 = ar.mark()
        w1 = self.din["mlp_w1"][l]
        w2 = self.din["mlp_w2"][l]
        if not hasattr(self, "W1C"):
            self.W1C = self.scr("W1C", [16, 128, 16 * 512], BF16)
            self.W2C = self.scr("W2C", [32, 128, 4 * 1024], BF16)
        r_W1C = [self.R(f"w1c{j}") for j in range(16)]
        r_W2C = [self.R(f"w2c{j}") for j in range(32)]
        G, r_G = self.load_grep(l, 1)
        W = self.work_bufs()
        xt = [ar.alloc([D_MODEL]) for _ in range(2)]
        r_xt = [self.R(f"xt{i}", sem=True) for i in range(2)]
        hTs = [ar.alloc([16, 512], BF16) for _ in range(2)]; r_hTs = [self.R(f"hT{i}") for i in range(2)]
        ysbs = [h_.rearrange("p a b -> p (a b)").bitcast(F32).rearrange("p (s n) -> p s n", s=4) for h_ in hTs]
        aT = ar.alloc([64, 512], BF16); r_aT = self.R("aT")
        w1b = [ar.alloc([16, 512], BF16) for _ in range(2)]
        r_w1b = [self.R(f"w1b{i}", sw=True) for i in range(2)]
        w2b = [ar.alloc([4, 1024], BF16) for _ in range(2)]
        r_w2b = [self.R(f"w2b{i}", sw=True) for i in range(2)]
        w1s = [self.R(f"w1s{i}", sem=True).sem for i in range(2)]
        w2s = [self.R(f"w2s{i}", sem=True).sem for i in range(2)]
        rt = [ar.alloc([512]) for _ in range(2)]
        r_rt = [self.R(f"rt{i}") for i in range(2)]
        AB, r_AB = self.AB[l][1], self.r_AB[l][1]
        xi = 0
        w1i = 0
        w2i = 0
        NBLK = self.cfg.get('nblk', 6)
        xi_ = [0]

        def pre_one(blk_, sub_, part=0):
            g_ = 0 if blk_ < 2 else 1
            tile_ = blk_ * 4 + sub_
            if part != 2:
                b_ = xi_[0] % 2; xi_[0] += 1
                k.dma('sp', xt[b_], Xin[tile_ * 128:(tile_ + 1) * 128, :], reads=[r_Xin[tile_]], writes=[r_xt[b_]])
            else:
                b_ = 0
            self.pre_tile(xt[b_], r_xt[b_], hTs[blk_ % 2], r_hTs[blk_ % 2], sub_ * 128, AB, r_AB, g_, W, part=part)

        for sub in range(4):
            pre_one(0, sub)
        pending = []
        for blk in range(NBLK):
            g = 0 if blk < 2 else 1
            hT, r_hT, ysb = hTs[blk % 2], r_hTs[blk % 2], ysbs[blk % 2]
            if self.cfg.get("mlp_stop") == 1:
                k.barrier(); ar.release(m0); return
            for j in range(16):
                wb_i = w1i % 2; w1i += 1
                if blk == 0:
                    k.dma('pool', w1b[wb_i], w1[:, j * 512:(j + 1) * 512].rearrange("(k p) n -> p k n", p=128), writes=[r_w1b[wb_i]])
                    k.dma('sp', self.W1C[j], w1b[wb_i].rearrange("p a b -> p (a b)"), reads=[r_w1b[wb_i]], writes=[r_W1C[j]], sem=w1s[wb_i])
                else:
                    k.dma('pool', w1b[wb_i].rearrange("p a b -> p (a b)"), self.W1C[j], reads=[r_W1C[j]], writes=[r_w1b[wb_i]])
                for cc in range(4):
                    c = j * 4 + cc
                    pb = c % 2
                    for kk in range(16):
                        k.op('pe', lambda e, kk=kk, cc=cc, wb_i=wb_i, pb=pb, hT=hT: e.matmul(self.psb(pb), lhsT=w1b[wb_i][:, kk, cc * 128:(cc + 1) * 128], rhs=hT[:, kk, :], start=(kk == 0), stop=(kk == 15)),
                             reads=[r_w1b[wb_i], r_hT], writes=[self.psr[pb]])
                    k.op('act', lambda e, pb=pb: e.activation(out=rt[pb], in_=self.psb(pb), func=AF.Relu), reads=[self.psr[pb]], writes=[r_rt[pb]])
                    k.op('dve', lambda e, pb=pb, c=c: e.tensor_tensor(out=aT[:, c, :], in0=rt[pb], in1=rt[pb], op=ALU.mult), reads=[r_rt[pb]], writes=[r_aT])
                if j < 3 and pending:
                    post_one(*pending.pop(0))
                if j % 4 == 1 and blk + 1 < NBLK:
                    pre_one(blk + 1, j // 4, part=1)
                if j % 4 == 3 and blk + 1 < NBLK:
                    pre_one(blk + 1, j // 4, part=2)
            if self.cfg.get("mlp_stop") == 2:
                k.barrier(); ar.release(m0); return
            for half in range(2):
                for j in range(16):
                    wb_i = w2i % 2; w2i += 1
                    if blk == 0:
                        k.dma('pool', w2b[wb_i], w2[j * 512:(j + 1) * 512, half * 1024:(half + 1) * 1024].rearrange("(c p) n -> p c n", p=128), writes=[r_w2b[wb_i]])
                        k.dma('sp', self.W2C[half * 16 + j], w2b[wb_i].rearrange("p a b -> p (a b)"), reads=[r_w2b[wb_i]], writes=[r_W2C[half * 16 + j]], sem=w2s[wb_i])
                    else:
                        k.dma('pool', w2b[wb_i].rearrange("p a b -> p (a b)"), self.W2C[half * 16 + j], reads=[r_W2C[half * 16 + j]], writes=[r_w2b[wb_i]])
                    for cc in range(4):
                        c = j * 4 + cc
                        for sub in range(4):
                            for n in range(2):
                                pb = sub * 2 + n
                                k.op('pe', lambda e, c=c, cc=cc, sub=sub, n=n, pb=pb, wb_i=wb_i: e.matmul(self.psb(pb), lhsT=aT[:, c, sub * 128:(sub + 1) * 128], rhs=w2b[wb_i][:, cc, n * 512:(n + 1) * 512], start=(c == 0), stop=(c == 63)),
                                     reads=[r_aT, r_w2b[wb_i]] + ([r_hT] if False else []), writes=[self.psr[pb]])
                if self.cfg.get("mlp_stop") == 3:
                    k.barrier(); ar.release(m0); return
                if half == 0:
                    for sub in range(4):
                        for n in range(2):
                            pb = sub * 2 + n
                            eng = 'act' if n == 0 else 'dve'
                            if eng == 'act':
                                k.op('act', lambda e, sub=sub, n=n, pb=pb, ysb=ysb: e.copy(out=ysb[:, sub, n * 512:(n + 1) * 512], in_=self.psb(pb)), reads=[self.psr[pb]], writes=[r_hT])
                            else:
                                k.op('dve', lambda e, sub=sub, n=n, pb=pb, ysb=ysb: e.tensor_copy(out=ysb[:, sub, n * 512:(n + 1) * 512], in_=self.psb(pb)), reads=[self.psr[pb]], writes=[r_hT])
            if self.cfg.get("mlp_stop") == 4:
                k.barrier(); ar.release(m0); return
            def post_one(blk_, sub_, ysb_, r_hT_, g_):
                tile_ = blk_ * 4 + sub_
                b_ = xi_[0] % 2; xi_[0] += 1
                k.dma('sp', xt[b_], Xin[tile_ * 128:(tile_ + 1) * 128, :], reads=[r_Xin[tile_]], writes=[r_xt[b_]])
                yp_ = [(ysb_[:, sub_, :], r_hT_, 1024), (self.psb(sub_ * 2), self.psr[sub_ * 2], 512), (self.psb(sub_ * 2 + 1), self.psr[sub_ * 2 + 1], 512)]
                self.post_tile(xt[b_], r_xt[b_], yp_, G[g_], r_G[g_], W)
                k.dma('sp', Xout[tile_ * 128:(tile_ + 1) * 128, :], xt[b_], reads=[r_xt[b_]], writes=[r_Xout[tile_]], is_output=out_is_output)

            post_one(blk, 0, ysb, r_hT, g)
            for sub in range(1, 4):
                pending.append((blk, sub, ysb, r_hT, g))
        while pending:
            post_one(*pending.pop(0))
        k.barrier()
        ar.release(m0)


def _even_scratch(self):
    if hasattr(self, "QA"):
        return
    s = self.scr
    self.QA = s("QA", [8, 128, NT]); self.FF = s("FF", [8, 128, NT]); self.FB = s("FB", [8, 128, NT]); self.GA = s("GA", [8, 128, NT])
    self.QB = s("QB", [8, 128, NT], BF16); self.KB = s("KB", [8, 128, NT], BF16)
    self.VA = s("VA", [NT, 1024], BF16); self.VB = s("VB", [NT, 1024], BF16)
    self.OA = s("OA", [16, 128, NT], BF16) if not self.cfg.get("dump_oa") else self.out("OA", [16, 128, NT], BF16)
    self.r_E1 = self.R("E1out")
    self.r_OA = self.R("OAres")


def even_inproj(self, Xin, r_Xin):
    k, ar = self.k, self.ar
    self.phase()
    _even_scratch(self)
    m0 = ar.mark()
    l = 0
    w_in = self.din["w_in_even"]
    W = self.work_bufs()
    xt = [ar.alloc([D_MODEL]) for _ in range(2)]
    r_xt = [self.R(f"e1xt{i}", sem=True) for i in range(2)]
    hT = ar.alloc([16, NT], BF16); r_hT = self.R("hTall")
    wb = [ar.alloc([16, 512], BF16) for _ in range(2)]
    r_wb = [self.R(f"e1w{i}", sw=True) for i in range(2)]
    NST = 4
    stg = [ar.alloc([512]) for _ in range(NST)]
    r_stg = [self.R(f"e1stg{i}", sem=True) for i in range(NST)]
    AB, r_AB = self.AB[l][0], self.r_AB[l][0]
    for tile in range(24):
        g = 0 if tile < 8 else 1
        b = tile % 2
        k.dma('sp', xt[b], Xin[tile * 128:(tile + 1) * 128, :], reads=[r_Xin[tile]], writes=[r_xt[b]])
        self.pre_tile(xt[b], r_xt[b], hT, r_hT, tile * 128, AB, r_AB, g, W)
    fm_dst = {0: self.QA, 1: self.FF, 2: self.FB, 4: self.GA, 5: self.QB, 6: self.KB}
    si = 0
    pbi = 0
    nak, nav = self.dout["nak"], self.dout["nav"]
    for j in self.cfg.get('e1_js', range(16)):
        grp = j // 2
        wi = j % 2
        k.dma('pool', wb[wi], w_in[:, j * 512:(j + 1) * 512].rearrange("(k p) n -> p k n", p=128), writes=[r_wb[wi]])
        if grp in fm_dst:
            dst = fm_dst[grp]
            isbf = grp in (5, 6)
            for cc in range(4):
                head = (j % 2) * 4 + cc
                for tb in range(6):
                    pb = pbi % 4; pbi += 1
                    for kk in range(16):
                        k.op('pe', lambda e, kk=kk, cc=cc, wi=wi, tb=tb, pb=pb: e.matmul(self.psb(pb), lhsT=wb[wi][:, kk, cc * 128:(cc + 1) * 128], rhs=hT[:, kk, tb * 512:(tb + 1) * 512], start=(kk == 0), stop=(kk == 15)),
                             reads=[r_wb[wi], r_hT], writes=[self.psr[pb]])
                    s_ = si % NST; si += 1
                    so = stg[s_] if not isbf else stg[s_].bitcast(BF16)[:, 0:512]
                    scale = (128.0 ** -0.5) if grp == 5 else 1.0
                    if si % 2 == 0:
                        k.op('act', lambda e, so=so, pb=pb, scale=scale: e.activation(out=so, in_=self.psb(pb), func=AF.Copy, scale=scale), reads=[self.psr[pb]], writes=[r_stg[s_]])
                    else:
                        k.op('dve', lambda e, so=so, pb=pb, scale=scale: e.tensor_scalar_mul(out=so, in0=self.psb(pb), scalar1=scale), reads=[self.psr[pb]], writes=[r_stg[s_]])
                    k.dma('sp', dst[head, :, tb * 512:(tb + 1) * 512], so, reads=[r_stg[s_]], writes=[self.r_E1])
        if grp in (3, 6, 7):
            ntile = 8 if grp == 6 else 24
            col0 = (j % 2) * 512
            for t in range(ntile):
                pb = pbi % 4; pbi += 1
                for kk in range(16):
                    k.op('pe', lambda e, kk=kk, wi=wi, t=t, pb=pb: e.matmul(self.psb(pb), lhsT=hT[:, kk, t * 128:(t + 1) * 128], rhs=wb[wi][:, kk, :], start=(kk == 0), stop=(kk == 15)),
                         reads=[r_wb[wi], r_hT], writes=[self.psr[pb]])
                if grp in (3, 7):
                    s_ = si % NST; si += 1
                    so = stg[s_].bitcast(BF16)[:, 0:512]
                    k.op('act', lambda e, so=so, pb=pb: e.copy(out=so, in_=self.psb(pb)), reads=[self.psr[pb]], writes=[r_stg[s_]])
                    d = self.VA if grp == 3 else self.VB
                    k.dma('sp', d[t * 128:(t + 1) * 128, col0:col0 + 512], so, reads=[r_stg[s_]], writes=[self.r_E1])
                if grp in (6, 7) and t < 8:
                    s_ = si % NST; si += 1
                    so = stg[s_]
                    k.op('act', lambda e, so=so, pb=pb: e.copy(out=so, in_=self.psb(pb)), reads=[self.psr[pb]], writes=[r_stg[s_]])
                    d = nak if grp == 6 else nav
                    k.dma('sp', d[t * 128:(t + 1) * 128, col0:col0 + 512], so, reads=[r_stg[s_]], is_output=True)
    k.barrier()
    ar.release(m0)


Builder.even_inproj = even_inproj


def hgrn_phase(self):
    k, ar = self.k, self.ar
    self.phase()
    _even_scratch(self)
    m0 = ar.mark()
    CH = 64
    BT = 512
    cmask = ar.alloc([1024]); r_cmask = self.R("cmask", sem=True)
    trim = ar.alloc([128], parts=64); r_trim = self.R("trim", sem=True)
    k.dma('sp', cmask, self.din["cmask"], writes=[r_cmask])
    k.dma('sp', trim, self.din["trimask"], writes=[r_trim])
    gn = ar.alloc([1]); r_gn = self.R("hgn", sem=True)
    with self.nc.allow_non_contiguous_dma(reason="tiny"):
        k.dma('sp', gn, self.din["hgn"].rearrange("o p -> p o"), writes=[r_gn])
    lbr = ar.alloc([2, 1024], parts=3); r_lbr = self.R("lbr", sem=True)
    k.dma('sp', lbr, self.din["lbrows"].rearrange("d r c -> r d c"), writes=[r_lbr])
    k.op('act', I("activation", out=lbr, in_=lbr, func=AF.Exp), reads=[r_lbr], writes=[r_lbr])
    pf = self.psb(7)
    for d in range(2):
        for h in range(8):
            k.op('pe', I("transpose", out=pf[:, (d * 8 + h) * 3:(d * 8 + h) * 3 + 3], in_=lbr[0:3, d, h * 128:(h + 1) * 128], identity=self.ident_f[0:3, 0:3]),
                 reads=[r_lbr, self.r_ident_f], writes=[self.psr[7]])
    lbe = ar.alloc([16, 3]); r_lbe = self.R("lbe")
    omlb = ar.alloc([16]); r_omlb = self.R("omlb")
    lsum = ar.alloc([16]); r_lsum = self.R("lsum")
    k.op('dve', I("tensor_copy", out=lbe.rearrange("p a r -> p (a r)"), in_=pf[:, 0:48]), reads=[self.psr[7]], writes=[r_lbe])
    k.op('dve', I("tensor_reduce", out=lsum, in_=lbe, axis=AX.X, op=ALU.add), reads=[r_lbe], writes=[r_lsum])
    k.op('dve', I("reciprocal", out=lsum, in_=lsum), reads=[r_lsum], writes=[r_lsum])
    k.op('dve', I("tensor_tensor", out=omlb, in0=lbe[:, :, 0], in1=lsum, op=ALU.mult), reads=[r_lbe, r_lsum], writes=[r_omlb])
    lbv = ar.alloc([16]); lnom = ar.alloc([16])
    k.op('dve', I("tensor_copy", out=lbv, in_=omlb), reads=[r_omlb], writes=[r_omlb])
    k.op('dve', I("tensor_scalar", out=omlb, in0=omlb, scalar1=-1.0, scalar2=1.0, op0=ALU.mult, op1=ALU.add), reads=[r_omlb], writes=[r_omlb])
    k.op('act', I("activation", out=lnom, in_=omlb, func=AF.Ln), reads=[r_omlb], writes=[r_omlb])
    lnsc = ar.alloc([1])
    k.op('dve', I("memset", lnsc, float(math.log(128.0 ** -0.5))), writes=[r_omlb])
    k.barrier()
    NC_ = 4
    def f32buf(n=BT): return ar.alloc([n])
    qT = [f32buf() for _ in range(3)]; r_qT = [self.R(f"hq{i}", sem=True) for i in range(3)]
    fT = [f32buf() for _ in range(3)]; r_fT = [self.R(f"hf{i}", sem=True) for i in range(3)]
    kt = f32buf(); r_kt = self.R("kt")
    lf = f32buf(); r_lf = self.R("lf")
    bc = f32buf(); r_bc = self.R("bc")
    rv = f32buf(); r_rv = self.R("rv")
    sq = f32buf(); r_sq = self.R("sq")
    tm = f32buf(); r_tm = self.R("tm")
    tm2 = f32buf(); r_tm2 = self.R("tm2")
    vt = [[ar.alloc([8, 128], BF16, parts=64) for _ in range(2)] for _ in range(NC_)]
    r_vt = [[self.R(f"hv{c}{i}", sem=True) for i in range(2)] for c in range(NC_)]
    eb = [[f32buf() for _ in range(2)] for _ in range(NC_)]; r_eb = [[self.R(f"eb{c}{i}") for i in range(2)] for c in range(NC_)]
    qb = [[ar.alloc([BT], BF16) for _ in range(2)] for _ in range(NC_)]; r_qb = [[self.R(f"qb{c}{i}") for i in range(2)] for c in range(NC_)]
    kb = [[ar.alloc([BT], BF16) for _ in range(2)] for _ in range(NC_)]; r_kb = [[self.R(f"kb{c}{i}") for i in range(2)] for c in range(NC_)]
    kd = [[ar.alloc([BT], BF16) for _ in range(2)] for _ in range(NC_)]; r_kd = [[self.R(f"kd{c}{i}") for i in range(2)] for c in range(NC_)]
    S32 = [ar.alloc([128]) for _ in range(NC_)]; r_S32 = [self.R(f"S32{c}", sem=True) for c in range(NC_)]
    Sbf = [ar.alloc([128], BF16) for _ in range(NC_)]; r_Sbf = [self.R(f"Sbf{c}") for c in range(NC_)]
    Asb = [[ar.alloc([CH], BF16, parts=64) for _ in range(2)] for _ in range(NC_)]; r_Asb = [[self.R(f"Asb{c}{i}") for i in range(2)] for c in range(NC_)]
    kdt = [[ar.alloc([128], BF16, parts=64) for _ in range(2)] for _ in range(NC_)]; r_kdt = [[self.R(f"kdt{c}{i}") for i in range(2)] for c in range(NC_)]
    Oacc = [ar.alloc([2048]) for _ in range(2)]; r_Oacc = [self.R(f"Oacc{i}") for i in range(2)]
    Oacb = [ar.alloc([2048]) for _ in range(2)]; r_Oacb = [self.R(f"Oacb{i}") for i in range(2)]
    gT = [f32buf() for _ in range(2)]; r_gT = [self.R(f"hg{i}", sem=True) for i in range(2)]
    sq16 = ar.alloc([BT], BF16); r_sq16 = self.R("sq16")
    rstd = f32buf(); r_rstd = self.R("rstdh")
    ost = [ar.alloc([BT], BF16) for _ in range(2)]; r_ost = [self.R(f"host{i}", sem=True) for i in range(2)]
    def regA(c, p): return self.ps[0:CH, 2 * c, p * 64:p * 64 + CH]
    def regU(c, p): return self.ps[:, 2 * c, 128 + p * 128:256 + p * 128]
    def regOd(c, p): return self.ps[:, 2 * c, 384 + p * 64:448 + p * 64]
    def regK(c, p): return self.psb16(2 * c + 1)[0:CH, p * 128:(p + 1) * 128]
    def regOa(c, p): return self.ps[:, 2 * c + 1, 128 + p * 64:192 + p * 64]
    r_bD = [self.psr[2 * c] for c in range(NC_)]
    r_bA = [self.psr[2 * c + 1] for c in range(NC_)]
    r_pA = [[r_bD[c] for p in range(2)] for c in range(NC_)]
    r_pU = [[r_bD[c] for p in range(2)] for c in range(NC_)]
    r_pK = [[r_bA[c] for p in range(2)] for c in range(NC_)]
    r_pO = [[(r_bA[c] if c % 2 == 0 else r_bD[c]) for p in range(2)] for c in range(NC_)]
    r_fin = self.R("pfin")
    srcs = {0: self.FF, 1: self.FB}
    cnt = {"ld": 0, "fin": 0}
    stepc = [0] * NC_
    chc = [0] * NC_
    seqs = [(i * 256, 256, True, i) for i in range(4)] + [(1024, 2048, False, 0)]
    seqs = seqs[self.cfg.get('hg_s0', 0):self.cfg.get('hg_s1', 5)]
    def prep_chain(c, h, d, t0, bt, nblk, ncb, step, sl, par, blkof):
        blk = step if d == 0 else nblk - 1 - step
        blkof[c] = blk
        c0 = t0 + blk * bt
        li = cnt["ld"] % 3; cnt["ld"] += 1
        pi = stepc[c] % 2; stepc[c] += 1
        par[c] = pi
        k.dma('sp', qT[li][:, sl], self.QA[h, :, c0:c0 + bt], reads=[self.r_E1], writes=[r_qT[li]])
        k.dma('sp', fT[li][:, sl], srcs[d][h, :, c0:c0 + bt], reads=[self.r_E1], writes=[r_fT[li]])
        k.dma('sp', vt[c][pi][:, 0:ncb, :], self.VA[c0:c0 + bt, h * 128:(h + 1) * 128].rearrange("(c p) v -> p c v", p=CH), reads=[self.r_E1], writes=[r_vt[c][pi]])
        q_, f_ = qT[li][:, sl], fT[li][:, sl]
        hd = d * 8 + h
        lb_ap, lno_ap = lbv[:, hd:hd + 1], lnom[:, hd:hd + 1]
        k.op('act', I("activation", out=kt[:, sl], in_=f_, func=AF.Exp), reads=[r_fT[li]], writes=[r_kt])
        k.op('act', I("activation", out=tm[:, sl], in_=kt[:, sl], func=AF.Ln, bias=1.0), reads=[r_kt], writes=[r_tm])
        k.op('act', I("activation", out=lf[:, sl], in_=kt[:, sl], func=AF.Ln, bias=lb_ap), reads=[r_kt, r_omlb], writes=[r_lf])
        k.op('dve', I("tensor_tensor", out=lf[:, sl], in0=lf[:, sl], in1=tm[:, sl], op=ALU.subtract), reads=[r_lf, r_tm], writes=[r_lf])
        mF, mB = cmask[:, 0:bt], cmask[:, 512:512 + bt]
        if d == 0:
            k.op('dve', I("tensor_tensor_scan", out=bc[:, sl], data0=mF, data1=lf[:, sl], initial=0.0, op0=ALU.mult, op1=ALU.add), reads=[r_lf, r_cmask], writes=[r_bc])
            k.op('dve', I("tensor_tensor_scan", out=rv[:, sl][:, ::-1], data0=mB[:, ::-1], data1=lf[:, sl][:, ::-1], initial=0.0, op0=ALU.mult, op1=ALU.add), reads=[r_lf, r_cmask], writes=[r_rv])
        else:
            k.op('dve', I("tensor_tensor_scan", out=bc[:, sl][:, ::-1], data0=mB[:, ::-1], data1=lf[:, sl][:, ::-1], initial=0.0, op0=ALU.mult, op1=ALU.add), reads=[r_lf, r_cmask], writes=[r_bc])
            k.op('dve', I("tensor_tensor_scan", out=rv[:, sl], data0=mF, data1=lf[:, sl], initial=0.0, op0=ALU.mult, op1=ALU.add), reads=[r_lf, r_cmask], writes=[r_rv])
        dcol0 = CH - 1 if d == 0 else 0
        k.op('act', I("activation", out=eb[c][pi][:, 0:ncb], in_=bc[:, dcol0:bt:CH], func=AF.Exp), reads=[r_bc], writes=[r_eb[c][pi]])
        k.op('dve', I("tensor_tensor", out=rv[:, sl], in0=rv[:, sl], in1=lf[:, sl], op=ALU.subtract), reads=[r_rv, r_lf], writes=[r_rv])
        k.op('dve', I("tensor_tensor", out=rv[:, sl], in0=rv[:, sl], in1=tm[:, sl], op=ALU.subtract), reads=[r_rv, r_tm], writes=[r_rv])
        k.op('act', I("activation", out=kd[c][pi][:, sl], in_=rv[:, sl], func=AF.Exp, bias=lno_ap), reads=[r_rv, r_omlb], writes=[r_kd[c][pi]])
        k.op('dve', I("tensor_tensor", out=tm[:, sl], in0=tm[:, sl], in1=bc[:, sl], op=ALU.add), reads=[r_tm, r_bc], writes=[r_tm])
        k.op('act', I("activation", out=kb[c][pi][:, sl], in_=tm[:, sl], func=AF.Exp, scale=-1.0, bias=lno_ap), reads=[r_tm, r_omlb], writes=[r_kb[c][pi]])
        k.op('act', I("activation", out=sq[:, sl], in_=q_, func=AF.Exp, scale=-1.0), reads=[r_qT[li]], writes=[r_sq])
        k.op('act', I("activation", out=sq[:, sl], in_=sq[:, sl], func=AF.Ln, bias=1.0), reads=[r_sq], writes=[r_sq])
        k.op('dve', I("tensor_tensor", out=sq[:, sl], in0=bc[:, sl], in1=sq[:, sl], op=ALU.subtract), reads=[r_bc, r_sq], writes=[r_sq])
        k.op('act', I("activation", out=tm2[:, sl], in_=sq[:, sl], func=AF.Exp, bias=lnsc[:, 0:1]), reads=[r_sq, r_omlb], writes=[r_tm2])
        k.op('dve', I("tensor_tensor", out=qb[c][pi][:, sl], in0=q_, in1=tm2[:, sl], op=ALU.mult), reads=[r_qT[li], r_tm2], writes=[r_qb[c][pi]])

    def chunk_step(cstep, chains, bt, ncb, par, blkof):
        info = []
        for c, (h, d) in enumerate(chains):
            cc = cstep if d == 0 else ncb - 1 - cstep
            pp = chc[c] % 2; chc[c] += 1
            pi = par[c]
            cs = cc * CH
            info.append((c, h, d, cc, pp, pi, cs))
        for (c, h, d, cc, pp, pi, cs) in info:
            qbc, kbc, kdc = qb[c][pi][:, cs:cs + CH], kb[c][pi][:, cs:cs + CH], kd[c][pi][:, cs:cs + CH]
            k.op('pe', I("matmul", regA(c, pp), lhsT=kbc, rhs=qbc, start=True, stop=True), reads=[r_kb[c][pi], r_qb[c][pi]], writes=[r_pA[c][pp]])
            k.op('pe', I("transpose", out=regK(c, pp), in_=kdc, identity=self.ident_b), reads=[r_kd[c][pi], self.r_ident_b], writes=[r_pK[c][pp]])
        for (c, h, d, cc, pp, pi, cs) in info:
            mk = trim[:, 0:CH] if d == 0 else trim[:, CH:2 * CH]
            k.op('dve', I("tensor_tensor", out=Asb[c][pp], in0=regA(c, pp), in1=mk, op=ALU.mult), reads=[r_pA[c][pp], r_trim], writes=[r_Asb[c][pp]])
            k.op('act', I("copy", out=kdt[c][pp], in_=regK(c, pp)), reads=[r_pK[c][pp]], writes=[r_kdt[c][pp]])
        for (c, h, d, cc, pp, pi, cs) in info:
            qbc = qb[c][pi][:, cs:cs + CH]
            vch = vt[c][pi][:, cc, :]
            ro = regOa(c, pp) if d == 0 else regOd(c, pp)
            k.op('pe', I("matmul", ro, lhsT=Sbf[c], rhs=qbc, start=True, stop=False), reads=[r_Sbf[c], r_qb[c][pi]], writes=[r_pO[c][pp]])
            k.op('pe', I("matmul", ro, lhsT=vch, rhs=Asb[c][pp], start=False, stop=True), reads=[r_vt[c][pi], r_Asb[c][pp]], writes=[r_pO[c][pp]])
            k.op('pe', I("matmul", regU(c, pp), lhsT=kdt[c][pp], rhs=vch, start=True, stop=True), reads=[r_kdt[c][pp], r_vt[c][pi]], writes=[r_pU[c][pp]])
        for (c, h, d, cc, pp, pi, cs) in info:
            blk = blkof[c]
            oc = Oacc[c // 2][:, blk * bt + cs: blk * bt + cs + CH]
            if d == 0:
                k.op('act', I("copy", out=oc, in_=regOa(c, pp)), reads=[r_pO[c][pp]], writes=[r_Oacc[c // 2]])
            dec = eb[c][pi][:, cc:cc + 1]
            k.op('dve', I("scalar_tensor_tensor", out=S32[c], in0=S32[c], scalar=dec, in1=regU(c, pp), op0=ALU.mult, op1=ALU.add), reads=[r_S32[c], r_eb[c][pi], r_pU[c][pp]], writes=[r_S32[c]])
            k.op('act', I("copy", out=Sbf[c], in_=S32[c]), reads=[r_S32[c]], writes=[r_Sbf[c]])
        for (c, h, d, cc, pp, pi, cs) in info:
            if d == 1:
                blk = blkof[c]
                oc = Oacb[c // 2][:, blk * bt + cs: blk * bt + cs + CH]
                k.op('dve', I("tensor_copy", out=oc, in_=regOd(c, pp)), reads=[r_pO[c][pp]], writes=[r_Oacb[c // 2]])

    def finish_group(chains, t0, bt, nblk, sl, is_ctx, sidx, hp):
        if is_ctx:
            for c, (h, d) in enumerate(chains):
                dst = self.dout["nsf" if d == 0 else "nsb"]
                self.store(dst[sidx, h], S32[c], r_S32[c], reads=[r_S32[c]], writes=[], is_output=True)
        for hh in range(0 if self.cfg.get('hg_nofin') else 2):
            h = 2 * hp + hh
            for blk in range(nblk):
                c0 = t0 + blk * bt
                gi = cnt["fin"] % 2; cnt["fin"] += 1
                ob = Oacc[hh][:, blk * bt:(blk + 1) * bt]
                obb = Oacb[hh][:, blk * bt:(blk + 1) * bt]
                k.dma('sp', gT[gi][:, sl], self.GA[h, :, c0:c0 + bt], reads=[self.r_E1], writes=[r_gT[gi]])
                k.op('dve', I("tensor_tensor", out=ob, in0=ob, in1=obb, op=ALU.add), reads=[r_Oacc[hh], r_Oacb[hh]], writes=[r_Oacc[hh]])
                k.op('act', I("activation", out=sq16[:, sl], in_=ob, func=AF.Square), reads=[r_Oacc[hh]], writes=[r_sq16])
                pfin = self.ps[:, 7, 0:bt]
                fin_res = [r_bA[3]]
                k.op('pe', I("matmul", pfin, lhsT=self.ones_b, rhs=sq16[:, sl], start=True, stop=True), reads=[self.r_ones, r_sq16], writes=fin_res)
                k.op('act', I("activation", out=rstd[:, sl], in_=pfin, func=AF.Ln, scale=1.0 / 128, bias=EPS), reads=fin_res, writes=[r_rstd])
                g_ = gT[gi][:, sl]
                k.op('act', I("activation", out=tm[:, sl], in_=g_, func=AF.Exp, scale=-1.0), reads=[r_gT[gi]], writes=[r_tm])
                k.op('act', I("activation", out=tm[:, sl], in_=tm[:, sl], func=AF.Ln, bias=1.0), reads=[r_tm], writes=[r_tm])
                k.op('dve', I("scalar_tensor_tensor", out=rstd[:, sl], in0=rstd[:, sl], scalar=-0.5, in1=tm[:, sl], op0=ALU.mult, op1=ALU.subtract), reads=[r_rstd, r_tm], writes=[r_rstd])
                k.op('act', I("activation", out=rstd[:, sl], in_=rstd[:, sl], func=AF.Exp), reads=[r_rstd], writes=[r_rstd])
                k.op('dve', I("tensor_tensor", out=tm[:, sl], in0=ob, in1=g_, op=ALU.mult), reads=[r_Oacc[hh], r_gT[gi], r_tm], writes=[r_tm])
                k.op('dve', I("scalar_tensor_tensor", out=ost[gi][:, sl], in0=tm[:, sl], scalar=gn[:, 0:1], in1=rstd[:, sl], op0=ALU.mult, op1=ALU.mult), reads=[r_tm, r_rstd, r_gn], writes=[r_ost[gi]])
                self.store(self.OA[h, :, c0:c0 + bt], ost[gi][:, sl], r_ost[gi], reads=[r_ost[gi]], writes=[self.r_OA], is_output=bool(self.cfg.get("dump_oa")))

    items = []
    for (t0, T, is_ctx, sidx) in seqs:
        bt = min(BT, T)
        nblk = T // bt
        for hp in range(self.cfg.get('hg_nhp', 4)):
            for step in range(nblk):
                items.append((t0, T, is_ctx, sidx, hp, step))
    pars = [[0] * NC_ for _ in items]
    blkofs = [[0] * NC_ for _ in items]

    def do_prep(ii, c):
        (t0, T, is_ctx, sidx, hp, step) = items[ii]
        bt = min(BT, T); nblk = T // bt; ncb = bt // CH
        h, d = 2 * hp + (c // 2), c % 2
        prep_chain(c, h, d, t0, bt, nblk, ncb, step, slice(0, bt), pars[ii], blkofs[ii])

    for c in range(NC_):
        do_prep(0, c)
    for ii, (t0, T, is_ctx, sidx, hp, step) in enumerate(items):
        bt = min(BT, T); nblk = T // bt; ncb = bt // CH
        sl = slice(0, bt)
        chains = [(2 * hp + (c // 2), c % 2) for c in range(NC_)]
        if step == 0:
            for c, (h, d) in enumerate(chains):
                if is_ctx:
                    k.op('dve', I("memset", S32[c], 0.0), writes=[r_S32[c]])
                    k.op('dve', I("memset", Sbf[c], 0.0), writes=[r_Sbf[c]])
                else:
                    k.dma('sp', S32[c], self.din["st_f" if d == 0 else "st_b"][h], writes=[r_S32[c]])
                    k.op('act', I("copy", out=Sbf[c], in_=S32[c]), reads=[r_S32[c]], writes=[r_Sbf[c]])
        nxt = ii + 1 if ii + 1 < len(items) else None
        done = 0
        for cstep in range(ncb):
            chunk_step(cstep, chains, bt, ncb, pars[ii], blkofs[ii])
            if nxt is not None:
                want = ((cstep + 1) * NC_) // ncb
                while done < want:
                    do_prep(nxt, done); done += 1
        if nxt is not None:
            while done < NC_:
                do_prep(nxt, done); done += 1
        if step == nblk - 1:
            finish_group(chains, t0, bt, nblk, sl, is_ctx, sidx, hp)
    k.barrier()
    ar.release(m0)


Builder.hgrn_phase = hgrn_phase


def _attn_finish(self, po, r_po, rec, r_rec, ob16, r_ob16, pT, r_pT, ostg_slice, r_ostg):
    k = self.k
    k.op('dve', I("reciprocal", out=rec, in_=po[:, 128:129]), reads=[r_po], writes=[r_rec])
    k.op('dve', I("tensor_scalar_mul", out=ob16, in0=po[:, 0:128], scalar1=rec), reads=[r_po, r_rec], writes=[r_ob16])
    k.op('pe', I("transpose", out=pT, in_=ob16, identity=self.ident_b), reads=[r_ob16, self.r_ident_b], writes=[r_pT])
    k.op('act', I("copy", out=ostg_slice, in_=pT), reads=[r_pT], writes=[r_ostg])


def na_phase(self):
    k, ar = self.k, self.ar
    self.phase()
    _even_scratch(self)
    m0 = ar.mark()
    QT = [ar.alloc([2048], BF16) for _ in range(2)]; r_QT = [self.R(f"naQ{i}", sem=True) for i in range(2)]
    KT = [ar.alloc([2048], BF16) for _ in range(2)]; r_KT = [self.R(f"naK{i}", sem=True) for i in range(2)]
    vaug = [ar.alloc([16, 129], BF16) for _ in range(2)]; r_vaug = [self.R(f"naV{i}", sem=True) for i in range(2)]
    for i in range(2):
        k.op('pool', I("memset", vaug[i][:, :, 128:129], 1.0), writes=[r_vaug[i]])
    rec = [ar.alloc([1]) for _ in range(2)]; r_rec = [self.R(f"narec{i}") for i in range(2)]
    ob16 = [ar.alloc([128], BF16) for _ in range(2)]; r_ob16 = [self.R(f"naob{i}") for i in range(2)]
    ostg = [ar.alloc([512], BF16) for _ in range(2)]; r_ostg = [self.R(f"naost{i}", sem=True) for i in range(2)]
    Pc = [ar.alloc([4, 512], BF16) for _ in range(2)]; r_Pc = [self.R(f"naPc{i}") for i in range(2)]
    Praw = [ar.alloc([128], BF16) for _ in range(3)]; r_Praw = [self.R(f"naPr{i}") for i in range(3)]
    Pacc = [ar.alloc([512]) for _ in range(2)]; r_Pacc = [self.R(f"naPacc{i}") for i in range(2)]
    recb = ar.alloc([512]); r_recb = self.R("narecb")
    Pl = [ar.alloc([128], BF16) for _ in range(6)]; r_Pl = [self.R(f"naPl{i}") for i in range(6)]
    cnt = {"h": 0, "o": 0, "f": 0, "pl": 0, "pr": 0}
    for sq_ in range(4):
        t0 = sq_ * 256
        for h in range(8):
            hi = cnt["h"] % 2; cnt["h"] += 1
            k.dma('sp', QT[hi][:, 0:256], self.QB[h, :, t0:t0 + 256], reads=[self.r_E1], writes=[r_QT[hi]])
            k.dma('sp', KT[hi][:, 0:256], self.KB[h, :, t0:t0 + 256], reads=[self.r_E1], writes=[r_KT[hi]])
            k.dma('sp', vaug[hi][:, 0:2, 0:128], self.VB[t0:t0 + 256, h * 128:(h + 1) * 128].rearrange("(c p) v -> p c v", p=128), reads=[self.r_E1], writes=[r_vaug[hi]])
            pci = hi
            for kc in range(2):
                pb = kc
                k.op('pe', I("matmul", self.ps[:, pb, 0:256], lhsT=KT[hi][:, kc * 128:(kc + 1) * 128], rhs=QT[hi][:, 0:256], start=True, stop=True),
                     reads=[r_KT[hi], r_QT[hi]], writes=[self.psr[pb]])
                k.op('act', I("activation", out=Pc[pci][:, kc, 0:256], in_=self.ps[:, pb, 0:256], func=AF.Exp), reads=[self.psr[pb]], writes=[r_Pc[pci]])
            oi = cnt["o"] % 2; cnt["o"] += 1
            for qt in range(2):
                fi = cnt["f"] % 2; cnt["f"] += 1
                pv = 4 + fi
                po = self.ps[:, pv, 0:129]
                for kc in range(2):
                    k.op('pe', I("matmul", po, lhsT=Pc[pci][:, kc, qt * 128:(qt + 1) * 128], rhs=vaug[hi][:, kc, :], start=(kc == 0), stop=(kc == 1)),
                         reads=[r_Pc[pci], r_vaug[hi]], writes=[self.psr[pv]])
                pT = self.psb16(6)[:, fi * 128:(fi + 1) * 128]
                _attn_finish(self, po, self.psr[pv], rec[fi], r_rec[fi], ob16[fi], r_ob16[fi], pT, self.psr[6], ostg[oi][:, qt * 128:(qt + 1) * 128], r_ostg[oi])
            self.store(self.OA[8 + h, :, t0:t0 + 256], ostg[oi][:, 0:256], r_ostg[oi], reads=[r_ostg[oi]], writes=[self.r_OA], is_output=bool(self.cfg.get("dump_oa")))
    colm = ar.alloc([64]); r_colm = self.R("colm", sem=True)
    k.dma('sp', colm, self.din["colmask"], writes=[r_colm])
    Traw = ar.alloc([14, 64]); r_Traw = self.R("Traw", sem=True)
    Cexp = ar.alloc([14, 64], BF16); r_Cexp = self.R("Cexp")
    def ws(r): return min(max(r - 4, 0), 24)
    types = {}
    plan = []
    for j in range(16):
        lo = ws(2 * j) // 2
        hi_ = (ws(2 * j + 1) + 7) // 2
        lst = []
        for kc in range(lo, hi_ + 1):
            dl = 2 * kc - 2 * j
            valid = tuple(tuple(ws(2 * j + b) <= 2 * kc + a <= ws(2 * j + b) + 7 for b in range(2)) for a in range(2))
            key = (dl, valid)
            if key not in types:
                types[key] = len(types)
            lst.append((kc, types[key]))
        plan.append(lst)
    ntyp = len(types)
    EBt = ar.alloc([ntyp, 128], BF16); r_EBt = self.R("EBt")
    Kc32 = ar.alloc([4, 128]); r_Kc32 = self.R("Kc32", sem=True)
    Kc16 = ar.alloc([4, 128], BF16); r_Kc16 = self.R("Kc16")
    KcT = ar.alloc([512], BF16); r_KcT = self.R("KcT")
    Vc32 = ar.alloc([4, 128]); r_Vc32 = self.R("Vc32", sem=True)
    vaugc = ar.alloc([4, 129], BF16); r_vaugc = self.R("vaugc")
    k.op('pool', I("memset", vaugc[:, :, 128:129], 1.0), writes=[r_vaugc])
    rpbr = self.din["rpbr"]
    T0 = 1024
    for h in range(8):
        hi = cnt["h"] % 2; cnt["h"] += 1
        k.dma('sp', QT[hi], self.QB[h, :, T0:T0 + 2048], reads=[self.r_E1], writes=[r_QT[hi]])
        k.dma('sp', KT[hi], self.KB[h, :, T0:T0 + 2048], reads=[self.r_E1], writes=[r_KT[hi]])
        k.dma('sp', vaug[hi][:, :, 0:128], self.VB[T0:T0 + 2048, h * 128:(h + 1) * 128].rearrange("(c p) v -> p c v", p=128), reads=[self.r_E1], writes=[r_vaug[hi]])
        k.dma('sp', Kc32, self.din["cnak"][:, h, :].rearrange("(c p) d -> p c d", p=128), writes=[r_Kc32])
        k.dma('sp', Vc32, self.din["cnav"][:, h, :].rearrange("(c p) d -> p c d", p=128), writes=[r_Vc32])
        k.op('pool', I("tensor_copy", out=Kc16, in_=Kc32), reads=[r_Kc32], writes=[r_Kc16])
        k.op('pool', I("tensor_copy", out=vaugc[:, :, 0:128], in_=Vc32), reads=[r_Vc32], writes=[r_vaugc])
        for c in range(4):
            pT = self.psb16(6)[:, c * 128:(c + 1) * 128]
            k.op('pe', I("transpose", out=pT, in_=Kc16[:, c, :], identity=self.ident_b), reads=[r_Kc16, self.r_ident_b], writes=[self.psr[6]])
        k.op('act', I("copy", out=KcT, in_=self.psb16(6)[:, 0:512]), reads=[self.psr[6]], writes=[r_KcT])
        for half in range(2):
            src = bass.AP(tensor=rpbr.tensor, offset=h * 15 * 8192 + half * 8192 + 63, ap=[[127, 64], [8192, 14], [1, 64]])
            k.dma('sp', Traw[half * 64:(half + 1) * 64], src, writes=[r_Traw])
        k.op('act', I("activation", out=Traw, in_=Traw, func=AF.Exp), reads=[r_Traw], writes=[r_Traw])
        for i in range(14):
            k.op('dve', I("tensor_tensor", out=Cexp[:, i, :], in0=Traw[:, i, :], in1=colm, op=ALU.mult), reads=[r_Traw, r_colm], writes=[r_Cexp])
        for (dl, valid), ti in types.items():
            for b in range(2):
                k.op('pool', I("tensor_copy", out=EBt[:, ti, b * 64:(b + 1) * 64], in_=Cexp[:, dl - b + 7, :]), reads=[r_Cexp], writes=[r_EBt])
            for a in range(2):
                for b in range(2):
                    if not valid[a][b]:
                        k.op('pool', I("memset", EBt[a * 64:(a + 1) * 64, ti, b * 64:(b + 1) * 64], 0.0), reads=[r_EBt], writes=[r_EBt])
        for jg in range(4):
            pci = cnt["o"] % 2; cnt["o"] += 1
            po = self.ps[:, 4 + pci, :]
            r_po = self.psr[4 + pci]
            qs = slice(jg * 512, (jg + 1) * 512)
            for c in range(4):
                pb = c % 2
                k.op('pe', I("matmul", self.ps[:, pb, :], lhsT=KcT[:, c * 128:(c + 1) * 128], rhs=QT[hi][:, qs], start=True, stop=True),
                     reads=[r_KcT, r_QT[hi]], writes=[self.psr[pb]])
                k.op('act', I("activation", out=Pc[pci][:, c, :], in_=self.ps[:, pb, :], func=AF.Exp), reads=[self.psr[pb]], writes=[r_Pc[pci]])
                k.op('pe', I("matmul", po, lhsT=vaugc[:, c, 0:128], rhs=Pc[pci][:, c, :], start=(c == 0), stop=False), reads=[r_Pc[pci], r_vaugc], writes=[r_po])
                if c == 1:
                    k.op('dve', I("tensor_tensor", out=Pacc[pci], in0=Pc[pci][:, 0, :], in1=Pc[pci][:, 1, :], op=ALU.add), reads=[r_Pc[pci]], writes=[r_Pacc[pci]])
                elif c > 1:
                    k.op('dve', I("tensor_tensor", out=Pacc[pci], in0=Pacc[pci], in1=Pc[pci][:, c, :], op=ALU.add), reads=[r_Pc[pci], r_Pacc[pci]], writes=[r_Pacc[pci]])
            lat = []
            for jj in range(4):
                j = jg * 4 + jj
                for (kc, ti) in plan[j]:
                    lat.append((jj, j, kc, ti))
            NL = len(lat)
            sbk = []; pls_ = []; prs = []
            for n_ in range(NL):
                sbk.append((2, 3, 7)[cnt["pr"] % 3]); prs.append(cnt["pr"] % 3); cnt["pr"] += 1
                pls_.append(cnt["pl"] % 6); cnt["pl"] += 1

            def S_lat(n_):
                jj, j, kc, ti = lat[n_]
                pb = sbk[n_]
                k.op('pe', I("matmul", self.ps[:, pb, 0:128], lhsT=KT[hi][:, kc * 128:(kc + 1) * 128], rhs=QT[hi][:, j * 128:(j + 1) * 128], start=True, stop=True),
                     reads=[r_KT[hi], r_QT[hi]], writes=[self.psr[pb]])

            S_lat(0)
            if NL > 1:
                S_lat(1)
            for n_ in range(NL):
                jj, j, kc, ti = lat[n_]
                if n_ + 2 < NL:
                    S_lat(n_ + 2)
                pb, pr, pl = sbk[n_], prs[n_], pls_[n_]
                k.op('act', I("activation", out=Praw[pr], in_=self.ps[:, pb, 0:128], func=AF.Exp), reads=[self.psr[pb]], writes=[r_Praw[pr]])
                k.op('dve', I("tensor_tensor", out=Pl[pl], in0=Praw[pr], in1=EBt[:, ti, :], op=ALU.mult), reads=[r_Praw[pr], r_EBt], writes=[r_Pl[pl]])
                k.op('pe', I("matmul", po[:, jj * 128:(jj + 1) * 128], lhsT=vaug[hi][:, kc, 0:128], rhs=Pl[pl], start=False, stop=(n_ == NL - 1)),
                     reads=[r_Pl[pl], r_vaug[hi]], writes=[r_po])
                k.op('dve', I("tensor_tensor", out=Pacc[pci][:, jj * 128:(jj + 1) * 128], in0=Pacc[pci][:, jj * 128:(jj + 1) * 128], in1=Pl[pl], op=ALU.add), reads=[r_Pacc[pci], r_Pl[pl]], writes=[r_Pacc[pci]])
            pden = self.ps[:, 6, :]
            k.op('pe', I("matmul", pden, lhsT=self.ones_f, rhs=Pacc[pci], start=True, stop=True), reads=[self.r_ones_f, r_Pacc[pci]], writes=[self.psr[6]])
            k.op('dve', I("reciprocal", out=recb, in_=pden), reads=[self.psr[6]], writes=[r_recb])
            k.op('dve', I("tensor_tensor", out=ostg[pci], in0=po, in1=recb, op=ALU.mult), reads=[r_po, r_recb], writes=[r_ostg[pci]])
            self.store(self.OA[8 + h, :, T0 + jg * 512:T0 + (jg + 1) * 512], ostg[pci], r_ostg[pci], reads=[r_ostg[pci]], writes=[self.r_OA], is_output=bool(self.cfg.get("dump_oa")))
    k.barrier()
    ar.release(m0)


Builder.na_phase = na_phase


def outproj_phase(self, l, OA, r_OA, w_out, Xin, r_Xin, Xout, r_Xout):
    k, ar = self.k, self.ar
    self.phase()
    m0 = ar.mark()
    G, r_G = self.load_grep(l, 0)
    W = self.work_bufs()
    wo = ar.alloc([16, 2048], BF16); r_wo = self.R("wo", sw=True)
    for n in range(4):
        k.dma('pool', wo[:, :, n * 512:(n + 1) * 512], w_out[:, n * 512:(n + 1) * 512].rearrange("(k p) n -> p k n", p=128), writes=[r_wo])
    ob = [ar.alloc([16, 512], BF16) for _ in range(2)]; r_ob = [self.R(f"opo{i}", sem=True) for i in range(2)]
    xt = [ar.alloc([D_MODEL]) for _ in range(2)]; r_xt = [self.R(f"opx{i}", sem=True) for i in range(2)]
    xi = 0
    for blk in range(6):
        g = 0 if blk < 2 else 1
        bi = blk % 2
        k.dma('sp', ob[bi], OA[:, :, blk * 512:(blk + 1) * 512].rearrange("c p t -> p c t"), reads=[r_OA], writes=[r_ob[bi]])
        for sub in range(4):
            tile = blk * 4 + sub
            pb0 = (tile % 2) * 4
            for n in range(4):
                for kk in range(16):
                    k.op('pe', I("matmul", self.psb(pb0 + n), lhsT=ob[bi][:, kk, sub * 128:(sub + 1) * 128], rhs=wo[:, kk, n * 512:(n + 1) * 512], start=(kk == 0), stop=(kk == 15)),
                         reads=[r_ob[bi], r_wo], writes=[self.psr[pb0 + n]])
            b = xi % 2; xi += 1
            k.dma('sp', xt[b], Xin[tile * 128:(tile + 1) * 128, :], reads=[r_Xin[tile]], writes=[r_xt[b]])
            yp = [(self.psb(pb0 + n), self.psr[pb0 + n], 512) for n in range(4)]
            self.post_tile(xt[b], r_xt[b], yp, G[g], r_G[g], W)
            self.store(Xout[tile * 128:(tile + 1) * 128, :], xt[b], r_xt[b], reads=[r_xt[b]], writes=[r_Xout[tile]])
    k.barrier()
    ar.release(m0)


Builder.outproj_phase = outproj_phase


NTK = NT + 512
MLA_SCALE = 192.0 ** -0.5


def _mla_scratch(self):
    if hasattr(self, "CQT"):
        return
    s = self.scr
    self.CQT = s("CQT", [4, 128, NT], BF16)
    self.CKVT = s("CKVT", [4, 128, NTK], BF16)
    self.KPET = s("KPET", [64, NTK], BF16)
    self.KROT = s("KROT", [64, 2048], BF16)
    self.QN = s("QN", [16, 128, NT], BF16)
    self.QPE = s("QPE", [16, 64, NT], BF16)
    self.QROT = s("QROT", [16, 64, 2048], BF16)
    self.KN = s("KN", [16, 128, NTK], BF16)
    self.VM = s("VM", [NTK, 16, 128], BF16)
    self.OA2 = s("OA2", [16, 128, NT], BF16) if not self.cfg.get("dump_oa2") else self.out("OA2", [16, 128, NT], BF16)
    self.r_O1 = self.R("O1out"); self.r_O2 = self.R("O2out"); self.r_OA2 = self.R("OA2res")


def _rmsnorm_free(self, src_ps, r_src, n, gq, r_gq, out32, out16, r_out, W, st, r_st):
    k = self.k
    k.op('dve', I("memset", st[:, 0:1], 0.0), writes=[r_st])
    k.op('act', I("activation", out=W["junk"][:, 0:n], in_=src_ps, func=AF.Square, accum_out=st[:, 0:1]), reads=[r_src, r_st], writes=[W["r_junk"], r_st])
    k.op('act', I("activation", out=st[:, 1:2], in_=st[:, 0:1], func=AF.Ln, scale=1.0 / n, bias=EPS), reads=[r_st], writes=[r_st])
    k.op('act', I("activation", out=st[:, 2:3], in_=st[:, 1:2], func=AF.Exp, scale=-0.5), reads=[r_st], writes=[r_st])
    if out32 is not None:
        k.op('dve', I("scalar_tensor_tensor", out=out32, in0=src_ps, scalar=st[:, 2:3], in1=gq, op0=ALU.mult, op1=ALU.mult), reads=[r_src, r_st, r_gq], writes=[r_out])
        k.op('pool', I("tensor_copy", out=out16, in_=out32), reads=[r_out], writes=[r_out])
    else:
        k.op('dve', I("scalar_tensor_tensor", out=out16, in0=src_ps, scalar=st[:, 2:3], in1=gq, op0=ALU.mult, op1=ALU.mult), reads=[r_src, r_st, r_gq], writes=[r_out])


def mla_inproj(self, Xin, r_Xin):
    k, ar = self.k, self.ar
    self.phase()
    _mla_scratch(self)
    m0 = ar.mark()
    l = 1
    W = self.work_bufs()
    w_in = self.din["w_in_odd"]
    wb = ar.alloc([16, 1088], BF16); r_wb = self.R("o1w", sw=True)
    for (c0, c1) in ((0, 512), (512, 1024), (1024, 1088)):
        k.dma('pool', wb[:, :, c0:c1], w_in[:, c0:c1].rearrange("(k p) n -> p k n", p=128), writes=[r_wb])
    gq = ar.alloc([512]); r_gq = self.R("gq", sem=True)
    gkv = ar.alloc([512]); r_gkv = self.R("gkv", sem=True)
    k.dma('sp', gq, self.din["mla_qg"].to_broadcast([128, 512]), writes=[r_gq])
    k.dma('sp', gkv, self.din["mla_kvg"].to_broadcast([128, 512]), writes=[r_gkv])
    ropeP32 = ar.alloc([64], parts=64); r_ropeP32 = self.R("ropeP32", sem=True)
    ropeP = ar.alloc([64], BF16, parts=64); r_ropeP = self.R("ropeP")
    k.dma('sp', ropeP32, self.din["ropeP"], writes=[r_ropeP32])
    k.op('dve', I("tensor_copy", out=ropeP, in_=ropeP32), reads=[r_ropeP32], writes=[r_ropeP])
    cs = ar.alloc([2, 128], parts=64)
    r_cs = self.R("ropecs", sem=True)
    xt = [ar.alloc([D_MODEL]) for _ in range(2)]; r_xt = [self.R(f"o1x{i}", sem=True) for i in range(2)]
    hT = [ar.alloc([16, 128], BF16) for _ in range(2)]; r_hT = [self.R(f"o1h{i}") for i in range(2)]
    st = ar.alloc([8]); r_st = self.R("o1st")
    cq16 = [ar.alloc([512], BF16) for _ in range(2)]; r_cq16 = [self.R(f"cq16{i}") for i in range(2)]
    kv32 = [ar.alloc([512]) for _ in range(2)]; r_kv32 = [self.R(f"kv32{i}", sem=True) for i in range(2)]
    kv16 = [ar.alloc([512], BF16) for _ in range(2)]
    kp32 = [ar.alloc([64]) for _ in range(2)]; r_kp32 = [self.R(f"kp32{i}", sem=True) for i in range(2)]
    kp16 = [ar.alloc([64], BF16) for _ in range(2)]
    tq = [ar.alloc([4, 128], BF16) for _ in range(2)]; r_tq = [self.R(f"tq{i}", sem=True) for i in range(2)]
    tkv = [ar.alloc([4, 128], BF16) for _ in range(2)]; r_tkv = [self.R(f"tkv{i}", sem=True) for i in range(2)]
    tkp = [ar.alloc([128], BF16, parts=64) for _ in range(2)]; r_tkp = [self.R(f"tkp{i}", sem=True) for i in range(2)]
    trot = [ar.alloc([128], BF16, parts=64) for _ in range(2)]; r_trot = [self.R(f"trot{i}", sem=True) for i in range(2)]
    t1 = ar.alloc([128], parts=64); r_t1 = self.R("ropet1")
    t2 = ar.alloc([128], parts=64); r_t2 = self.R("ropet2")
    AB, r_AB = self.AB[l][0], self.r_AB[l][0]
    nckv, nkpe = self.dout["nckv"], self.dout["nkpe"]
    for tile in range(28):
        b = tile % 2
        own = tile < 24
        if own:
            g = 0 if tile < 8 else 1
            k.dma('sp', xt[b], Xin[tile * 128:(tile + 1) * 128, :], reads=[r_Xin[tile]], writes=[r_xt[b]])
            self.pre_tile(xt[b], r_xt[b], hT[b], r_hT[b], 0, AB, r_AB, g, W)
            for n, (c0, c1) in enumerate(((0, 512), (512, 1024), (1024, 1088))):
                for kk in range(16):
                    k.op('pe', I("matmul", self.ps[:, n, 0:c1 - c0], lhsT=hT[b][:, kk, :], rhs=wb[:, kk, c0:c1], start=(kk == 0), stop=(kk == 15)),
                         reads=[r_hT[b], r_wb], writes=[self.psr[n]])
            _rmsnorm_free(self, self.ps[:, 0, :], self.psr[0], 512, gq, r_gq, None, cq16[b], r_cq16[b], W, st, r_st)
            _rmsnorm_free(self, self.ps[:, 1, :], self.psr[1], 512, gkv, r_gkv, kv32[b], kv16[b], r_kv32[b], W, st, r_st)
            k.op('act', I("copy", out=kp32[b], in_=self.ps[:, 2, 0:64]), reads=[self.psr[2]], writes=[r_kp32[b]])
            k.op('pool', I("tensor_copy", out=kp16[b], in_=kp32[b]), reads=[r_kp32[b]], writes=[r_kp32[b]])
            if tile < 8:
                self.store(nckv[tile * 128:(tile + 1) * 128, :], kv32[b], r_kv32[b], reads=[r_kv32[b]], writes=[], is_output=True)
                self.store(nkpe[tile * 128:(tile + 1) * 128, :], kp32[b], r_kp32[b], reads=[r_kp32[b]], writes=[], is_output=True)
        else:
            ct = tile - 24
            k.dma('sp', kv32[b], self.din["cckv"][ct * 128:(ct + 1) * 128, :], writes=[r_kv32[b]])
            k.dma('sp', kp32[b], self.din["ckpe"][ct * 128:(ct + 1) * 128, :], writes=[r_kp32[b]])
            k.op('pool', I("tensor_copy", out=kv16[b], in_=kv32[b]), reads=[r_kv32[b]], writes=[r_kv32[b]])
            k.op('pool', I("tensor_copy", out=kp16[b], in_=kp32[b]), reads=[r_kp32[b]], writes=[r_kp32[b]])
        if own:
            p3 = self.psb16(3)
            for c in range(4):
                k.op('pe', I("transpose", out=p3[:, c * 128:(c + 1) * 128], in_=cq16[b][:, c * 128:(c + 1) * 128], identity=self.ident_b), reads=[r_cq16[b], self.r_ident_b], writes=[self.psr[3]])
            k.op('act', I("copy", out=tq[b].rearrange("p a b -> p (a b)"), in_=p3[:, 0:512]), reads=[self.psr[3]], writes=[r_tq[b]])
            self.store(self.CQT[:, :, tile * 128:(tile + 1) * 128].rearrange("c p t -> p c t"), tq[b], r_tq[b], reads=[r_tq[b]], writes=[self.r_O1])
        p4 = self.psb16(4)
        for c in range(4):
            k.op('pe', I("transpose", out=p4[:, c * 128:(c + 1) * 128], in_=kv16[b][:, c * 128:(c + 1) * 128], identity=self.ident_b), reads=[r_kv32[b], self.r_ident_b], writes=[self.psr[4]])
        k.op('act', I("copy", out=tkv[b].rearrange("p a b -> p (a b)"), in_=p4[:, 0:512]), reads=[self.psr[4]], writes=[r_tkv[b]])
        self.store(self.CKVT[:, :, tile * 128:(tile + 1) * 128].rearrange("c p t -> p c t"), tkv[b], r_tkv[b], reads=[r_tkv[b]], writes=[self.r_O1])
        p5 = self.psb16(5)
        k.op('pe', I("transpose", out=p5[0:64, 0:128], in_=kp16[b], identity=self.ident_b), reads=[r_kp32[b], self.r_ident_b], writes=[self.psr[5]])
        k.op('act', I("copy", out=tkp[b], in_=p5[0:64, 0:128]), reads=[self.psr[5]], writes=[r_tkp[b]])
        self.store(self.KPET[:, tile * 128:(tile + 1) * 128], tkp[b], r_tkp[b], reads=[r_tkp[b]], writes=[self.r_O1])
        if own and tile >= 8:
            lt = tile - 8
            k.dma('sp', cs, self.din["ropecs"][:, :, lt * 128:(lt + 1) * 128], writes=[r_cs])
            k.op('pe', I("matmul", self.ps[0:64, 6, 0:128], lhsT=ropeP, rhs=tkp[b], start=True, stop=True), reads=[r_ropeP, r_tkp[b]], writes=[self.psr[6]])
            k.op('dve', I("tensor_tensor", out=t1, in0=self.ps[0:64, 6, 0:128], in1=cs[:, 1, :], op=ALU.mult), reads=[self.psr[6], r_cs], writes=[r_t1])
            k.op('pool', I("tensor_tensor", out=t2, in0=tkp[b], in1=cs[:, 0, :], op=ALU.mult), reads=[r_tkp[b], r_cs], writes=[r_t2])
            k.op('pool', I("tensor_tensor", out=trot[b], in0=t1, in1=t2, op=ALU.add), reads=[r_t1, r_t2], writes=[r_trot[b]])
            self.store(self.KROT[:, lt * 128:(lt + 1) * 128], trot[b], r_trot[b], reads=[r_trot[b]], writes=[self.r_O1])
    k.barrier()
    ar.release(m0)


def mla_proj(self):
    k, ar = self.k, self.ar
    self.phase()
    _mla_scratch(self)
    m0 = ar.mark()
    wq = ar.alloc([4, 3072], BF16); r_wq = self.R("wq", sw=True)
    wkv = ar.alloc([4, 4096], BF16); r_wkv = self.R("wkv", sw=True)
    for n in range(6):
        k.dma('pool', wq[:, :, n * 512:(n + 1) * 512], self.din["w_uq"][:, n * 512:(n + 1) * 512].rearrange("(k p) n -> p k n", p=128), writes=[r_wq])
    for n in range(8):
        k.dma('pool', wkv[:, :, n * 512:(n + 1) * 512], self.din["w_ukv"][:, n * 512:(n + 1) * 512].rearrange("(k p) n -> p k n", p=128), writes=[r_wkv])
    cqT = ar.alloc([4, NT], BF16); r_cqT = self.R("cqT", sem=True)
    ckvT = ar.alloc([4, NTK], BF16); r_ckvT = self.R("ckvT", sem=True)
    k.dma('sp', cqT, self.CQT.rearrange("c p t -> p c t"), reads=[self.r_O1], writes=[r_cqT])
    k.dma('sp', ckvT, self.CKVT.rearrange("c p t -> p c t"), reads=[self.r_O1], writes=[r_ckvT])
    ropeP32 = ar.alloc([64], parts=64); r_ropeP32 = self.R("ropeP32b", sem=True)
    ropeP = ar.alloc([64], BF16, parts=64); r_ropeP = self.R("ropePb")
    k.dma('sp', ropeP32, self.din["ropeP"], writes=[r_ropeP32])
    k.op('dve', I("tensor_copy", out=ropeP, in_=ropeP32), reads=[r_ropeP32], writes=[r_ropeP])
    cs = ar.alloc([2, 2048], parts=64); r_cs = self.R("ropecs2", sem=True)
    k.dma('sp', cs, self.din["ropecs"], writes=[r_cs])
    NST = 4
    stg = [ar.alloc([512], BF16) for _ in range(NST)]; r_stg = [self.R(f"o2s{i}", sem=True) for i in range(NST)]
    t1 = ar.alloc([512], parts=64); r_t1 = self.R("o2t1")
    t2 = ar.alloc([512], parts=64); r_t2 = self.R("o2t2")
    si = 0; pbi = 0
    wqv = wq.rearrange("p k (h d) -> p k h d", h=16)
    wkvv = wkv.rearrange("p k (h d) -> p k h d", h=16)
    for h in range(16):
        for tb in range(6):
            ts = slice(tb * 512, (tb + 1) * 512)
            pb = pbi % 3; pbi += 1
            for kk in range(4):
                k.op('pe', I("matmul", self.psb(pb), lhsT=wqv[:, kk, h, 0:128], rhs=cqT[:, kk, ts], start=(kk == 0), stop=(kk == 3)), reads=[r_wq, r_cqT], writes=[self.psr[pb]])
            s_ = si % NST; si += 1
            k.op('act', I("activation", out=stg[s_], in_=self.psb(pb), func=AF.Copy, scale=MLA_SCALE), reads=[self.psr[pb]], writes=[r_stg[s_]])
            k.dma('sp', self.QN[h, :, ts], stg[s_], reads=[r_stg[s_]], writes=[self.r_O2])
            pb = pbi % 3; pbi += 1
            for kk in range(4):
                k.op('pe', I("matmul", self.ps[0:64, pb, :], lhsT=wqv[:, kk, h, 128:192], rhs=cqT[:, kk, ts], start=(kk == 0), stop=(kk == 3)), reads=[r_wq, r_cqT], writes=[self.psr[pb]])
            s_ = si % NST; si += 1
            qpe = stg[s_][0:64, :]
            k.op('act', I("activation", out=qpe, in_=self.ps[0:64, pb, :], func=AF.Copy, scale=MLA_SCALE), reads=[self.psr[pb]], writes=[r_stg[s_]])
            k.dma('sp', self.QPE[h, :, ts], qpe, reads=[r_stg[s_]], writes=[self.r_O2])
            if tb >= 2:
                lt = tb - 2
                ls = slice(lt * 512, (lt + 1) * 512)
                k.op('pe', I("matmul", self.ps[0:64, 6, :], lhsT=ropeP, rhs=qpe, start=True, stop=True), reads=[r_ropeP, r_stg[s_]], writes=[self.psr[6]])
                k.op('dve', I("tensor_tensor", out=t1, in0=self.ps[0:64, 6, :], in1=cs[:, 1, ls], op=ALU.mult), reads=[self.psr[6], r_cs], writes=[r_t1])
                k.op('pool', I("tensor_tensor", out=t2, in0=qpe, in1=cs[:, 0, ls], op=ALU.mult), reads=[r_stg[s_], r_cs], writes=[r_t2])
                s2 = si % NST; si += 1
                qrot = stg[s2][0:64, :]
                k.op('pool', I("tensor_tensor", out=qrot, in0=t1, in1=t2, op=ALU.add), reads=[r_t1, r_t2], writes=[r_stg[s2]])
                k.dma('sp', self.QROT[h, :, ls], qrot, reads=[r_stg[s2]], writes=[self.r_O2])
        for tb in range(7):
            ts = slice(tb * 512, (tb + 1) * 512)
            pb = pbi % 3; pbi += 1
            for kk in range(4):
                k.op('pe', I("matmul", self.psb(pb), lhsT=wkvv[:, kk, h, 0:128], rhs=ckvT[:, kk, ts], start=(kk == 0), stop=(kk == 3)), reads=[r_wkv, r_ckvT], writes=[self.psr[pb]])
            s_ = si % NST; si += 1
            k.op('act', I("copy", out=stg[s_], in_=self.psb(pb)), reads=[self.psr[pb]], writes=[r_stg[s_]])
            k.dma('sp', self.KN[h, :, ts], stg[s_], reads=[r_stg[s_]], writes=[self.r_O2])
    for t in range(28):
        for hg in range(4):
            pb = 3 + (pbi % 2); pbi += 1
            for kk in range(4):
                k.op('pe', I("matmul", self.psb(pb).rearrange("p (h d) -> p h d", h=4), lhsT=ckvT[:, kk, t * 128:(t + 1) * 128], rhs=wkvv[:, kk, hg * 4:(hg + 1) * 4, 128:256], start=(kk == 0), stop=(kk == 3)),
                     reads=[r_wkv, r_ckvT], writes=[self.psr[pb]])
            s_ = si % NST; si += 1
            k.op('act', I("copy", out=stg[s_], in_=self.psb(pb)), reads=[self.psr[pb]], writes=[r_stg[s_]])
            k.dma('sp', self.VM[t * 128:(t + 1) * 128, hg * 4:(hg + 1) * 4, :], stg[s_].rearrange("p (h d) -> p h d", h=4), reads=[r_stg[s_]], writes=[self.r_O2])
    k.barrier()
    ar.release(m0)


def mla_attn(self):
    k, ar = self.k, self.ar
    self.phase()
    _mla_scratch(self)
    m0 = ar.mark()
    kpeT_f = ar.alloc([NTK], BF16); r_kpeT = self.R("kpeTall", sem=True)
    krot_f = ar.alloc([2048], BF16); r_krot = self.R("krotall", sem=True)
    k.op('dve', I("memset", kpeT_f[64:128], 0.0), writes=[r_kpeT])
    k.op('dve', I("memset", krot_f[64:128], 0.0), writes=[r_krot])
    k.dma('sp', kpeT_f[0:64], self.KPET, reads=[self.r_O1], writes=[r_kpeT])
    k.dma('sp', krot_f[0:64], self.KROT, reads=[self.r_O1], writes=[r_krot])
    kpeT, krot = kpeT_f, krot_f
    QN = [ar.alloc([NT], BF16) for _ in range(2)]; r_QN = [self.R(f"aQN{i}", sem=True) for i in range(2)]
    QP = [ar.alloc([NT], BF16) for _ in range(2)]; r_QP = [self.R(f"aQP{i}", sem=True) for i in range(2)]
    QR = [ar.alloc([2048], BF16) for _ in range(2)]; r_QR = [self.R(f"aQR{i}", sem=True) for i in range(2)]
    for i in range(2):
        k.op('dve', I("memset", QP[i][64:128], 0.0), writes=[r_QP[i]])
        k.op('dve', I("memset", QR[i][64:128], 0.0), writes=[r_QR[i]])
    KN = [ar.alloc([NTK], BF16) for _ in range(2)]; r_KN = [self.R(f"aKN{i}", sem=True) for i in range(2)]
    va = [ar.alloc([28, 129], BF16) for _ in range(2)]; r_va = [self.R(f"aV{i}", sem=True) for i in range(2)]
    for i in range(2):
        k.op('pool', I("memset", va[i][:, :, 128:129], 1.0), writes=[r_va[i]])
    PT = [ar.alloc([512], BF16) for _ in range(5)]; r_PT = [self.R(f"aPT{i}") for i in range(5)]
    rec = [ar.alloc([1]) for _ in range(2)]; r_rec = [self.R(f"arec{i}") for i in range(2)]
    ob16 = [ar.alloc([128], BF16) for _ in range(2)]; r_ob16 = [self.R(f"aob{i}") for i in range(2)]
    ostg = [ar.alloc([512], BF16) for _ in range(2)]; r_ostg = [self.R(f"aost{i}", sem=True) for i in range(2)]
    Pacc = [ar.alloc([512]) for _ in range(2)]; r_Pacc = [self.R(f"aPacc{i}") for i in range(2)]
    recb = [ar.alloc([512]) for _ in range(2)]; r_recb = [self.R(f"arecb{i}") for i in range(2)]
    cnt = {"p": 0, "s": 0, "o": 0, "f": 0}
    def load_head(h):
        hi = h % 2
        k.dma('sp', QN[hi], self.QN[h], reads=[self.r_O2], writes=[r_QN[hi]])
        k.dma('sp', QP[hi][0:64], self.QPE[h], reads=[self.r_O2], writes=[r_QP[hi]])
        k.dma('sp', QR[hi][0:64], self.QROT[h], reads=[self.r_O2], writes=[r_QR[hi]])
        k.dma('sp', KN[hi], self.KN[h], reads=[self.r_O2], writes=[r_KN[hi]])
        k.dma('sp', va[hi][:, :, 0:128], self.VM[:, h, :].rearrange("(c p) d -> p c d", p=128), reads=[self.r_O2], writes=[r_va[hi]])

    flat = []
    jobinfo = {}
    jid = 0
    for h in range(16):
        hi = h % 2
        jobs = []
        for sq_ in range(4):
            t0 = sq_ * 256
            keys = [(t0 // 128 + c, kpeT[:, t0 + c * 128:t0 + (c + 1) * 128], QP[hi][:, t0:t0 + 256]) for c in range(2)]
            jobs.append((t0, 256, keys))
        for qb in range(4):
            q0 = 1024 + qb * 512
            lq = slice(qb * 512, (qb + 1) * 512)
            keys = [(8 + c, krot[:, c * 128:(c + 1) * 128], QR[hi][:, lq]) for c in range(16)]
            keys += [(24 + c, kpeT[:, NT + c * 128:NT + (c + 1) * 128], QP[hi][:, q0:q0 + 512]) for c in range(4)]
            jobs.append((q0, 512, keys))
        for (q0, nq, keys) in jobs:
            for n_, (kt, kr, qr) in enumerate(keys):
                flat.append((h, jid, q0, nq, n_, len(keys), kt, kr, qr))
            jid += 1
    NF = len(flat)
    sbs = [(0, 1, 7)[i % 3] for i in range(NF)]
    pis = [i % 5 for i in range(NF)]
    loaded = set()

    def S_ops(i):
        (h, jid_, q0, nq, n_, NK, kt, kr, qr) = flat[i]
        hi = h % 2
        if h not in loaded:
            load_head(h); loaded.add(h)
        sb_ = sbs[i]
        k.op('pe', I("matmul", self.ps[:, sb_, 0:nq], lhsT=KN[hi][:, kt * 128:(kt + 1) * 128], rhs=QN[hi][:, q0:q0 + nq], start=True, stop=False),
             reads=[r_KN[hi], r_QN[hi]], writes=[self.psr[sb_]])
        k.op('pe', I("matmul", self.ps[:, sb_, 0:nq], lhsT=kr, rhs=qr, start=False, stop=True),
             reads=[r_kpeT, r_krot, r_QP[hi], r_QR[hi]], writes=[self.psr[sb_]])

    load_head(0); loaded.add(0)
    S_ops(0); S_ops(1)
    for i in range(NF):
        (h, jid_, q0, nq, n_, NK, kt, kr, qr) = flat[i]
        hi = h % 2
        ji = jid_ % 2
        po = self.ps[:, 2 + ji, 0:nq]
        r_po = self.psr[2 + ji]
        if n_ == 0 and h + 1 < 16 and (h + 1) not in loaded and q0 == 256:
            load_head(h + 1); loaded.add(h + 1)
        if i + 2 < NF:
            S_ops(i + 2)
        sb_, pi = sbs[i], pis[i]
        k.op('act', I("activation", out=PT[pi][:, 0:nq], in_=self.ps[:, sb_, 0:nq], func=AF.Exp), reads=[self.psr[sb_]], writes=[r_PT[pi]])
        k.op('pe', I("matmul", po, lhsT=va[hi][:, kt, 0:128], rhs=PT[pi][:, 0:nq], start=(n_ == 0), stop=(n_ == NK - 1)),
             reads=[r_PT[pi], r_va[hi]], writes=[r_po])
        if n_ == 0:
            k.op('dve', I("tensor_copy", out=Pacc[ji][:, 0:nq], in_=PT[pi][:, 0:nq]), reads=[r_PT[pi]], writes=[r_Pacc[ji]])
        else:
            k.op('dve', I("tensor_tensor", out=Pacc[ji][:, 0:nq], in0=Pacc[ji][:, 0:nq], in1=PT[pi][:, 0:nq], op=ALU.add), reads=[r_Pacc[ji], r_PT[pi]], writes=[r_Pacc[ji]])
        if n_ == NK - 1:
            pden = self.ps[:, 4 + ji, 0:nq]
            k.op('pe', I("matmul", pden, lhsT=self.ones_f, rhs=Pacc[ji][:, 0:nq], start=True, stop=True), reads=[self.r_ones_f, r_Pacc[ji]], writes=[self.psr[4 + ji]])
            k.op('act', I("activation", out=recb[ji][:, 0:nq], in_=pden, func=AF.Ln), reads=[self.psr[4 + ji]], writes=[r_recb[ji]])
            k.op('act', I("activation", out=recb[ji][:, 0:nq], in_=recb[ji][:, 0:nq], func=AF.Exp, scale=-1.0), reads=[r_recb[ji]], writes=[r_recb[ji]])
            k.op('dve', I("tensor_tensor", out=ostg[ji][:, 0:nq], in0=po, in1=recb[ji][:, 0:nq], op=ALU.mult), reads=[r_po, r_recb[ji]], writes=[r_ostg[ji]])
            self.store(self.OA2[h, :, q0:q0 + nq], ostg[ji][:, 0:nq], r_ostg[ji], reads=[r_ostg[ji]], writes=[self.r_OA2], is_output=bool(self.cfg.get("dump_oa2")))
    k.barrier()
    ar.release(m0)


Builder.mla_inproj = mla_inproj
Builder.mla_proj = mla_proj
Builder.mla_attn = mla_attn


def _host_consts():
    cm = np.ones((128, 1024), np.float32)
    t = np.arange(512)
    cm[:, 0:512][:, t % 64 == 0] = 0
    cm[:, 512:][:, t % 64 == 63] = 0
    s = np.arange(64)[:, None]
    tt = np.arange(64)[None, :]
    tri = np.concatenate([(s <= tt), (s >= tt)], 1).astype(np.float32)
    col = np.arange(64)
    cs_ = np.clip(col - 8, 0, 48)
    ok = (col[None, :] >= cs_[:, None]) & (col[None, :] < cs_[:, None] + 16)
    m = ok.T.astype(np.float32)
    colmask = np.concatenate([m, m], 0)
    P = np.zeros((64, 64), np.float32)
    for base in (0, 32):
        for i in range(16):
            P[base + i, base + i + 16] = -1.0
            P[base + i + 16, base + i] = 1.0
    ropeP = np.ascontiguousarray(P.T)
    tq = np.arange(2048)
    inv = np.power(np.float32(10000.0), -np.arange(0, 32, 2, dtype=np.float32) / np.float32(32)).astype(np.float32)
    ang_r = (tq // 64).astype(np.float32)[:, None] * inv
    ang_c = (tq % 64).astype(np.float32)[:, None] * inv
    ang = np.concatenate([ang_r, ang_r, ang_c, ang_c], 1).T
    ropecs = np.ascontiguousarray(np.stack([np.cos(ang), np.sin(ang)], 1).astype(np.float32))
    return {"ident": np.eye(128, dtype=np.float32), "cmask": cm, "trimask": tri, "colmask": colmask, "ropeP": ropeP, "ropecs": ropecs}


def build_program(cfg=None):
    B = Builder(cfg or {})
    B.inp("cvec", [2, 2048]); B.inp("ada_w", [2, 2048, 12288]); B.inp("ada_b", [2, 12288]); B.inp("norm_g", [2, 4, 2048])
    B.inp("w_in_even", [2048, 8192]); B.inp("cmask", [128, 1024]); B.inp("trimask", [64, 128]); B.inp("hgn", [1, 128]); B.inp("lbrows", [2, 3, 1024])
    B.inp("st_f", [8, 128, 128]); B.inp("st_b", [8, 128, 128]); B.inp("rpbr", [8, 15, 8192]); B.inp("colmask", [128, 64])
    B.inp("cnak", [512, 8, 128]); B.inp("cnav", [512, 8, 128]); B.inp("w_out_even", [2048, 2048])
    B.inp("mlp_w1", [2, 2048, 8192]); B.inp("mlp_w2", [2, 8192, 2048])
    B.inp("w_in_odd", [2048, 1088]); B.inp("mla_qg", [1, 512]); B.inp("mla_kvg", [1, 512]); B.inp("w_uq", [512, 3072]); B.inp("w_ukv", [512, 4096]); B.inp("w_out_odd", [2048, 2048])
    B.inp("cckv", [512, 512]); B.inp("ckpe", [512, 64]); B.inp("ropeP", [64, 64]); B.inp("ropecs", [64, 2, 2048])
    X0 = B.inp("xin", [NT, 2048])
    X4 = B.out("yout", [NT, 2048])
    B.out("nsf", [4, 8, 128, 128]); B.out("nsb", [4, 8, 128, 128]); B.out("nak", [1024, 1024]); B.out("nav", [1024, 1024])
    B.out("nckv", [1024, 512]); B.out("nkpe", [1024, 64])
    X1 = B.scr("X1", [NT, 2048]); X2 = B.scr("X2", [NT, 2048]); X3 = B.scr("X3", [NT, 2048])
    rX = [[B.R(f"X{i}_{t}") for t in range(24)] for i in range(5)]
    B.consts()
    B.modulation(0)
    B.modulation(1)
    B.even_inproj(X0, rX[0])
    B.hgrn_phase()
    B.na_phase()
    B.outproj_phase(0, B.OA, B.r_OA, B.din["w_out_even"], X0, rX[0], X1, rX[1])
    B.mlp_phase(0, X1, rX[1], X2, rX[2])
    B.mla_inproj(X2, rX[2])
    B.mla_proj()
    B.mla_attn()
    B.outproj_phase(1, B.OA2, B.r_OA2, B.din["w_out_odd"], X2, rX[2], X3, rX[3])
    B.mlp_phase(1, X3, rX[3], X4, rX[4], out_is_output=True)
    B.k.emit()
    return B


def kernel(x_prompt, x_sample, state_hgrn_fwd, state_hgrn_bwd, cache_na_k, cache_na_v, cache_mla_ckv, cache_mla_kpe,
           c, c_ctx, ada_w, ada_b, norm_g, hgrn_lb_fwd, hgrn_lb_bwd, w_in_even, hgrn_norm_g, na_rpb, w_out_even, w_in_odd,
           mla_q_norm_g, w_uq, mla_kv_norm_g, w_ukv, w_out_odd, mlp_w1, mlp_w2):
    f32 = lambda a: np.ascontiguousarray(np.asarray(a, dtype=np.float32))
    NCORE = 8
    B = build_program()
    hc = _host_consts()
    rpb = f32(na_rpb)[0]
    rp = np.zeros((8, 15, 128), np.float32)
    rp[:, :, 48:79] = rpb[:, :, ::-1]
    rpbr = np.ascontiguousarray(np.broadcast_to(rp[:, :, None, :], (8, 15, 64, 128))).reshape(8, 15, 8192)
    shared = {
        "ada_w": f32(ada_w), "ada_b": f32(ada_b), "norm_g": f32(norm_g), "w_in_even": f32(w_in_even)[0],
        "hgn": f32(hgrn_norm_g)[0:1], "lbrows": np.ascontiguousarray(np.stack([f32(hgrn_lb_fwd), f32(hgrn_lb_bwd)], 0)),
        "rpbr": rpbr, "w_out_even": f32(w_out_even)[0], "mlp_w1": f32(mlp_w1), "mlp_w2": f32(mlp_w2),
        "w_in_odd": f32(w_in_odd)[0], "mla_qg": f32(mla_q_norm_g)[0:1], "mla_kvg": f32(mla_kv_norm_g)[0:1],
        "w_uq": f32(w_uq)[0], "w_ukv": f32(w_ukv)[0], "w_out_odd": f32(w_out_odd)[0],
    }
    shared.update(hc)
    xp, xs = f32(x_prompt), f32(x_sample)
    sf, sb_ = f32(state_hgrn_fwd), f32(state_hgrn_bwd)
    cnk, cnv = f32(cache_na_k), f32(cache_na_v)
    cck, ckp = f32(cache_mla_ckv), f32(cache_mla_kpe)
    cc, cctx = f32(c), f32(c_ctx)
    in_maps = []
    for i in range(NCORE):
        m = dict(shared)
        m["xin"] = np.ascontiguousarray(np.concatenate([xp[4 * i:4 * i + 4].reshape(1024, 2048), xs[i]], 0))
        m["cvec"] = np.ascontiguousarray(np.stack([cctx, cc[i]], 0))
        m["st_f"] = np.ascontiguousarray(sf[i, 0]); m["st_b"] = np.ascontiguousarray(sb_[i, 0])
        m["cnak"] = np.ascontiguousarray(cnk[i, 0]); m["cnav"] = np.ascontiguousarray(cnv[i, 0])
        m["cckv"] = np.ascontiguousarray(cck[i, 0]); m["ckpe"] = np.ascontiguousarray(ckp[i, 0])
        in_maps.append(m)
    res = run_bass_kernel_spmd(B.nc, in_maps, core_ids=list(range(NCORE)))
    R_ = res.results
    y_prompt = np.concatenate([np.asarray(r["yout"])[:1024].reshape(4, 256, 2048) for r in R_], 0).astype(np.float32)
    y_sample = np.stack([np.asarray(r["yout"])[1024:] for r in R_], 0).astype(np.float32)
    nsf = np.concatenate([np.asarray(r["nsf"]).reshape(4, 1, 8, 128, 128) for r in R_], 0).astype(np.float32)
    nsb = np.concatenate([np.asarray(r["nsb"]).reshape(4, 1, 8, 128, 128) for r in R_], 0).astype(np.float32)
    nak = np.concatenate([np.asarray(r["nak"]).reshape(4, 1, 256, 8, 128) for r in R_], 0).astype(np.float32)
    nav = np.concatenate([np.asarray(r["nav"]).reshape(4, 1, 256, 8, 128) for r in R_], 0).astype(np.float32)
    nckv = np.concatenate([np.asarray(r["nckv"]).reshape(4, 1, 256, 512) for r in R_], 0).astype(np.float32)
    nkpe = np.concatenate([np.asarray(r["nkpe"]).reshape(4, 1, 256, 64) for r in R_], 0).astype(np.float32)
    return (y_prompt, y_sample, nsf, nsb, nak, nav, nckv, nkpe)
```

```python
import numpy as np
import concourse.bass as bass
import concourse.mybir as mybir
from concourse.bass_utils import run_bass_kernel_spmd

F32 = mybir.dt.float32
BF16 = mybir.dt.bfloat16
I32 = mybir.dt.int32
AF = mybir.ActivationFunctionType
ALU = mybir.AluOpType
AX = mybir.AxisListType

COMPUTE = ('pe', 'act', 'dve', 'pool')
ALLENG = ('pe', 'act', 'dve', 'pool', 'sp')


class DmaSem:
    __slots__ = ('name', 'count', 'handle')

    def __init__(self, name):
        self.name = name
        self.count = 0
        self.handle = None


class Res:
    __slots__ = ('name', 'w_ops', 'w_dma', 'r_ops', 'r_dma', 'had_read', 'sem', '_stsem', '_stphase')

    def __init__(self, name, sem=None):
        self.name = name
        self.w_ops = {}
        self.w_dma = {}
        self.r_ops = {}
        self.r_dma = {}
        self.had_read = False
        self.sem = sem


class Op:
    __slots__ = ('eng', 'fn', 'dep_ops', 'dep_dma', 'idx', 'signal', 'dma_sem')

    def __init__(self, eng, fn):
        self.eng = eng
        self.fn = fn
        self.dep_ops = {}
        self.dep_dma = {}
        self.idx = -1
        self.signal = False
        self.dma_sem = None


class K:
    def __init__(self, nc):
        self.nc = nc
        self.ops = {e: [] for e in ALLENG}
        self.known_ops = {e: {f: -1 for f in ALLENG} for e in ALLENG}
        self.known_dma = {e: {} for e in ALLENG}
        self.dma_sems = []
        self.out_sems = set()
        self.nres = 0

    def res(self, name=None):
        self.nres += 1
        return Res(name or f"r{self.nres}")

    def dsem(self, name):
        s = DmaSem(name)
        self.dma_sems.append(s)
        return s

    def _collect(self, eng, reads, writes):
        raw_ops, raw_dma, war_ops, war_dma = {}, {}, {}, {}
        for r in reads:
            for e, i in r.w_ops.items():
                if raw_ops.get(e, -1) < i:
                    raw_ops[e] = i
            for s, v in r.w_dma.items():
                if raw_dma.get(s, 0) < v:
                    raw_dma[s] = v
        for w in writes:
            for e, i in w.r_ops.items():
                if war_ops.get(e, -1) < i:
                    war_ops[e] = i
            for s, v in w.r_dma.items():
                if war_dma.get(s, 0) < v:
                    war_dma[s] = v
        dep_ops = {}
        for e, i in raw_ops.items():
            if e == eng and eng in ('pe', 'sp'):
                continue
            dep_ops[e] = i
        for e, i in war_ops.items():
            if e == eng:
                continue
            if dep_ops.get(e, -1) < i:
                dep_ops[e] = i
        dep_dma = dict(raw_dma)
        for s, v in war_dma.items():
            if dep_dma.get(s, 0) < v:
                dep_dma[s] = v
        ko = self.known_ops[eng]
        kd = self.known_dma[eng]
        dep_ops = {e: i for e, i in dep_ops.items() if ko[e] < i}
        dep_dma = {s: v for s, v in dep_dma.items() if kd.get(s, 0) < v}
        for e, i in dep_ops.items():
            ko[e] = i
        for s, v in dep_dma.items():
            kd[s] = v
        return dep_ops, dep_dma

    def _register(self, op, reads, writes, dma_evt=None):
        eng, idx = op.eng, op.idx
        for r in reads:
            r.had_read = True
            if dma_evt is None:
                r.r_ops[eng] = idx
            else:
                r.r_dma[dma_evt[0]] = dma_evt[1]
        for w in writes:
            if w.had_read:
                w.w_ops = {}
                w.w_dma = {}
                w.r_ops = {}
                w.r_dma = {}
                w.had_read = False
            if dma_evt is None:
                w.w_ops[eng] = idx
            else:
                w.w_dma[dma_evt[0]] = dma_evt[1]

    def op(self, eng, fn, reads=(), writes=()):
        o = Op(eng, fn)
        o.dep_ops, o.dep_dma = self._collect(eng, reads, writes)
        o.idx = len(self.ops[eng])
        self.ops[eng].append(o)
        self._register(o, reads, writes)
        return o

    def dma(self, q, out, in_, reads=(), writes=(), sem=None, is_output=False, **kw):
        if sem is None:
            for r in list(writes) + list(reads):
                if r.sem is not None:
                    sem = r.sem
                    break
        assert sem is not None, "dma needs a semaphore-bearing resource"
        o = Op(q, lambda e, out=out, in_=in_, kw=kw: e.dma_start(out=out, in_=in_, **kw))
        o.dep_ops, o.dep_dma = self._collect(q, reads, writes)
        o.idx = len(self.ops[q])
        self.ops[q].append(o)
        sem.count += 16
        o.dma_sem = sem
        self._register(o, reads, writes, dma_evt=(sem, sem.count))
        if is_output:
            self.out_sems.add(sem)
        return o

    def barrier(self):
        o = Op('sp', lambda e: e.nop())
        for e in COMPUTE:
            n = len(self.ops[e])
            if n > 0 and self.known_ops['sp'][e] < n - 1:
                last = n - 1
                while last >= 0 and (self.ops[e][last].fn is None or self.ops[e][last].dma_sem is not None):
                    last -= 1
                if last >= 0 and self.known_ops['sp'][e] < last:
                    o.dep_ops[e] = last
        for s in self.dma_sems:
            if self.known_dma['sp'].get(s, 0) < s.count:
                o.dep_dma[s] = s.count
        o.idx = len(self.ops['sp'])
        self.ops['sp'].append(o)
        for e in COMPUTE:
            w = Op(e, None)
            w.dep_ops = {'sp': o.idx}
            w.idx = len(self.ops[e])
            self.ops[e].append(w)
        for e in ALLENG:
            for f in ALLENG:
                self.known_ops[e][f] = len(self.ops[f]) - 1
            for s in self.dma_sems:
                self.known_dma[e][s] = s.count
            self.known_ops[e]['sp'] = o.idx

    def emit(self):
        nc = self.nc
        self.barrier()
        sig = {e: set() for e in ALLENG}
        for e in ALLENG:
            for o in self.ops[e]:
                for f, i in o.dep_ops.items():
                    sig[f].add(i)
        val = {}
        for e in ALLENG:
            val[e] = {i: k + 1 for k, i in enumerate(sorted(sig[e]))}
        self._cm = nc.cleanup_on_exit()
        self._cm.__enter__()
        esem = {e: nc.alloc_semaphore(f"eng_{e}") for e in ALLENG}
        for s in self.dma_sems:
            if s.count > 0:
                s.handle = nc.alloc_semaphore(f"d_{s.name}")
        engobj = {'pe': 'tensor', 'act': 'scalar', 'dve': 'vector', 'pool': 'gpsimd', 'sp': 'sync'}
        stats = {}

        def run(e):
            def body(eng):
                nw = 0
                for o in self.ops[e]:
                    for f, i in o.dep_ops.items():
                        eng.wait_ge(esem[f], val[f][i])
                        nw += 1
                    for s, v in o.dep_dma.items():
                        eng.wait_ge(s.handle, v)
                        nw += 1
                    if o.fn is None:
                        continue
                    ins = o.fn(eng)
                    if o.dma_sem is not None:
                        ins.then_inc(o.dma_sem.handle, 16)
                    elif o.idx in val[e]:
                        ins.then_inc(esem[e], 1)
                stats[e] = (len(self.ops[e]), nw)
            return body

        with nc.Block() as block:
            for e in ALLENG:
                getattr(block, engobj[e])(run(e))
        nc.all_engine_barrier()
        self._cm.__exit__(None, None, None)
        self.stats = stats
        return stats


import math

D_MODEL = 2048
NT = 3072
NCH = 16
D_FF = 8192
EPS = 1e-6


def I(method, *a, **kw):
    return lambda e: getattr(e, method)(*a, **kw)


class Arena:
    def __init__(self, nc, nbytes):
        self.t = nc.alloc_sbuf_tensor("arena", [128, nbytes // 4], F32)
        self.n = nbytes
        self.top = 0
        self.peak = 0

    def alloc(self, shape, dt=F32, parts=128):
        esz = 4 if dt == F32 else 2
        n = 1
        for s in shape:
            n *= s
        nb = (n * esz + 63) // 64 * 64
        assert self.top + nb <= self.n, f"arena overflow {self.top}+{nb}>{self.n}"
        o4 = self.top // 4
        a = self.t[0:parts, o4:o4 + nb // 4]
        self.top += nb
        self.peak = max(self.peak, self.top)
        if dt != F32:
            a = a.bitcast(dt)
        a = a[:, 0:n]
        if len(shape) == 2:
            a = a.rearrange("p (a b) -> p a b", a=shape[0])
        elif len(shape) == 3:
            a = a.rearrange("p (a b c) -> p a b c", a=shape[0], b=shape[1])
        return a

    def mark(self):
        return self.top

    def release(self, m):
        self.top = m


class Builder:
    def __init__(self, cfg):
        self.cfg = cfg
        nc = self.nc = bass.Bass("TRN2", target_bir_lowering=False)
        self.k = K(nc)
        self.din = {}
        self.dout = {}
        self.ar = Arena(nc, 204800)
        self.ps = nc.alloc_psum_tensor("ps", [128, 8, 512], F32)
        self.psr = [self.k.res(f"psum{b}") for b in range(8)]
        self.rr = {}

    def inp(self, name, shape, dt=F32):
        self.din[name] = self.nc.dram_tensor(name, list(shape), dt, kind="ExternalInput").ap()
        return self.din[name]

    def out(self, name, shape, dt=F32):
        self.dout[name] = self.nc.dram_tensor(name, list(shape), dt, kind="ExternalOutput").ap()
        return self.dout[name]

    def scr(self, name, shape, dt=F32):
        return self.nc.dram_tensor(name, list(shape), dt, kind="Internal").ap()

    def R(self, name, sem=False, sw=False):
        r = self.k.res(name)
        if not hasattr(self, "sem_pool"):
            self.sem_pool = []
            self.sem_i = 0
            self.sw_pool = []
            self.sw_i = 0
        if sw:
            if self.sw_i >= len(self.sw_pool):
                self.sw_pool.append(self.k.dsem(f"w{len(self.sw_pool)}"))
            r.sem = self.sw_pool[self.sw_i]
            self.sw_i += 1
        elif sem:
            if self.sem_i >= len(self.sem_pool):
                self.sem_pool.append(self.k.dsem(f"p{len(self.sem_pool)}"))
            r.sem = self.sem_pool[self.sem_i]
            self.sem_i += 1
        return r

    def phase(self):
        self.sem_i = self.sem_keep
        self.sw_i = 0
        self.phase_id = getattr(self, "phase_id", 0) + 1

    def keep_sems(self):
        self.sem_keep = getattr(self, "sem_i", 0)

    def store(self, out, in_, src_res, reads, writes, is_output=False):
        if not hasattr(src_res, "_stsem") or src_res._stphase != self.phase_id:
            src_res._stsem = self.R("stsem", sw=True).sem
            src_res._stphase = self.phase_id
        return self.k.dma('pool', out, in_, reads=reads, writes=writes, sem=src_res._stsem, is_output=is_output)

    def psb(self, b):
        return self.ps[:, b, :]

    def psb16(self, b):
        return self.ps[:, b, :].bitcast(BF16)

    def consts(self):
        k, ar = self.k, self.ar
        idf = self.inp("ident", [128, 128])
        self.ident_f = ar.alloc([128])
        self.ident_b = ar.alloc([128], BF16)
        self.ones_b = ar.alloc([128], BF16)
        self.r_ident_f = self.R("ident_f", sem=True)
        self.r_ident_b = self.R("ident_b")
        self.r_ones = self.R("ones_b")
        k.dma('sp', self.ident_f, idf, writes=[self.r_ident_f])
        k.op('dve', lambda e: e.tensor_copy(out=self.ident_b, in_=self.ident_f), reads=[self.r_ident_f], writes=[self.r_ident_b])
        k.op('dve', lambda e: e.memset(self.ones_b, 1.0), writes=[self.r_ones])
        self.ones_f = ar.alloc([128]); self.r_ones_f = self.R("ones_f")
        k.op('dve', lambda e: e.memset(self.ones_f, 1.0), writes=[self.r_ones_f])
        self.AB = [[ar.alloc([16, 4]) for s in range(2)] for l in range(2)]
        self.r_AB = [[self.R(f"AB{l}{s}") for s in range(2)] for l in range(2)]
        self.GROW = [[self.scr(f"grow{l}{s}", [2, D_MODEL]) for s in range(2)] for l in range(2)]
        self.r_GROW = [[self.R(f"grow{l}{s}") for s in range(2)] for l in range(2)]
        self.keep_sems()

    def modulation(self, l):
        k, ar, nc = self.k, self.ar, self.nc
        self.phase()
        m0 = ar.mark()
        cvec = self.din["cvec"]
        ada_w = self.din["ada_w"]
        ada_b = self.din["ada_b"]
        norm_g = self.din["norm_g"]
        crow = ar.alloc([D_MODEL], F32, parts=2)
        tmp = ar.alloc([D_MODEL], F32, parts=2)
        scb = ar.alloc([D_MODEL], BF16, parts=2)
        scT = ar.alloc([16, 2], BF16)
        mod = ar.alloc([6, D_MODEL], F32, parts=2)
        gn = ar.alloc([4, D_MODEL], F32, parts=2)
        wb = [ar.alloc([16, 512], BF16) for _ in range(3)]
        r_crow = self.R("crow", sem=True); r_tmp = self.R("tmp"); r_scb = self.R("scb"); r_scT = self.R("scT")
        r_mod = self.R("modrow", sem=True); r_gn = self.R("gn", sem=True)
        r_wb = [self.R(f"modw{i}", sw=True) for i in range(3)]
        k.dma('sp', crow, cvec, writes=[r_crow])
        k.dma('sp', mod, ada_b[l].rearrange("(o s d) -> o s d", o=1, s=6).to_broadcast([2, 6, D_MODEL]), writes=[r_mod])
        k.dma('sp', gn, norm_g[l].rearrange("(o s) d -> o s d", o=1).to_broadcast([2, 4, D_MODEL]), writes=[r_gn])
        if self.cfg.get("mod_stop") == 1:
            k.dma('sp', self.dout["dbg_mod0"], mod, reads=[r_mod], is_output=True); k.barrier(); ar.release(m0); return
        k.op('act', lambda e: e.activation(out=tmp, in_=crow, func=AF.Exp, scale=-1.0), reads=[r_crow], writes=[r_tmp])
        k.op('dve', lambda e: e.tensor_scalar_add(out=tmp, in0=tmp, scalar1=1.0), reads=[r_tmp], writes=[r_tmp])
        k.op('dve', lambda e: e.reciprocal(out=tmp, in_=tmp), reads=[r_tmp], writes=[r_tmp])
        k.op('dve', lambda e: e.tensor_tensor(out=scb, in0=tmp, in1=crow, op=ALU.mult), reads=[r_tmp, r_crow], writes=[r_scb])
        if self.cfg.get("mod_stop") == 2:
            k.dma('sp', self.dout["dbg_mod0"], mod, reads=[r_mod], is_output=True); k.barrier(); ar.release(m0); return
        pst = self.psb16(7)
        for c in range(16):
            k.op('pe', lambda e, c=c: e.transpose(out=pst[:, c * 2:c * 2 + 2], in_=scb[0:2, c * 128:(c + 1) * 128], identity=self.ident_b[0:2, 0:2]),
                 reads=[r_scb, self.r_ident_b], writes=[self.psr[7]])
        k.op('dve', lambda e: e.tensor_copy(out=scT.rearrange("p a b -> p (a b)"), in_=pst[:, 0:32]), reads=[self.psr[7]], writes=[r_scT])
        if self.cfg.get("mod_stop") == 3:
            k.dma('sp', self.dout["dbg_mod0"], mod, reads=[r_mod], is_output=True); k.barrier(); ar.release(m0); return
        for j in range(24):
            b = j % 3
            k.dma('pool', wb[b], ada_w[l, :, j * 512:(j + 1) * 512].rearrange("(k p) n -> p k n", p=128), writes=[r_wb[b]])
            pb = j % 2
            for kk in range(16):
                k.op('pe', lambda e, kk=kk, b=b, pb=pb: e.matmul(self.ps[0:2, pb, :], lhsT=scT[:, kk, :], rhs=wb[b][:, kk, :], start=(kk == 0), stop=(kk == 15)),
                     reads=[r_scT, r_wb[b]], writes=[self.psr[pb]])
            s_, off = divmod(j * 512, D_MODEL)
            k.op('dve', lambda e, pb=pb, s_=s_, off=off: e.tensor_tensor(out=mod[:, s_, off:off + 512], in0=self.ps[0:2, pb, :], in1=mod[:, s_, off:off + 512], op=ALU.add),
                 reads=[self.psr[pb], r_mod], writes=[r_mod])
        if self.cfg.get("mod_stop") == 4:
            k.dma('sp', self.dout["dbg_mod0"], mod, reads=[r_mod], is_output=True); k.barrier(); ar.release(m0); return
        for s in range(2):
            o = 3 * s
            k.op('dve', lambda e, o=o, s=s: e.scalar_tensor_tensor(out=mod[:, o + 1, :], in0=mod[:, o + 1, :], scalar=1.0, in1=gn[:, 2 * s, :], op0=ALU.add, op1=ALU.mult),
                 reads=[r_mod, r_gn], writes=[r_mod])
            k.op('dve', lambda e, o=o, s=s: e.tensor_tensor(out=mod[:, o + 2, :], in0=mod[:, o + 2, :], in1=gn[:, 2 * s + 1, :], op=ALU.mult),
                 reads=[r_mod, r_gn], writes=[r_mod])
        if self.cfg.get("mod_stop") == 5:
            k.dma('sp', self.dout["dbg_mod0"], mod, reads=[r_mod], is_output=True); k.barrier(); ar.release(m0); return
        for s in range(2):
            o = 3 * s
            k.dma('sp', self.GROW[l][s], mod[:, o + 2, :], reads=[r_mod], writes=[self.r_GROW[l][s]])
            if self.cfg.get("mod_stop") == 6:
                k.dma('sp', self.dout["dbg_mod0"], mod, reads=[r_mod], is_output=True); k.barrier(); ar.release(m0); return
            pf = self.psb(6)
            for c in range(16):
                for ab in range(2):
                    k.op('pe', lambda e, c=c, ab=ab, o=o: e.transpose(out=pf[:, c * 4 + 2 * ab:c * 4 + 2 * ab + 2], in_=mod[0:2, o + 1 - ab, c * 128:(c + 1) * 128], identity=self.ident_f[0:2, 0:2]),
                         reads=[r_mod, self.r_ident_f], writes=[self.psr[6]])
            if self.cfg.get("mod_stop") == 7:
                k.dma('sp', self.dout["dbg_mod0"], mod, reads=[r_mod], is_output=True); k.barrier(); ar.release(m0); return
            k.op('dve', lambda e, s=s: e.tensor_copy(out=self.AB[l][s].rearrange("p a b -> p (a b)"), in_=pf[:, 0:64]), reads=[self.psr[6]], writes=[self.r_AB[l][s]])
        if self.cfg.get("mod_stop") == 8:
            k.dma('sp', self.dout["dbg_mod0"], mod, reads=[r_mod], is_output=True); k.barrier(); ar.release(m0); return
        if self.cfg.get("dump_mod"):
            k.dma('sp', self.dout[f"dbg_mod{l}"], mod, reads=[r_mod], is_output=True)
        k.barrier()
        ar.release(m0)

    def pre_tile(self, xt, r_xt, hT, r_hT, col0, AB, r_AB, g, W):
        k = self.k
        junk, xn, st = W["junk"], W["xn"], W["st"]
        r_junk, r_xn, r_st = W["r_junk"], W["r_xn"], W["r_st"]
        ps_ = self.cfg.get('pre_stop', 99)
        if ps_ == 0: return
        k.op('dve', lambda e: e.memset(st[:, 0:1], 0.0), writes=[r_st])
        k.op('act', lambda e: e.activation(out=junk, in_=xt, func=AF.Square, accum_out=st[:, 0:1]), reads=[r_xt, r_st], writes=[r_junk, r_st])
        if ps_ == 1: return
        k.op('act', lambda e: e.activation(out=st[:, 1:2], in_=st[:, 0:1], func=AF.Ln, scale=1.0 / D_MODEL, bias=EPS), reads=[r_st], writes=[r_st])
        k.op('act', lambda e: e.activation(out=st[:, 2:3], in_=st[:, 1:2], func=AF.Exp, scale=-0.5), reads=[r_st], writes=[r_st])
        if ps_ == 2: return
        k.op('dve', lambda e: e.tensor_scalar_mul(out=xn, in0=xt, scalar1=st[:, 2:3]), reads=[r_xt, r_st], writes=[r_xn])
        if ps_ == 3: return
        for half in range(2):
            pb = 6 + half
            pst = self.psb16(pb)
            for c8 in range(8):
                c = half * 8 + c8
                k.op('pe', lambda e, c=c, c8=c8, pst=pst: e.transpose(out=pst[:, c8 * 128:(c8 + 1) * 128], in_=xn[:, c * 128:(c + 1) * 128], identity=self.ident_b),
                     reads=[r_xn, self.r_ident_b], writes=[self.psr[pb]])
            if ps_ == 4: return
            for c8 in range(8):
                c = half * 8 + c8
                if ps_ == 5 and c8 == 1: return
                if ps_ == 6 and c8 == 2: return
                src = pst[:, c8 * 128:(c8 + 1) * 128]
                dst = hT[:, c, col0:col0 + 128]
                if True:
                    k.op('act', lambda e, src=src, dst=dst, c=c: e.activation(out=dst, in_=src, func=AF.Identity, scale=AB[:, c, g:g + 1], bias=AB[:, c, 2 + g:3 + g]),
                         reads=[self.psr[pb], r_AB], writes=[r_hT])
                else:
                    k.op('dve', lambda e, src=src, dst=dst, c=c: e.tensor_scalar(out=dst, in0=src, scalar1=AB[:, c, g:g + 1], scalar2=AB[:, c, 2 + g:3 + g], op0=ALU.mult, op1=ALU.add),
                         reads=[self.psr[pb], r_AB], writes=[r_hT])

    def post_tile(self, xt, r_xt, ypieces, Grep, r_G, W):
        k = self.k
        junk, st, t = W["junk32"], W["st2"], W["t"]
        r_junk, r_st, r_t = W["r_junk32"], W["r_st2"], W["r_t"]
        npc = len(ypieces)
        k.op('dve', lambda e: e.memset(st[:, 0:4], 0.0), writes=[r_st])
        off = 0
        for i, (yp, r_y, n) in enumerate(ypieces):
            k.op('act', lambda e, yp=yp, i=i, n=n: e.activation(out=junk[:, 0:n], in_=yp, func=AF.Square, accum_out=st[:, i:i + 1]),
                 reads=[r_y, r_st], writes=[r_junk, r_st])
        k.op('dve', lambda e: e.tensor_reduce(out=st[:, 4:5], in_=st[:, 0:4], axis=AX.X, op=ALU.add), reads=[r_st], writes=[r_st])
        k.op('act', lambda e: e.activation(out=st[:, 5:6], in_=st[:, 4:5], func=AF.Ln, scale=1.0 / D_MODEL, bias=EPS), reads=[r_st], writes=[r_st])
        k.op('act', lambda e: e.activation(out=st[:, 6:7], in_=st[:, 5:6], func=AF.Exp, scale=-0.5), reads=[r_st], writes=[r_st])
        off = 0
        for i, (yp, r_y, n) in enumerate(ypieces):
            k.op('dve', lambda e, yp=yp, off=off, n=n: e.scalar_tensor_tensor(out=t[:, off:off + n], in0=yp, scalar=st[:, 6:7], in1=Grep[:, off:off + n], op0=ALU.mult, op1=ALU.mult),
                 reads=[r_y, r_st, r_G], writes=[r_t])
            off += n
        k.op('dve', lambda e: e.tensor_tensor(out=xt, in0=xt, in1=t, op=ALU.add), reads=[r_xt, r_t], writes=[r_xt])

    def load_grep(self, l, s):
        k, ar = self.k, self.ar
        G = [ar.alloc([D_MODEL]) for g in range(2)]
        r_G = [self.R(f"Grep{g}", sem=True) for g in range(2)]
        for g in range(2):
            k.dma('sp', G[g], self.GROW[l][s][g:g + 1, :].to_broadcast([128, D_MODEL]), reads=[self.r_GROW[l][s]], writes=[r_G[g]])
        return G, r_G

    def work_bufs(self):
        ar = self.ar
        W = {}
        W["junk"] = ar.alloc([D_MODEL], BF16); W["r_junk"] = self.R("junk")
        W["xn"] = ar.alloc([D_MODEL], BF16); W["r_xn"] = self.R("xn")
        W["st"] = ar.alloc([8]); W["r_st"] = self.R("st")
        W["st2"] = ar.alloc([8]); W["r_st2"] = self.R("st2")
        W["junk32"] = W["junk"]; W["r_junk32"] = W["r_junk"]
        W["t"] = ar.alloc([D_MODEL]); W["r_t"] = self.R("t")
        return W

    def mlp_phase(self, l, Xin, r_Xin, Xout, r_Xout, out_is_output=False):
        k, ar = self.k, self.ar
        self.phase()
        m0 = ar.mark()
        w1 = self.din["mlp_w1"][l]
        w2 = self.din["mlp_w2"][l]
        if not hasattr(self, "W1C"):
            self.W1C = self.scr("W1C", [16, 128, 16 * 512], BF16)
            self.W2C = self.scr("W2C", [32, 128, 4 * 1024], BF16)
        r_W1C = [self.R(f"w1c{j}") for j in range(16)]
        r_W2C = [self.R(f"w2c{j}") for j in range(32)]
        G, r_G = self.load_grep(l, 1)
        W = self.work_bufs()
        xt = [ar.alloc([D_MODEL]) for _ in range(2)]
        r_xt = [self.R(f"xt{i}", sem=True) for i in range(2)]
        hTs = [ar.alloc([16, 512], BF16) for _ in range(2)]; r_hTs = [self.R(f"hT{i}") for i in range(2)]
        ysbs = [h_.rearrange("p a b -> p (a b)").bitcast(F32).rearrange("p (s n) -> p s n", s=4) for h_ in hTs]
        aT = ar.alloc([64, 512], BF16); r_aT = self.R("aT")
        w1b = [ar.alloc([16, 512], BF16) for _ in range(2)]
        r_w1b = [self.R(f"w1b{i}", sw=True) for i in range(2)]
        w2b = [ar.alloc([4, 1024], BF16) for _ in range(2)]
        r_w2b = [self.R(f"w2b{i}", sw=True) for i in range(2)]
        w1s = [self.R(f"w1s{i}", sem=True).sem for i in range(2)]
        w2s = [self.R(f"w2s{i}", sem=True).sem for i in range(2)]
        rt = [ar.alloc([512]) for _ in range(2)]
        r_rt = [self.R(f"rt{i}") for i in range(2)]
        AB, r_AB = self.AB[l][1], self.r_AB[l][1]
        xi = 0
        w1i = 0
        w2i = 0
        NBLK = self.cfg.get('nblk', 6)
        xi_ = [0]

        def pre_one(blk_, sub_):
            g_ = 0 if blk_ < 2 else 1
            tile_ = blk_ * 4 + sub_
            b_ = xi_[0] % 2; xi_[0] += 1
            k.dma('sp', xt[b_], Xin[tile_ * 128:(tile_ + 1) * 128, :], reads=[r_Xin[tile_]], writes=[r_xt[b_]])
            self.pre_tile(xt[b_], r_xt[b_], hTs[blk_ % 2], r_hTs[blk_ % 2], sub_ * 128, AB, r_AB, g_, W)

        for sub in range(4):
            pre_one(0, sub)
        pending = []
        for blk in range(NBLK):
            g = 0 if blk < 2 else 1
            hT, r_hT, ysb = hTs[blk % 2], r_hTs[blk % 2], ysbs[blk % 2]
            if self.cfg.get("mlp_stop") == 1:
                k.barrier(); ar.release(m0); return
            for j in range(16):
                wb_i = w1i % 2; w1i += 1
                if blk == 0:
                    k.dma('pool', w1b[wb_i], w1[:, j * 512:(j + 1) * 512].rearrange("(k p) n -> p k n", p=128), writes=[r_w1b[wb_i]])
                    k.dma('sp', self.W1C[j], w1b[wb_i].rearrange("p a b -> p (a b)"), reads=[r_w1b[wb_i]], writes=[r_W1C[j]], sem=w1s[wb_i])
                else:
                    k.dma('pool', w1b[wb_i].rearrange("p a b -> p (a b)"), self.W1C[j], reads=[r_W1C[j]], writes=[r_w1b[wb_i]])
                for cc in range(4):
                    c = j * 4 + cc
                    pb = c % 2
                    for kk in range(16):
                        k.op('pe', lambda e, kk=kk, cc=cc, wb_i=wb_i, pb=pb, hT=hT: e.matmul(self.psb(pb), lhsT=w1b[wb_i][:, kk, cc * 128:(cc + 1) * 128], rhs=hT[:, kk, :], start=(kk == 0), stop=(kk == 15)),
                             reads=[r_w1b[wb_i], r_hT], writes=[self.psr[pb]])
                    k.op('act', lambda e, pb=pb: e.activation(out=rt[pb], in_=self.psb(pb), func=AF.Relu), reads=[self.psr[pb]], writes=[r_rt[pb]])
                    k.op('dve', lambda e, pb=pb, c=c: e.tensor_tensor(out=aT[:, c, :], in0=rt[pb], in1=rt[pb], op=ALU.mult), reads=[r_rt[pb]], writes=[r_aT])
                if j < 3 and pending:
                    post_one(*pending.pop(0))
                if j % 4 == 3 and blk + 1 < NBLK:
                    pre_one(blk + 1, j // 4)
            if self.cfg.get("mlp_stop") == 2:
                k.barrier(); ar.release(m0); return
            for half in range(2):
                for j in range(16):
                    wb_i = w2i % 2; w2i += 1
                    if blk == 0:
                        k.dma('pool', w2b[wb_i], w2[j * 512:(j + 1) * 512, half * 1024:(half + 1) * 1024].rearrange("(c p) n -> p c n", p=128), writes=[r_w2b[wb_i]])
                        k.dma('sp', self.W2C[half * 16 + j], w2b[wb_i].rearrange("p a b -> p (a b)"), reads=[r_w2b[wb_i]], writes=[r_W2C[half * 16 + j]], sem=w2s[wb_i])
                    else:
                        k.dma('pool', w2b[wb_i].rearrange("p a b -> p (a b)"), self.W2C[half * 16 + j], reads=[r_W2C[half * 16 + j]], writes=[r_w2b[wb_i]])
                    for cc in range(4):
                        c = j * 4 + cc
                        for sub in range(4):
                            for n in range(2):
                                pb = sub * 2 + n
                                k.op('pe', lambda e, c=c, cc=cc, sub=sub, n=n, pb=pb, wb_i=wb_i: e.matmul(self.psb(pb), lhsT=aT[:, c, sub * 128:(sub + 1) * 128], rhs=w2b[wb_i][:, cc, n * 512:(n + 1) * 512], start=(c == 0), stop=(c == 63)),
                                     reads=[r_aT, r_w2b[wb_i]] + ([r_hT] if False else []), writes=[self.psr[pb]])
                if self.cfg.get("mlp_stop") == 3:
                    k.barrier(); ar.release(m0); return
                if half == 0:
                    for sub in range(4):
                        for n in range(2):
                            pb = sub * 2 + n
                            eng = 'act' if n == 0 else 'dve'
                            if eng == 'act':
                                k.op('act', lambda e, sub=sub, n=n, pb=pb, ysb=ysb: e.copy(out=ysb[:, sub, n * 512:(n + 1) * 512], in_=self.psb(pb)), reads=[self.psr[pb]], writes=[r_hT])
                            else:
                                k.op('dve', lambda e, sub=sub, n=n, pb=pb, ysb=ysb: e.tensor_copy(out=ysb[:, sub, n * 512:(n + 1) * 512], in_=self.psb(pb)), reads=[self.psr[pb]], writes=[r_hT])
            if self.cfg.get("mlp_stop") == 4:
                k.barrier(); ar.release(m0); return
            def post_one(blk_, sub_, ysb_, r_hT_, g_):
                tile_ = blk_ * 4 + sub_
                b_ = xi_[0] % 2; xi_[0] += 1
                k.dma('sp', xt[b_], Xin[tile_ * 128:(tile_ + 1) * 128, :], reads=[r_Xin[tile_]], writes=[r_xt[b_]])
                yp_ = [(ysb_[:, sub_, :], r_hT_, 1024), (self.psb(sub_ * 2), self.psr[sub_ * 2], 512), (self.psb(sub_ * 2 + 1), self.psr[sub_ * 2 + 1], 512)]
                self.post_tile(xt[b_], r_xt[b_], yp_, G[g_], r_G[g_], W)
                k.dma('sp', Xout[tile_ * 128:(tile_ + 1) * 128, :], xt[b_], reads=[r_xt[b_]], writes=[r_Xout[tile_]], is_output=out_is_output)

            post_one(blk, 0, ysb, r_hT, g)
            for sub in range(1, 4):
                pending.append((blk, sub, ysb, r_hT, g))
        while pending:
            post_one(*pending.pop(0))
        k.barrier()
        ar.release(m0)


def _even_scratch(self):
    if hasattr(self, "QA"):
        return
    s = self.scr
    self.QA = s("QA", [8, 128, NT]); self.FF = s("FF", [8, 128, NT]); self.FB = s("FB", [8, 128, NT]); self.GA = s("GA", [8, 128, NT])
    self.QB = s("QB", [8, 128, NT], BF16); self.KB = s("KB", [8, 128, NT], BF16)
    self.VA = s("VA", [NT, 1024], BF16); self.VB = s("VB", [NT, 1024], BF16)
    self.OA = s("OA", [16, 128, NT], BF16) if not self.cfg.get("dump_oa") else self.out("OA", [16, 128, NT], BF16)
    self.r_E1 = self.R("E1out")
    self.r_OA = self.R("OAres")


def even_inproj(self, Xin, r_Xin):
    k, ar = self.k, self.ar
    self.phase()
    _even_scratch(self)
    m0 = ar.mark()
    l = 0
    w_in = self.din["w_in_even"]
    W = self.work_bufs()
    xt = [ar.alloc([D_MODEL]) for _ in range(2)]
    r_xt = [self.R(f"e1xt{i}", sem=True) for i in range(2)]
    hT = ar.alloc([16, NT], BF16); r_hT = self.R("hTall")
    wb = [ar.alloc([16, 512], BF16) for _ in range(2)]
    r_wb = [self.R(f"e1w{i}", sw=True) for i in range(2)]
    NST = 4
    stg = [ar.alloc([512]) for _ in range(NST)]
    r_stg = [self.R(f"e1stg{i}", sem=True) for i in range(NST)]
    AB, r_AB = self.AB[l][0], self.r_AB[l][0]
    for tile in range(24):
        g = 0 if tile < 8 else 1
        b = tile % 2
        k.dma('sp', xt[b], Xin[tile * 128:(tile + 1) * 128, :], reads=[r_Xin[tile]], writes=[r_xt[b]])
        self.pre_tile(xt[b], r_xt[b], hT, r_hT, tile * 128, AB, r_AB, g, W)
    fm_dst = {0: self.QA, 1: self.FF, 2: self.FB, 4: self.GA, 5: self.QB, 6: self.KB}
    si = 0
    pbi = 0
    nak, nav = self.dout["nak"], self.dout["nav"]
    for j in self.cfg.get('e1_js', range(16)):
        grp = j // 2
        wi = j % 2
        k.dma('pool', wb[wi], w_in[:, j * 512:(j + 1) * 512].rearrange("(k p) n -> p k n", p=128), writes=[r_wb[wi]])
        if grp in fm_dst:
            dst = fm_dst[grp]
            isbf = grp in (5, 6)
            for cc in range(4):
                head = (j % 2) * 4 + cc
                for tb in range(6):
                    pb = pbi % 4; pbi += 1
                    for kk in range(16):
                        k.op('pe', lambda e, kk=kk, cc=cc, wi=wi, tb=tb, pb=pb: e.matmul(self.psb(pb), lhsT=wb[wi][:, kk, cc * 128:(cc + 1) * 128], rhs=hT[:, kk, tb * 512:(tb + 1) * 512], start=(kk == 0), stop=(kk == 15)),
                             reads=[r_wb[wi], r_hT], writes=[self.psr[pb]])
                    s_ = si % NST; si += 1
                    so = stg[s_] if not isbf else stg[s_].bitcast(BF16)[:, 0:512]
                    scale = (128.0 ** -0.5) if grp == 5 else 1.0
                    if si % 2 == 0:
                        k.op('act', lambda e, so=so, pb=pb, scale=scale: e.activation(out=so, in_=self.psb(pb), func=AF.Copy, scale=scale), reads=[self.psr[pb]], writes=[r_stg[s_]])
                    else:
                        k.op('dve', lambda e, so=so, pb=pb, scale=scale: e.tensor_scalar_mul(out=so, in0=self.psb(pb), scalar1=scale), reads=[self.psr[pb]], writes=[r_stg[s_]])
                    k.dma('sp', dst[head, :, tb * 512:(tb + 1) * 512], so, reads=[r_stg[s_]], writes=[self.r_E1])
        if grp in (3, 6, 7):
            ntile = 8 if grp == 6 else 24
            col0 = (j % 2) * 512
            for t in range(ntile):
                pb = pbi % 4; pbi += 1
                for kk in range(16):
                    k.op('pe', lambda e, kk=kk, wi=wi, t=t, pb=pb: e.matmul(self.psb(pb), lhsT=hT[:, kk, t * 128:(t + 1) * 128], rhs=wb[wi][:, kk, :], start=(kk == 0), stop=(kk == 15)),
                         reads=[r_wb[wi], r_hT], writes=[self.psr[pb]])
                if grp in (3, 7):
                    s_ = si % NST; si += 1
                    so = stg[s_].bitcast(BF16)[:, 0:512]
                    k.op('act', lambda e, so=so, pb=pb: e.copy(out=so, in_=self.psb(pb)), reads=[self.psr[pb]], writes=[r_stg[s_]])
                    d = self.VA if grp == 3 else self.VB
                    k.dma('sp', d[t * 128:(t + 1) * 128, col0:col0 + 512], so, reads=[r_stg[s_]], writes=[self.r_E1])
                if grp in (6, 7) and t < 8:
                    s_ = si % NST; si += 1
                    so = stg[s_]
                    k.op('act', lambda e, so=so, pb=pb: e.copy(out=so, in_=self.psb(pb)), reads=[self.psr[pb]], writes=[r_stg[s_]])
                    d = nak if grp == 6 else nav
                    k.dma('sp', d[t * 128:(t + 1) * 128, col0:col0 + 512], so, reads=[r_stg[s_]], is_output=True)
    k.barrier()
    ar.release(m0)


Builder.even_inproj = even_inproj


def hgrn_phase(self):
    k, ar = self.k, self.ar
    self.phase()
    _even_scratch(self)
    m0 = ar.mark()
    CH = 64
    BT = 512
    cmask = ar.alloc([1024]); r_cmask = self.R("cmask", sem=True)
    trim = ar.alloc([128], parts=64); r_trim = self.R("trim", sem=True)
    k.dma('sp', cmask, self.din["cmask"], writes=[r_cmask])
    k.dma('sp', trim, self.din["trimask"], writes=[r_trim])
    gn = ar.alloc([1]); r_gn = self.R("hgn", sem=True)
    with self.nc.allow_non_contiguous_dma(reason="tiny"):
        k.dma('sp', gn, self.din["hgn"].rearrange("o p -> p o"), writes=[r_gn])
    lbr = ar.alloc([2, 1024], parts=3); r_lbr = self.R("lbr", sem=True)
    k.dma('sp', lbr, self.din["lbrows"].rearrange("d r c -> r d c"), writes=[r_lbr])
    k.op('act', I("activation", out=lbr, in_=lbr, func=AF.Exp), reads=[r_lbr], writes=[r_lbr])
    pf = self.psb(7)
    for d in range(2):
        for h in range(8):
            k.op('pe', I("transpose", out=pf[:, (d * 8 + h) * 3:(d * 8 + h) * 3 + 3], in_=lbr[0:3, d, h * 128:(h + 1) * 128], identity=self.ident_f[0:3, 0:3]),
                 reads=[r_lbr, self.r_ident_f], writes=[self.psr[7]])
    lbe = ar.alloc([16, 3]); r_lbe = self.R("lbe")
    omlb = ar.alloc([16]); r_omlb = self.R("omlb")
    lsum = ar.alloc([16]); r_lsum = self.R("lsum")
    k.op('dve', I("tensor_copy", out=lbe.rearrange("p a r -> p (a r)"), in_=pf[:, 0:48]), reads=[self.psr[7]], writes=[r_lbe])
    k.op('dve', I("tensor_reduce", out=lsum, in_=lbe, axis=AX.X, op=ALU.add), reads=[r_lbe], writes=[r_lsum])
    k.op('dve', I("reciprocal", out=lsum, in_=lsum), reads=[r_lsum], writes=[r_lsum])
    k.op('dve', I("tensor_tensor", out=omlb, in0=lbe[:, :, 0], in1=lsum, op=ALU.mult), reads=[r_lbe, r_lsum], writes=[r_omlb])
    lbv = ar.alloc([16]); lnom = ar.alloc([16])
    k.op('dve', I("tensor_copy", out=lbv, in_=omlb), reads=[r_omlb], writes=[r_omlb])
    k.op('dve', I("tensor_scalar", out=omlb, in0=omlb, scalar1=-1.0, scalar2=1.0, op0=ALU.mult, op1=ALU.add), reads=[r_omlb], writes=[r_omlb])
    k.op('act', I("activation", out=lnom, in_=omlb, func=AF.Ln), reads=[r_omlb], writes=[r_omlb])
    lnsc = ar.alloc([1])
    k.op('dve', I("memset", lnsc, float(math.log(128.0 ** -0.5))), writes=[r_omlb])
    NC_ = 4
    def f32buf(n=BT): return ar.alloc([n])
    qT = [f32buf() for _ in range(3)]; r_qT = [self.R(f"hq{i}", sem=True) for i in range(3)]
    fT = [f32buf() for _ in range(3)]; r_fT = [self.R(f"hf{i}", sem=True) for i in range(3)]
    kt = f32buf(); r_kt = self.R("kt")
    lf = f32buf(); r_lf = self.R("lf")
    bc = f32buf(); r_bc = self.R("bc")
    rv = f32buf(); r_rv = self.R("rv")
    sq = f32buf(); r_sq = self.R("sq")
    tm = f32buf(); r_tm = self.R("tm")
    tm2 = f32buf(); r_tm2 = self.R("tm2")
    vt = [[ar.alloc([8, 128], BF16, parts=64) for _ in range(2)] for _ in range(NC_)]
    r_vt = [[self.R(f"hv{c}{i}", sem=True) for i in range(2)] for c in range(NC_)]
    eb = [[f32buf() for _ in range(2)] for _ in range(NC_)]; r_eb = [[self.R(f"eb{c}{i}") for i in range(2)] for c in range(NC_)]
    qb = [[ar.alloc([BT], BF16) for _ in range(2)] for _ in range(NC_)]; r_qb = [[self.R(f"qb{c}{i}") for i in range(2)] for c in range(NC_)]
    kb = [[ar.alloc([BT], BF16) for _ in range(2)] for _ in range(NC_)]; r_kb = [[self.R(f"kb{c}{i}") for i in range(2)] for c in range(NC_)]
    kd = [[ar.alloc([BT], BF16) for _ in range(2)] for _ in range(NC_)]; r_kd = [[self.R(f"kd{c}{i}") for i in range(2)] for c in range(NC_)]
    S32 = [ar.alloc([128]) for _ in range(NC_)]; r_S32 = [self.R(f"S32{c}", sem=True) for c in range(NC_)]
    Sbf = [ar.alloc([128], BF16) for _ in range(NC_)]; r_Sbf = [self.R(f"Sbf{c}") for c in range(NC_)]
    Asb = [[ar.alloc([CH], BF16, parts=64) for _ in range(2)] for _ in range(NC_)]; r_Asb = [[self.R(f"Asb{c}{i}") for i in range(2)] for c in range(NC_)]
    kdt = [[ar.alloc([128], BF16, parts=64) for _ in range(2)] for _ in range(NC_)]; r_kdt = [[self.R(f"kdt{c}{i}") for i in range(2)] for c in range(NC_)]
    Oacc = [ar.alloc([2048]) for _ in range(2)]; r_Oacc = [self.R(f"Oacc{i}") for i in range(2)]
    Oacb = [ar.alloc([2048]) for _ in range(2)]; r_Oacb = [self.R(f"Oacb{i}") for i in range(2)]
    gT = [f32buf() for _ in range(2)]; r_gT = [self.R(f"hg{i}", sem=True) for i in range(2)]
    sq16 = ar.alloc([BT], BF16); r_sq16 = self.R("sq16")
    rstd = f32buf(); r_rstd = self.R("rstdh")
    ost = [ar.alloc([BT], BF16) for _ in range(2)]; r_ost = [self.R(f"host{i}", sem=True) for i in range(2)]
    def regA(c, p): return self.ps[0:CH, 2 * c, p * 64:p * 64 + CH]
    def regU(c, p): return self.ps[:, 2 * c, 128 + p * 128:256 + p * 128]
    def regOd(c, p): return self.ps[:, 2 * c, 384 + p * 64:448 + p * 64]
    def regK(c, p): return self.psb16(2 * c + 1)[0:CH, p * 128:(p + 1) * 128]
    def regOa(c, p): return self.ps[:, 2 * c + 1, 128 + p * 64:192 + p * 64]
    r_bD = [self.psr[2 * c] for c in range(NC_)]
    r_bA = [self.psr[2 * c + 1] for c in range(NC_)]
    r_pA = [[r_bD[c] for p in range(2)] for c in range(NC_)]
    r_pU = [[r_bD[c] for p in range(2)] for c in range(NC_)]
    r_pK = [[r_bA[c] for p in range(2)] for c in range(NC_)]
    r_pO = [[(r_bA[c] if c % 2 == 0 else r_bD[c]) for p in range(2)] for c in range(NC_)]
    r_fin = self.R("pfin")
    srcs = {0: self.FF, 1: self.FB}
    cnt = {"ld": 0, "fin": 0}
    stepc = [0] * NC_
    chc = [0] * NC_
    seqs = [(i * 256, 256, True, i) for i in range(4)] + [(1024, 2048, False, 0)]
    seqs = seqs[self.cfg.get('hg_s0', 0):self.cfg.get('hg_s1', 5)]
    def prep_chain(c, h, d, t0, bt, nblk, ncb, step, sl, par, blkof):
        blk = step if d == 0 else nblk - 1 - step
        blkof[c] = blk
        c0 = t0 + blk * bt
        li = cnt["ld"] % 3; cnt["ld"] += 1
        pi = stepc[c] % 2; stepc[c] += 1
        par[c] = pi
        k.dma('sp', qT[li][:, sl], self.QA[h, :, c0:c0 + bt], reads=[self.r_E1], writes=[r_qT[li]])
        k.dma('sp', fT[li][:, sl], srcs[d][h, :, c0:c0 + bt], reads=[self.r_E1], writes=[r_fT[li]])
        k.dma('sp', vt[c][pi][:, 0:ncb, :], self.VA[c0:c0 + bt, h * 128:(h + 1) * 128].rearrange("(c p) v -> p c v", p=CH), reads=[self.r_E1], writes=[r_vt[c][pi]])
        q_, f_ = qT[li][:, sl], fT[li][:, sl]
        hd = d * 8 + h
        lb_ap, lno_ap = lbv[:, hd:hd + 1], lnom[:, hd:hd + 1]
        k.op('act', I("activation", out=kt[:, sl], in_=f_, func=AF.Exp), reads=[r_fT[li]], writes=[r_kt])
        k.op('act', I("activation", out=tm[:, sl], in_=kt[:, sl], func=AF.Ln, bias=1.0), reads=[r_kt], writes=[r_tm])
        k.op('act', I("activation", out=lf[:, sl], in_=kt[:, sl], func=AF.Ln, bias=lb_ap), reads=[r_kt, r_omlb], writes=[r_lf])
        k.op('dve', I("tensor_tensor", out=lf[:, sl], in0=lf[:, sl], in1=tm[:, sl], op=ALU.subtract), reads=[r_lf, r_tm], writes=[r_lf])
        mF, mB = cmask[:, 0:bt], cmask[:, 512:512 + bt]
        if d == 0:
            k.op('dve', I("tensor_tensor_scan", out=bc[:, sl], data0=mF, data1=lf[:, sl], initial=0.0, op0=ALU.mult, op1=ALU.add), reads=[r_lf, r_cmask], writes=[r_bc])
            k.op('dve', I("tensor_tensor_scan", out=rv[:, sl][:, ::-1], data0=mB[:, ::-1], data1=lf[:, sl][:, ::-1], initial=0.0, op0=ALU.mult, op1=ALU.add), reads=[r_lf, r_cmask], writes=[r_rv])
        else:
            k.op('dve', I("tensor_tensor_scan", out=bc[:, sl][:, ::-1], data0=mB[:, ::-1], data1=lf[:, sl][:, ::-1], initial=0.0, op0=ALU.mult, op1=ALU.add), reads=[r_lf, r_cmask], writes=[r_bc])
            k.op('dve', I("tensor_tensor_scan", out=rv[:, sl], data0=mF, data1=lf[:, sl], initial=0.0, op0=ALU.mult, op1=ALU.add), reads=[r_lf, r_cmask], writes=[r_rv])
        dcol0 = CH - 1 if d == 0 else 0
        k.op('act', I("activation", out=eb[c][pi][:, 0:ncb], in_=bc[:, dcol0:bt:CH], func=AF.Exp), reads=[r_bc], writes=[r_eb[c][pi]])
        k.op('dve', I("tensor_tensor", out=rv[:, sl], in0=rv[:, sl], in1=lf[:, sl], op=ALU.subtract), reads=[r_rv, r_lf], writes=[r_rv])
        k.op('dve', I("tensor_tensor", out=rv[:, sl], in0=rv[:, sl], in1=tm[:, sl], op=ALU.subtract), reads=[r_rv, r_tm], writes=[r_rv])
        k.op('act', I("activation", out=kd[c][pi][:, sl], in_=rv[:, sl], func=AF.Exp, bias=lno_ap), reads=[r_rv, r_omlb], writes=[r_kd[c][pi]])
        k.op('dve', I("tensor_tensor", out=tm[:, sl], in0=tm[:, sl], in1=bc[:, sl], op=ALU.add), reads=[r_tm, r_bc], writes=[r_tm])
        k.op('act', I("activation", out=kb[c][pi][:, sl], in_=tm[:, sl], func=AF.Exp, scale=-1.0, bias=lno_ap), reads=[r_tm, r_omlb], writes=[r_kb[c][pi]])
        k.op('act', I("activation", out=sq[:, sl], in_=q_, func=AF.Exp, scale=-1.0), reads=[r_qT[li]], writes=[r_sq])
        k.op('act', I("activation", out=sq[:, sl], in_=sq[:, sl], func=AF.Ln, bias=1.0), reads=[r_sq], writes=[r_sq])
        k.op('dve', I("tensor_tensor", out=sq[:, sl], in0=bc[:, sl], in1=sq[:, sl], op=ALU.subtract), reads=[r_bc, r_sq], writes=[r_sq])
        k.op('act', I("activation", out=tm2[:, sl], in_=sq[:, sl], func=AF.Exp, bias=lnsc[:, 0:1]), reads=[r_sq, r_omlb], writes=[r_tm2])
        k.op('dve', I("tensor_tensor", out=qb[c][pi][:, sl], in0=q_, in1=tm2[:, sl], op=ALU.mult), reads=[r_qT[li], r_tm2], writes=[r_qb[c][pi]])

    def chunk_step(cstep, chains, bt, ncb, par, blkof):
        info = []
        for c, (h, d) in enumerate(chains):
            cc = cstep if d == 0 else ncb - 1 - cstep
            pp = chc[c] % 2; chc[c] += 1
            pi = par[c]
            cs = cc * CH
            info.append((c, h, d, cc, pp, pi, cs))
        for (c, h, d, cc, pp, pi, cs) in info:
            qbc, kbc, kdc = qb[c][pi][:, cs:cs + CH], kb[c][pi][:, cs:cs + CH], kd[c][pi][:, cs:cs + CH]
            k.op('pe', I("matmul", regA(c, pp), lhsT=kbc, rhs=qbc, start=True, stop=True), reads=[r_kb[c][pi], r_qb[c][pi]], writes=[r_pA[c][pp]])
            k.op('pe', I("transpose", out=regK(c, pp), in_=kdc, identity=self.ident_b), reads=[r_kd[c][pi], self.r_ident_b], writes=[r_pK[c][pp]])
        for (c, h, d, cc, pp, pi, cs) in info:
            mk = trim[:, 0:CH] if d == 0 else trim[:, CH:2 * CH]
            k.op('dve', I("tensor_tensor", out=Asb[c][pp], in0=regA(c, pp), in1=mk, op=ALU.mult), reads=[r_pA[c][pp], r_trim], writes=[r_Asb[c][pp]])
            k.op('act', I("copy", out=kdt[c][pp], in_=regK(c, pp)), reads=[r_pK[c][pp]], writes=[r_kdt[c][pp]])
        for (c, h, d, cc, pp, pi, cs) in info:
            qbc = qb[c][pi][:, cs:cs + CH]
            vch = vt[c][pi][:, cc, :]
            ro = regOa(c, pp) if d == 0 else regOd(c, pp)
            k.op('pe', I("matmul", ro, lhsT=Sbf[c], rhs=qbc, start=True, stop=False), reads=[r_Sbf[c], r_qb[c][pi]], writes=[r_pO[c][pp]])
            k.op('pe', I("matmul", ro, lhsT=vch, rhs=Asb[c][pp], start=False, stop=True), reads=[r_vt[c][pi], r_Asb[c][pp]], writes=[r_pO[c][pp]])
            k.op('pe', I("matmul", regU(c, pp), lhsT=kdt[c][pp], rhs=vch, start=True, stop=True), reads=[r_kdt[c][pp], r_vt[c][pi]], writes=[r_pU[c][pp]])
        for (c, h, d, cc, pp, pi, cs) in info:
            blk = blkof[c]
            oc = Oacc[c // 2][:, blk * bt + cs: blk * bt + cs + CH]
            if d == 0:
                k.op('act', I("copy", out=oc, in_=regOa(c, pp)), reads=[r_pO[c][pp]], writes=[r_Oacc[c // 2]])
            dec = eb[c][pi][:, cc:cc + 1]
            k.op('dve', I("scalar_tensor_tensor", out=S32[c], in0=S32[c], scalar=dec, in1=regU(c, pp), op0=ALU.mult, op1=ALU.add), reads=[r_S32[c], r_eb[c][pi], r_pU[c][pp]], writes=[r_S32[c]])
            k.op('act', I("copy", out=Sbf[c], in_=S32[c]), reads=[r_S32[c]], writes=[r_Sbf[c]])
        for (c, h, d, cc, pp, pi, cs) in info:
            if d == 1:
                blk = blkof[c]
                oc = Oacb[c // 2][:, blk * bt + cs: blk * bt + cs + CH]
                k.op('dve', I("tensor_copy", out=oc, in_=regOd(c, pp)), reads=[r_pO[c][pp]], writes=[r_Oacb[c // 2]])

    def finish_group(chains, t0, bt, nblk, sl, is_ctx, sidx, hp):
        if is_ctx:
            for c, (h, d) in enumerate(chains):
                dst = self.dout["nsf" if d == 0 else "nsb"]
                self.store(dst[sidx, h], S32[c], r_S32[c], reads=[r_S32[c]], writes=[], is_output=True)
        for hh in range(0 if self.cfg.get('hg_nofin') else 2):
            h = 2 * hp + hh
            for blk in range(nblk):
                c0 = t0 + blk * bt
                gi = cnt["fin"] % 2; cnt["fin"] += 1
                ob = Oacc[hh][:, blk * bt:(blk + 1) * bt]
                obb = Oacb[hh][:, blk * bt:(blk + 1) * bt]
                k.dma('sp', gT[gi][:, sl], self.GA[h, :, c0:c0 + bt], reads=[self.r_E1], writes=[r_gT[gi]])
                k.op('dve', I("tensor_tensor", out=ob, in0=ob, in1=obb, op=ALU.add), reads=[r_Oacc[hh], r_Oacb[hh]], writes=[r_Oacc[hh]])
                k.op('act', I("activation", out=sq16[:, sl], in_=ob, func=AF.Square), reads=[r_Oacc[hh]], writes=[r_sq16])
                pfin = self.ps[:, 7, 0:bt]
                fin_res = [r_bA[3]]
                k.op('pe', I("matmul", pfin, lhsT=self.ones_b, rhs=sq16[:, sl], start=True, stop=True), reads=[self.r_ones, r_sq16], writes=fin_res)
                k.op('act', I("activation", out=rstd[:, sl], in_=pfin, func=AF.Ln, scale=1.0 / 128, bias=EPS), reads=fin_res, writes=[r_rstd])
                g_ = gT[gi][:, sl]
                k.op('act', I("activation", out=tm[:, sl], in_=g_, func=AF.Exp, scale=-1.0), reads=[r_gT[gi]], writes=[r_tm])
                k.op('act', I("activation", out=tm[:, sl], in_=tm[:, sl], func=AF.Ln, bias=1.0), reads=[r_tm], writes=[r_tm])
                k.op('dve', I("scalar_tensor_tensor", out=rstd[:, sl], in0=rstd[:, sl], scalar=-0.5, in1=tm[:, sl], op0=ALU.mult, op1=ALU.subtract), reads=[r_rstd, r_tm], writes=[r_rstd])
                k.op('act', I("activation", out=rstd[:, sl], in_=rstd[:, sl], func=AF.Exp), reads=[r_rstd], writes=[r_rstd])
                k.op('dve', I("tensor_tensor", out=tm[:, sl], in0=ob, in1=g_, op=ALU.mult), reads=[r_Oacc[hh], r_gT[gi], r_tm], writes=[r_tm])
                k.op('dve', I("scalar_tensor_tensor", out=ost[gi][:, sl], in0=tm[:, sl], scalar=gn[:, 0:1], in1=rstd[:, sl], op0=ALU.mult, op1=ALU.mult), reads=[r_tm, r_rstd, r_gn], writes=[r_ost[gi]])
                self.store(self.OA[h, :, c0:c0 + bt], ost[gi][:, sl], r_ost[gi], reads=[r_ost[gi]], writes=[self.r_OA], is_output=bool(self.cfg.get("dump_oa")))

    items = []
    for (t0, T, is_ctx, sidx) in seqs:
        bt = min(BT, T)
        nblk = T // bt
        for hp in range(self.cfg.get('hg_nhp', 4)):
            for step in range(nblk):
                items.append((t0, T, is_ctx, sidx, hp, step))
    pars = [[0] * NC_ for _ in items]
    blkofs = [[0] * NC_ for _ in items]

    def do_prep(ii, c):
        (t0, T, is_ctx, sidx, hp, step) = items[ii]
        bt = min(BT, T); nblk = T // bt; ncb = bt // CH
        h, d = 2 * hp + (c // 2), c % 2
        prep_chain(c, h, d, t0, bt, nblk, ncb, step, slice(0, bt), pars[ii], blkofs[ii])

    for c in range(NC_):
        do_prep(0, c)
    for ii, (t0, T, is_ctx, sidx, hp, step) in enumerate(items):
        bt = min(BT, T); nblk = T // bt; ncb = bt // CH
        sl = slice(0, bt)
        chains = [(2 * hp + (c // 2), c % 2) for c in range(NC_)]
        if step == 0:
            for c, (h, d) in enumerate(chains):
                if is_ctx:
                    k.op('dve', I("memset", S32[c], 0.0), writes=[r_S32[c]])
                    k.op('dve', I("memset", Sbf[c], 0.0), writes=[r_Sbf[c]])
                else:
                    k.dma('sp', S32[c], self.din["st_f" if d == 0 else "st_b"][h], writes=[r_S32[c]])
                    k.op('act', I("copy", out=Sbf[c], in_=S32[c]), reads=[r_S32[c]], writes=[r_Sbf[c]])
        nxt = ii + 1 if ii + 1 < len(items) else None
        done = 0
        for cstep in range(ncb):
            chunk_step(cstep, chains, bt, ncb, pars[ii], blkofs[ii])
            if nxt is not None:
                want = ((cstep + 1) * NC_) // ncb
                while done < want:
                    do_prep(nxt, done); done += 1
        if nxt is not None:
            while done < NC_:
                do_prep(nxt, done); done += 1
        if step == nblk - 1:
            finish_group(chains, t0, bt, nblk, sl, is_ctx, sidx, hp)
    k.barrier()
    ar.release(m0)


Builder.hgrn_phase = hgrn_phase


def _attn_finish(self, po, r_po, rec, r_rec, ob16, r_ob16, pT, r_pT, ostg_slice, r_ostg):
    k = self.k
    k.op('dve', I("reciprocal", out=rec, in_=po[:, 128:129]), reads=[r_po], writes=[r_rec])
    k.op('dve', I("tensor_scalar_mul", out=ob16, in0=po[:, 0:128], scalar1=rec), reads=[r_po, r_rec], writes=[r_ob16])
    k.op('pe', I("transpose", out=pT, in_=ob16, identity=self.ident_b), reads=[r_ob16, self.r_ident_b], writes=[r_pT])
    k.op('act', I("copy", out=ostg_slice, in_=pT), reads=[r_pT], writes=[r_ostg])


def na_phase(self):
    k, ar = self.k, self.ar
    self.phase()
    _even_scratch(self)
    m0 = ar.mark()
    QT = [ar.alloc([2048], BF16) for _ in range(2)]; r_QT = [self.R(f"naQ{i}", sem=True) for i in range(2)]
    KT = [ar.alloc([2048], BF16) for _ in range(2)]; r_KT = [self.R(f"naK{i}", sem=True) for i in range(2)]
    vaug = [ar.alloc([16, 129], BF16) for _ in range(2)]; r_vaug = [self.R(f"naV{i}", sem=True) for i in range(2)]
    for i in range(2):
        k.op('pool', I("memset", vaug[i][:, :, 128:129], 1.0), writes=[r_vaug[i]])
    rec = [ar.alloc([1]) for _ in range(2)]; r_rec = [self.R(f"narec{i}") for i in range(2)]
    ob16 = [ar.alloc([128], BF16) for _ in range(2)]; r_ob16 = [self.R(f"naob{i}") for i in range(2)]
    ostg = [ar.alloc([512], BF16) for _ in range(2)]; r_ostg = [self.R(f"naost{i}", sem=True) for i in range(2)]
    Pc = [ar.alloc([4, 512], BF16) for _ in range(2)]; r_Pc = [self.R(f"naPc{i}") for i in range(2)]
    Praw = [ar.alloc([128], BF16) for _ in range(3)]; r_Praw = [self.R(f"naPr{i}") for i in range(3)]
    Pacc = [ar.alloc([512]) for _ in range(2)]; r_Pacc = [self.R(f"naPacc{i}") for i in range(2)]
    recb = ar.alloc([512]); r_recb = self.R("narecb")
    Pl = [ar.alloc([128], BF16) for _ in range(6)]; r_Pl = [self.R(f"naPl{i}") for i in range(6)]
    cnt = {"h": 0, "o": 0, "f": 0, "pl": 0, "pr": 0}
    for sq_ in range(4):
        t0 = sq_ * 256
        for h in range(8):
            hi = cnt["h"] % 2; cnt["h"] += 1
            k.dma('sp', QT[hi][:, 0:256], self.QB[h, :, t0:t0 + 256], reads=[self.r_E1], writes=[r_QT[hi]])
            k.dma('sp', KT[hi][:, 0:256], self.KB[h, :, t0:t0 + 256], reads=[self.r_E1], writes=[r_KT[hi]])
            k.dma('sp', vaug[hi][:, 0:2, 0:128], self.VB[t0:t0 + 256, h * 128:(h + 1) * 128].rearrange("(c p) v -> p c v", p=128), reads=[self.r_E1], writes=[r_vaug[hi]])
            pci = hi
            for kc in range(2):
                pb = kc
                k.op('pe', I("matmul", self.ps[:, pb, 0:256], lhsT=KT[hi][:, kc * 128:(kc + 1) * 128], rhs=QT[hi][:, 0:256], start=True, stop=True),
                     reads=[r_KT[hi], r_QT[hi]], writes=[self.psr[pb]])
                k.op('act', I("activation", out=Pc[pci][:, kc, 0:256], in_=self.ps[:, pb, 0:256], func=AF.Exp), reads=[self.psr[pb]], writes=[r_Pc[pci]])
            oi = cnt["o"] % 2; cnt["o"] += 1
            for qt in range(2):
                fi = cnt["f"] % 2; cnt["f"] += 1
                pv = 4 + fi
                po = self.ps[:, pv, 0:129]
                for kc in range(2):
                    k.op('pe', I("matmul", po, lhsT=Pc[pci][:, kc, qt * 128:(qt + 1) * 128], rhs=vaug[hi][:, kc, :], start=(kc == 0), stop=(kc == 1)),
                         reads=[r_Pc[pci], r_vaug[hi]], writes=[self.psr[pv]])
                pT = self.psb16(6)[:, fi * 128:(fi + 1) * 128]
                _attn_finish(self, po, self.psr[pv], rec[fi], r_rec[fi], ob16[fi], r_ob16[fi], pT, self.psr[6], ostg[oi][:, qt * 128:(qt + 1) * 128], r_ostg[oi])
            self.store(self.OA[8 + h, :, t0:t0 + 256], ostg[oi][:, 0:256], r_ostg[oi], reads=[r_ostg[oi]], writes=[self.r_OA], is_output=bool(self.cfg.get("dump_oa")))
    colm = ar.alloc([64]); r_colm = self.R("colm", sem=True)
    k.dma('sp', colm, self.din["colmask"], writes=[r_colm])
    Traw = ar.alloc([14, 64]); r_Traw = self.R("Traw", sem=True)
    Cexp = ar.alloc([14, 64], BF16); r_Cexp = self.R("Cexp")
    def ws(r): return min(max(r - 4, 0), 24)
    types = {}
    plan = []
    for j in range(16):
        lo = ws(2 * j) // 2
        hi_ = (ws(2 * j + 1) + 7) // 2
        lst = []
        for kc in range(lo, hi_ + 1):
            dl = 2 * kc - 2 * j
            valid = tuple(tuple(ws(2 * j + b) <= 2 * kc + a <= ws(2 * j + b) + 7 for b in range(2)) for a in range(2))
            key = (dl, valid)
            if key not in types:
                types[key] = len(types)
            lst.append((kc, types[key]))
        plan.append(lst)
    ntyp = len(types)
    EBt = ar.alloc([ntyp, 128], BF16); r_EBt = self.R("EBt")
    Kc32 = ar.alloc([4, 128]); r_Kc32 = self.R("Kc32", sem=True)
    Kc16 = ar.alloc([4, 128], BF16); r_Kc16 = self.R("Kc16")
    KcT = ar.alloc([512], BF16); r_KcT = self.R("KcT")
    Vc32 = ar.alloc([4, 128]); r_Vc32 = self.R("Vc32", sem=True)
    vaugc = ar.alloc([4, 129], BF16); r_vaugc = self.R("vaugc")
    k.op('pool', I("memset", vaugc[:, :, 128:129], 1.0), writes=[r_vaugc])
    rpbr = self.din["rpbr"]
    T0 = 1024
    for h in range(8):
        hi = cnt["h"] % 2; cnt["h"] += 1
        k.dma('sp', QT[hi], self.QB[h, :, T0:T0 + 2048], reads=[self.r_E1], writes=[r_QT[hi]])
        k.dma('sp', KT[hi], self.KB[h, :, T0:T0 + 2048], reads=[self.r_E1], writes=[r_KT[hi]])
        k.dma('sp', vaug[hi][:, :, 0:128], self.VB[T0:T0 + 2048, h * 128:(h + 1) * 128].rearrange("(c p) v -> p c v", p=128), reads=[self.r_E1], writes=[r_vaug[hi]])
        k.dma('sp', Kc32, self.din["cnak"][:, h, :].rearrange("(c p) d -> p c d", p=128), writes=[r_Kc32])
        k.dma('sp', Vc32, self.din["cnav"][:, h, :].rearrange("(c p) d -> p c d", p=128), writes=[r_Vc32])
        k.op('pool', I("tensor_copy", out=Kc16, in_=Kc32), reads=[r_Kc32], writes=[r_Kc16])
        k.op('pool', I("tensor_copy", out=vaugc[:, :, 0:128], in_=Vc32), reads=[r_Vc32], writes=[r_vaugc])
        for c in range(4):
            pT = self.psb16(6)[:, c * 128:(c + 1) * 128]
            k.op('pe', I("transpose", out=pT, in_=Kc16[:, c, :], identity=self.ident_b), reads=[r_Kc16, self.r_ident_b], writes=[self.psr[6]])
        k.op('act', I("copy", out=KcT, in_=self.psb16(6)[:, 0:512]), reads=[self.psr[6]], writes=[r_KcT])
        for half in range(2):
            src = bass.AP(tensor=rpbr.tensor, offset=h * 15 * 8192 + half * 8192 + 63, ap=[[127, 64], [8192, 14], [1, 64]])
            k.dma('sp', Traw[half * 64:(half + 1) * 64], src, writes=[r_Traw])
        k.op('act', I("activation", out=Traw, in_=Traw, func=AF.Exp), reads=[r_Traw], writes=[r_Traw])
        for i in range(14):
            k.op('dve', I("tensor_tensor", out=Cexp[:, i, :], in0=Traw[:, i, :], in1=colm, op=ALU.mult), reads=[r_Traw, r_colm], writes=[r_Cexp])
        for (dl, valid), ti in types.items():
            for b in range(2):
                k.op('pool', I("tensor_copy", out=EBt[:, ti, b * 64:(b + 1) * 64], in_=Cexp[:, dl - b + 7, :]), reads=[r_Cexp], writes=[r_EBt])
            for a in range(2):
                for b in range(2):
                    if not valid[a][b]:
                        k.op('pool', I("memset", EBt[a * 64:(a + 1) * 64, ti, b * 64:(b + 1) * 64], 0.0), reads=[r_EBt], writes=[r_EBt])
        for jg in range(4):
            pci = cnt["o"] % 2; cnt["o"] += 1
            po = self.ps[:, 4 + pci, :]
            r_po = self.psr[4 + pci]
            qs = slice(jg * 512, (jg + 1) * 512)
            for c in range(4):
                pb = c % 2
                k.op('pe', I("matmul", self.ps[:, pb, :], lhsT=KcT[:, c * 128:(c + 1) * 128], rhs=QT[hi][:, qs], start=True, stop=True),
                     reads=[r_KcT, r_QT[hi]], writes=[self.psr[pb]])
                k.op('act', I("activation", out=Pc[pci][:, c, :], in_=self.ps[:, pb, :], func=AF.Exp), reads=[self.psr[pb]], writes=[r_Pc[pci]])
                k.op('pe', I("matmul", po, lhsT=vaugc[:, c, 0:128], rhs=Pc[pci][:, c, :], start=(c == 0), stop=False), reads=[r_Pc[pci], r_vaugc], writes=[r_po])
                if c == 1:
                    k.op('dve', I("tensor_tensor", out=Pacc[pci], in0=Pc[pci][:, 0, :], in1=Pc[pci][:, 1, :], op=ALU.add), reads=[r_Pc[pci]], writes=[r_Pacc[pci]])
                elif c > 1:
                    k.op('dve', I("tensor_tensor", out=Pacc[pci], in0=Pacc[pci], in1=Pc[pci][:, c, :], op=ALU.add), reads=[r_Pc[pci], r_Pacc[pci]], writes=[r_Pacc[pci]])
            lat = []
            for jj in range(4):
                j = jg * 4 + jj
                for (kc, ti) in plan[j]:
                    lat.append((jj, j, kc, ti))
            NL = len(lat)
            sbk = []; pls_ = []; prs = []
            for n_ in range(NL):
                sbk.append((2, 3, 7)[cnt["pr"] % 3]); prs.append(cnt["pr"] % 3); cnt["pr"] += 1
                pls_.append(cnt["pl"] % 6); cnt["pl"] += 1

            def S_lat(n_):
                jj, j, kc, ti = lat[n_]
                pb = sbk[n_]
                k.op('pe', I("matmul", self.ps[:, pb, 0:128], lhsT=KT[hi][:, kc * 128:(kc + 1) * 128], rhs=QT[hi][:, j * 128:(j + 1) * 128], start=True, stop=True),
                     reads=[r_KT[hi], r_QT[hi]], writes=[self.psr[pb]])

            S_lat(0)
            if NL > 1:
                S_lat(1)
            for n_ in range(NL):
                jj, j, kc, ti = lat[n_]
                if n_ + 2 < NL:
                    S_lat(n_ + 2)
                pb, pr, pl = sbk[n_], prs[n_], pls_[n_]
                k.op('act', I("activation", out=Praw[pr], in_=self.ps[:, pb, 0:128], func=AF.Exp), reads=[self.psr[pb]], writes=[r_Praw[pr]])
                k.op('dve', I("tensor_tensor", out=Pl[pl], in0=Praw[pr], in1=EBt[:, ti, :], op=ALU.mult), reads=[r_Praw[pr], r_EBt], writes=[r_Pl[pl]])
                k.op('pe', I("matmul", po[:, jj * 128:(jj + 1) * 128], lhsT=vaug[hi][:, kc, 0:128], rhs=Pl[pl], start=False, stop=(n_ == NL - 1)),
                     reads=[r_Pl[pl], r_vaug[hi]], writes=[r_po])
                k.op('dve', I("tensor_tensor", out=Pacc[pci][:, jj * 128:(jj + 1) * 128], in0=Pacc[pci][:, jj * 128:(jj + 1) * 128], in1=Pl[pl], op=ALU.add), reads=[r_Pacc[pci], r_Pl[pl]], writes=[r_Pacc[pci]])
            pden = self.ps[:, 6, :]
            k.op('pe', I("matmul", pden, lhsT=self.ones_f, rhs=Pacc[pci], start=True, stop=True), reads=[self.r_ones_f, r_Pacc[pci]], writes=[self.psr[6]])
            k.op('dve', I("reciprocal", out=recb, in_=pden), reads=[self.psr[6]], writes=[r_recb])
            k.op('dve', I("tensor_tensor", out=ostg[pci], in0=po, in1=recb, op=ALU.mult), reads=[r_po, r_recb], writes=[r_ostg[pci]])
            self.store(self.OA[8 + h, :, T0 + jg * 512:T0 + (jg + 1) * 512], ostg[pci], r_ostg[pci], reads=[r_ostg[pci]], writes=[self.r_OA], is_output=bool(self.cfg.get("dump_oa")))
    k.barrier()
    ar.release(m0)


Builder.na_phase = na_phase


def outproj_phase(self, l, OA, r_OA, w_out, Xin, r_Xin, Xout, r_Xout):
    k, ar = self.k, self.ar
    self.phase()
    m0 = ar.mark()
    G, r_G = self.load_grep(l, 0)
    W = self.work_bufs()
    wo = ar.alloc([16, 2048], BF16); r_wo = self.R("wo", sw=True)
    for n in range(4):
        k.dma('pool', wo[:, :, n * 512:(n + 1) * 512], w_out[:, n * 512:(n + 1) * 512].rearrange("(k p) n -> p k n", p=128), writes=[r_wo])
    ob = [ar.alloc([16, 512], BF16) for _ in range(2)]; r_ob = [self.R(f"opo{i}", sem=True) for i in range(2)]
    xt = [ar.alloc([D_MODEL]) for _ in range(2)]; r_xt = [self.R(f"opx{i}", sem=True) for i in range(2)]
    xi = 0
    for blk in range(6):
        g = 0 if blk < 2 else 1
        bi = blk % 2
        k.dma('sp', ob[bi], OA[:, :, blk * 512:(blk + 1) * 512].rearrange("c p t -> p c t"), reads=[r_OA], writes=[r_ob[bi]])
        for sub in range(4):
            tile = blk * 4 + sub
            pb0 = (tile % 2) * 4
            for n in range(4):
                for kk in range(16):
                    k.op('pe', I("matmul", self.psb(pb0 + n), lhsT=ob[bi][:, kk, sub * 128:(sub + 1) * 128], rhs=wo[:, kk, n * 512:(n + 1) * 512], start=(kk == 0), stop=(kk == 15)),
                         reads=[r_ob[bi], r_wo], writes=[self.psr[pb0 + n]])
            b = xi % 2; xi += 1
            k.dma('sp', xt[b], Xin[tile * 128:(tile + 1) * 128, :], reads=[r_Xin[tile]], writes=[r_xt[b]])
            yp = [(self.psb(pb0 + n), self.psr[pb0 + n], 512) for n in range(4)]
            self.post_tile(xt[b], r_xt[b], yp, G[g], r_G[g], W)
            self.store(Xout[tile * 128:(tile + 1) * 128, :], xt[b], r_xt[b], reads=[r_xt[b]], writes=[r_Xout[tile]])
    k.barrier()
    ar.release(m0)


Builder.outproj_phase = outproj_phase


NTK = NT + 512
MLA_SCALE = 192.0 ** -0.5


def _mla_scratch(self):
    if hasattr(self, "CQT"):
        return
    s = self.scr
    self.CQT = s("CQT", [4, 128, NT], BF16)
    self.CKVT = s("CKVT", [4, 128, NTK], BF16)
    self.KPET = s("KPET", [64, NTK], BF16)
    self.KROT = s("KROT", [64, 2048], BF16)
    self.QN = s("QN", [16, 128, NT], BF16)
    self.QPE = s("QPE", [16, 64, NT], BF16)
    self.QROT = s("QROT", [16, 64, 2048], BF16)
    self.KN = s("KN", [16, 128, NTK], BF16)
    self.VM = s("VM", [NTK, 16, 128], BF16)
    self.OA2 = s("OA2", [16, 128, NT], BF16) if not self.cfg.get("dump_oa2") else self.out("OA2", [16, 128, NT], BF16)
    self.r_O1 = self.R("O1out"); self.r_O2 = self.R("O2out"); self.r_OA2 = self.R("OA2res")


def _rmsnorm_free(self, src_ps, r_src, n, gq, r_gq, out32, out16, r_out, W, st, r_st):
    k = self.k
    k.op('dve', I("memset", st[:, 0:1], 0.0), writes=[r_st])
    k.op('act', I("activation", out=W["junk"][:, 0:n], in_=src_ps, func=AF.Square, accum_out=st[:, 0:1]), reads=[r_src, r_st], writes=[W["r_junk"], r_st])
    k.op('act', I("activation", out=st[:, 1:2], in_=st[:, 0:1], func=AF.Ln, scale=1.0 / n, bias=EPS), reads=[r_st], writes=[r_st])
    k.op('act', I("activation", out=st[:, 2:3], in_=st[:, 1:2], func=AF.Exp, scale=-0.5), reads=[r_st], writes=[r_st])
    if out32 is not None:
        k.op('dve', I("scalar_tensor_tensor", out=out32, in0=src_ps, scalar=st[:, 2:3], in1=gq, op0=ALU.mult, op1=ALU.mult), reads=[r_src, r_st, r_gq], writes=[r_out])
        k.op('pool', I("tensor_copy", out=out16, in_=out32), reads=[r_out], writes=[r_out])
    else:
        k.op('dve', I("scalar_tensor_tensor", out=out16, in0=src_ps, scalar=st[:, 2:3], in1=gq, op0=ALU.mult, op1=ALU.mult), reads=[r_src, r_st, r_gq], writes=[r_out])


def mla_inproj(self, Xin, r_Xin):
    k, ar = self.k, self.ar
    self.phase()
    _mla_scratch(self)
    m0 = ar.mark()
    l = 1
    W = self.work_bufs()
    w_in = self.din["w_in_odd"]
    wb = ar.alloc([16, 1088], BF16); r_wb = self.R("o1w", sw=True)
    for (c0, c1) in ((0, 512), (512, 1024), (1024, 1088)):
        k.dma('pool', wb[:, :, c0:c1], w_in[:, c0:c1].rearrange("(k p) n -> p k n", p=128), writes=[r_wb])
    gq = ar.alloc([512]); r_gq = self.R("gq", sem=True)
    gkv = ar.alloc([512]); r_gkv = self.R("gkv", sem=True)
    k.dma('sp', gq, self.din["mla_qg"].to_broadcast([128, 512]), writes=[r_gq])
    k.dma('sp', gkv, self.din["mla_kvg"].to_broadcast([128, 512]), writes=[r_gkv])
    ropeP32 = ar.alloc([64], parts=64); r_ropeP32 = self.R("ropeP32", sem=True)
    ropeP = ar.alloc([64], BF16, parts=64); r_ropeP = self.R("ropeP")
    k.dma('sp', ropeP32, self.din["ropeP"], writes=[r_ropeP32])
    k.op('dve', I("tensor_copy", out=ropeP, in_=ropeP32), reads=[r_ropeP32], writes=[r_ropeP])
    cs = ar.alloc([2, 128], parts=64)
    r_cs = self.R("ropecs", sem=True)
    xt = [ar.alloc([D_MODEL]) for _ in range(2)]; r_xt = [self.R(f"o1x{i}", sem=True) for i in range(2)]
    hT = [ar.alloc([16, 128], BF16) for _ in range(2)]; r_hT = [self.R(f"o1h{i}") for i in range(2)]
    st = ar.alloc([8]); r_st = self.R("o1st")
    cq16 = [ar.alloc([512], BF16) for _ in range(2)]; r_cq16 = [self.R(f"cq16{i}") for i in range(2)]
    kv32 = [ar.alloc([512]) for _ in range(2)]; r_kv32 = [self.R(f"kv32{i}", sem=True) for i in range(2)]
    kv16 = [ar.alloc([512], BF16) for _ in range(2)]
    kp32 = [ar.alloc([64]) for _ in range(2)]; r_kp32 = [self.R(f"kp32{i}", sem=True) for i in range(2)]
    kp16 = [ar.alloc([64], BF16) for _ in range(2)]
    tq = [ar.alloc([4, 128], BF16) for _ in range(2)]; r_tq = [self.R(f"tq{i}", sem=True) for i in range(2)]
    tkv = [ar.alloc([4, 128], BF16) for _ in range(2)]; r_tkv = [self.R(f"tkv{i}", sem=True) for i in range(2)]
    tkp = [ar.alloc([128], BF16, parts=64) for _ in range(2)]; r_tkp = [self.R(f"tkp{i}", sem=True) for i in range(2)]
    trot = [ar.alloc([128], BF16, parts=64) for _ in range(2)]; r_trot = [self.R(f"trot{i}", sem=True) for i in range(2)]
    t1 = ar.alloc([128], parts=64); r_t1 = self.R("ropet1")
    t2 = ar.alloc([128], parts=64); r_t2 = self.R("ropet2")
    AB, r_AB = self.AB[l][0], self.r_AB[l][0]
    nckv, nkpe = self.dout["nckv"], self.dout["nkpe"]
    for tile in range(28):
        b = tile % 2
        own = tile < 24
        if own:
            g = 0 if tile < 8 else 1
            k.dma('sp', xt[b], Xin[tile * 128:(tile + 1) * 128, :], reads=[r_Xin[tile]], writes=[r_xt[b]])
            self.pre_tile(xt[b], r_xt[b], hT[b], r_hT[b], 0, AB, r_AB, g, W)
            for n, (c0, c1) in enumerate(((0, 512), (512, 1024), (1024, 1088))):
                for kk in range(16):
                    k.op('pe', I("matmul", self.ps[:, n, 0:c1 - c0], lhsT=hT[b][:, kk, :], rhs=wb[:, kk, c0:c1], start=(kk == 0), stop=(kk == 15)),
                         reads=[r_hT[b], r_wb], writes=[self.psr[n]])
            _rmsnorm_free(self, self.ps[:, 0, :], self.psr[0], 512, gq, r_gq, None, cq16[b], r_cq16[b], W, st, r_st)
            _rmsnorm_free(self, self.ps[:, 1, :], self.psr[1], 512, gkv, r_gkv, kv32[b], kv16[b], r_kv32[b], W, st, r_st)
            k.op('act', I("copy", out=kp32[b], in_=self.ps[:, 2, 0:64]), reads=[self.psr[2]], writes=[r_kp32[b]])
            k.op('pool', I("tensor_copy", out=kp16[b], in_=kp32[b]), reads=[r_kp32[b]], writes=[r_kp32[b]])
            if tile < 8:
                self.store(nckv[tile * 128:(tile + 1) * 128, :], kv32[b], r_kv32[b], reads=[r_kv32[b]], writes=[], is_output=True)
                self.store(nkpe[tile * 128:(tile + 1) * 128, :], kp32[b], r_kp32[b], reads=[r_kp32[b]], writes=[], is_output=True)
        else:
            ct = tile - 24
            k.dma('sp', kv32[b], self.din["cckv"][ct * 128:(ct + 1) * 128, :], writes=[r_kv32[b]])
            k.dma('sp', kp32[b], self.din["ckpe"][ct * 128:(ct + 1) * 128, :], writes=[r_kp32[b]])
            k.op('pool', I("tensor_copy", out=kv16[b], in_=kv32[b]), reads=[r_kv32[b]], writes=[r_kv32[b]])
            k.op('pool', I("tensor_copy", out=kp16[b], in_=kp32[b]), reads=[r_kp32[b]], writes=[r_kp32[b]])
        if own:
            p3 = self.psb16(3)
            for c in range(4):
                k.op('pe', I("transpose", out=p3[:, c * 128:(c + 1) * 128], in_=cq16[b][:, c * 128:(c + 1) * 128], identity=self.ident_b), reads=[r_cq16[b], self.r_ident_b], writes=[self.psr[3]])
            k.op('act', I("copy", out=tq[b].rearrange("p a b -> p (a b)"), in_=p3[:, 0:512]), reads=[self.psr[3]], writes=[r_tq[b]])
            self.store(self.CQT[:, :, tile * 128:(tile + 1) * 128].rearrange("c p t -> p c t"), tq[b], r_tq[b], reads=[r_tq[b]], writes=[self.r_O1])
        p4 = self.psb16(4)
        for c in range(4):
            k.op('pe', I("transpose", out=p4[:, c * 128:(c + 1) * 128], in_=kv16[b][:, c * 128:(c + 1) * 128], identity=self.ident_b), reads=[r_kv32[b], self.r_ident_b], writes=[self.psr[4]])
        k.op('act', I("copy", out=tkv[b].rearrange("p a b -> p (a b)"), in_=p4[:, 0:512]), reads=[self.psr[4]], writes=[r_tkv[b]])
        self.store(self.CKVT[:, :, tile * 128:(tile + 1) * 128].rearrange("c p t -> p c t"), tkv[b], r_tkv[b], reads=[r_tkv[b]], writes=[self.r_O1])
        p5 = self.psb16(5)
        k.op('pe', I("transpose", out=p5[0:64, 0:128], in_=kp16[b], identity=self.ident_b), reads=[r_kp32[b], self.r_ident_b], writes=[self.psr[5]])
        k.op('act', I("copy", out=tkp[b], in_=p5[0:64, 0:128]), reads=[self.psr[5]], writes=[r_tkp[b]])
        self.store(self.KPET[:, tile * 128:(tile + 1) * 128], tkp[b], r_tkp[b], reads=[r_tkp[b]], writes=[self.r_O1])
        if own and tile >= 8:
            lt = tile - 8
            k.dma('sp', cs, self.din["ropecs"][:, :, lt * 128:(lt + 1) * 128], writes=[r_cs])
            k.op('pe', I("matmul", self.ps[0:64, 6, 0:128], lhsT=ropeP, rhs=tkp[b], start=True, stop=True), reads=[r_ropeP, r_tkp[b]], writes=[self.psr[6]])
            k.op('dve', I("tensor_tensor", out=t1, in0=self.ps[0:64, 6, 0:128], in1=cs[:, 1, :], op=ALU.mult), reads=[self.psr[6], r_cs], writes=[r_t1])
            k.op('pool', I("tensor_tensor", out=t2, in0=tkp[b], in1=cs[:, 0, :], op=ALU.mult), reads=[r_tkp[b], r_cs], writes=[r_t2])
            k.op('pool', I("tensor_tensor", out=trot[b], in0=t1, in1=t2, op=ALU.add), reads=[r_t1, r_t2], writes=[r_trot[b]])
            self.store(self.KROT[:, lt * 128:(lt + 1) * 128], trot[b], r_trot[b], reads=[r_trot[b]], writes=[self.r_O1])
    k.barrier()
    ar.release(m0)


def mla_proj(self):
    k, ar = self.k, self.ar
    self.phase()
    _mla_scratch(self)
    m0 = ar.mark()
    wq = ar.alloc([4, 3072], BF16); r_wq = self.R("wq", sw=True)
    wkv = ar.alloc([4, 4096], BF16); r_wkv = self.R("wkv", sw=True)
    for n in range(6):
        k.dma('pool', wq[:, :, n * 512:(n + 1) * 512], self.din["w_uq"][:, n * 512:(n + 1) * 512].rearrange("(k p) n -> p k n", p=128), writes=[r_wq])
    for n in range(8):
        k.dma('pool', wkv[:, :, n * 512:(n + 1) * 512], self.din["w_ukv"][:, n * 512:(n + 1) * 512].rearrange("(k p) n -> p k n", p=128), writes=[r_wkv])
    cqT = ar.alloc([4, NT], BF16); r_cqT = self.R("cqT", sem=True)
    ckvT = ar.alloc([4, NTK], BF16); r_ckvT = self.R("ckvT", sem=True)
    k.dma('sp', cqT, self.CQT.rearrange("c p t -> p c t"), reads=[self.r_O1], writes=[r_cqT])
    k.dma('sp', ckvT, self.CKVT.rearrange("c p t -> p c t"), reads=[self.r_O1], writes=[r_ckvT])
    ropeP32 = ar.alloc([64], parts=64); r_ropeP32 = self.R("ropeP32b", sem=True)
    ropeP = ar.alloc([64], BF16, parts=64); r_ropeP = self.R("ropePb")
    k.dma('sp', ropeP32, self.din["ropeP"], writes=[r_ropeP32])
    k.op('dve', I("tensor_copy", out=ropeP, in_=ropeP32), reads=[r_ropeP32], writes=[r_ropeP])
    cs = ar.alloc([2, 2048], parts=64); r_cs = self.R("ropecs2", sem=True)
    k.dma('sp', cs, self.din["ropecs"], writes=[r_cs])
    NST = 4
    stg = [ar.alloc([512], BF16) for _ in range(NST)]; r_stg = [self.R(f"o2s{i}", sem=True) for i in range(NST)]
    t1 = ar.alloc([512], parts=64); r_t1 = self.R("o2t1")
    t2 = ar.alloc([512], parts=64); r_t2 = self.R("o2t2")
    si = 0; pbi = 0
    wqv = wq.rearrange("p k (h d) -> p k h d", h=16)
    wkvv = wkv.rearrange("p k (h d) -> p k h d", h=16)
    for h in range(16):
        for tb in range(6):
            ts = slice(tb * 512, (tb + 1) * 512)
            pb = pbi % 3; pbi += 1
            for kk in range(4):
                k.op('pe', I("matmul", self.psb(pb), lhsT=wqv[:, kk, h, 0:128], rhs=cqT[:, kk, ts], start=(kk == 0), stop=(kk == 3)), reads=[r_wq, r_cqT], writes=[self.psr[pb]])
            s_ = si % NST; si += 1
            k.op('act', I("activation", out=stg[s_], in_=self.psb(pb), func=AF.Copy, scale=MLA_SCALE), reads=[self.psr[pb]], writes=[r_stg[s_]])
            k.dma('sp', self.QN[h, :, ts], stg[s_], reads=[r_stg[s_]], writes=[self.r_O2])
            pb = pbi % 3; pbi += 1
            for kk in range(4):
                k.op('pe', I("matmul", self.ps[0:64, pb, :], lhsT=wqv[:, kk, h, 128:192], rhs=cqT[:, kk, ts], start=(kk == 0), stop=(kk == 3)), reads=[r_wq, r_cqT], writes=[self.psr[pb]])
            s_ = si % NST; si += 1
            qpe = stg[s_][0:64, :]
            k.op('act', I("activation", out=qpe, in_=self.ps[0:64, pb, :], func=AF.Copy, scale=MLA_SCALE), reads=[self.psr[pb]], writes=[r_stg[s_]])
            k.dma('sp', self.QPE[h, :, ts], qpe, reads=[r_stg[s_]], writes=[self.r_O2])
            if tb >= 2:
                lt = tb - 2
                ls = slice(lt * 512, (lt + 1) * 512)
                k.op('pe', I("matmul", self.ps[0:64, 6, :], lhsT=ropeP, rhs=qpe, start=True, stop=True), reads=[r_ropeP, r_stg[s_]], writes=[self.psr[6]])
                k.op('dve', I("tensor_tensor", out=t1, in0=self.ps[0:64, 6, :], in1=cs[:, 1, ls], op=ALU.mult), reads=[self.psr[6], r_cs], writes=[r_t1])
                k.op('pool', I("tensor_tensor", out=t2, in0=qpe, in1=cs[:, 0, ls], op=ALU.mult), reads=[r_stg[s_], r_cs], writes=[r_t2])
                s2 = si % NST; si += 1
                qrot = stg[s2][0:64, :]
                k.op('pool', I("tensor_tensor", out=qrot, in0=t1, in1=t2, op=ALU.add), reads=[r_t1, r_t2], writes=[r_stg[s2]])
                k.dma('sp', self.QROT[h, :, ls], qrot, reads=[r_stg[s2]], writes=[self.r_O2])
        for tb in range(7):
            ts = slice(tb * 512, (tb + 1) * 512)
            pb = pbi % 3; pbi += 1
            for kk in range(4):
                k.op('pe', I("matmul", self.psb(pb), lhsT=wkvv[:, kk, h, 0:128], rhs=ckvT[:, kk, ts], start=(kk == 0), stop=(kk == 3)), reads=[r_wkv, r_ckvT], writes=[self.psr[pb]])
            s_ = si % NST; si += 1
            k.op('act', I("copy", out=stg[s_], in_=self.psb(pb)), reads=[self.psr[pb]], writes=[r_stg[s_]])
            k.dma('sp', self.KN[h, :, ts], stg[s_], reads=[r_stg[s_]], writes=[self.r_O2])
    for t in range(28):
        for hg in range(4):
            pb = 3 + (pbi % 2); pbi += 1
            for kk in range(4):
                k.op('pe', I("matmul", self.psb(pb).rearrange("p (h d) -> p h d", h=4), lhsT=ckvT[:, kk, t * 128:(t + 1) * 128], rhs=wkvv[:, kk, hg * 4:(hg + 1) * 4, 128:256], start=(kk == 0), stop=(kk == 3)),
                     reads=[r_wkv, r_ckvT], writes=[self.psr[pb]])
            s_ = si % NST; si += 1
            k.op('act', I("copy", out=stg[s_], in_=self.psb(pb)), reads=[self.psr[pb]], writes=[r_stg[s_]])
            k.dma('sp', self.VM[t * 128:(t + 1) * 128, hg * 4:(hg + 1) * 4, :], stg[s_].rearrange("p (h d) -> p h d", h=4), reads=[r_stg[s_]], writes=[self.r_O2])
    k.barrier()
    ar.release(m0)


def mla_attn(self):
    k, ar = self.k, self.ar
    self.phase()
    _mla_scratch(self)
    m0 = ar.mark()
    kpeT_f = ar.alloc([NTK], BF16); r_kpeT = self.R("kpeTall", sem=True)
    krot_f = ar.alloc([2048], BF16); r_krot = self.R("krotall", sem=True)
    k.op('dve', I("memset", kpeT_f[64:128], 0.0), writes=[r_kpeT])
    k.op('dve', I("memset", krot_f[64:128], 0.0), writes=[r_krot])
    k.dma('sp', kpeT_f[0:64], self.KPET, reads=[self.r_O1], writes=[r_kpeT])
    k.dma('sp', krot_f[0:64], self.KROT, reads=[self.r_O1], writes=[r_krot])
    kpeT, krot = kpeT_f, krot_f
    QN = [ar.alloc([NT], BF16) for _ in range(2)]; r_QN = [self.R(f"aQN{i}", sem=True) for i in range(2)]
    QP = [ar.alloc([NT], BF16) for _ in range(2)]; r_QP = [self.R(f"aQP{i}", sem=True) for i in range(2)]
    QR = [ar.alloc([2048], BF16) for _ in range(2)]; r_QR = [self.R(f"aQR{i}", sem=True) for i in range(2)]
    for i in range(2):
        k.op('dve', I("memset", QP[i][64:128], 0.0), writes=[r_QP[i]])
        k.op('dve', I("memset", QR[i][64:128], 0.0), writes=[r_QR[i]])
    KN = [ar.alloc([NTK], BF16) for _ in range(2)]; r_KN = [self.R(f"aKN{i}", sem=True) for i in range(2)]
    va = [ar.alloc([28, 129], BF16) for _ in range(2)]; r_va = [self.R(f"aV{i}", sem=True) for i in range(2)]
    for i in range(2):
        k.op('pool', I("memset", va[i][:, :, 128:129], 1.0), writes=[r_va[i]])
    PT = [ar.alloc([512], BF16) for _ in range(5)]; r_PT = [self.R(f"aPT{i}") for i in range(5)]
    rec = [ar.alloc([1]) for _ in range(2)]; r_rec = [self.R(f"arec{i}") for i in range(2)]
    ob16 = [ar.alloc([128], BF16) for _ in range(2)]; r_ob16 = [self.R(f"aob{i}") for i in range(2)]
    ostg = [ar.alloc([512], BF16) for _ in range(2)]; r_ostg = [self.R(f"aost{i}", sem=True) for i in range(2)]
    Pacc = [ar.alloc([512]) for _ in range(2)]; r_Pacc = [self.R(f"aPacc{i}") for i in range(2)]
    recb = [ar.alloc([512]) for _ in range(2)]; r_recb = [self.R(f"arecb{i}") for i in range(2)]
    cnt = {"p": 0, "s": 0, "o": 0, "f": 0}
    def load_head(h):
        hi = h % 2
        k.dma('sp', QN[hi], self.QN[h], reads=[self.r_O2], writes=[r_QN[hi]])
        k.dma('sp', QP[hi][0:64], self.QPE[h], reads=[self.r_O2], writes=[r_QP[hi]])
        k.dma('sp', QR[hi][0:64], self.QROT[h], reads=[self.r_O2], writes=[r_QR[hi]])
        k.dma('sp', KN[hi], self.KN[h], reads=[self.r_O2], writes=[r_KN[hi]])
        k.dma('sp', va[hi][:, :, 0:128], self.VM[:, h, :].rearrange("(c p) d -> p c d", p=128), reads=[self.r_O2], writes=[r_va[hi]])

    flat = []
    jobinfo = {}
    jid = 0
    for h in range(16):
        hi = h % 2
        jobs = []
        for sq_ in range(4):
            t0 = sq_ * 256
            keys = [(t0 // 128 + c, kpeT[:, t0 + c * 128:t0 + (c + 1) * 128], QP[hi][:, t0:t0 + 256]) for c in range(2)]
            jobs.append((t0, 256, keys))
        for qb in range(4):
            q0 = 1024 + qb * 512
            lq = slice(qb * 512, (qb + 1) * 512)
            keys = [(8 + c, krot[:, c * 128:(c + 1) * 128], QR[hi][:, lq]) for c in range(16)]
            keys += [(24 + c, kpeT[:, NT + c * 128:NT + (c + 1) * 128], QP[hi][:, q0:q0 + 512]) for c in range(4)]
            jobs.append((q0, 512, keys))
        for (q0, nq, keys) in jobs:
            for n_, (kt, kr, qr) in enumerate(keys):
                flat.append((h, jid, q0, nq, n_, len(keys), kt, kr, qr))
            jid += 1
    NF = len(flat)
    sbs = [(0, 1, 7)[i % 3] for i in range(NF)]
    pis = [i % 5 for i in range(NF)]
    loaded = set()

    def S_ops(i):
        (h, jid_, q0, nq, n_, NK, kt, kr, qr) = flat[i]
        hi = h % 2
        if h not in loaded:
            load_head(h); loaded.add(h)
        sb_ = sbs[i]
        k.op('pe', I("matmul", self.ps[:, sb_, 0:nq], lhsT=KN[hi][:, kt * 128:(kt + 1) * 128], rhs=QN[hi][:, q0:q0 + nq], start=True, stop=False),
             reads=[r_KN[hi], r_QN[hi]], writes=[self.psr[sb_]])
        k.op('pe', I("matmul", self.ps[:, sb_, 0:nq], lhsT=kr, rhs=qr, start=False, stop=True),
             reads=[r_kpeT, r_krot, r_QP[hi], r_QR[hi]], writes=[self.psr[sb_]])

    load_head(0); loaded.add(0)
    S_ops(0); S_ops(1)
    for i in range(NF):
        (h, jid_, q0, nq, n_, NK, kt, kr, qr) = flat[i]
        hi = h % 2
        ji = jid_ % 2
        po = self.ps[:, 2 + ji, 0:nq]
        r_po = self.psr[2 + ji]
        if n_ == 0 and h + 1 < 16 and (h + 1) not in loaded and q0 == 256:
            load_head(h + 1); loaded.add(h + 1)
        if i + 2 < NF:
            S_ops(i + 2)
        sb_, pi = sbs[i], pis[i]
        k.op('act', I("activation", out=PT[pi][:, 0:nq], in_=self.ps[:, sb_, 0:nq], func=AF.Exp), reads=[self.psr[sb_]], writes=[r_PT[pi]])
        k.op('pe', I("matmul", po, lhsT=va[hi][:, kt, 0:128], rhs=PT[pi][:, 0:nq], start=(n_ == 0), stop=(n_ == NK - 1)),
             reads=[r_PT[pi], r_va[hi]], writes=[r_po])
        if n_ == 0:
            k.op('dve', I("tensor_copy", out=Pacc[ji][:, 0:nq], in_=PT[pi][:, 0:nq]), reads=[r_PT[pi]], writes=[r_Pacc[ji]])
        else:
            k.op('dve', I("tensor_tensor", out=Pacc[ji][:, 0:nq], in0=Pacc[ji][:, 0:nq], in1=PT[pi][:, 0:nq], op=ALU.add), reads=[r_Pacc[ji], r_PT[pi]], writes=[r_Pacc[ji]])
        if n_ == NK - 1:
            pden = self.ps[:, 4 + ji, 0:nq]
            k.op('pe', I("matmul", pden, lhsT=self.ones_f, rhs=Pacc[ji][:, 0:nq], start=True, stop=True), reads=[self.r_ones_f, r_Pacc[ji]], writes=[self.psr[4 + ji]])
            k.op('act', I("activation", out=recb[ji][:, 0:nq], in_=pden, func=AF.Ln), reads=[self.psr[4 + ji]], writes=[r_recb[ji]])
            k.op('act', I("activation", out=recb[ji][:, 0:nq], in_=recb[ji][:, 0:nq], func=AF.Exp, scale=-1.0), reads=[r_recb[ji]], writes=[r_recb[ji]])
            k.op('dve', I("tensor_tensor", out=ostg[ji][:, 0:nq], in0=po, in1=recb[ji][:, 0:nq], op=ALU.mult), reads=[r_po, r_recb[ji]], writes=[r_ostg[ji]])
            self.store(self.OA2[h, :, q0:q0 + nq], ostg[ji][:, 0:nq], r_ostg[ji], reads=[r_ostg[ji]], writes=[self.r_OA2], is_output=bool(self.cfg.get("dump_oa2")))
    k.barrier()
    ar.release(m0)


Builder.mla_inproj = mla_inproj
Builder.mla_proj = mla_proj
Builder.mla_attn = mla_attn


def _host_consts():
    cm = np.ones((128, 1024), np.float32)
    t = np.arange(512)
    cm[:, 0:512][:, t % 64 == 0] = 0
    cm[:, 512:][:, t % 64 == 63] = 0
    s = np.arange(64)[:, None]
    tt = np.arange(64)[None, :]
    tri = np.concatenate([(s <= tt), (s >= tt)], 1).astype(np.float32)
    col = np.arange(64)
    cs_ = np.clip(col - 8, 0, 48)
    ok = (col[None, :] >= cs_[:, None]) & (col[None, :] < cs_[:, None] + 16)
    m = ok.T.astype(np.float32)
    colmask = np.concatenate([m, m], 0)
    P = np.zeros((64, 64), np.float32)
    for base in (0, 32):
        for i in range(16):
            P[base + i, base + i + 16] = -1.0
            P[base + i + 16, base + i] = 1.0
    ropeP = np.ascontiguousarray(P.T)
    tq = np.arange(2048)
    inv = np.power(np.float32(10000.0), -np.arange(0, 32, 2, dtype=np.float32) / np.float32(32)).astype(np.float32)
    ang_r = (tq // 64).astype(np.float32)[:, None] * inv
    ang_c = (tq % 64).astype(np.float32)[:, None] * inv
    ang = np.concatenate([ang_r, ang_r, ang_c, ang_c], 1).T
    ropecs = np.ascontiguousarray(np.stack([np.cos(ang), np.sin(ang)], 1).astype(np.float32))
    return {"ident": np.eye(128, dtype=np.float32), "cmask": cm, "trimask": tri, "colmask": colmask, "ropeP": ropeP, "ropecs": ropecs}


def build_program(cfg=None):
    B = Builder(cfg or {})
    B.inp("cvec", [2, 2048]); B.inp("ada_w", [2, 2048, 12288]); B.inp("ada_b", [2, 12288]); B.inp("norm_g", [2, 4, 2048])
    B.inp("w_in_even", [2048, 8192]); B.inp("cmask", [128, 1024]); B.inp("trimask", [64, 128]); B.inp("hgn", [1, 128]); B.inp("lbrows", [2, 3, 1024])
    B.inp("st_f", [8, 128, 128]); B.inp("st_b", [8, 128, 128]); B.inp("rpbr", [8, 15, 8192]); B.inp("colmask", [128, 64])
    B.inp("cnak", [512, 8, 128]); B.inp("cnav", [512, 8, 128]); B.inp("w_out_even", [2048, 2048])
    B.inp("mlp_w1", [2, 2048, 8192]); B.inp("mlp_w2", [2, 8192, 2048])
    B.inp("w_in_odd", [2048, 1088]); B.inp("mla_qg", [1, 512]); B.inp("mla_kvg", [1, 512]); B.inp("w_uq", [512, 3072]); B.inp("w_ukv", [512, 4096]); B.inp("w_out_odd", [2048, 2048])
    B.inp("cckv", [512, 512]); B.inp("ckpe", [512, 64]); B.inp("ropeP", [64, 64]); B.inp("ropecs", [64, 2, 2048])
    X0 = B.inp("xin", [NT, 2048])
    X4 = B.out("yout", [NT, 2048])
    B.out("nsf", [4, 8, 128, 128]); B.out("nsb", [4, 8, 128, 128]); B.out("nak", [1024, 1024]); B.out("nav", [1024, 1024])
    B.out("nckv", [1024, 512]); B.out("nkpe", [1024, 64])
    X1 = B.scr("X1", [NT, 2048]); X2 = B.scr("X2", [NT, 2048]); X3 = B.scr("X3", [NT, 2048])
    rX = [[B.R(f"X{i}_{t}") for t in range(24)] for i in range(5)]
    B.consts()
    B.modulation(0)
    B.modulation(1)
    B.even_inproj(X0, rX[0])
    B.hgrn_phase()
    B.na_phase()
    B.outproj_phase(0, B.OA, B.r_OA, B.din["w_out_even"], X0, rX[0], X1, rX[1])
    B.mlp_phase(0, X1, rX[1], X2, rX[2])
    B.mla_inproj(X2, rX[2])
    B.mla_proj()
    B.mla_attn()
    B.outproj_phase(1, B.OA2, B.r_OA2, B.din["w_out_odd"], X2, rX[2], X3, rX[3])
    B.mlp_phase(1, X3, rX[3], X4, rX[4], out_is_output=True)
    B.k.emit()
    return B


def kernel(x_prompt, x_sample, state_hgrn_fwd, state_hgrn_bwd, cache_na_k, cache_na_v, cache_mla_ckv, cache_mla_kpe,
           c, c_ctx, ada_w, ada_b, norm_g, hgrn_lb_fwd, hgrn_lb_bwd, w_in_even, hgrn_norm_g, na_rpb, w_out_even, w_in_odd,
           mla_q_norm_g, w_uq, mla_kv_norm_g, w_ukv, w_out_odd, mlp_w1, mlp_w2):
    f32 = lambda a: np.ascontiguousarray(np.asarray(a, dtype=np.float32))
    NCORE = 8
    B = build_program()
    hc = _host_consts()
    rpb = f32(na_rpb)[0]
    rp = np.zeros((8, 15, 128), np.float32)
    rp[:, :, 48:79] = rpb[:, :, ::-1]
    rpbr = np.ascontiguousarray(np.broadcast_to(rp[:, :, None, :], (8, 15, 64, 128))).reshape(8, 15, 8192)
    shared = {
        "ada_w": f32(ada_w), "ada_b": f32(ada_b), "norm_g": f32(norm_g), "w_in_even": f32(w_in_even)[0],
        "hgn": f32(hgrn_norm_g)[0:1], "lbrows": np.ascontiguousarray(np.stack([f32(hgrn_lb_fwd), f32(hgrn_lb_bwd)], 0)),
        "rpbr": rpbr, "w_out_even": f32(w_out_even)[0], "mlp_w1": f32(mlp_w1), "mlp_w2": f32(mlp_w2),
        "w_in_odd": f32(w_in_odd)[0], "mla_qg": f32(mla_q_norm_g)[0:1], "mla_kvg": f32(mla_kv_norm_g)[0:1],
        "w_uq": f32(w_uq)[0], "w_ukv": f32(w_ukv)[0], "w_out_odd": f32(w_out_odd)[0],
    }
    shared.update(hc)
    xp, xs = f32(x_prompt), f32(x_sample)
    sf, sb_ = f32(state_hgrn_fwd), f32(state_hgrn_bwd)
    cnk, cnv = f32(cache_na_k), f32(cache_na_v)
    cck, ckp = f32(cache_mla_ckv), f32(cache_mla_kpe)
    cc, cctx = f32(c), f32(c_ctx)
    in_maps = []
    for i in range(NCORE):
        m = dict(shared)
        m["xin"] = np.ascontiguousarray(np.concatenate([xp[4 * i:4 * i + 4].reshape(1024, 2048), xs[i]], 0))
        m["cvec"] = np.ascontiguousarray(np.stack([cctx, cc[i]], 0))
        m["st_f"] = np.ascontiguousarray(sf[i, 0]); m["st_b"] = np.ascontiguousarray(sb_[i, 0])
        m["cnak"] = np.ascontiguousarray(cnk[i, 0]); m["cnav"] = np.ascontiguousarray(cnv[i, 0])
        m["cckv"] = np.ascontiguousarray(cck[i, 0]); m["ckpe"] = np.ascontiguousarray(ckp[i, 0])
        in_maps.append(m)
    res = run_bass_kernel_spmd(B.nc, in_maps, core_ids=list(range(NCORE)))
    R_ = res.results
    y_prompt = np.concatenate([np.asarray(r["yout"])[:1024].reshape(4, 256, 2048) for r in R_], 0).astype(np.float32)
    y_sample = np.stack([np.asarray(r["yout"])[1024:] for r in R_], 0).astype(np.float32)
    nsf = np.concatenate([np.asarray(r["nsf"]).reshape(4, 1, 8, 128, 128) for r in R_], 0).astype(np.float32)
    nsb = np.concatenate([np.asarray(r["nsb"]).reshape(4, 1, 8, 128, 128) for r in R_], 0).astype(np.float32)
    nak = np.concatenate([np.asarray(r["nak"]).reshape(4, 1, 256, 8, 128) for r in R_], 0).astype(np.float32)
    nav = np.concatenate([np.asarray(r["nav"]).reshape(4, 1, 256, 8, 128) for r in R_], 0).astype(np.float32)
    nckv = np.concatenate([np.asarray(r["nckv"]).reshape(4, 1, 256, 512) for r in R_], 0).astype(np.float32)
    nkpe = np.concatenate([np.asarray(r["nkpe"]).reshape(4, 1, 256, 64) for r in R_], 0).astype(np.float32)
    return (y_prompt, y_sample, nsf, nsb, nak, nav, nckv, nkpe)
```
